# Optimizing a Trainium2 kernel written in Bass

```python
import jax, jax.numpy as jnp
from jax import lax
import numpy as np

D_MODEL = 1024
BATCH = 16
SEQ = 2048
DEPTH = 4

SB_HEADS = 8
SB_HEAD_DIM = 64
SB_WIDTH = SB_HEADS * SB_HEAD_DIM
SB_BLOCK = 128
DN_HEADS = 4
DN_HEAD_DIM = 128
DN_WIDTH = DN_HEADS * DN_HEAD_DIM
DN_CHUNK = 64
CONV_WIDTH = 4
MIX_WIDTH = SB_WIDTH + DN_WIDTH
IN_WIDTH = 3 * SB_WIDTH + 4 * DN_WIDTH + 2 * DN_HEADS
D_FF = 4 * D_MODEL
N_MOD = 6
EPS = 1e-6

kernel_name = "stickbreak_gdn_hybrid_adaln"


def rmsnorm(x, gain):
    xf = x.astype(jnp.float32)
    y = xf * lax.rsqrt(jnp.mean(xf * xf, axis=-1, keepdims=True) + EPS)
    return (y * gain.astype(jnp.float32)).astype(x.dtype)


def l2norm(x):
    return x * lax.rsqrt(jnp.sum(x * x, axis=-1, keepdims=True) + EPS)


def causal_depthwise_conv(x, w):
    k_w, ch = w.shape
    return lax.conv_general_dilated(
        x, w[:, None, :].astype(x.dtype), window_strides=(1,), padding=((k_w - 1, 0),),
        dimension_numbers=("NWC", "WIO", "NWC"), feature_group_count=ch)


def stick_breaking_attention(q, k, v):
    t_len = q.shape[2]
    scale = SB_HEAD_DIM ** -0.5
    outs = []
    for blk in range(t_len // SB_BLOCK):
        start = blk * SB_BLOCK
        end = start + SB_BLOCK
        qb = q[:, :, start:end]
        kb = k[:, :, :end]
        vb = v[:, :, :end]
        z = jnp.einsum("bhqd,bhkd->bhqk", qb, kb, preferred_element_type=jnp.float32) * scale
        t_idx = start + jnp.arange(SB_BLOCK)[:, None]
        s_idx = jnp.arange(end)[None, :]
        causal = s_idx < t_idx
        log_1m_beta = jnp.where(causal, jax.nn.log_sigmoid(-z), 0.0)
        suffix = lax.cumsum(log_1m_beta, axis=3, reverse=True) - log_1m_beta
        att = jnp.where(causal, jnp.exp(jax.nn.log_sigmoid(z) + suffix), 0.0)
        outs.append(jnp.einsum("bhqk,bhkd->bhqd", att.astype(vb.dtype), vb))
    return jnp.concatenate(outs, axis=2)


def chunk_gated_delta_rule(q, k, v, g, beta):
    b, h, t_len, dk = q.shape
    dv = v.shape[-1]
    n = t_len // DN_CHUNK
    c = DN_CHUNK
    q = q * dk ** -0.5
    qc = q.reshape(b, h, n, c, dk)
    kc = k.reshape(b, h, n, c, dk)
    vc = v.reshape(b, h, n, c, dv)
    bc = beta.reshape(b, h, n, c)
    gc = jnp.cumsum(g.reshape(b, h, n, c), axis=-1)
    tril_incl = jnp.tril(jnp.ones((c, c), dtype=bool))
    tril_strict = jnp.tril(jnp.ones((c, c), dtype=bool), -1)
    diff = jnp.where(tril_incl, gc[..., :, None] - gc[..., None, :], 0.0)
    decay = jnp.where(tril_incl, jnp.exp(diff), 0.0)
    k_beta = kc * bc[..., None]
    v_beta = vc * bc[..., None]
    lower = jnp.where(tril_strict, jnp.einsum("bhnid,bhnjd->bhnij", k_beta, kc) * decay, 0.0)
    eye = jnp.eye(c, dtype=jnp.float32)
    a_mat = lower + eye
    t_mat = lax.linalg.triangular_solve(a_mat, jnp.broadcast_to(eye, a_mat.shape),
                                        left_side=True, lower=True, unit_diagonal=True)
    u = jnp.einsum("bhnij,bhnje->bhnie", t_mat, v_beta)
    w = jnp.einsum("bhnij,bhnjd->bhnid", t_mat, k_beta * jnp.exp(gc)[..., None])
    intra = jnp.where(tril_incl, jnp.einsum("bhnid,bhnjd->bhnij", qc, kc) * decay, 0.0)

    def step(state, inp):
        q_n, k_n, u_n, w_n, g_n, a_n = inp
        v_new = u_n - jnp.einsum("bhcd,bhde->bhce", w_n, state)
        o_n = (jnp.einsum("bhcd,bhde->bhce", q_n * jnp.exp(g_n)[..., None], state)
               + jnp.einsum("bhij,bhje->bhie", a_n, v_new))
        g_last = g_n[..., -1:]
        state = (state * jnp.exp(g_last)[..., None]
                 + jnp.einsum("bhcd,bhce->bhde", k_n * jnp.exp(g_last - g_n)[..., None], v_new))
        return state, o_n

    xs = tuple(jnp.moveaxis(a, 2, 0) for a in (qc, kc, u, w, gc, intra))
    state0 = jnp.zeros((b, h, dk, dv), jnp.float32)
    _, o = lax.scan(step, state0, xs)
    return jnp.moveaxis(o, 0, 2).reshape(b, h, t_len, dv)


def hybrid_mixer(h, w_in, sb_q_norm, sb_k_norm, conv_w, a_log, dt_bias, dn_out_norm, w_out):
    b, t_len, _ = h.shape
    proj = h @ w_in
    cuts = [int(i) for i in np.cumsum([SB_WIDTH, SB_WIDTH, SB_WIDTH, 3 * DN_WIDTH, DN_WIDTH, DN_HEADS])]
    sb_q, sb_k, sb_v, dn_qkv, dn_z, dn_a, dn_b = jnp.split(proj, cuts, axis=-1)

    def sb_heads(a):
        return a.reshape(b, t_len, SB_HEADS, SB_HEAD_DIM)
    q_a = rmsnorm(sb_heads(sb_q), sb_q_norm).transpose(0, 2, 1, 3)
    k_a = rmsnorm(sb_heads(sb_k), sb_k_norm).transpose(0, 2, 1, 3)
    v_a = sb_heads(sb_v).transpose(0, 2, 1, 3)
    o_a = stick_breaking_attention(q_a, k_a, v_a)
    o_a = o_a.transpose(0, 2, 1, 3).reshape(b, t_len, SB_WIDTH)

    qkv = jax.nn.silu(causal_depthwise_conv(dn_qkv, conv_w)).astype(jnp.float32)
    q_b, k_b, v_b = jnp.split(qkv, 3, axis=-1)
    def dn_heads(a):
        return a.reshape(b, t_len, DN_HEADS, DN_HEAD_DIM).transpose(0, 2, 1, 3)
    q_b = l2norm(dn_heads(q_b))
    k_b = l2norm(dn_heads(k_b))
    v_b = dn_heads(v_b)
    beta = jax.nn.sigmoid(dn_b.astype(jnp.float32)).transpose(0, 2, 1)
    g = (-jnp.exp(a_log.astype(jnp.float32))
         * jax.nn.softplus(dn_a.astype(jnp.float32) + dt_bias.astype(jnp.float32))).transpose(0, 2, 1)
    o_b = chunk_gated_delta_rule(q_b, k_b, v_b, g, beta)
    o_b = o_b.transpose(0, 2, 1, 3).astype(h.dtype)
    z = dn_z.reshape(b, t_len, DN_HEADS, DN_HEAD_DIM)
    o_b = (rmsnorm(o_b, dn_out_norm) * jax.nn.silu(z)).reshape(b, t_len, DN_WIDTH)

    return jnp.concatenate([o_a, o_b], axis=-1) @ w_out


def setup_inputs(seed: int = 0) -> dict:
    key = jax.random.key(seed)
    ks = jax.random.split(key, 20)
    f32 = jnp.float32
    x = jax.random.normal(ks[0], (BATCH, SEQ, D_MODEL), f32)
    c = jax.random.normal(ks[1], (BATCH, D_MODEL), f32)
    w_ada = jax.random.normal(ks[2], (DEPTH, D_MODEL, N_MOD * D_MODEL), f32) * (0.5 * D_MODEL ** -0.5)
    b_ada = jax.random.normal(ks[3], (DEPTH, N_MOD * D_MODEL), f32) * 0.01
    norm_mix = 1.0 + 0.02 * jax.random.normal(ks[4], (DEPTH, D_MODEL), f32)
    norm_mlp = 1.0 + 0.02 * jax.random.normal(ks[5], (DEPTH, D_MODEL), f32)
    w_in = jax.random.normal(ks[6], (DEPTH, D_MODEL, IN_WIDTH), f32) * D_MODEL ** -0.5
    sb_q_norm = 1.0 + 0.02 * jax.random.normal(ks[7], (DEPTH, SB_HEAD_DIM), f32)
    sb_k_norm = 1.0 + 0.02 * jax.random.normal(ks[8], (DEPTH, SB_HEAD_DIM), f32)
    conv_w = jax.random.normal(ks[9], (DEPTH, CONV_WIDTH, 3 * DN_WIDTH), f32) * CONV_WIDTH ** -0.5
    a_log = jnp.log(jax.random.uniform(ks[10], (DEPTH, DN_HEADS), f32, 1.0, 16.0))
    dt = jnp.exp(jax.random.uniform(ks[11], (DEPTH, DN_HEADS), f32, np.log(1e-3), np.log(1e-1)))
    dt_bias = dt + jnp.log(-jnp.expm1(-dt))
    dn_out_norm = 1.0 + 0.02 * jax.random.normal(ks[12], (DEPTH, DN_HEAD_DIM), f32)
    w_out = jax.random.normal(ks[13], (DEPTH, MIX_WIDTH, D_MODEL), f32) * MIX_WIDTH ** -0.5
    w_ff1 = jax.random.normal(ks[14], (DEPTH, D_MODEL, D_FF), f32) * D_MODEL ** -0.5
    w_ff2 = jax.random.normal(ks[15], (DEPTH, D_FF, D_MODEL), f32) * D_FF ** -0.5
    return {"x": x, "c": c, "w_ada": w_ada, "b_ada": b_ada, "norm_mix": norm_mix,
            "norm_mlp": norm_mlp, "w_in": w_in, "sb_q_norm": sb_q_norm, "sb_k_norm": sb_k_norm,
            "conv_w": conv_w, "a_log": a_log, "dt_bias": dt_bias, "dn_out_norm": dn_out_norm,
            "w_out": w_out, "w_ff1": w_ff1, "w_ff2": w_ff2}


def reference(x, c, w_ada, b_ada, norm_mix, norm_mlp, w_in, sb_q_norm, sb_k_norm,
              conv_w, a_log, dt_bias, dn_out_norm, w_out, w_ff1, w_ff2):
    cond = jax.nn.silu(c)
    for l in range(DEPTH):
        mod = cond @ w_ada[l] + b_ada[l]
        sh_a, sc_a, g_a, sh_m, sc_m, g_m = [m[:, None, :] for m in jnp.split(mod, N_MOD, axis=-1)]
        h = rmsnorm(x, norm_mix[l]) * (1.0 + sc_a) + sh_a
        x = x + g_a * hybrid_mixer(h, w_in[l], sb_q_norm[l], sb_k_norm[l], conv_w[l],
                                   a_log[l], dt_bias[l], dn_out_norm[l], w_out[l])
        h = rmsnorm(x, norm_mlp[l]) * (1.0 + sc_m) + sh_m
        x = x + g_m * (jnp.square(jax.nn.relu(h @ w_ff1[l])) @ w_ff2[l])
    return x
```

```python
import math
import numpy as np
import concourse.bass as bass
import concourse.mybir as mybir
from concourse.bass_utils import run_bass_kernel_spmd

F32 = mybir.dt.float32
BF16 = mybir.dt.bfloat16
AF = mybir.ActivationFunctionType
ALU = mybir.AluOpType

D = 1024
NCH = 8
DEPTH = 4
SEQ = 2048
NCORES = 8
SB_H, SB_D = 8, 64
DN_H, DN_D = 4, 128
IN_W = 3592
DFF = 4096
EPS = 1e-6
ENGS = ("pe", "act", "dve", "pool", "sp")


class Tile:
    __slots__ = ("name", "lw", "rd", "sem", "semcnt", "psum")

    def __init__(self, name):
        self.name = name
        self.psum = False
        self.lw = None
        self.rd = []
        self.sem = None
        self.semcnt = 0


class Op:
    __slots__ = ("eng", "fn", "deps", "signal", "sigval", "dma", "dsem", "dval")

    def __init__(self, eng, fn):
        self.eng = eng
        self.fn = fn
        self.deps = []
        self.signal = False
        self.sigval = 0
        self.dma = False
        self.dsem = None
        self.dval = 0


class Sched:
    def __init__(self, nc):
        self.nc = nc
        self.ops = {e: [] for e in ENGS}
        self.ndsem = 0

    def _add(self, op, reads, writes):
        deps = []
        for t in reads:
            if t.lw is not None:
                deps.append(t.lw)
        for t in writes:
            if t.lw is not None:
                deps.append(t.lw)
            deps.extend(t.rd)
        for t in reads:
            t.rd.append(op)
        for t in writes:
            t.lw = op
            t.rd = []
        seen = set()
        for d in deps:
            if d is op or id(d) in seen:
                continue
            seen.add(id(d))
            if (not d.dma) and (not op.dma) and d.eng == "pe" and op.eng == "pe":
                continue
            op.deps.append(d)
            d.signal = True
        self.ops[op.eng].append(op)
        return op

    def op(self, eng, fn, reads=(), writes=()):
        if getattr(self, "cap", None) is not None:
            self.cap.append((eng, fn, list(reads), list(writes)))
            return None
        reads, writes = list(reads), list(writes)
        pr = [t for t in reads if t.psum]
        if pr:
            reads = [t for t in reads if not t.psum]
            writes = writes + [t for t in pr if t not in writes]
        return self._add(Op(eng, fn), reads, writes)

    def dma(self, eng, out, in_, reads=(), writes=()):
        o = Op(eng, None)
        o.dma = True
        tiles = list(writes) + list(reads)
        st = tiles[0]
        if st.sem is None:
            st.sem = self.nc.alloc_semaphore(f"dsem{self.ndsem}")
            self.ndsem += 1
        st.semcnt += 16
        o.dsem = st.sem
        o.dval = st.semcnt
        o.signal = True
        o.fn = lambda e, out=out, in_=in_: e.dma_start(out=out, in_=in_)
        return self._add(o, [], tiles)

    def capture(self, f):
        assert getattr(self, "cap", None) is None
        self.cap = []
        f()
        lst, self.cap = self.cap, None
        return lst

    def merge(self, *lists):
        idx = [0] * len(lists)
        total = sum(len(x) for x in lists)
        for _ in range(total):
            best, bf = None, None
            for k, lst in enumerate(lists):
                if idx[k] < len(lst):
                    fr = idx[k] / len(lst)
                    if bf is None or fr < bf:
                        best, bf = k, fr
            eng, fn, reads, writes = lists[best][idx[best]]
            idx[best] += 1
            self.op(eng, fn, reads, writes)

    def emit(self, final_wait_ops=()):
        nc = self.nc
        esem = {e: nc.alloc_semaphore(f"sem_{e}") for e in ENGS}
        for e in ENGS:
            c = 0
            for o in self.ops[e]:
                if (not o.dma) and o.signal:
                    c += 1
                    o.sigval = c
        stats = {}

        def run(e, eng):
            known = {}
            nw = 0
            for o in self.ops[e]:
                for d in o.deps:
                    if d.dma:
                        key, val = d.dsem, d.dval
                    else:
                        key, val = esem[d.eng], d.sigval
                    if known.get(key.num, 0) >= val:
                        continue
                    known[key.num] = val
                    eng.wait_ge(key, val)
                    nw += 1
                ins = o.fn(eng)
                if o.dma:
                    ins.then_inc(o.dsem, 16)
                elif o.signal:
                    ins.then_inc(esem[e], 1)
            if e == "sp":
                for o in final_wait_ops:
                    if o.dma:
                        eng.wait_ge(o.dsem, o.dval)
                    else:
                        eng.wait_ge(esem[o.eng], o.sigval)
            stats[e] = (len(self.ops[e]), nw)

        with nc.Block() as block:
            @block.tensor
            def _(eng):
                run("pe", eng)

            @block.scalar
            def _(eng):
                run("act", eng)

            @block.vector
            def _(eng):
                run("dve", eng)

            @block.gpsimd
            def _(eng):
                run("pool", eng)

            @block.sync
            def _(eng):
                run("sp", eng)
        return stats


class Buf:
    def __init__(self, t, name, grid=None):
        self.t = t
        if grid is None:
            self.T = Tile(name)
        else:
            self.T = np.empty(grid, dtype=object)
            for idx in np.ndindex(*grid):
                self.T[idx] = Tile(f"{name}{idx}")

    def tl(self, *idx):
        if isinstance(self.T, Tile):
            return [self.T]
        sub = self.T[idx] if idx else self.T
        if isinstance(sub, Tile):
            return [sub]
        return list(sub.ravel())


def _consts():
    i = np.arange(128)
    same = (i[:, None] // 64) == (i[None, :] // 64)
    c = {}
    c["ident"] = np.eye(128, dtype=np.float32)
    c["ones"] = np.ones((128, 128), np.float32)
    blk = np.zeros((128, 128), np.float32)
    blk[:64, :64] = 1.0
    blk[64:, 64:] = 1.0
    c["blk64"] = blk
    c["tril"] = (i[:, None] >= i[None, :]).astype(np.float32)
    c["mincl"] = ((i[:, None] <= i[None, :]) & same).astype(np.float32)
    c["mrev"] = ((i[:, None] > i[None, :]) & same).astype(np.float32)
    c["mbI"] = np.where((i[None, :] <= i[:, None]) & same, 0.0, 30000.0).astype(np.float32)
    c["strict"] = (i[None, :] < i[:, None]).astype(np.float32)
    b16 = (i[:, None] // 16) == (i[None, :] // 16)
    b32 = (i[:, None] // 32) == (i[None, :] // 32)
    low = i[None, :] < i[:, None]
    c["m16"] = (b16 & low).astype(np.float32)
    c["m32"] = (b32 & ~b16 & low).astype(np.float32)
    c["m64"] = (same & ~b32 & low).astype(np.float32)
    c["m16T"] = np.ascontiguousarray(c["m16"].T)
    c["m32T"] = np.ascontiguousarray(c["m32"].T)
    n32 = ["ident", "mincl", "mrev", "mbI", "strict", "m16", "m32", "m64", "m16T", "m32T"]
    n16 = ["ident", "ones", "blk64", "tril"]
    return (n32, np.concatenate([c[n] for n in n32], axis=1), n16, np.concatenate([c[n] for n in n16], axis=1))


class WStream:
    def __init__(self, bld, bufs, srcs, depth):
        self.b, self.bufs, self.srcs, self.depth = bld, bufs, list(srcs), depth
        assert len(bufs) > depth
        self.n = 0
        self.issued = 0
        self.live = []
        for _ in range(min(depth, len(self.srcs))):
            self._issue()

    def _issue(self):
        buf = self.bufs[self.issued % len(self.bufs)]
        self.b.S.dma("pool", buf.t[:], self.srcs[self.issued], writes=buf.tl())
        self.live.append(buf)
        self.issued += 1

    def next(self):
        buf = self.live[self.n]
        self.n += 1
        if self.issued < len(self.srcs):
            self._issue()
        return buf


AR_EL = 29184


class Builder:
    def __init__(self, T=SEQ, L=DEPTH, NSEQ=2, dbg=(), stop=None):
        self.stop = stop
        self.Tn = T
        self.L = L
        self.NSEQ = NSEQ
        self.NT = T // 512
        self.NB = T // 128
        self.dbg_names = dbg
        nc = self.nc = bass.Bass("TRN2", target_bir_lowering=False)
        self.S = Sched(nc)
        self.cnt = {}
        self.final = []
        self.phase = {}
        self.build()

    def sb(self, name, shape, dt, grid=None):
        return Buf(self.nc.alloc_sbuf_tensor("s_" + name, list(shape), dt), name, grid)

    def av(self, phase, name, shape, dt, grid=None, base=None):
        ph = self.phase.setdefault(phase, {"off": 0, "tiles": []})
        nel = int(np.prod(shape[1:]))
        nbf = nel * (2 if dt == F32 else 1)
        off = (ph["off"] + 15) // 16 * 16
        ph["off"] = off + nbf
        assert ph["off"] <= AR_EL, (phase, name, ph["off"])
        ap = self.arena[:, off:off + nbf]
        if dt == F32:
            ap = ap.bitcast(F32)
        if len(shape) == 3:
            ap = ap.rearrange("p (a b) -> p a b", a=shape[1], b=shape[2])
        if shape[0] < 128:
            ap = ap[0:shape[0]]
        b = Buf(ap, name, grid)
        ph["tiles"].extend(b.tl())
        return b

    def ring(self, lst, key):
        v = self.cnt.get(key, 0)
        self.cnt[key] = v + 1
        return lst[v % len(lst)]

    def barrier(self, *phases, extra=()):
        tiles = list(extra)
        for p in phases:
            tiles.extend(self.phase[p]["tiles"])
        self.S.op("sp", lambda e: e.nop(), writes=tiles)

    def din(self, name, shape, dt=F32):
        return self.nc.dram_tensor(name, list(shape), dt, kind="ExternalInput").ap()

    def dump(self, name, ap, tiles, shape, dt=F32):
        if name not in self.dbg_names:
            return
        d = self.nc.dram_tensor("dbg_" + name, list(shape), dt, kind="ExternalOutput").ap()
        self.final.append(self.S.dma("sp", d, ap, reads=tiles))

    def build(self):
        nc, S, T, L, NSEQ, NT, NB = self.nc, self.S, self.Tn, self.L, self.NSEQ, self.NT, self.NB
        d_xT = self.din("xT", [NSEQ, 128, NCH, T])
        d_cT = self.din("cT", [128, NCH, 2])
        d_wada = self.din("w_ada", [L, 128, NCH, 6 * D])
        d_bada = self.din("b_ada", [128, L, 48])
        d_nmix = self.din("norm_mix", [128, L, NCH])
        d_nmlp = self.din("norm_mlp", [128, L, NCH])
        d_win = self.din("w_in", [L, 128, NCH, IN_W])
        d_sbq = self.din("sb_q_norm", [128, L])
        d_sbk = self.din("sb_k_norm", [128, L])
        d_conv = self.din("conv_w", [128, L, 12, 4])
        d_gate = self.din("gate_p", [128, L, 8])
        d_dno = self.din("dn_out_norm", [128, L])
        d_wout = self.din("w_out", [L, 128, NCH, D])
        d_ff1 = self.din("w_ff1", [L, 128, NCH, DFF])
        d_ff2 = self.din("w_ff2", [L, 128, 32, D])
        n32, c32m, n16, c16m = _consts()
        d_c32 = self.din("consts32", [128, c32m.shape[1]])
        d_c16 = self.din("consts16", [128, c16m.shape[1]])
        d_out = nc.dram_tensor("outT", [NSEQ, 128, NCH, T], F32, kind="ExternalOutput").ap()
        self.d = dict(win=d_win, wout=d_wout, ff1=d_ff1, ff2=d_ff2)

        self.arena = nc.alloc_sbuf_tensor("arena", [128, AR_EL], BF16)

        c32 = self.sb("c32", [128, c32m.shape[1]], F32)
        c16 = self.sb("c16", [128, c16m.shape[1]], BF16)
        S.dma("sp", c32.t[:], d_c32, writes=c32.tl())
        S.dma("pool", c16.t[:], d_c16, writes=c16.tl())
        self.c32, self.c16 = c32, c16
        self.C32 = lambda n: c32.t[:, n32.index(n) * 128:(n32.index(n) + 1) * 128]
        self.C16 = lambda n: c16.t[:, n16.index(n) * 128:(n16.index(n) + 1) * 128]

        prm = self.sb("prm", [128, 512], F32)
        PT = prm.tl()
        self.prm, self.PT = prm, PT
        o = [0]

        def pslot(n):
            r = (o[0], o[0] + n)
            o[0] += n
            return r
        s_bada, s_nmix, s_nmlp = pslot(L * 48), pslot(L * NCH), pslot(L * NCH)
        s_sbq, s_sbk, s_conv, s_dno, s_gate = pslot(L), pslot(L), pslot(L * 48), pslot(L), pslot(L * 8)
        assert o[0] <= 512

        def pv(s, *shape):
            ap = prm.t[:, s[0]:s[1]]
            if len(shape) == 2:
                ap = ap.rearrange("p (a b) -> p a b", a=shape[0], b=shape[1])
            elif len(shape) == 3:
                ap = ap.rearrange("p (a b c) -> p a b c", a=shape[0], b=shape[1], c=shape[2])
            return ap
        self.pv = pv
        self.slots = dict(sbq=s_sbq, sbk=s_sbk, conv=s_conv, dno=s_dno, gate=s_gate)
        S.dma("sp", pv(s_bada, L, 48), d_bada, writes=PT)
        S.dma("sp", pv(s_nmix, L, NCH), d_nmix, writes=PT)
        S.dma("sp", pv(s_nmlp, L, NCH), d_nmlp, writes=PT)
        S.dma("sp", prm.t[:, s_sbq[0]:s_sbq[1]], d_sbq, writes=PT)
        S.dma("sp", prm.t[:, s_sbk[0]:s_sbk[1]], d_sbk, writes=PT)
        S.dma("sp", pv(s_conv, L, 12, 4), d_conv, writes=PT)
        S.dma("sp", prm.t[:, s_dno[0]:s_dno[1]], d_dno, writes=PT)
        S.dma("sp", pv(s_gate, L, 8), d_gate, writes=PT)
        nexpA = self.sb("nexpA", [128, L, 4], F32)
        self.nexpA = nexpA
        S.op("act", lambda e: e.activation(nexpA.t[:], pv(s_gate, L, 8)[:, :, 0:4], AF.Exp), reads=PT, writes=nexpA.tl())
        S.op("dve", lambda e: e.tensor_scalar(nexpA.t[:], nexpA.t[:], -1.0, None, ALU.mult), reads=nexpA.tl(), writes=nexpA.tl())

        self.PS = PS = [Buf(nc.alloc_psum_tensor(f"ps{i}", [128, 512], F32), f"ps{i}") for i in range(8)]
        for b in PS:
            b.T.psum = True

        cT = self.sb("cT", [128, NCH, 2], F32)
        ctmp = self.sb("ctmp", [128, NCH, 2], F32)
        cond = self.sb("cond", [128, NCH, 2], BF16)
        S.dma("sp", cT.t[:], d_cT, writes=cT.tl())
        S.op("act", lambda e: e.activation(ctmp.t[:], cT.t[:], AF.Exp, scale=-1.0), reads=cT.tl(), writes=ctmp.tl())
        S.op("dve", lambda e: e.tensor_scalar(ctmp.t[:], ctmp.t[:], 1.0, None, ALU.add), reads=ctmp.tl(), writes=ctmp.tl())
        S.op("dve", lambda e: e.reciprocal(ctmp.t[:], ctmp.t[:]), reads=ctmp.tl(), writes=ctmp.tl())
        S.op("dve", lambda e: e.tensor_tensor(cond.t[:], cT.t[:], ctmp.t[:], ALU.mult), reads=cT.tl() + ctmp.tl(), writes=cond.tl())
        mod = self.sb("mod", [128, L, 48, 2], F32)
        self.mod = mod
        wada = [self.av("setup", f"wada{i}", [128, NCH, 512], BF16) for i in range(2)]

        def ada_piece(l, pc):
            wb = self.ring(wada, "wada")
            S.dma("pool", wb.t[:], d_wada[l, :, :, pc * 512:(pc + 1) * 512], writes=wb.tl())
            for jj in range(4):
                j = pc * 4 + jj
                for kc in range(NCH):
                    S.op("pe", lambda e, jj=jj, kc=kc, j=j: e.matmul(
                        PS[0].t[:, j * 2:j * 2 + 2], wb.t[:, kc, jj * 128:(jj + 1) * 128], cond.t[:, kc, :],
                        start=(kc == 0), stop=(kc == NCH - 1)), reads=wb.tl() + cond.tl(), writes=PS[0].tl())

        def ada_evac(l, b):
            S.op("dve", lambda e: e.tensor_tensor(
                mod.t[:, l, :, b], PS[0].t[:, 0:96].rearrange("p (j b) -> p j b", b=2)[:, :, b],
                pv(s_bada, L, 48)[:, l, :], ALU.add), reads=PS[0].tl() + PT, writes=mod.tl())
        for l in range(L):
            for pc in range(12):
                ada_piece(l, pc)
            for b in range(2):
                ada_evac(l, b)
        gains = self.sb("gains", [128, L, 2, NCH, 2], F32)
        self.gains = gains

        def gain_op(l, which, sl, m, b):
            S.op("dve", lambda e: e.scalar_tensor_tensor(
                gains.t[:, l, which, :, b], mod.t[:, l, m * 8:(m + 1) * 8, b], 1.0, pv(sl, L, NCH)[:, l, :], ALU.add, ALU.mult),
                reads=mod.tl() + PT, writes=gains.tl())
        for l in range(L):
            for which, (sl, m) in enumerate(((s_nmix, 1), (s_nmlp, 4))):
                for b in range(2):
                    gain_op(l, which, sl, m, b)
        self.dump("mod", mod.t[:], mod.tl(), [128, L, 48, 2])

        self.xT = self.sb("xT", [128, NCH, T], F32, grid=(NCH, NT))
        self.hT = self.sb("hT", [128, NCH, T], BF16, grid=(NCH, NT))
        self.oT = self.sb("oT", [128, NCH, T], BF16, grid=(NCH, NT))
        self.sqb = [self.sb(f"sqb{i}", [128, 512], BF16) for i in range(2)]
        self.wring = [self.sb(f"wring{i}", [128, NCH, 128], BF16) for i in range(4)]
        self.alloc_phases()

        xT = self.xT
        self.cur = "setup"
        for s in range(NSEQ):
            for c in range(NCH):
                S.dma("sp", xT.t[:, c, :], d_xT[s, :, c, :], writes=xT.tl(c))
            for l in range(L):
                self.layer(l, s)
            for c in range(NCH):
                self.final.append(S.dma("sp", d_out[s, :, c, :], xT.t[:, c, :], reads=xT.tl(c)))
        self.stats = S.emit(final_wait_ops=self.final)

    def switch(self, new, extra=()):
        self.barrier(self.cur, new, extra=extra)
        self.cur = new

    def alloc_phases(self):
        T, NB, NT = self.Tn, self.NB, self.NT
        av = self.av
        self.qa = av("sb", "qa", [128, 2, T], BF16, grid=(2, NT))
        self.ka = av("sb", "ka", [128, 2, T], BF16, grid=(2, NT))
        self.va = av("sb", "va", [128, NB, 256], BF16, grid=(NB,))
        self.wv = av("sb", "wv", [128, NCH, 256], BF16)
        self.a_e = [av("sb", f"a_e{i}", [128, 512], F32) for i in range(3)]
        self.a_x = [av("sb", f"a_x{i}", [128, 512], F32) for i in range(2)]
        self.a_sp = [av("sb", f"a_sp{i}", [128, 512], BF16) for i in range(2)]
        self.a_att = [av("sb", f"a_att{i}", [128, 512], BF16) for i in range(2)]
        self.a_R = [av("sb", f"a_R{i}", [1, 512], BF16) for i in range(2)]
        self.raw32 = [av("sb", f"raw32_{i}", [128, 512], F32) for i in range(2)]
        g = "gdn"
        self.raw = av(g, "raw", [128, T + 4], F32)
        self.CH = min(T, 1024)
        self.cacc = av(g, "cacc", [128, self.CH], F32)
        self.tA = av(g, "tA", [128, self.CH], F32)
        self.kT = av(g, "kT", [128, T], BF16)
        self.qT = av(g, "qT", [128, T], BF16)
        self.vT = av(g, "vT", [128, T], BF16)
        self.zs = av(g, "zs", [128, T], BF16)
        self.gtok = av(g, "gtok", [128, NB, 8], F32)
        self.gcrc = av(g, "gcrc", [128, NB, 8], F32)
        self.gsc = av(g, "gsc", [128, NB, 8], F32)
        self.wgate = av(g, "wgate", [128, NCH, 8], BF16)

        def mk(n, k, dt):
            return [av(g, f"{n}{i}", [128, 128], dt) for i in range(k)]
        self.g_ebc, self.g_tI, self.g_DL, self.g_DLs = mk("ebc", 2, F32), mk("tI", 2, F32), mk("DL", 2, F32), mk("DLs", 2, F32)
        self.g_L32, self.g_Lc, self.g_Uc = mk("L32", 2, F32), mk("Lc", 3, F32), mk("Uc", 3, F32)
        self.g_Xc, self.g_Yc = mk("Xc", 3, F32), mk("Yc", 3, F32)
        self.g_C32, self.g_C32T, self.g_C64 = mk("C32", 2, BF16), mk("C32T", 2, BF16), mk("C64", 2, BF16)
        self.g_T16T, self.g_T16, self.g_Ya, self.g_Yb = mk("T16T", 2, BF16), mk("T16", 2, BF16), mk("Ya", 2, BF16), mk("Yb", 2, BF16)
        self.g_T32T, self.g_T32, self.g_Yd, self.g_TT = mk("T32T", 2, BF16), mk("T32", 2, BF16), mk("Yd", 2, BF16), mk("TT", 2, BF16)
        self.g_A, self.g_AT, self.g_kbg, self.g_kd = mk("A", 2, BF16), mk("AT", 2, BF16), mk("kbg", 2, BF16), mk("kd", 2, BF16)
        self.g_vb, self.g_u, self.g_wT = mk("vb", 2, BF16), mk("u", 2, F32), mk("wT", 2, BF16)
        self.g_qg, self.g_vn = mk("qg", 2, BF16), mk("vn", 2, BF16)
        self.S32 = av(g, "S32", [128, 128], F32)
        self.Sbf = av(g, "Sbf", [128, 128], BF16)
        self.wo = [av("op", f"wo{i}", [128, NCH, 512], BF16) for i in range(2)]
        self.HT = min(1024, T)
        tph = self.HT // 512
        self.hidA = av("mlp", "hidA", [128, 16, self.HT], BF16, grid=(16, tph))
        self.w2 = [av("mlp", f"w2_{i}", [128, 32, 128], BF16) for i in range(2)]
        self.relu = [av("mlp", f"relu{i}", [128, 512], F32) for i in range(2)]
        if T == SEQ:
            ap = self.oT.t[:].rearrange("p c t -> p (c t)").rearrange("p (a b) -> p a b", a=16, b=self.HT)
            self.hidB = Buf(ap, "hidB", grid=(16, tph))
        else:
            self.hidB = self.sb("hidB", [128, 16, self.HT], BF16, grid=(16, tph))
        self.phase["mlp"]["tiles"].extend(self.hidB.tl())

    def hid(self, j):
        return (self.hidA, j) if j < 16 else (self.hidB, j - 16)

    def mmps(self):
        return self.ring([self.PS[0], self.PS[1]], "mmps")


    def norm_to_hT(self, l, s, which):
        S, NT, PS = self.S, self.NT, self.PS
        xT, hT, gains, mod = self.xT, self.hT, self.gains, self.mod
        msh = 0 if which == 0 else 3
        for tt in range(NT):
            ts = slice(tt * 512, (tt + 1) * 512)
            ps = self.mmps()
            for c in range(NCH):
                self._sq_mm(xT.t[:, c, ts], xT.tl(c, tt), ps, "ones", c == 0, c == NCH - 1, eng=("act" if c % 2 else "pool"))
            S.op("act", lambda e, ps=ps: e.activation(ps.t[:, :], ps.t[:, :], AF.Ln, bias=EPS, scale=1.0 / D), reads=ps.tl(), writes=ps.tl())
            S.op("act", lambda e, ps=ps: e.activation(ps.t[:, :], ps.t[:, :], AF.Exp, scale=-0.5), reads=ps.tl(), writes=ps.tl())
            for c in range(NCH):
                tmp = PS[2 + c % 2]
                S.op("dve", lambda e, c=c, ts=ts, ps=ps, tmp=tmp: e.tensor_tensor(tmp.t[:, :], xT.t[:, c, ts], ps.t[:, :], ALU.mult),
                     reads=xT.tl(c, tt) + ps.tl(), writes=tmp.tl())
                S.op("dve", lambda e, c=c, ts=ts, tmp=tmp: e.tensor_scalar(
                    hT.t[:, c, ts], tmp.t[:, :], gains.t[:, l, which, c, s:s + 1], mod.t[:, l, msh * 8 + c, s:s + 1], ALU.mult, ALU.add),
                    reads=tmp.tl() + gains.tl() + mod.tl(), writes=hT.tl(c, tt))

    def _sq_mm(self, src_ap, src_tiles, ps, ones_name, start, stop, eng="pool"):
        S = self.S
        sq = self.ring(self.sqb, "sqb")
        if eng == "act":
            S.op("act", lambda e: e.activation(sq.t[:], src_ap, AF.Square), reads=src_tiles, writes=sq.tl())
        else:
            S.op("pool", lambda e: e.tensor_tensor(sq.t[:], src_ap, src_ap, ALU.mult), reads=src_tiles, writes=sq.tl())
        S.op("pe", lambda e: e.matmul(ps.t[:, :], self.C16(ones_name), sq.t[:], start=start, stop=stop),
             reads=sq.tl() + self.c16.tl(), writes=ps.tl())

    def group_norm(self, src_ap, src_tiles, ones_name, nfeat, gain_ap, out_ap, out_tiles, bias2=0.0):
        S = self.S
        ps = self.mmps()
        self._sq_mm(src_ap, src_tiles, ps, ones_name, True, True, eng="act")
        sc = 1.0 if nfeat is None else 1.0 / nfeat
        S.op("act", lambda e: e.activation(ps.t[:, :], ps.t[:, :], AF.Ln, bias=EPS, scale=sc), reads=ps.tl(), writes=ps.tl())
        S.op("act", lambda e: e.activation(ps.t[:, :], ps.t[:, :], AF.Exp, scale=-0.5, bias=bias2), reads=ps.tl(), writes=ps.tl())
        if gain_ap is None:
            S.op("dve", lambda e: e.tensor_tensor(out_ap, src_ap, ps.t[:, :], ALU.mult), reads=src_tiles + ps.tl(), writes=out_tiles)
        else:
            S.op("dve", lambda e: e.scalar_tensor_tensor(out_ap, src_ap, gain_ap, ps.t[:, :], ALU.mult, ALU.mult),
                 reads=src_tiles + ps.tl() + self.PT, writes=out_tiles)

    def proj_chunk(self, l, col0, evac):
        S, hT = self.S, self.hT
        assert self.win_cols[self.win_stream.n] == col0
        wb = self.win_stream.next()
        for tt in range(self.NT):
            ts = slice(tt * 512, (tt + 1) * 512)
            ps = self.mmps()
            for kc in range(NCH):
                S.op("pe", lambda e, ps=ps, kc=kc, ts=ts: e.matmul(ps.t[:, :], wb.t[:, kc, :], hT.t[:, kc, ts],
                                                                  start=(kc == 0), stop=(kc == NCH - 1)),
                     reads=wb.tl() + hT.tl(kc, tt), writes=ps.tl())
            evac(tt, ps)

    def layer(self, l, s):
        T = self.Tn
        cols = []
        for half in range(2):
            for which in range(2):
                for cc in range(2):
                    cols.append(which * 512 + (half * 2 + cc) * 128)
        for hd in range(DN_H):
            for base in (1536, 2048, 2560, 3072):
                cols.append(base + hd * 128)
        self.win_cols = cols
        self.win_stream = WStream(self, self.wring, [self.d["win"][l, :, :, c0:c0 + 128] for c0 in cols], 3)
        if self.stop == "setup":
            return
        self.norm_to_hT(l, s, 0)
        self.dump(f"h1_{l}", self.hT.t[:], self.hT.tl(), [128, NCH, T], BF16)
        if self.stop == "norm":
            return
        self.switch("sb", extra=self.oT.tl())
        for half in range(2):
            self.sb_proj(l, half)
            if self.stop == "sbproj":
                return
            self.sb_attn(l, half)
        if self.stop == "attn":
            return
        self.switch("gdn")
        self.gdn(l)
        self.dump(f"oT_{l}", self.oT.t[:], self.oT.tl(), [128, NCH, T], BF16)
        if self.stop == "gdn":
            return
        self.switch("op")
        self.out_proj(l, s)
        self.dump(f"x1_{l}", self.xT.t[:], self.xT.tl(), [128, NCH, T])
        self.norm_to_hT(l, s, 1)
        self.switch("mlp", extra=self.oT.tl())
        self.mlp(l, s)
        self.dump(f"x2_{l}", self.xT.t[:], self.xT.tl(), [128, NCH, T])

    def sb_proj(self, l, half):
        S, NT, NB, hT = self.S, self.NT, self.NB, self.hT
        for which, dst, slot in ((0, self.qa, self.slots["sbq"]), (1, self.ka, self.slots["sbk"])):
            for cc in range(2):
                def evac(tt, ps, cc=cc, dst=dst, slot=slot):
                    r = self.raw32[tt % 2]
                    S.op("act", lambda e: e.activation(r.t[:], ps.t[:, :], AF.Copy), reads=ps.tl(), writes=r.tl())
                    self.group_norm(r.t[:], r.tl(), "blk64", SB_D, self.prm.t[:, slot[0] + l:slot[0] + l + 1],
                                    dst.t[:, cc, tt * 512:(tt + 1) * 512], dst.tl(cc, tt))
                self.proj_chunk(l, which * 512 + (half * 2 + cc) * 128, evac)
        wv, va = self.wv, self.va
        S.dma("pool", wv.t[:], self.d["win"][l, :, :, 1024 + half * 256:1024 + (half + 1) * 256], writes=wv.tl())

        def vblk(blk):
            ps = self.mmps()
            tt = blk // 4
            for kc in range(NCH):
                S.op("pe", lambda e, kc=kc: e.matmul(ps.t[:, 0:256], hT.t[:, kc, blk * 128:(blk + 1) * 128], wv.t[:, kc, :],
                                                     start=(kc == 0), stop=(kc == NCH - 1)),
                     reads=wv.tl() + hT.tl(kc, tt), writes=ps.tl())
            if blk % 2 == 0:
                S.op("dve", lambda e: e.tensor_copy(va.t[:, blk, :], ps.t[:, 0:256]), reads=ps.tl(), writes=va.tl(blk))
            else:
                S.op("act", lambda e: e.activation(va.t[:, blk, :], ps.t[:, 0:256], AF.Copy), reads=ps.tl(), writes=va.tl(blk))
        for blk in range(NB):
            vblk(blk)
        self.dump(f"qa_{l}_{half}", self.qa.t[:], self.qa.tl(), [128, 2, self.Tn], BF16)
        self.dump(f"ka_{l}_{half}", self.ka.t[:], self.ka.tl(), [128, 2, self.Tn], BF16)
        self.dump(f"va_{l}_{half}", self.va.t[:], self.va.tl(), [128, NB, 256], BF16)

    def sb_attn(self, l, half):
        S, NT = self.S, self.NT
        PS, C16 = self.PS, self.C16
        qa, ka, va, oT = self.qa, self.ka, self.va, self.oT
        c16t = self.c16.tl()
        scale = SB_D ** -0.5
        items = []
        for hh in range(4):
            for qt in range(NT):
                for kb in range(4 * qt + 3, -1, -1):
                    items.append((hh, qt, kb))
        n_it = len(items)
        grp = {}
        g = -1
        for n, it in enumerate(items):
            if n == 0 or items[n - 1][:2] != it[:2]:
                g += 1
            grp[n] = g

        def geom(n):
            hh, qt, kb = items[n]
            i = kb - 4 * qt
            c0 = 128 * i if i > 0 else 0
            return hh, qt, kb, i, c0, 512 - c0

        def s1(n):
            hh, qt, kb, i, c0, w = geom(n)
            cc, p0 = hh // 2, (hh % 2) * 64
            zp = PS[2 + n % 2]
            S.op("pe", lambda e: e.matmul(zp.t[:, 0:w], ka.t[p0:p0 + 64, cc, kb * 128:(kb + 1) * 128],
                                          qa.t[p0:p0 + 64, cc, qt * 512 + c0:(qt + 1) * 512], start=True, stop=True),
                 reads=ka.tl(cc, kb // 4) + qa.tl(cc, qt), writes=zp.tl())

        def s2(n):
            hh, qt, kb, i, c0, w = geom(n)
            zp = PS[2 + n % 2]
            eb, sp = self.a_e[n % 3], self.a_sp[n % 2]
            S.op("act", lambda e: e.activation(eb.t[:, 0:w], zp.t[:, 0:w], AF.Exp, scale=scale), reads=zp.tl(), writes=eb.tl())
            S.op("act", lambda e: e.activation(sp.t[:, 0:w], eb.t[:, 0:w], AF.Ln, bias=1.0), reads=eb.tl(), writes=sp.tl())
            if i >= 0:
                S.op("pool", lambda e: e.affine_select(sp.t[:, 0:128], sp.t[:, 0:128], [[1, 128]], ALU.is_gt, 0.0,
                                                       base=0, channel_multiplier=-1), reads=sp.tl(), writes=sp.tl())

        def s3(n):
            hh, qt, kb, i, c0, w = geom(n)
            cp = PS[4 + n % 2]
            sp = self.a_sp[n % 2]
            R = self.a_R[grp[n] % 2]
            first = (kb == 4 * qt + 3)
            S.op("pe", lambda e: e.matmul(cp.t[:, 0:w], C16("tril"), sp.t[:, 0:w], start=True, stop=first),
                 reads=sp.tl() + c16t, writes=cp.tl())
            if not first:
                r0 = 128 if i >= 0 else 0
                S.op("pe", lambda e: e.matmul(cp.t[:, r0:w], C16("ones")[0:1, :], R.t[0:1, c0 + r0:512], start=False, stop=True),
                     reads=R.tl() + c16t, writes=cp.tl())
            if kb > 0:
                S.op("dve", lambda e: e.tensor_copy(R.t[0:1, c0:512], cp.t[0:1, 0:w]), reads=cp.tl(), writes=R.tl())

        def s4(n):
            hh, qt, kb, i, c0, w = geom(n)
            cp = PS[4 + n % 2]
            eb, xb, at = self.a_e[n % 3], self.a_x[n % 2], self.a_att[n % 2]
            import os
            sub = os.environ.get("S4SUB", "abc")
            if "a" in sub:
                S.op("act", lambda e: e.activation(xb.t[:, 0:w], cp.t[:, 0:w], AF.Exp, scale=-1.0), reads=cp.tl(), writes=xb.tl())
            if "b" in sub:
                S.op("dve", lambda e: e.tensor_tensor(at.t[:, 0:w], eb.t[:, 0:w], xb.t[:, 0:w], ALU.mult),
                     reads=eb.tl() + xb.tl(), writes=at.tl())
            if i >= 0 and "c" in sub:
                S.op("pool", lambda e: e.affine_select(at.t[:, 0:128], at.t[:, 0:128], [[1, 128]], ALU.is_gt, 0.0,
                                                       base=0, channel_multiplier=-1), reads=at.tl(), writes=at.tl())

        def s5(n):
            hh, qt, kb, i, c0, w = geom(n)
            cc, p0 = hh // 2, (hh % 2) * 64
            c = half * 2 + cc
            op_ = PS[6 + grp[n] % 2]
            at = self.a_att[n % 2]
            last = (kb == 0)
            first = (kb == 4 * qt + 3)
            vl = va.t[:, kb, cc * 128:(cc + 1) * 128]
            if i >= 0 and w > 128:
                S.op("pe", lambda e: e.matmul(op_.t[:, c0:c0 + 128], vl, at.t[:, 0:128], start=first, stop=False),
                     reads=at.tl() + va.tl(kb), writes=op_.tl())
                S.op("pe", lambda e: e.matmul(op_.t[:, c0 + 128:512], vl, at.t[:, 128:w], start=False, stop=last),
                     reads=at.tl() + va.tl(kb), writes=op_.tl())
            else:
                S.op("pe", lambda e: e.matmul(op_.t[:, c0:512], vl, at.t[:, 0:w], start=first, stop=last),
                     reads=at.tl() + va.tl(kb), writes=op_.tl())
            if last:
                S.op("dve", lambda e: e.tensor_copy(oT.t[p0:p0 + 64, c, qt * 512:(qt + 1) * 512], op_.t[p0:p0 + 64, :]),
                     reads=op_.tl(), writes=oT.tl(c, qt))

        import os
        stg = os.environ.get("ATT_STAGES", "12345")
        n_it = min(n_it, int(os.environ.get("ATT_ITEMS", n_it)))
        for n in range(n_it + 2):
            if n < n_it:
                if "1" in stg:
                    s1(n)
                if "2" in stg:
                    s2(n)
            if 0 <= n - 1 < n_it:
                if "3" in stg:
                    s3(n - 1)
                if "4" in stg:
                    s4(n - 1)
            if 0 <= n - 2 < n_it:
                if "5" in stg:
                    s5(n - 2)

    def gdn(self, l):
        S, NT, NB, T = self.S, self.NT, self.NB, self.Tn
        PS, C32, hT = self.PS, self.C32, self.hT
        c32t = self.c32.tl()
        gtok, gcrc, gsc, wg = self.gtok, self.gcrc, self.gsc, self.wgate
        PT = self.PT
        gpar = self.pv(self.slots["gate"], self.L, 8)
        nexpA = self.nexpA
        S.dma("pool", wg.t[:], self.d["win"][l, :, :, 3584:3592], writes=wg.tl())
        ps = self.mmps()

        def gproj(blk):
            for kc in range(NCH):
                S.op("pe", lambda e, kc=kc: e.matmul(ps.t[:, blk * 8:blk * 8 + 8], hT.t[:, kc, blk * 128:(blk + 1) * 128], wg.t[:, kc, :],
                                                     start=(kc == 0), stop=(kc == NCH - 1)),
                     reads=wg.tl() + hT.tl(kc, blk // 4), writes=ps.tl())
        for blk in range(NB):
            gproj(blk)
        psv = ps.t[:, 0:NB * 8].rearrange("p (a b) -> p a b", b=8)

        def ghead(h):
            S.op("act", lambda e: e.activation(gtok.t[:, :, h], psv[:, :, h], AF.Exp, bias=gpar[:, l, 4 + h:5 + h]),
                 reads=ps.tl() + PT, writes=gtok.tl())
            S.op("act", lambda e: e.activation(gtok.t[:, :, h], gtok.t[:, :, h], AF.Ln, bias=1.0), reads=gtok.tl(), writes=gtok.tl())
            S.op("dve", lambda e: e.tensor_scalar(gtok.t[:, :, h], gtok.t[:, :, h], nexpA.t[:, l, h:h + 1], None, ALU.mult),
                 reads=gtok.tl() + nexpA.tl(), writes=gtok.tl())
        for h in range(DN_H):
            ghead(h)
        S.op("act", lambda e: e.activation(gtok.t[:, :, 4:8], psv[:, :, 4:8], AF.Exp, scale=-1.0), reads=ps.tl(), writes=gtok.tl())
        S.op("dve", lambda e: e.tensor_scalar(gtok.t[:, :, 4:8], gtok.t[:, :, 4:8], 1.0, None, ALU.add), reads=gtok.tl(), writes=gtok.tl())
        S.op("dve", lambda e: e.reciprocal(gtok.t[:, :, 4:8], gtok.t[:, :, 4:8]), reads=gtok.tl(), writes=gtok.tl())
        ps2 = self.mmps()

        def gcs(blk):
            S.op("pe", lambda e: e.matmul(ps2.t[:, blk * 8:blk * 8 + 4], C32("mincl"), gtok.t[:, blk, 0:4], start=True, stop=True),
                 reads=gtok.tl() + c32t, writes=ps2.tl())
            S.op("pe", lambda e: e.matmul(ps2.t[:, blk * 8 + 4:blk * 8 + 8], C32("mrev"), gtok.t[:, blk, 0:4], start=True, stop=True),
                 reads=gtok.tl() + c32t, writes=ps2.tl())
        for blk in range(NB):
            gcs(blk)
        S.op("dve", lambda e: e.tensor_copy(gcrc.t[:].rearrange("p a b -> p (a b)"), ps2.t[:, 0:NB * 8]), reads=ps2.tl(), writes=gcrc.tl())
        S.op("act", lambda e: e.activation(gsc.t[:], gcrc.t[:], AF.Exp), reads=gcrc.tl(), writes=gsc.tl())
        S.op("dve", lambda e: e.tensor_tensor(gsc.t[:, :, 0:4], gsc.t[:, :, 0:4], gtok.t[:, :, 4:8], ALU.mult),
             reads=gsc.tl() + gtok.tl(), writes=gsc.tl())
        self.dump(f"gtok_{l}", gtok.t[:], gtok.tl(), [128, NB, 8])
        self.dump(f"gcrc_{l}", gcrc.t[:], gcrc.tl(), [128, NB, 8])
        S.op("pool", lambda e: e.memset(self.raw.t[:, 0:3], 0.0), writes=self.raw.tl())
        for hd in range(DN_H):
            self.gdn_head(l, hd)

    def sigmoid_tA(self, src_ap, src_tiles):
        S, tA = self.S, self.tA
        S.op("act", lambda e: e.activation(tA.t[:], src_ap, AF.Exp, scale=-1.0), reads=src_tiles, writes=tA.tl())
        S.op("act", lambda e: e.activation(tA.t[:], tA.t[:], AF.Ln, bias=1.0), reads=tA.tl(), writes=tA.tl())
        S.op("act", lambda e: e.activation(tA.t[:], tA.t[:], AF.Exp, scale=-1.0), reads=tA.tl(), writes=tA.tl())

    def gdn_stream(self, l, hd, kind):
        S, NT, T = self.S, self.NT, self.Tn
        raw, acc, tA = self.raw, self.cacc, self.tA
        col0 = {"q": 1536, "k": 2048, "v": 2560, "z": 3072}[kind] + hd * 128

        def evac(tt, ps):
            if tt % 2 == 0:
                S.op("dve", lambda e: e.tensor_copy(raw.t[:, 3 + tt * 512:3 + (tt + 1) * 512], ps.t[:, :]), reads=ps.tl(), writes=raw.tl())
            else:
                S.op("act", lambda e: e.activation(raw.t[:, 3 + tt * 512:3 + (tt + 1) * 512], ps.t[:, :], AF.Copy), reads=ps.tl(), writes=raw.tl())
        self.proj_chunk(l, col0, evac)
        CH = self.CH
        cw = self.pv(self.slots["conv"], self.L, 12, 4)
        PT = self.PT
        j = {"q": 0, "k": 4, "v": 8, "z": 0}[kind] + hd
        b2 = math.log(DN_D ** -0.5) if kind == "q" else 0.0
        for hf in range(T // CH):
            h0 = hf * CH
            if kind == "z":
                self.sigmoid_tA(raw.t[:, 3 + h0:3 + h0 + CH], raw.tl())
                S.op("dve", lambda e, h0=h0: e.tensor_tensor(self.zs.t[:, h0:h0 + CH], raw.t[:, 3 + h0:3 + h0 + CH], tA.t[:], ALU.mult),
                     reads=raw.tl() + tA.tl(), writes=self.zs.tl())
                continue
            S.op("dve", lambda e, h0=h0: e.tensor_scalar(acc.t[:], raw.t[:, h0:h0 + CH], cw[:, l, j, 0:1], None, ALU.mult),
                 reads=raw.tl() + PT, writes=acc.tl())
            for k in range(1, 4):
                S.op("dve", lambda e, k=k, h0=h0: e.scalar_tensor_tensor(
                    acc.t[:], raw.t[:, h0 + k:h0 + k + CH], cw[:, l, j, k:k + 1], acc.t[:], ALU.mult, ALU.add),
                    reads=raw.tl() + acc.tl() + PT, writes=acc.tl())
            self.sigmoid_tA(acc.t[:], acc.tl())
            if kind == "v":
                S.op("dve", lambda e, h0=h0: e.tensor_tensor(self.vT.t[:, h0:h0 + CH], acc.t[:], tA.t[:], ALU.mult),
                     reads=acc.tl() + tA.tl(), writes=self.vT.tl())
                continue
            S.op("dve", lambda e: e.tensor_tensor(acc.t[:], acc.t[:], tA.t[:], ALU.mult), reads=acc.tl() + tA.tl(), writes=acc.tl())
            dst = self.qT if kind == "q" else self.kT
            for t2 in range(CH // 512):
                self.group_norm(acc.t[:, t2 * 512:(t2 + 1) * 512], acc.tl(), "ones", None, None,
                                dst.t[:, h0 + t2 * 512:h0 + (t2 + 1) * 512], dst.tl(), bias2=b2)

    def tri_inverse(self, L32, psr):
        S, ring, C32c, C16c = self.S, self.ring, self.C32, self.C16
        c32t = self.c32.tl()
        ident32 = C32c("ident")
        pu = psr()
        S.op("pe", lambda e: e.matmul(pu.t[:, 0:128], L32.t[:], ident32, start=True, stop=True), reads=L32.tl() + c32t, writes=pu.tl())
        L16, U16 = ring(self.g_Lc, "Lc"), ring(self.g_Uc, "Uc")
        C32, C32T, C64 = ring(self.g_C32, "C32"), ring(self.g_C32T, "C32T"), ring(self.g_C64, "C64")
        X0, Y0 = ring(self.g_Xc, "Xc"), ring(self.g_Yc, "Yc")
        S.op("dve", lambda e: e.tensor_tensor(L16.t[:], L32.t[:], C32c("m16"), ALU.mult), reads=L32.tl() + c32t, writes=L16.tl())
        S.op("dve", lambda e: e.tensor_tensor(C32.t[:], L32.t[:], C32c("m32"), ALU.mult), reads=L32.tl() + c32t, writes=C32.tl())
        S.op("pool", lambda e: e.tensor_tensor(C64.t[:], L32.t[:], C32c("m64"), ALU.mult), reads=L32.tl() + c32t, writes=C64.tl())
        S.op("dve", lambda e: e.tensor_tensor(U16.t[:], pu.t[:, 0:128], C32c("m16T"), ALU.mult), reads=pu.tl() + c32t, writes=U16.tl())
        S.op("dve", lambda e: e.tensor_tensor(C32T.t[:], pu.t[:, 0:128], C32c("m32T"), ALU.mult), reads=pu.tl() + c32t, writes=C32T.tl())
        S.op("pool", lambda e: e.tensor_tensor(Y0.t[:], ident32, L16.t[:], ALU.subtract), reads=L16.tl() + c32t, writes=Y0.tl())
        S.op("dve", lambda e: e.tensor_tensor(X0.t[:], ident32, U16.t[:], ALU.subtract), reads=U16.tl() + c32t, writes=X0.tl())
        Lk, Uk, Xk, Yk = L16, U16, X0, Y0
        first_call = not getattr(self, "_tri_dbg", False)
        self._tri_dbg = True
        if first_call:
            self.dump("dbgX0", Xk.t[:], Xk.tl(), [128, 128])
            self.dump("dbgU16", Uk.t[:], Uk.tl(), [128, 128])
            self.dump("dbgL16", Lk.t[:], Lk.tl(), [128, 128])
        T16T = T16 = None
        for k in range(3):
            last = (k == 2)
            pq = psr()
            S.op("pe", lambda e, Uk=Uk, Lk=Lk, pq=pq: e.matmul(pq.t[:, 0:128], Uk.t[:], Lk.t[:], start=True, stop=True),
                 reads=Uk.tl() + Lk.tl(), writes=pq.tl())
            S.op("pe", lambda e, Uk=Uk, Lk=Lk, pq=pq: e.matmul(pq.t[:, 128:256], Lk.t[:], Uk.t[:], start=True, stop=True),
                 reads=Uk.tl() + Lk.tl(), writes=pq.tl())
            Ln_, Un_ = ring(self.g_Lc, "Lc"), ring(self.g_Uc, "Uc")
            S.op("act", lambda e, Ln_=Ln_, pq=pq: e.activation(Ln_.t[:], pq.t[:, 0:128], AF.Copy), reads=pq.tl(), writes=Ln_.tl())
            S.op("dve", lambda e, Un_=Un_, pq=pq: e.tensor_copy(Un_.t[:], pq.t[:, 128:256]), reads=pq.tl(), writes=Un_.tl())
            S.op("pe", lambda e, Ln_=Ln_, Xk=Xk, pq=pq: e.matmul(pq.t[:, 256:384], Ln_.t[:], Xk.t[:], start=True, stop=True),
                 reads=Ln_.tl() + Xk.tl(), writes=pq.tl())
            S.op("pe", lambda e, Un_=Un_, Yk=Yk, pq=pq: e.matmul(pq.t[:, 384:512], Un_.t[:], Yk.t[:], start=True, stop=True),
                 reads=Un_.tl() + Yk.tl(), writes=pq.tl())
            if last:
                Xn, Yn = ring(self.g_T16T, "T16T"), ring(self.g_T16, "T16")
            else:
                Xn, Yn = ring(self.g_Xc, "Xc"), ring(self.g_Yc, "Yc")
            S.op("dve", lambda e, Xn=Xn, Xk=Xk, pq=pq: e.tensor_tensor(Xn.t[:], pq.t[:, 256:384], Xk.t[:], ALU.add),
                 reads=pq.tl() + Xk.tl(), writes=Xn.tl())
            S.op("dve", lambda e, Yn=Yn, Yk=Yk, pq=pq: e.tensor_tensor(Yn.t[:], pq.t[:, 384:512], Yk.t[:], ALU.add),
                 reads=pq.tl() + Yk.tl(), writes=Yn.tl())
            Uk, Lk, Xk, Yk = Un_, Ln_, Xn, Yn
        T16T, T16 = Xk, Yk
        if first_call:
            self.dump("dbgT16T", T16T.t[:], T16T.tl(), [128, 128], BF16)
        pd = psr()
        S.op("pe", lambda e: e.matmul(pd.t[:, 0:128], C32.t[:], T16T.t[:], start=True, stop=True), reads=C32.tl() + T16T.tl(), writes=pd.tl())
        S.op("pe", lambda e: e.matmul(pd.t[:, 128:256], C32T.t[:], T16.t[:], start=True, stop=True), reads=C32T.tl() + T16.tl(), writes=pd.tl())
        Ya, Yb = ring(self.g_Ya, "Ya"), ring(self.g_Yb, "Yb")
        S.op("act", lambda e: e.activation(Ya.t[:], pd.t[:, 0:128], AF.Copy), reads=pd.tl(), writes=Ya.tl())
        S.op("dve", lambda e: e.tensor_copy(Yb.t[:], pd.t[:, 128:256]), reads=pd.tl(), writes=Yb.tl())
        pd2 = psr()
        S.op("pe", lambda e: e.matmul(pd2.t[:, 0:128], T16.t[:], Ya.t[:], start=True, stop=True), reads=T16.tl() + Ya.tl(), writes=pd2.tl())
        S.op("pe", lambda e: e.matmul(pd2.t[:, 128:256], T16T.t[:], Yb.t[:], start=True, stop=True), reads=T16T.tl() + Yb.tl(), writes=pd2.tl())
        T32T, T32 = ring(self.g_T32T, "T32T"), ring(self.g_T32, "T32")
        S.op("dve", lambda e: e.scalar_tensor_tensor(T32T.t[:], pd2.t[:, 0:128], -1.0, T16T.t[:], ALU.mult, ALU.add),
             reads=pd2.tl() + T16T.tl(), writes=T32T.tl())
        S.op("dve", lambda e: e.scalar_tensor_tensor(T32.t[:], pd2.t[:, 128:256], -1.0, T16.t[:], ALU.mult, ALU.add),
             reads=pd2.tl() + T16.tl(), writes=T32.tl())
        pe1 = psr()
        S.op("pe", lambda e: e.matmul(pe1.t[:, 0:128], C64.t[:], T32T.t[:], start=True, stop=True), reads=C64.tl() + T32T.tl(), writes=pe1.tl())
        Yd = ring(self.g_Yd, "Yd")
        S.op("act", lambda e: e.activation(Yd.t[:], pe1.t[:, 0:128], AF.Copy), reads=pe1.tl(), writes=Yd.tl())
        S.op("pe", lambda e: e.matmul(pe1.t[:, 128:256], T32.t[:], Yd.t[:], start=True, stop=True), reads=T32.tl() + Yd.tl(), writes=pe1.tl())
        TT = ring(self.g_TT, "TT")
        S.op("dve", lambda e: e.scalar_tensor_tensor(TT.t[:], pe1.t[:, 128:256], -1.0, T32T.t[:], ALU.mult, ALU.add),
             reads=pe1.tl() + T32T.tl(), writes=TT.tl())
        return TT

    def gdn_head(self, l, hd):
        S, NT, NB, T = self.S, self.NT, self.NB, self.Tn
        PS, C16, C32 = self.PS, self.C16, self.C32
        c16t, c32t = self.c16.tl(), self.c32.tl()
        kT, qT, vT, zs = self.kT, self.qT, self.vT, self.zs
        gtok, gcrc, gsc = self.gtok, self.gcrc, self.gsc
        ring = self.ring
        for kind in ("q", "k", "v", "z"):
            self.gdn_stream(l, hd, kind)
        self.dump(f"qT_{l}_{hd}", qT.t[:], qT.tl(), [128, T], BF16)
        self.dump(f"kT_{l}_{hd}", kT.t[:], kT.tl(), [128, T], BF16)
        self.dump(f"vT_{l}_{hd}", vT.t[:], vT.tl(), [128, T], BF16)
        self.dump(f"zs_{l}_{hd}", zs.t[:], zs.tl(), [128, T], BF16)

        S32, Sbf = self.S32, self.Sbf
        S.op("dve", lambda e: e.memset(S32.t[:], 0.0), writes=S32.tl())
        S.op("pool", lambda e: e.memset(Sbf.t[:], 0.0), writes=Sbf.tl())
        PB = [PS[2], PS[3], PS[4], PS[5]]
        ident16 = C16("ident")

        def psr():
            return ring(PB, "gps")
        prep_out = {}

        def prep(blk):
            bs = slice(blk * 128, (blk + 1) * 128)
            gcol = gtok.t[:, blk, hd:hd + 1]
            bcol = gtok.t[:, blk, 4 + hd:5 + hd]
            gccol = gcrc.t[:, blk, hd:hd + 1]
            bgcol = gsc.t[:, blk, hd:hd + 1]
            erccol = gsc.t[:, blk, 4 + hd:5 + hd]
            p1 = psr()
            S.op("pe", lambda e: e.matmul(p1.t[:, 0:128], gcol.to_broadcast([128, 128]), C32("mincl"), start=True, stop=True),
                 reads=gtok.tl() + c32t, writes=p1.tl())
            ebc, tI, DL, DLs = ring(self.g_ebc, "ebc"), ring(self.g_tI, "tI"), ring(self.g_DL, "DL"), ring(self.g_DLs, "DLs")
            S.op("act", lambda e: e.activation(ebc.t[:], p1.t[:, 0:128], AF.Exp), reads=p1.tl(), writes=ebc.tl())
            S.op("dve", lambda e: e.scalar_tensor_tensor(tI.t[:], p1.t[:, 0:128], gccol, C32("mbI"), ALU.subtract, ALU.add),
                 reads=p1.tl() + gcrc.tl() + c32t, writes=tI.tl())
            S.op("act", lambda e: e.activation(DL.t[:], tI.t[:], AF.Exp, scale=-1.0), reads=tI.tl(), writes=DL.tl())
            S.op("dve", lambda e: e.tensor_tensor(DLs.t[:], DL.t[:], C32("strict"), ALU.mult), reads=DL.tl() + c32t, writes=DLs.tl())
            p2 = psr()
            S.op("pe", lambda e: e.matmul(p2.t[:, 0:128], kT.t[:, bs], kT.t[:, bs], start=True, stop=True), reads=kT.tl(), writes=p2.tl())
            S.op("pe", lambda e: e.matmul(p2.t[:, 128:256], qT.t[:, bs], kT.t[:, bs], start=True, stop=True), reads=kT.tl() + qT.tl(), writes=p2.tl())
            L32, Ab = ring(self.g_L32, "L32"), ring(self.g_A, "A")
            S.op("dve", lambda e: e.scalar_tensor_tensor(L32.t[:], p2.t[:, 0:128], bcol, DLs.t[:], ALU.mult, ALU.mult),
                 reads=p2.tl() + gtok.tl() + DLs.tl(), writes=L32.tl())
            S.op("dve", lambda e: e.tensor_tensor(Ab.t[:], p2.t[:, 128:256], DL.t[:], ALU.mult), reads=p2.tl() + DL.tl(), writes=Ab.tl())
            p3 = psr()
            p3b = p3.t[:, :].bitcast(BF16)
            S.op("pe", lambda e: e.transpose(p3b[:, 128:256], Ab.t[:], ident16), reads=Ab.tl() + c16t, writes=p3.tl())
            S.op("pe", lambda e: e.transpose(p3b[:, 256:384], kT.t[:, bs], ident16), reads=kT.tl() + c16t, writes=p3.tl())
            S.op("pe", lambda e: e.transpose(p3b[:, 384:512], vT.t[:, bs], ident16), reads=vT.tl() + c16t, writes=p3.tl())
            AT = ring(self.g_AT, "AT")
            kbg, kd, vb = ring(self.g_kbg, "kbg"), ring(self.g_kd, "kd"), ring(self.g_vb, "vb")
            S.op("act", lambda e: e.activation(AT.t[:], p3b[:, 128:256], AF.Copy), reads=p3.tl(), writes=AT.tl())
            S.op("dve", lambda e: e.tensor_scalar(kbg.t[:], p3b[:, 256:384], bgcol, None, ALU.mult), reads=p3.tl() + gsc.tl(), writes=kbg.tl())
            S.op("dve", lambda e: e.tensor_scalar(kd.t[:], p3b[:, 256:384], erccol, None, ALU.mult), reads=p3.tl() + gsc.tl(), writes=kd.tl())
            S.op("dve", lambda e: e.tensor_scalar(vb.t[:], p3b[:, 384:512], bcol, None, ALU.mult), reads=p3.tl() + gtok.tl(), writes=vb.tl())
            qg = ring(self.g_qg, "qg")
            S.op("dve", lambda e: e.tensor_tensor(qg.t[:], qT.t[:, bs], ebc.t[:], ALU.mult), reads=qT.tl() + ebc.tl(), writes=qg.tl())
            TT = self.tri_inverse(L32, psr)
            if blk == 0:
                self.dump(f"L32_{l}_{hd}", L32.t[:], L32.tl(), [128, 128])
                self.dump(f"TT_{l}_{hd}", TT.t[:], TT.tl(), [128, 128], BF16)
            p4 = psr()
            S.op("pe", lambda e: e.matmul(p4.t[:, 0:128], TT.t[:], vb.t[:], start=True, stop=True), reads=TT.tl() + vb.tl(), writes=p4.tl())
            S.op("pe", lambda e: e.matmul(p4.t[:, 128:256], kbg.t[:], TT.t[:], start=True, stop=True), reads=TT.tl() + kbg.tl(), writes=p4.tl())
            u, wT = ring(self.g_u, "u"), ring(self.g_wT, "wT")
            S.op("dve", lambda e: e.tensor_copy(u.t[:], p4.t[:, 0:128]), reads=p4.tl(), writes=u.tl())
            S.op("act", lambda e: e.activation(wT.t[:], p4.t[:, 128:256], AF.Copy), reads=p4.tl(), writes=wT.tl())
            prep_out[blk] = (u, wT, kd, AT, qg, ebc)

        opsum, pw = PS[6], PS[7]

        def chunk(blk, ch, u, wT, kd, AT, qg, ebc):
            r0 = ch * 64
            rs = slice(r0, r0 + 64)
            cs = slice((blk % 4) * 128 + r0, (blk % 4) * 128 + r0 + 64)
            S.op("pe", lambda e: e.matmul(pw.t[:, 0:128], wT.t[:], Sbf.t[:], start=True, stop=True), reads=wT.tl() + Sbf.tl(), writes=pw.tl())
            vn = ring(self.g_vn, "vn")
            S.op("dve", lambda e: e.tensor_tensor(vn.t[rs, :], u.t[rs, :], pw.t[rs, 0:128], ALU.subtract), reads=u.tl() + pw.tl(), writes=vn.tl())
            S.op("pe", lambda e: e.matmul(opsum.t[:, cs], Sbf.t[:], qg.t[:, rs], start=True, stop=False), reads=Sbf.tl() + qg.tl(), writes=opsum.tl())
            S.op("pe", lambda e: e.matmul(opsum.t[:, cs], vn.t[rs, :], AT.t[rs, rs], start=False, stop=True), reads=vn.tl() + AT.tl(), writes=opsum.tl())
            S.op("pe", lambda e: e.matmul(pw.t[:, 128:256], kd.t[rs, :], vn.t[rs, :], start=True, stop=True), reads=kd.tl() + vn.tl(), writes=pw.tl())
            S.op("dve", lambda e: e.scalar_tensor_tensor(S32.t[:], S32.t[:], ebc.t[:, r0 + 63:r0 + 64], pw.t[:, 128:256], ALU.mult, ALU.add),
                 reads=S32.tl() + ebc.tl() + pw.tl(), writes=S32.tl())
            S.op("act", lambda e: e.activation(Sbf.t[:], S32.t[:], AF.Copy), reads=S32.tl(), writes=Sbf.tl())

        def finish(tt):
            ts = slice(tt * 512, (tt + 1) * 512)
            go = self.tA.t[:, 0:512]
            got = self.tA.tl()
            S.op("act", lambda e: e.activation(go, opsum.t[:, :], AF.Copy), reads=opsum.tl(), writes=got)
            dno = self.slots["dno"]
            self.group_norm(go, got, "ones", DN_D, self.prm.t[:, dno[0] + l:dno[0] + l + 1], go, got)
            S.op("dve", lambda e: e.tensor_tensor(self.oT.t[:, 4 + hd, ts], go, zs.t[:, ts], ALU.mult),
                 reads=got + zs.tl(), writes=self.oT.tl(4 + hd, tt))

        def chain(blk):
            args = prep_out.pop(blk)
            for ch in range(2):
                chunk(blk, ch, *args)
            if blk % 4 == 3:
                finish(blk // 4)

        prep(0)
        for blk in range(NB):
            la = S.capture(lambda: prep(blk + 1)) if blk + 1 < NB else []
            lb = S.capture(lambda: chain(blk))
            S.merge(la, lb)

    def out_proj(self, l, s):
        S, NT = self.S, self.NT
        xT, oT, mod = self.xT, self.oT, self.mod

        def one(wo, cc, c, tt):
            ts = slice(tt * 512, (tt + 1) * 512)
            ps = self.mmps()
            for kc in range(NCH):
                S.op("pe", lambda e, kc=kc: e.matmul(ps.t[:, :], wo.t[:, kc, cc * 128:(cc + 1) * 128], oT.t[:, kc, ts],
                                                     start=(kc == 0), stop=(kc == NCH - 1)),
                     reads=wo.tl() + oT.tl(kc, tt), writes=ps.tl())
            S.op("dve", lambda e: e.scalar_tensor_tensor(xT.t[:, c, ts], ps.t[:, :], mod.t[:, l, 16 + c, s:s + 1], xT.t[:, c, ts], ALU.mult, ALU.add),
                 reads=ps.tl() + mod.tl() + xT.tl(c, tt), writes=xT.tl(c, tt))
        for half in range(2):
            wo = self.wo[half]
            S.dma("pool", wo.t[:], self.d["wout"][l, :, :, half * 512:(half + 1) * 512], writes=wo.tl())
        for half in range(2):
            for cc in range(4):
                for tt in range(NT):
                    one(self.wo[half], cc, half * 4 + cc, tt)

    def mlp(self, l, s):
        S, NT, HT = self.S, self.NT, self.HT
        xT, hT, mod = self.xT, self.hT, self.mod
        nh = self.Tn // HT
        tph = HT // 512

        def ff1(wb, j, tt, t2):
            ts = slice(tt * 512, (tt + 1) * 512)
            ps = self.mmps()
            for kc in range(NCH):
                S.op("pe", lambda e, kc=kc: e.matmul(ps.t[:, :], wb.t[:, kc, :], hT.t[:, kc, ts], start=(kc == 0), stop=(kc == NCH - 1)),
                     reads=wb.tl() + hT.tl(kc, tt), writes=ps.tl())
            rl = self.ring(self.relu, "relu")
            hb, jj = self.hid(j)
            S.op("act", lambda e: e.activation(rl.t[:], ps.t[:, :], AF.Relu), reads=ps.tl(), writes=rl.tl())
            S.op("pool", lambda e: e.tensor_tensor(hb.t[:, jj, t2 * 512:(t2 + 1) * 512], rl.t[:], rl.t[:], ALU.mult),
                 reads=rl.tl(), writes=hb.tl(jj, t2))

        def ff2(w2, c, tt, t2):
            ts = slice(tt * 512, (tt + 1) * 512)
            ps = self.mmps()
            for j in range(32):
                hb, jj = self.hid(j)
                S.op("pe", lambda e, j=j, hb=hb, jj=jj: e.matmul(ps.t[:, :], w2.t[:, j, :], hb.t[:, jj, t2 * 512:(t2 + 1) * 512],
                                                                start=(j == 0), stop=(j == 31)),
                     reads=w2.tl() + hb.tl(jj, t2), writes=ps.tl())
            S.op("dve", lambda e: e.scalar_tensor_tensor(xT.t[:, c, ts], ps.t[:, :], mod.t[:, l, 40 + c, s:s + 1], xT.t[:, c, ts], ALU.mult, ALU.add),
                 reads=ps.tl() + mod.tl() + xT.tl(c, tt), writes=xT.tl(c, tt))

        s1 = WStream(self, self.wring, [self.d["ff1"][l, :, :, j * 128:(j + 1) * 128] for hf in range(nh) for j in range(32)], 3)
        s2 = WStream(self, self.w2, [self.d["ff2"][l, :, :, c * 128:(c + 1) * 128] for hf in range(nh) for c in range(NCH)], 1)
        for hf in range(nh):
            for j in range(32):
                wb = s1.next()
                for t2 in range(tph):
                    ff1(wb, j, hf * tph + t2, t2)
            for c in range(NCH):
                w2 = s2.next()
                for t2 in range(tph):
                    ff2(w2, c, hf * tph + t2, t2)


def _chunked(w, L):
    Lk, K, N = w.shape
    return np.ascontiguousarray(w.reshape(Lk, K // 128, 128, N).transpose(0, 2, 1, 3))


def prep_shared(inp, L):
    f = lambda a: np.ascontiguousarray(np.asarray(a, dtype=np.float32))
    sh = {}
    sh["w_ada"] = _chunked(f(inp["w_ada"])[:L], L)
    sh["b_ada"] = f(f(inp["b_ada"])[:L].reshape(L, 48, 128).transpose(2, 0, 1))
    sh["norm_mix"] = f(f(inp["norm_mix"])[:L].reshape(L, NCH, 128).transpose(2, 0, 1))
    sh["norm_mlp"] = f(f(inp["norm_mlp"])[:L].reshape(L, NCH, 128).transpose(2, 0, 1))
    sh["w_in"] = _chunked(f(inp["w_in"])[:L], L)
    sh["sb_q_norm"] = f(np.tile(f(inp["sb_q_norm"])[:L], (1, 2)).T)
    sh["sb_k_norm"] = f(np.tile(f(inp["sb_k_norm"])[:L], (1, 2)).T)
    sh["conv_w"] = f(f(inp["conv_w"])[:L].reshape(L, 4, 12, 128).transpose(3, 0, 2, 1))
    gp = np.concatenate([f(inp["a_log"])[:L], f(inp["dt_bias"])[:L]], axis=1)
    sh["gate_p"] = f(np.broadcast_to(gp[None], (128, L, 8)))
    sh["dn_out_norm"] = f(f(inp["dn_out_norm"])[:L].T)
    sh["w_out"] = _chunked(f(inp["w_out"])[:L], L)
    sh["w_ff1"] = _chunked(f(inp["w_ff1"])[:L], L)
    sh["w_ff2"] = _chunked(f(inp["w_ff2"])[:L], L)
    cc = _consts()
    sh["consts32"], sh["consts16"] = cc[1], cc[3]
    return sh


def prep_core(x2, c2, T):
    ns = x2.shape[0]
    xT = np.ascontiguousarray(np.asarray(x2, np.float32).reshape(ns, T, NCH, 128).transpose(0, 3, 2, 1))
    cT = np.ascontiguousarray(np.asarray(c2, np.float32).reshape(2, NCH, 128).transpose(2, 1, 0))
    return {"xT": xT, "cT": cT}


_CACHE = {}


def kernel(**inputs):
    x = np.asarray(inputs["x"], np.float32)
    c = np.asarray(inputs["c"], np.float32)
    B, T, _ = x.shape
    key = (T, DEPTH, 2)
    if key not in _CACHE:
        _CACHE[key] = Builder(T=T, L=DEPTH, NSEQ=2)
    bld = _CACHE[key]
    sh = prep_shared(inputs, DEPTH)
    in_maps = []
    for core in range(NCORES):
        m = dict(sh)
        m.update(prep_core(x[2 * core:2 * core + 2], c[2 * core:2 * core + 2], T))
        in_maps.append(m)
    res = run_bass_kernel_spmd(bld.nc, in_maps, core_ids=list(range(NCORES)))
    out = np.empty((B, T, D), np.float32)
    for core in range(NCORES):
        oT = res.results[core]["outT"]
        out[2 * core:2 * core + 2] = oT.transpose(0, 3, 2, 1).reshape(2, T, D)
    return out
```

```python
import math
import numpy as np
import concourse.bass as bass
import concourse.mybir as mybir
from concourse.bass_utils import run_bass_kernel_spmd

F32 = mybir.dt.float32
BF16 = mybir.dt.bfloat16
AF = mybir.ActivationFunctionType
ALU = mybir.AluOpType

D = 1024
NCH = 8
DEPTH = 4
SEQ = 2048
NCORES = 8
SB_H, SB_D = 8, 64
DN_H, DN_D = 4, 128
IN_W = 3592
DFF = 4096
EPS = 1e-6
ENGS = ("pe", "act", "dve", "pool", "sp")


class Tile:
    __slots__ = ("name", "lw", "rd", "sem", "semcnt", "psum")

    def __init__(self, name):
        self.name = name
        self.psum = False
        self.lw = None
        self.rd = []
        self.sem = None
        self.semcnt = 0


class Op:
    __slots__ = ("eng", "fn", "deps", "signal", "sigval", "dma", "dsem", "dval")

    def __init__(self, eng, fn):
        self.eng = eng
        self.fn = fn
        self.deps = []
        self.signal = False
        self.sigval = 0
        self.dma = False
        self.dsem = None
        self.dval = 0


class Sched:
    def __init__(self, nc):
        self.nc = nc
        self.ops = {e: [] for e in ENGS}
        self.ndsem = 0

    def _add(self, op, reads, writes):
        deps = []
        for t in reads:
            if t.lw is not None:
                deps.append(t.lw)
        for t in writes:
            if t.lw is not None:
                deps.append(t.lw)
            deps.extend(t.rd)
        for t in reads:
            t.rd.append(op)
        for t in writes:
            t.lw = op
            t.rd = []
        seen = set()
        for d in deps:
            if d is op or id(d) in seen:
                continue
            seen.add(id(d))
            if (not d.dma) and (not op.dma) and d.eng == "pe" and op.eng == "pe":
                continue
            op.deps.append(d)
            d.signal = True
        self.ops[op.eng].append(op)
        return op

    def op(self, eng, fn, reads=(), writes=()):
        if getattr(self, "cap", None) is not None:
            self.cap.append((eng, fn, list(reads), list(writes)))
            return None
        reads, writes = list(reads), list(writes)
        pr = [t for t in reads if t.psum]
        if pr:
            reads = [t for t in reads if not t.psum]
            writes = writes + [t for t in pr if t not in writes]
        return self._add(Op(eng, fn), reads, writes)

    def dma(self, eng, out, in_, reads=(), writes=()):
        o = Op(eng, None)
        o.dma = True
        tiles = list(writes) + list(reads)
        st = tiles[0]
        if st.sem is None:
            st.sem = self.nc.alloc_semaphore(f"dsem{self.ndsem}")
            self.ndsem += 1
        st.semcnt += 16
        o.dsem = st.sem
        o.dval = st.semcnt
        o.signal = True
        o.fn = lambda e, out=out, in_=in_: e.dma_start(out=out, in_=in_)
        return self._add(o, [], tiles)

    def capture(self, f):
        assert getattr(self, "cap", None) is None
        self.cap = []
        f()
        lst, self.cap = self.cap, None
        return lst

    def merge(self, *lists):
        idx = [0] * len(lists)
        total = sum(len(x) for x in lists)
        for _ in range(total):
            best, bf = None, None
            for k, lst in enumerate(lists):
                if idx[k] < len(lst):
                    fr = idx[k] / len(lst)
                    if bf is None or fr < bf:
                        best, bf = k, fr
            eng, fn, reads, writes = lists[best][idx[best]]
            idx[best] += 1
            self.op(eng, fn, reads, writes)

    def emit(self, final_wait_ops=()):
        nc = self.nc
        esem = {e: nc.alloc_semaphore(f"sem_{e}") for e in ENGS}
        for e in ENGS:
            c = 0
            for o in self.ops[e]:
                if (not o.dma) and o.signal:
                    c += 1
                    o.sigval = c
        stats = {}

        def run(e, eng):
            known = {}
            nw = 0
            for o in self.ops[e]:
                for d in o.deps:
                    if d.dma:
                        key, val = d.dsem, d.dval
                    else:
                        key, val = esem[d.eng], d.sigval
                    if known.get(key.num, 0) >= val:
                        continue
                    known[key.num] = val
                    eng.wait_ge(key, val)
                    nw += 1
                ins = o.fn(eng)
                if o.dma:
                    ins.then_inc(o.dsem, 16)
                elif o.signal:
                    ins.then_inc(esem[e], 1)
            if e == "sp":
                for o in final_wait_ops:
                    if o.dma:
                        eng.wait_ge(o.dsem, o.dval)
                    else:
                        eng.wait_ge(esem[o.eng], o.sigval)
            stats[e] = (len(self.ops[e]), nw)

        with nc.Block() as block:
            @block.tensor
            def _(eng):
                run("pe", eng)

            @block.scalar
            def _(eng):
                run("act", eng)

            @block.vector
            def _(eng):
                run("dve", eng)

            @block.gpsimd
            def _(eng):
                run("pool", eng)

            @block.sync
            def _(eng):
                run("sp", eng)
        return stats


class Buf:
    def __init__(self, t, name, grid=None):
        self.t = t
        if grid is None:
            self.T = Tile(name)
        else:
            self.T = np.empty(grid, dtype=object)
            for idx in np.ndindex(*grid):
                self.T[idx] = Tile(f"{name}{idx}")

    def tl(self, *idx):
        if isinstance(self.T, Tile):
            return [self.T]
        sub = self.T[idx] if idx else self.T
        if isinstance(sub, Tile):
            return [sub]
        return list(sub.ravel())


def _consts():
    i = np.arange(128)
    same = (i[:, None] // 64) == (i[None, :] // 64)
    c = {}
    c["ident"] = np.eye(128, dtype=np.float32)
    c["ones"] = np.ones((128, 128), np.float32)
    blk = np.zeros((128, 128), np.float32)
    blk[:64, :64] = 1.0
    blk[64:, 64:] = 1.0
    c["blk64"] = blk
    c["tril"] = (i[:, None] >= i[None, :]).astype(np.float32)
    c["mincl"] = ((i[:, None] <= i[None, :]) & same).astype(np.float32)
    c["mrev"] = ((i[:, None] > i[None, :]) & same).astype(np.float32)
    c["mbI"] = np.where((i[None, :] <= i[:, None]) & same, 0.0, 30000.0).astype(np.float32)
    c["strict"] = (i[None, :] < i[:, None]).astype(np.float32)
    b16 = (i[:, None] // 16) == (i[None, :] // 16)
    b32 = (i[:, None] // 32) == (i[None, :] // 32)
    low = i[None, :] < i[:, None]
    c["m16"] = (b16 & low).astype(np.float32)
    c["m32"] = (b32 & ~b16 & low).astype(np.float32)
    c["m64"] = (same & ~b32 & low).astype(np.float32)
    c["m16T"] = np.ascontiguousarray(c["m16"].T)
    c["m32T"] = np.ascontiguousarray(c["m32"].T)
    n32 = ["ident", "mincl", "mrev", "mbI", "strict", "m16", "m32", "m64", "m16T", "m32T"]
    n16 = ["ident", "ones", "blk64", "tril"]
    return (n32, np.concatenate([c[n] for n in n32], axis=1), n16, np.concatenate([c[n] for n in n16], axis=1))


class WStream:
    def __init__(self, bld, bufs, srcs, depth):
        self.b, self.bufs, self.srcs, self.depth = bld, bufs, list(srcs), depth
        assert len(bufs) > depth
        self.n = 0
        self.issued = 0
        self.live = []
        for _ in range(min(depth, len(self.srcs))):
            self._issue()

    def _issue(self):
        buf = self.bufs[self.issued % len(self.bufs)]
        self.b.S.dma("pool", buf.t[:], self.srcs[self.issued], writes=buf.tl())
        self.live.append(buf)
        self.issued += 1

    def next(self):
        buf = self.live[self.n]
        self.n += 1
        if self.issued < len(self.srcs):
            self._issue()
        return buf


AR_EL = 30208


class Builder:
    def __init__(self, T=SEQ, L=DEPTH, NSEQ=2, dbg=(), stop=None):
        self.stop = stop
        self.Tn = T
        self.L = L
        self.NSEQ = NSEQ
        self.NT = T // 512
        self.NB = T // 128
        self.dbg_names = dbg
        nc = self.nc = bass.Bass("TRN2", target_bir_lowering=False)
        self.S = Sched(nc)
        self.cnt = {}
        self.final = []
        self.phase = {}
        self.build()

    def sb(self, name, shape, dt, grid=None):
        return Buf(self.nc.alloc_sbuf_tensor("s_" + name, list(shape), dt), name, grid)

    def av(self, phase, name, shape, dt, grid=None, base=None):
        ph = self.phase.setdefault(phase, {"off": 0, "tiles": []})
        nel = int(np.prod(shape[1:]))
        nbf = nel * (2 if dt == F32 else 1)
        off = (ph["off"] + 15) // 16 * 16
        ph["off"] = off + nbf
        assert ph["off"] <= AR_EL, (phase, name, ph["off"])
        ap = self.arena[:, off:off + nbf]
        if dt == F32:
            ap = ap.bitcast(F32)
        if len(shape) == 3:
            ap = ap.rearrange("p (a b) -> p a b", a=shape[1], b=shape[2])
        if shape[0] < 128:
            ap = ap[0:shape[0]]
        b = Buf(ap, name, grid)
        ph["tiles"].extend(b.tl())
        return b

    def ring(self, lst, key):
        v = self.cnt.get(key, 0)
        self.cnt[key] = v + 1
        return lst[v % len(lst)]

    def barrier(self, *phases, extra=()):
        tiles = list(extra)
        for p in phases:
            tiles.extend(self.phase[p]["tiles"])
        self.S.op("sp", lambda e: e.nop(), writes=tiles)

    def din(self, name, shape, dt=F32):
        return self.nc.dram_tensor(name, list(shape), dt, kind="ExternalInput").ap()

    def dump(self, name, ap, tiles, shape, dt=F32):
        if name not in self.dbg_names:
            return
        d = self.nc.dram_tensor("dbg_" + name, list(shape), dt, kind="ExternalOutput").ap()
        self.final.append(self.S.dma("sp", d, ap, reads=tiles))

    def build(self):
        nc, S, T, L, NSEQ, NT, NB = self.nc, self.S, self.Tn, self.L, self.NSEQ, self.NT, self.NB
        d_xT = self.din("xT", [NSEQ, 128, NCH, T])
        d_cT = self.din("cT", [128, NCH, 2])
        d_wada = self.din("w_ada", [L, 128, NCH, 6 * D])
        d_bada = self.din("b_ada", [128, L, 48])
        d_nmix = self.din("norm_mix", [128, L, NCH])
        d_nmlp = self.din("norm_mlp", [128, L, NCH])
        d_win = self.din("w_in", [L, 128, NCH, IN_W])
        d_sbq = self.din("sb_q_norm", [128, L])
        d_sbk = self.din("sb_k_norm", [128, L])
        d_conv = self.din("conv_w", [128, L, 12, 4])
        d_gate = self.din("gate_p", [128, L, 8])
        d_dno = self.din("dn_out_norm", [128, L])
        d_wout = self.din("w_out", [L, 128, NCH, D])
        d_ff1 = self.din("w_ff1", [L, 128, NCH, DFF])
        d_ff2 = self.din("w_ff2", [L, 128, 32, D])
        n32, c32m, n16, c16m = _consts()
        d_c32 = self.din("consts32", [128, c32m.shape[1]])
        d_c16 = self.din("consts16", [128, c16m.shape[1]])
        d_out = nc.dram_tensor("outT", [NSEQ, 128, NCH, T], F32, kind="ExternalOutput").ap()
        self.d = dict(win=d_win, wout=d_wout, ff1=d_ff1, ff2=d_ff2)

        self.arena = nc.alloc_sbuf_tensor("arena", [128, AR_EL], BF16)

        c32 = self.sb("c32", [128, c32m.shape[1]], F32)
        c16 = self.sb("c16", [128, c16m.shape[1]], BF16)
        S.dma("sp", c32.t[:], d_c32, writes=c32.tl())
        S.dma("pool", c16.t[:], d_c16, writes=c16.tl())
        self.c32, self.c16 = c32, c16
        self.C32 = lambda n: c32.t[:, n32.index(n) * 128:(n32.index(n) + 1) * 128]
        self.C16 = lambda n: c16.t[:, n16.index(n) * 128:(n16.index(n) + 1) * 128]

        prm = self.sb("prm", [128, 512], F32)
        PT = prm.tl()
        self.prm, self.PT = prm, PT
        o = [0]

        def pslot(n):
            r = (o[0], o[0] + n)
            o[0] += n
            return r
        s_bada, s_nmix, s_nmlp = pslot(L * 48), pslot(L * NCH), pslot(L * NCH)
        s_sbq, s_sbk, s_conv, s_dno, s_gate = pslot(L), pslot(L), pslot(L * 48), pslot(L), pslot(L * 8)
        assert o[0] <= 512

        def pv(s, *shape):
            ap = prm.t[:, s[0]:s[1]]
            if len(shape) == 2:
                ap = ap.rearrange("p (a b) -> p a b", a=shape[0], b=shape[1])
            elif len(shape) == 3:
                ap = ap.rearrange("p (a b c) -> p a b c", a=shape[0], b=shape[1], c=shape[2])
            return ap
        self.pv = pv
        self.slots = dict(sbq=s_sbq, sbk=s_sbk, conv=s_conv, dno=s_dno, gate=s_gate)
        S.dma("sp", pv(s_bada, L, 48), d_bada, writes=PT)
        S.dma("sp", pv(s_nmix, L, NCH), d_nmix, writes=PT)
        S.dma("sp", pv(s_nmlp, L, NCH), d_nmlp, writes=PT)
        S.dma("sp", prm.t[:, s_sbq[0]:s_sbq[1]], d_sbq, writes=PT)
        S.dma("sp", prm.t[:, s_sbk[0]:s_sbk[1]], d_sbk, writes=PT)
        S.dma("sp", pv(s_conv, L, 12, 4), d_conv, writes=PT)
        S.dma("sp", prm.t[:, s_dno[0]:s_dno[1]], d_dno, writes=PT)
        S.dma("sp", pv(s_gate, L, 8), d_gate, writes=PT)
        nexpA = self.sb("nexpA", [128, L, 4], F32)
        self.nexpA = nexpA
        S.op("act", lambda e: e.activation(nexpA.t[:], pv(s_gate, L, 8)[:, :, 0:4], AF.Exp), reads=PT, writes=nexpA.tl())
        S.op("dve", lambda e: e.tensor_scalar(nexpA.t[:], nexpA.t[:], -1.0, None, ALU.mult), reads=nexpA.tl(), writes=nexpA.tl())

        self.PS = PS = [Buf(nc.alloc_psum_tensor(f"ps{i}", [128, 512], F32), f"ps{i}") for i in range(8)]
        for b in PS:
            b.T.psum = True

        cT = self.sb("cT", [128, NCH, 2], F32)
        ctmp = self.sb("ctmp", [128, NCH, 2], F32)
        cond = self.sb("cond", [128, NCH, 2], BF16)
        S.dma("sp", cT.t[:], d_cT, writes=cT.tl())
        S.op("act", lambda e: e.activation(ctmp.t[:], cT.t[:], AF.Exp, scale=-1.0), reads=cT.tl(), writes=ctmp.tl())
        S.op("dve", lambda e: e.tensor_scalar(ctmp.t[:], ctmp.t[:], 1.0, None, ALU.add), reads=ctmp.tl(), writes=ctmp.tl())
        S.op("dve", lambda e: e.reciprocal(ctmp.t[:], ctmp.t[:]), reads=ctmp.tl(), writes=ctmp.tl())
        S.op("dve", lambda e: e.tensor_tensor(cond.t[:], cT.t[:], ctmp.t[:], ALU.mult), reads=cT.tl() + ctmp.tl(), writes=cond.tl())
        mod = self.sb("mod", [128, L, 48, 2], F32)
        self.mod = mod
        wada = [self.av("setup", f"wada{i}", [128, NCH, 512], BF16) for i in range(2)]

        def ada_piece(l, pc):
            wb = self.ring(wada, "wada")
            S.dma("pool", wb.t[:], d_wada[l, :, :, pc * 512:(pc + 1) * 512], writes=wb.tl())
            for jj in range(4):
                j = pc * 4 + jj
                for kc in range(NCH):
                    S.op("pe", lambda e, jj=jj, kc=kc, j=j: e.matmul(
                        PS[0].t[:, j * 2:j * 2 + 2], wb.t[:, kc, jj * 128:(jj + 1) * 128], cond.t[:, kc, :],
                        start=(kc == 0), stop=(kc == NCH - 1)), reads=wb.tl() + cond.tl(), writes=PS[0].tl())

        def ada_evac(l, b):
            S.op("dve", lambda e: e.tensor_tensor(
                mod.t[:, l, :, b], PS[0].t[:, 0:96].rearrange("p (j b) -> p j b", b=2)[:, :, b],
                pv(s_bada, L, 48)[:, l, :], ALU.add), reads=PS[0].tl() + PT, writes=mod.tl())
        for l in range(L):
            for pc in range(12):
                ada_piece(l, pc)
            for b in range(2):
                ada_evac(l, b)
        gains = self.sb("gains", [128, L, 2, NCH, 2], F32)
        self.gains = gains

        def gain_op(l, which, sl, m, b):
            S.op("dve", lambda e: e.scalar_tensor_tensor(
                gains.t[:, l, which, :, b], mod.t[:, l, m * 8:(m + 1) * 8, b], 1.0, pv(sl, L, NCH)[:, l, :], ALU.add, ALU.mult),
                reads=mod.tl() + PT, writes=gains.tl())
        for l in range(L):
            for which, (sl, m) in enumerate(((s_nmix, 1), (s_nmlp, 4))):
                for b in range(2):
                    gain_op(l, which, sl, m, b)
        self.dump("mod", mod.t[:], mod.tl(), [128, L, 48, 2])

        self.xT = self.sb("xT", [128, NCH, T], F32, grid=(NCH, NT))
        self.hT = self.sb("hT", [128, NCH, T], BF16, grid=(NCH, NT))
        self.oT = self.sb("oT", [128, NCH, T], BF16, grid=(NCH, NT))
        self.sqb = [self.sb(f"sqb{i}", [128, 512], BF16) for i in range(2)]
        self.wring = [self.sb(f"wring{i}", [128, NCH, 128], BF16) for i in range(3)]
        self.alloc_phases()

        xT = self.xT
        self.cur = "setup"
        for s in range(NSEQ):
            for c in range(NCH):
                S.dma("sp", xT.t[:, c, :], d_xT[s, :, c, :], writes=xT.tl(c))
            for l in range(L):
                self.layer(l, s)
            for c in range(NCH):
                self.final.append(S.dma("sp", d_out[s, :, c, :], xT.t[:, c, :], reads=xT.tl(c)))
        self.stats = S.emit(final_wait_ops=self.final)

    def switch(self, new, extra=()):
        self.barrier(self.cur, new, extra=extra)
        self.cur = new

    def alloc_phases(self):
        T, NB, NT = self.Tn, self.NB, self.NT
        av = self.av
        self.qa = av("sb", "qa", [128, 2, T], BF16, grid=(2, NT))
        self.ka = av("sb", "ka", [128, 2, T], BF16, grid=(2, NT))
        self.va = av("sb", "va", [128, NB, 256], BF16, grid=(NB,))
        self.wv = av("sb", "wv", [128, NCH, 256], BF16)
        self.a_e = [av("sb", f"a_e{i}", [128, 512], F32) for i in range(3)]
        self.a_x = [av("sb", f"a_x{i}", [128, 512], F32) for i in range(2)]
        self.a_sp = [av("sb", f"a_sp{i}", [128, 512], BF16) for i in range(2)]
        self.a_att = [av("sb", f"a_att{i}", [128, 512], BF16) for i in range(2)]
        self.a_R = [av("sb", f"a_R{i}", [1, 512], BF16) for i in range(2)]
        self.raw32 = [av("sb", f"raw32_{i}", [128, 512], F32) for i in range(2)]
        g = "gdn"
        self.raw = av(g, "raw", [128, T + 4], F32)
        self.CH = min(T, 1024)
        self.cacc = av(g, "cacc", [128, self.CH], F32)
        self.tA = av(g, "tA", [128, self.CH], F32)
        self.kT = av(g, "kT", [128, T], BF16)
        self.qT = av(g, "qT", [128, T], BF16)
        self.vT = av(g, "vT", [128, T], BF16)
        self.zs = av(g, "zs", [128, T], BF16)
        self.gtok = av(g, "gtok", [128, NB, 8], F32)
        self.gcrc = av(g, "gcrc", [128, NB, 8], F32)
        self.gsc = av(g, "gsc", [128, NB, 8], F32)
        self.wgate = av(g, "wgate", [128, NCH, 8], BF16)

        def mk(n, k, dt):
            return [av(g, f"{n}{i}", [128, 128], dt) for i in range(k)]
        self.g_ebc, self.g_tI, self.g_DL, self.g_DLs = mk("ebc", 3, F32), mk("tI", 1, F32), mk("DL", 2, F32), mk("DLs", 1, F32)
        self.g_L32, self.g_Lc, self.g_Uc = mk("L32", 2, F32), mk("Lc", 4, F32), mk("Uc", 4, F32)
        self.g_Xc, self.g_Yc = mk("Xc", 4, F32), mk("Yc", 4, F32)
        self.g_C32, self.g_C32T, self.g_C64 = mk("C32", 2, BF16), mk("C32T", 2, BF16), mk("C64", 2, BF16)
        self.g_T16T, self.g_T16, self.g_Ya, self.g_Yb = mk("T16T", 2, BF16), mk("T16", 2, BF16), mk("Ya", 2, BF16), mk("Yb", 2, BF16)
        self.g_T32T, self.g_T32, self.g_Yd, self.g_TT = mk("T32T", 2, BF16), mk("T32", 2, BF16), mk("Yd", 2, BF16), mk("TT", 2, BF16)
        self.g_A, self.g_AT, self.g_kbg, self.g_kd = mk("A", 2, BF16), mk("AT", 3, BF16), mk("kbg", 2, BF16), mk("kd", 3, BF16)
        self.g_vb, self.g_u, self.g_wT = mk("vb", 2, BF16), mk("u", 3, F32), mk("wT", 3, BF16)
        self.g_qg, self.g_vn = mk("qg", 3, BF16), mk("vn", 2, BF16)
        self.S32 = av(g, "S32", [128, 128], F32)
        self.Sbf = av(g, "Sbf", [128, 128], BF16)
        self.wo = [av("op", f"wo{i}", [128, NCH, 512], BF16) for i in range(2)]
        self.HT = min(1024, T)
        tph = self.HT // 512
        self.hidA = av("mlp", "hidA", [128, 16, self.HT], BF16, grid=(16, tph))
        self.w2 = [av("mlp", f"w2_{i}", [128, 32, 128], BF16) for i in range(2)]
        self.relu = [av("mlp", f"relu{i}", [128, 512], F32) for i in range(2)]
        if T == SEQ:
            ap = self.oT.t[:].rearrange("p c t -> p (c t)").rearrange("p (a b) -> p a b", a=16, b=self.HT)
            self.hidB = Buf(ap, "hidB", grid=(16, tph))
        else:
            self.hidB = self.sb("hidB", [128, 16, self.HT], BF16, grid=(16, tph))
        self.phase["mlp"]["tiles"].extend(self.hidB.tl())

    def hid(self, j):
        return (self.hidA, j) if j < 16 else (self.hidB, j - 16)

    def mmps(self):
        return self.ring([self.PS[0], self.PS[1]], "mmps")


    def norm_to_hT(self, l, s, which):
        S, NT, PS = self.S, self.NT, self.PS
        xT, hT, gains, mod = self.xT, self.hT, self.gains, self.mod
        msh = 0 if which == 0 else 3
        for tt in range(NT):
            ts = slice(tt * 512, (tt + 1) * 512)
            ps = self.mmps()
            for c in range(NCH):
                self._sq_mm(xT.t[:, c, ts], xT.tl(c, tt), ps, "ones", c == 0, c == NCH - 1, eng=("act" if c % 2 else "pool"))
            S.op("act", lambda e, ps=ps: e.activation(ps.t[:, :], ps.t[:, :], AF.Ln, bias=EPS, scale=1.0 / D), reads=ps.tl(), writes=ps.tl())
            S.op("act", lambda e, ps=ps: e.activation(ps.t[:, :], ps.t[:, :], AF.Exp, scale=-0.5), reads=ps.tl(), writes=ps.tl())
            for c in range(NCH):
                tmp = PS[2 + c % 2]
                S.op("dve", lambda e, c=c, ts=ts, ps=ps, tmp=tmp: e.tensor_tensor(tmp.t[:, :], xT.t[:, c, ts], ps.t[:, :], ALU.mult),
                     reads=xT.tl(c, tt) + ps.tl(), writes=tmp.tl())
                S.op("dve", lambda e, c=c, ts=ts, tmp=tmp: e.tensor_scalar(
                    hT.t[:, c, ts], tmp.t[:, :], gains.t[:, l, which, c, s:s + 1], mod.t[:, l, msh * 8 + c, s:s + 1], ALU.mult, ALU.add),
                    reads=tmp.tl() + gains.tl() + mod.tl(), writes=hT.tl(c, tt))

    def _sq_mm(self, src_ap, src_tiles, ps, ones_name, start, stop, eng="pool"):
        S = self.S
        sq = self.ring(self.sqb, "sqb")
        if eng == "act":
            S.op("act", lambda e: e.activation(sq.t[:], src_ap, AF.Square), reads=src_tiles, writes=sq.tl())
        else:
            S.op("pool", lambda e: e.tensor_tensor(sq.t[:], src_ap, src_ap, ALU.mult), reads=src_tiles, writes=sq.tl())
        S.op("pe", lambda e: e.matmul(ps.t[:, :], self.C16(ones_name), sq.t[:], start=start, stop=stop),
             reads=sq.tl() + self.c16.tl(), writes=ps.tl())

    def group_norm(self, src_ap, src_tiles, ones_name, nfeat, gain_ap, out_ap, out_tiles, bias2=0.0):
        S = self.S
        ps = self.mmps()
        self._sq_mm(src_ap, src_tiles, ps, ones_name, True, True, eng="act")
        sc = 1.0 if nfeat is None else 1.0 / nfeat
        S.op("act", lambda e: e.activation(ps.t[:, :], ps.t[:, :], AF.Ln, bias=EPS, scale=sc), reads=ps.tl(), writes=ps.tl())
        S.op("act", lambda e: e.activation(ps.t[:, :], ps.t[:, :], AF.Exp, scale=-0.5, bias=bias2), reads=ps.tl(), writes=ps.tl())
        if gain_ap is None:
            S.op("dve", lambda e: e.tensor_tensor(out_ap, src_ap, ps.t[:, :], ALU.mult), reads=src_tiles + ps.tl(), writes=out_tiles)
        else:
            S.op("dve", lambda e: e.scalar_tensor_tensor(out_ap, src_ap, gain_ap, ps.t[:, :], ALU.mult, ALU.mult),
                 reads=src_tiles + ps.tl() + self.PT, writes=out_tiles)

    def proj_chunk(self, l, col0, evac):
        S, hT = self.S, self.hT
        assert self.win_cols[self.win_stream.n] == col0
        wb = self.win_stream.next()
        for tt in range(self.NT):
            ts = slice(tt * 512, (tt + 1) * 512)
            ps = self.mmps()
            for kc in range(NCH):
                S.op("pe", lambda e, ps=ps, kc=kc, ts=ts: e.matmul(ps.t[:, :], wb.t[:, kc, :], hT.t[:, kc, ts],
                                                                  start=(kc == 0), stop=(kc == NCH - 1)),
                     reads=wb.tl() + hT.tl(kc, tt), writes=ps.tl())
            evac(tt, ps)

    def layer(self, l, s):
        T = self.Tn
        cols = []
        for half in range(2):
            for which in range(2):
                for cc in range(2):
                    cols.append(which * 512 + (half * 2 + cc) * 128)
        for hd in range(DN_H):
            for base in (1536, 2048, 2560, 3072):
                cols.append(base + hd * 128)
        self.win_cols = cols
        self.win_stream = WStream(self, self.wring, [self.d["win"][l, :, :, c0:c0 + 128] for c0 in cols], 2)
        if self.stop == "setup":
            return
        self.norm_to_hT(l, s, 0)
        self.dump(f"h1_{l}", self.hT.t[:], self.hT.tl(), [128, NCH, T], BF16)
        if self.stop == "norm":
            return
        self.switch("sb", extra=self.oT.tl())
        for half in range(2):
            self.sb_proj(l, half)
            if self.stop == "sbproj":
                return
            self.sb_attn(l, half)
        if self.stop == "attn":
            return
        self.switch("gdn")
        self.gdn(l)
        self.dump(f"oT_{l}", self.oT.t[:], self.oT.tl(), [128, NCH, T], BF16)
        if self.stop == "gdn":
            return
        self.switch("op")
        self.out_proj(l, s)
        self.dump(f"x1_{l}", self.xT.t[:], self.xT.tl(), [128, NCH, T])
        self.norm_to_hT(l, s, 1)
        self.switch("mlp", extra=self.oT.tl())
        self.mlp(l, s)
        self.dump(f"x2_{l}", self.xT.t[:], self.xT.tl(), [128, NCH, T])

    def sb_proj(self, l, half):
        S, NT, NB, hT = self.S, self.NT, self.NB, self.hT
        for which, dst, slot in ((0, self.qa, self.slots["sbq"]), (1, self.ka, self.slots["sbk"])):
            for cc in range(2):
                def evac(tt, ps, cc=cc, dst=dst, slot=slot):
                    r = self.raw32[tt % 2]
                    S.op("act", lambda e: e.activation(r.t[:], ps.t[:, :], AF.Copy), reads=ps.tl(), writes=r.tl())
                    self.group_norm(r.t[:], r.tl(), "blk64", SB_D, self.prm.t[:, slot[0] + l:slot[0] + l + 1],
                                    dst.t[:, cc, tt * 512:(tt + 1) * 512], dst.tl(cc, tt))
                self.proj_chunk(l, which * 512 + (half * 2 + cc) * 128, evac)
        wv, va = self.wv, self.va
        S.dma("pool", wv.t[:], self.d["win"][l, :, :, 1024 + half * 256:1024 + (half + 1) * 256], writes=wv.tl())

        def vblk(blk):
            ps = self.mmps()
            tt = blk // 4
            for kc in range(NCH):
                S.op("pe", lambda e, kc=kc: e.matmul(ps.t[:, 0:256], hT.t[:, kc, blk * 128:(blk + 1) * 128], wv.t[:, kc, :],
                                                     start=(kc == 0), stop=(kc == NCH - 1)),
                     reads=wv.tl() + hT.tl(kc, tt), writes=ps.tl())
            if blk % 2 == 0:
                S.op("dve", lambda e: e.tensor_copy(va.t[:, blk, :], ps.t[:, 0:256]), reads=ps.tl(), writes=va.tl(blk))
            else:
                S.op("act", lambda e: e.activation(va.t[:, blk, :], ps.t[:, 0:256], AF.Copy), reads=ps.tl(), writes=va.tl(blk))
        for blk in range(NB):
            vblk(blk)
        self.dump(f"qa_{l}_{half}", self.qa.t[:], self.qa.tl(), [128, 2, self.Tn], BF16)
        self.dump(f"ka_{l}_{half}", self.ka.t[:], self.ka.tl(), [128, 2, self.Tn], BF16)
        self.dump(f"va_{l}_{half}", self.va.t[:], self.va.tl(), [128, NB, 256], BF16)

    def sb_attn(self, l, half):
        S, NT = self.S, self.NT
        PS, C16 = self.PS, self.C16
        qa, ka, va, oT = self.qa, self.ka, self.va, self.oT
        c16t = self.c16.tl()
        scale = SB_D ** -0.5
        items = []
        for cc in range(2):
            for qt in range(NT):
                for kb in range(4 * qt + 3, -1, -1):
                    for hh in (2 * cc, 2 * cc + 1):
                        items.append((hh, qt, kb))
        n_it = len(items)
        grp = {n: items[n][0] % 2 for n in range(n_it)}

        def geom(n):
            hh, qt, kb = items[n]
            i = kb - 4 * qt
            c0 = 128 * i if i > 0 else 0
            return hh, qt, kb, i, c0, 512 - c0

        def s1(n):
            hh, qt, kb, i, c0, w = geom(n)
            cc, p0 = hh // 2, (hh % 2) * 64
            zp = PS[2 + n % 2]
            S.op("pe", lambda e: e.matmul(zp.t[:, 0:w], ka.t[p0:p0 + 64, cc, kb * 128:(kb + 1) * 128],
                                          qa.t[p0:p0 + 64, cc, qt * 512 + c0:(qt + 1) * 512], start=True, stop=True),
                 reads=ka.tl(cc, kb // 4) + qa.tl(cc, qt), writes=zp.tl())

        import os
        nfill = int(os.environ.get("FILL", "0"))

        def fill():
            for _ in range(nfill):
                S.op("pe", lambda e: e.matmul(PS[0].t[:, :], C16("ones"), self.c16.t[:, 0:512], start=True, stop=True),
                     reads=c16t, writes=PS[0].tl())

        def s2(n):
            hh, qt, kb, i, c0, w = geom(n)
            zp = PS[2 + n % 2]
            eb, sp = self.a_e[n % 3], self.a_sp[n % 2]
            S.op("act", lambda e: e.activation(eb.t[:, 0:w], zp.t[:, 0:w], AF.Exp, scale=scale), reads=zp.tl(), writes=eb.tl())
            S.op("act", lambda e: e.activation(sp.t[:, 0:w], eb.t[:, 0:w], AF.Ln, bias=1.0), reads=eb.tl(), writes=sp.tl())
            if i >= 0:
                S.op("pool", lambda e: e.affine_select(sp.t[:, 0:128], sp.t[:, 0:128], [[1, 128]], ALU.is_gt, 0.0,
                                                       base=0, channel_multiplier=-1), reads=sp.tl(), writes=sp.tl())

        def s3(n):
            hh, qt, kb, i, c0, w = geom(n)
            cp = PS[4 + n % 2]
            sp = self.a_sp[n % 2]
            R = self.a_R[grp[n] % 2]
            first = (kb == 4 * qt + 3)
            S.op("pe", lambda e: e.matmul(cp.t[:, 0:w], C16("tril"), sp.t[:, 0:w], start=True, stop=first),
                 reads=sp.tl() + c16t, writes=cp.tl())
            if not first:
                r0 = 128 if i >= 0 else 0
                S.op("pe", lambda e: e.matmul(cp.t[:, r0:w], C16("ones")[0:1, :], R.t[0:1, c0 + r0:512], start=False, stop=True),
                     reads=R.tl() + c16t, writes=cp.tl())

        def s4(n):
            hh, qt, kb, i, c0, w = geom(n)
            cp = PS[4 + n % 2]
            eb, xb, at = self.a_e[n % 3], self.a_x[n % 2], self.a_att[n % 2]
            import os
            sub = os.environ.get("S4SUB", "abc")
            if "a" in sub:
                S.op("act", lambda e: e.activation(xb.t[:, 0:w], cp.t[:, 0:w], AF.Exp, scale=-1.0), reads=cp.tl(), writes=xb.tl())
            if kb > 0:
                R = self.a_R[grp[n] % 2]
                S.op("dve", lambda e: e.tensor_copy(R.t[0:1, c0:512], cp.t[0:1, 0:w]), reads=cp.tl(), writes=R.tl())
            if "b" in sub:
                S.op("dve", lambda e: e.tensor_tensor(at.t[:, 0:w], eb.t[:, 0:w], xb.t[:, 0:w], ALU.mult),
                     reads=eb.tl() + xb.tl(), writes=at.tl())
            if i >= 0 and "c" in sub:
                S.op("pool", lambda e: e.affine_select(at.t[:, 0:128], at.t[:, 0:128], [[1, 128]], ALU.is_gt, 0.0,
                                                       base=0, channel_multiplier=-1), reads=at.tl(), writes=at.tl())

        def s5(n):
            hh, qt, kb, i, c0, w = geom(n)
            cc, p0 = hh // 2, (hh % 2) * 64
            c = half * 2 + cc
            op_ = PS[6 + grp[n] % 2]
            at = self.a_att[n % 2]
            last = (kb == 0)
            first = (kb == 4 * qt + 3)
            vl = va.t[:, kb, cc * 128:(cc + 1) * 128]
            if i >= 0 and w > 128:
                S.op("pe", lambda e: e.matmul(op_.t[:, c0:c0 + 128], vl, at.t[:, 0:128], start=first, stop=False),
                     reads=at.tl() + va.tl(kb), writes=op_.tl())
                S.op("pe", lambda e: e.matmul(op_.t[:, c0 + 128:512], vl, at.t[:, 128:w], start=False, stop=last),
                     reads=at.tl() + va.tl(kb), writes=op_.tl())
            else:
                S.op("pe", lambda e: e.matmul(op_.t[:, c0:512], vl, at.t[:, 0:w], start=first, stop=last),
                     reads=at.tl() + va.tl(kb), writes=op_.tl())
            if last:
                S.op("dve", lambda e: e.tensor_copy(oT.t[p0:p0 + 64, c, qt * 512:(qt + 1) * 512], op_.t[p0:p0 + 64, :]),
                     reads=op_.tl(), writes=oT.tl(c, qt))

        import os
        stg = os.environ.get("ATT_STAGES", "12345")
        n_it = min(n_it, int(os.environ.get("ATT_ITEMS", n_it)))
        for n in range(n_it + 2):
            if n < n_it:
                if "1" in stg:
                    s1(n)
                if "2" in stg:
                    s2(n)
            if 0 <= n - 1 < n_it:
                if "3" in stg:
                    s3(n - 1)
                if "4" in stg:
                    s4(n - 1)
            if 0 <= n - 2 < n_it:
                if "5" in stg:
                    s5(n - 2)
            fill()

    def gdn(self, l):
        S, NT, NB, T = self.S, self.NT, self.NB, self.Tn
        PS, C32, hT = self.PS, self.C32, self.hT
        c32t = self.c32.tl()
        gtok, gcrc, gsc, wg = self.gtok, self.gcrc, self.gsc, self.wgate
        PT = self.PT
        gpar = self.pv(self.slots["gate"], self.L, 8)
        nexpA = self.nexpA
        S.dma("pool", wg.t[:], self.d["win"][l, :, :, 3584:3592], writes=wg.tl())
        ps = self.mmps()

        def gproj(blk):
            for kc in range(NCH):
                S.op("pe", lambda e, kc=kc: e.matmul(ps.t[:, blk * 8:blk * 8 + 8], hT.t[:, kc, blk * 128:(blk + 1) * 128], wg.t[:, kc, :],
                                                     start=(kc == 0), stop=(kc == NCH - 1)),
                     reads=wg.tl() + hT.tl(kc, blk // 4), writes=ps.tl())
        for blk in range(NB):
            gproj(blk)
        psv = ps.t[:, 0:NB * 8].rearrange("p (a b) -> p a b", b=8)

        def ghead(h):
            S.op("act", lambda e: e.activation(gtok.t[:, :, h], psv[:, :, h], AF.Exp, bias=gpar[:, l, 4 + h:5 + h]),
                 reads=ps.tl() + PT, writes=gtok.tl())
            S.op("act", lambda e: e.activation(gtok.t[:, :, h], gtok.t[:, :, h], AF.Ln, bias=1.0), reads=gtok.tl(), writes=gtok.tl())
            S.op("dve", lambda e: e.tensor_scalar(gtok.t[:, :, h], gtok.t[:, :, h], nexpA.t[:, l, h:h + 1], None, ALU.mult),
                 reads=gtok.tl() + nexpA.tl(), writes=gtok.tl())
        for h in range(DN_H):
            ghead(h)
        S.op("act", lambda e: e.activation(gtok.t[:, :, 4:8], psv[:, :, 4:8], AF.Exp, scale=-1.0), reads=ps.tl(), writes=gtok.tl())
        S.op("dve", lambda e: e.tensor_scalar(gtok.t[:, :, 4:8], gtok.t[:, :, 4:8], 1.0, None, ALU.add), reads=gtok.tl(), writes=gtok.tl())
        S.op("dve", lambda e: e.reciprocal(gtok.t[:, :, 4:8], gtok.t[:, :, 4:8]), reads=gtok.tl(), writes=gtok.tl())
        ps2 = self.mmps()

        def gcs(blk):
            S.op("pe", lambda e: e.matmul(ps2.t[:, blk * 8:blk * 8 + 4], C32("mincl"), gtok.t[:, blk, 0:4], start=True, stop=True),
                 reads=gtok.tl() + c32t, writes=ps2.tl())
            S.op("pe", lambda e: e.matmul(ps2.t[:, blk * 8 + 4:blk * 8 + 8], C32("mrev"), gtok.t[:, blk, 0:4], start=True, stop=True),
                 reads=gtok.tl() + c32t, writes=ps2.tl())
        for blk in range(NB):
            gcs(blk)
        S.op("dve", lambda e: e.tensor_copy(gcrc.t[:].rearrange("p a b -> p (a b)"), ps2.t[:, 0:NB * 8]), reads=ps2.tl(), writes=gcrc.tl())
        S.op("act", lambda e: e.activation(gsc.t[:], gcrc.t[:], AF.Exp), reads=gcrc.tl(), writes=gsc.tl())
        S.op("dve", lambda e: e.tensor_tensor(gsc.t[:, :, 0:4], gsc.t[:, :, 0:4], gtok.t[:, :, 4:8], ALU.mult),
             reads=gsc.tl() + gtok.tl(), writes=gsc.tl())
        self.dump(f"gtok_{l}", gtok.t[:], gtok.tl(), [128, NB, 8])
        self.dump(f"gcrc_{l}", gcrc.t[:], gcrc.tl(), [128, NB, 8])
        S.op("pool", lambda e: e.memset(self.raw.t[:, 0:3], 0.0), writes=self.raw.tl())
        for hd in range(DN_H):
            self.gdn_head(l, hd)

    def sigmoid_tA(self, src_ap, src_tiles):
        S, tA = self.S, self.tA
        S.op("act", lambda e: e.activation(tA.t[:], src_ap, AF.Exp, scale=-1.0), reads=src_tiles, writes=tA.tl())
        S.op("act", lambda e: e.activation(tA.t[:], tA.t[:], AF.Ln, bias=1.0), reads=tA.tl(), writes=tA.tl())
        S.op("act", lambda e: e.activation(tA.t[:], tA.t[:], AF.Exp, scale=-1.0), reads=tA.tl(), writes=tA.tl())

    def gdn_stream(self, l, hd, kind):
        S, NT, T = self.S, self.NT, self.Tn
        raw, acc, tA = self.raw, self.cacc, self.tA
        col0 = {"q": 1536, "k": 2048, "v": 2560, "z": 3072}[kind] + hd * 128

        def evac(tt, ps):
            if tt % 2 == 0:
                S.op("dve", lambda e: e.tensor_copy(raw.t[:, 3 + tt * 512:3 + (tt + 1) * 512], ps.t[:, :]), reads=ps.tl(), writes=raw.tl())
            else:
                S.op("act", lambda e: e.activation(raw.t[:, 3 + tt * 512:3 + (tt + 1) * 512], ps.t[:, :], AF.Copy), reads=ps.tl(), writes=raw.tl())
        self.proj_chunk(l, col0, evac)
        CH = self.CH
        cw = self.pv(self.slots["conv"], self.L, 12, 4)
        PT = self.PT
        j = {"q": 0, "k": 4, "v": 8, "z": 0}[kind] + hd
        b2 = math.log(DN_D ** -0.5) if kind == "q" else 0.0
        for hf in range(T // CH):
            h0 = hf * CH
            if kind == "z":
                self.sigmoid_tA(raw.t[:, 3 + h0:3 + h0 + CH], raw.tl())
                S.op("dve", lambda e, h0=h0: e.tensor_tensor(self.zs.t[:, h0:h0 + CH], raw.t[:, 3 + h0:3 + h0 + CH], tA.t[:], ALU.mult),
                     reads=raw.tl() + tA.tl(), writes=self.zs.tl())
                continue
            S.op("dve", lambda e, h0=h0: e.tensor_scalar(acc.t[:], raw.t[:, h0:h0 + CH], cw[:, l, j, 0:1], None, ALU.mult),
                 reads=raw.tl() + PT, writes=acc.tl())
            for k in range(1, 4):
                S.op("dve", lambda e, k=k, h0=h0: e.scalar_tensor_tensor(
                    acc.t[:], raw.t[:, h0 + k:h0 + k + CH], cw[:, l, j, k:k + 1], acc.t[:], ALU.mult, ALU.add),
                    reads=raw.tl() + acc.tl() + PT, writes=acc.tl())
            self.sigmoid_tA(acc.t[:], acc.tl())
            if kind == "v":
                S.op("dve", lambda e, h0=h0: e.tensor_tensor(self.vT.t[:, h0:h0 + CH], acc.t[:], tA.t[:], ALU.mult),
                     reads=acc.tl() + tA.tl(), writes=self.vT.tl())
                continue
            S.op("dve", lambda e: e.tensor_tensor(acc.t[:], acc.t[:], tA.t[:], ALU.mult), reads=acc.tl() + tA.tl(), writes=acc.tl())
            dst = self.qT if kind == "q" else self.kT
            for t2 in range(CH // 512):
                self.group_norm(acc.t[:, t2 * 512:(t2 + 1) * 512], acc.tl(), "ones", None, None,
                                dst.t[:, h0 + t2 * 512:h0 + (t2 + 1) * 512], dst.tl(), bias2=b2)

    def tri_inverse(self, L32, psr):
        S, ring, C32c, C16c = self.S, self.ring, self.C32, self.C16
        c32t = self.c32.tl()
        ident32 = C32c("ident")
        pu = psr()
        S.op("pe", lambda e: e.matmul(pu.t[:, 0:128], L32.t[:], ident32, start=True, stop=True), reads=L32.tl() + c32t, writes=pu.tl())
        L16, U16 = ring(self.g_Lc, "Lc"), ring(self.g_Uc, "Uc")
        LB, UB = ring(self.g_Lc, "Lc"), ring(self.g_Uc, "Uc")
        C32, C32T, C64 = ring(self.g_C32, "C32"), ring(self.g_C32T, "C32T"), ring(self.g_C64, "C64")
        X0, Y0 = ring(self.g_Xc, "Xc"), ring(self.g_Yc, "Yc")
        XB, YB = ring(self.g_Xc, "Xc"), ring(self.g_Yc, "Yc")
        S.op("dve", lambda e: e.tensor_tensor(L16.t[:], L32.t[:], C32c("m16"), ALU.mult), reads=L32.tl() + c32t, writes=L16.tl())
        S.op("dve", lambda e: e.tensor_tensor(C32.t[:], L32.t[:], C32c("m32"), ALU.mult), reads=L32.tl() + c32t, writes=C32.tl())
        S.op("pool", lambda e: e.tensor_tensor(C64.t[:], L32.t[:], C32c("m64"), ALU.mult), reads=L32.tl() + c32t, writes=C64.tl())
        S.op("dve", lambda e: e.tensor_tensor(U16.t[:], pu.t[:, 0:128], C32c("m16T"), ALU.mult), reads=pu.tl() + c32t, writes=U16.tl())
        S.op("dve", lambda e: e.tensor_tensor(C32T.t[:], pu.t[:, 0:128], C32c("m32T"), ALU.mult), reads=pu.tl() + c32t, writes=C32T.tl())
        S.op("pool", lambda e: e.tensor_tensor(Y0.t[:], ident32, L16.t[:], ALU.subtract), reads=L16.tl() + c32t, writes=Y0.tl())
        S.op("dve", lambda e: e.tensor_tensor(X0.t[:], ident32, U16.t[:], ALU.subtract), reads=U16.tl() + c32t, writes=X0.tl())
        Lk, Uk, Xk, Yk = L16, U16, X0, Y0
        first_call = not getattr(self, "_tri_dbg", False)
        self._tri_dbg = True
        if first_call:
            self.dump("dbgX0", Xk.t[:], Xk.tl(), [128, 128])
            self.dump("dbgU16", Uk.t[:], Uk.tl(), [128, 128])
            self.dump("dbgL16", Lk.t[:], Lk.tl(), [128, 128])
        T16T = T16 = None
        for k in range(3):
            last = (k == 2)
            pq = psr()
            S.op("pe", lambda e, Uk=Uk, Lk=Lk, pq=pq: e.matmul(pq.t[:, 0:128], Uk.t[:], Lk.t[:], start=True, stop=True),
                 reads=Uk.tl() + Lk.tl(), writes=pq.tl())
            S.op("pe", lambda e, Uk=Uk, Lk=Lk, pq=pq: e.matmul(pq.t[:, 128:256], Lk.t[:], Uk.t[:], start=True, stop=True),
                 reads=Uk.tl() + Lk.tl(), writes=pq.tl())
            Ln_, Un_ = (LB, UB) if k % 2 == 0 else (L16, U16)
            S.op("act", lambda e, Ln_=Ln_, pq=pq: e.activation(Ln_.t[:], pq.t[:, 0:128], AF.Copy), reads=pq.tl(), writes=Ln_.tl())
            S.op("dve", lambda e, Un_=Un_, pq=pq: e.tensor_copy(Un_.t[:], pq.t[:, 128:256]), reads=pq.tl(), writes=Un_.tl())
            S.op("pe", lambda e, Ln_=Ln_, Xk=Xk, pq=pq: e.matmul(pq.t[:, 256:384], Ln_.t[:], Xk.t[:], start=True, stop=True),
                 reads=Ln_.tl() + Xk.tl(), writes=pq.tl())
            S.op("pe", lambda e, Un_=Un_, Yk=Yk, pq=pq: e.matmul(pq.t[:, 384:512], Un_.t[:], Yk.t[:], start=True, stop=True),
                 reads=Un_.tl() + Yk.tl(), writes=pq.tl())
            if last:
                Xn, Yn = ring(self.g_T16T, "T16T"), ring(self.g_T16, "T16")
            else:
                Xn, Yn = (XB, YB) if k % 2 == 0 else (X0, Y0)
            S.op("dve", lambda e, Xn=Xn, Xk=Xk, pq=pq: e.tensor_tensor(Xn.t[:], pq.t[:, 256:384], Xk.t[:], ALU.add),
                 reads=pq.tl() + Xk.tl(), writes=Xn.tl())
            S.op("dve", lambda e, Yn=Yn, Yk=Yk, pq=pq: e.tensor_tensor(Yn.t[:], pq.t[:, 384:512], Yk.t[:], ALU.add),
                 reads=pq.tl() + Yk.tl(), writes=Yn.tl())
            Uk, Lk, Xk, Yk = Un_, Ln_, Xn, Yn
        T16T, T16 = Xk, Yk
        if first_call:
            self.dump("dbgT16T", T16T.t[:], T16T.tl(), [128, 128], BF16)
        pd = psr()
        S.op("pe", lambda e: e.matmul(pd.t[:, 0:128], C32.t[:], T16T.t[:], start=True, stop=True), reads=C32.tl() + T16T.tl(), writes=pd.tl())
        S.op("pe", lambda e: e.matmul(pd.t[:, 128:256], C32T.t[:], T16.t[:], start=True, stop=True), reads=C32T.tl() + T16.tl(), writes=pd.tl())
        Ya, Yb = ring(self.g_Ya, "Ya"), ring(self.g_Yb, "Yb")
        S.op("act", lambda e: e.activation(Ya.t[:], pd.t[:, 0:128], AF.Copy), reads=pd.tl(), writes=Ya.tl())
        S.op("dve", lambda e: e.tensor_copy(Yb.t[:], pd.t[:, 128:256]), reads=pd.tl(), writes=Yb.tl())
        pd2 = psr()
        S.op("pe", lambda e: e.matmul(pd2.t[:, 0:128], T16.t[:], Ya.t[:], start=True, stop=True), reads=T16.tl() + Ya.tl(), writes=pd2.tl())
        S.op("pe", lambda e: e.matmul(pd2.t[:, 128:256], T16T.t[:], Yb.t[:], start=True, stop=True), reads=T16T.tl() + Yb.tl(), writes=pd2.tl())
        T32T, T32 = ring(self.g_T32T, "T32T"), ring(self.g_T32, "T32")
        S.op("dve", lambda e: e.scalar_tensor_tensor(T32T.t[:], pd2.t[:, 0:128], -1.0, T16T.t[:], ALU.mult, ALU.add),
             reads=pd2.tl() + T16T.tl(), writes=T32T.tl())
        S.op("dve", lambda e: e.scalar_tensor_tensor(T32.t[:], pd2.t[:, 128:256], -1.0, T16.t[:], ALU.mult, ALU.add),
             reads=pd2.tl() + T16.tl(), writes=T32.tl())
        pe1 = psr()
        S.op("pe", lambda e: e.matmul(pe1.t[:, 0:128], C64.t[:], T32T.t[:], start=True, stop=True), reads=C64.tl() + T32T.tl(), writes=pe1.tl())
        Yd = ring(self.g_Yd, "Yd")
        S.op("act", lambda e: e.activation(Yd.t[:], pe1.t[:, 0:128], AF.Copy), reads=pe1.tl(), writes=Yd.tl())
        S.op("pe", lambda e: e.matmul(pe1.t[:, 128:256], T32.t[:], Yd.t[:], start=True, stop=True), reads=T32.tl() + Yd.tl(), writes=pe1.tl())
        TT = ring(self.g_TT, "TT")
        S.op("dve", lambda e: e.scalar_tensor_tensor(TT.t[:], pe1.t[:, 128:256], -1.0, T32T.t[:], ALU.mult, ALU.add),
             reads=pe1.tl() + T32T.tl(), writes=TT.tl())
        return TT

    def gdn_head(self, l, hd):
        S, NT, NB, T = self.S, self.NT, self.NB, self.Tn
        PS, C16, C32 = self.PS, self.C16, self.C32
        c16t, c32t = self.c16.tl(), self.c32.tl()
        kT, qT, vT, zs = self.kT, self.qT, self.vT, self.zs
        gtok, gcrc, gsc = self.gtok, self.gcrc, self.gsc
        ring = self.ring
        for kind in ("q", "k", "v", "z"):
            self.gdn_stream(l, hd, kind)
        self.dump(f"qT_{l}_{hd}", qT.t[:], qT.tl(), [128, T], BF16)
        self.dump(f"kT_{l}_{hd}", kT.t[:], kT.tl(), [128, T], BF16)
        self.dump(f"vT_{l}_{hd}", vT.t[:], vT.tl(), [128, T], BF16)
        self.dump(f"zs_{l}_{hd}", zs.t[:], zs.tl(), [128, T], BF16)

        S32, Sbf = self.S32, self.Sbf
        S.op("dve", lambda e: e.memset(S32.t[:], 0.0), writes=S32.tl())
        S.op("pool", lambda e: e.memset(Sbf.t[:], 0.0), writes=Sbf.tl())
        PB = [PS[2], PS[3], PS[4], PS[5]]
        ident16 = C16("ident")

        def psr():
            return ring(PB, "gps")
        prep_out = {}

        def prep(blk):
            bs = slice(blk * 128, (blk + 1) * 128)
            gcol = gtok.t[:, blk, hd:hd + 1]
            bcol = gtok.t[:, blk, 4 + hd:5 + hd]
            gccol = gcrc.t[:, blk, hd:hd + 1]
            bgcol = gsc.t[:, blk, hd:hd + 1]
            erccol = gsc.t[:, blk, 4 + hd:5 + hd]
            p1 = psr()
            S.op("pe", lambda e: e.matmul(p1.t[:, 0:128], gcol.to_broadcast([128, 128]), C32("mincl"), start=True, stop=True),
                 reads=gtok.tl() + c32t, writes=p1.tl())
            ebc, tI, DL, DLs = ring(self.g_ebc, "ebc"), ring(self.g_tI, "tI"), ring(self.g_DL, "DL"), ring(self.g_DLs, "DLs")
            S.op("act", lambda e: e.activation(ebc.t[:], p1.t[:, 0:128], AF.Exp), reads=p1.tl(), writes=ebc.tl())
            S.op("dve", lambda e: e.scalar_tensor_tensor(tI.t[:], p1.t[:, 0:128], gccol, C32("mbI"), ALU.subtract, ALU.add),
                 reads=p1.tl() + gcrc.tl() + c32t, writes=tI.tl())
            S.op("act", lambda e: e.activation(DL.t[:], tI.t[:], AF.Exp, scale=-1.0), reads=tI.tl(), writes=DL.tl())
            S.op("dve", lambda e: e.tensor_tensor(DLs.t[:], DL.t[:], C32("strict"), ALU.mult), reads=DL.tl() + c32t, writes=DLs.tl())
            p2 = psr()
            S.op("pe", lambda e: e.matmul(p2.t[:, 0:128], kT.t[:, bs], kT.t[:, bs], start=True, stop=True), reads=kT.tl(), writes=p2.tl())
            S.op("pe", lambda e: e.matmul(p2.t[:, 128:256], qT.t[:, bs], kT.t[:, bs], start=True, stop=True), reads=kT.tl() + qT.tl(), writes=p2.tl())
            L32, Ab = ring(self.g_L32, "L32"), ring(self.g_A, "A")
            S.op("dve", lambda e: e.scalar_tensor_tensor(L32.t[:], p2.t[:, 0:128], bcol, DLs.t[:], ALU.mult, ALU.mult),
                 reads=p2.tl() + gtok.tl() + DLs.tl(), writes=L32.tl())
            S.op("dve", lambda e: e.tensor_tensor(Ab.t[:], p2.t[:, 128:256], DL.t[:], ALU.mult), reads=p2.tl() + DL.tl(), writes=Ab.tl())
            p3 = psr()
            p3b = p3.t[:, :].bitcast(BF16)
            S.op("pe", lambda e: e.transpose(p3b[:, 128:256], Ab.t[:], ident16), reads=Ab.tl() + c16t, writes=p3.tl())
            S.op("pe", lambda e: e.transpose(p3b[:, 256:384], kT.t[:, bs], ident16), reads=kT.tl() + c16t, writes=p3.tl())
            S.op("pe", lambda e: e.transpose(p3b[:, 384:512], vT.t[:, bs], ident16), reads=vT.tl() + c16t, writes=p3.tl())
            AT = ring(self.g_AT, "AT")
            kbg, kd, vb = ring(self.g_kbg, "kbg"), ring(self.g_kd, "kd"), ring(self.g_vb, "vb")
            S.op("act", lambda e: e.activation(AT.t[:], p3b[:, 128:256], AF.Copy), reads=p3.tl(), writes=AT.tl())
            S.op("act", lambda e: e.activation(kbg.t[:], p3b[:, 256:384], AF.Copy, scale=bgcol), reads=p3.tl() + gsc.tl(), writes=kbg.tl())
            S.op("dve", lambda e: e.tensor_scalar(kd.t[:], p3b[:, 256:384], erccol, None, ALU.mult), reads=p3.tl() + gsc.tl(), writes=kd.tl())
            S.op("act", lambda e: e.activation(vb.t[:], p3b[:, 384:512], AF.Copy, scale=bcol), reads=p3.tl() + gtok.tl(), writes=vb.tl())
            qg = ring(self.g_qg, "qg")
            S.op("dve", lambda e: e.tensor_tensor(qg.t[:], qT.t[:, bs], ebc.t[:], ALU.mult), reads=qT.tl() + ebc.tl(), writes=qg.tl())
            TT = self.tri_inverse(L32, psr)
            if blk == 0:
                self.dump(f"L32_{l}_{hd}", L32.t[:], L32.tl(), [128, 128])
                self.dump(f"TT_{l}_{hd}", TT.t[:], TT.tl(), [128, 128], BF16)
            p4 = psr()
            S.op("pe", lambda e: e.matmul(p4.t[:, 0:128], TT.t[:], vb.t[:], start=True, stop=True), reads=TT.tl() + vb.tl(), writes=p4.tl())
            S.op("pe", lambda e: e.matmul(p4.t[:, 128:256], kbg.t[:], TT.t[:], start=True, stop=True), reads=TT.tl() + kbg.tl(), writes=p4.tl())
            u, wT = ring(self.g_u, "u"), ring(self.g_wT, "wT")
            S.op("dve", lambda e: e.tensor_copy(u.t[:], p4.t[:, 0:128]), reads=p4.tl(), writes=u.tl())
            S.op("act", lambda e: e.activation(wT.t[:], p4.t[:, 128:256], AF.Copy), reads=p4.tl(), writes=wT.tl())
            prep_out[blk] = (u, wT, kd, AT, qg, ebc)

        opsum, pw = PS[6], PS[7]

        def chunk(blk, ch, u, wT, kd, AT, qg, ebc):
            r0 = ch * 64
            rs = slice(r0, r0 + 64)
            cs = slice((blk % 4) * 128 + r0, (blk % 4) * 128 + r0 + 64)
            S.op("pe", lambda e: e.matmul(pw.t[:, 0:128], wT.t[:], Sbf.t[:], start=True, stop=True), reads=wT.tl() + Sbf.tl(), writes=pw.tl())
            vn = ring(self.g_vn, "vn")
            S.op("dve", lambda e: e.tensor_tensor(vn.t[rs, :], u.t[rs, :], pw.t[rs, 0:128], ALU.subtract), reads=u.tl() + pw.tl(), writes=vn.tl())
            S.op("pe", lambda e: e.matmul(opsum.t[:, cs], Sbf.t[:], qg.t[:, rs], start=True, stop=False), reads=Sbf.tl() + qg.tl(), writes=opsum.tl())
            S.op("pe", lambda e: e.matmul(opsum.t[:, cs], vn.t[rs, :], AT.t[rs, rs], start=False, stop=True), reads=vn.tl() + AT.tl(), writes=opsum.tl())
            S.op("pe", lambda e: e.matmul(pw.t[:, 128:256], kd.t[rs, :], vn.t[rs, :], start=True, stop=True), reads=kd.tl() + vn.tl(), writes=pw.tl())
            S.op("dve", lambda e: e.scalar_tensor_tensor(S32.t[:], S32.t[:], ebc.t[:, r0 + 63:r0 + 64], pw.t[:, 128:256], ALU.mult, ALU.add),
                 reads=S32.tl() + ebc.tl() + pw.tl(), writes=S32.tl())
            S.op("act", lambda e: e.activation(Sbf.t[:], S32.t[:], AF.Copy), reads=S32.tl(), writes=Sbf.tl())

        def finish(tt):
            ts = slice(tt * 512, (tt + 1) * 512)
            go = self.tA.t[:, 0:512]
            got = self.tA.tl()
            S.op("act", lambda e: e.activation(go, opsum.t[:, :], AF.Copy), reads=opsum.tl(), writes=got)
            dno = self.slots["dno"]
            self.group_norm(go, got, "ones", DN_D, self.prm.t[:, dno[0] + l:dno[0] + l + 1], go, got)
            S.op("dve", lambda e: e.tensor_tensor(self.oT.t[:, 4 + hd, ts], go, zs.t[:, ts], ALU.mult),
                 reads=got + zs.tl(), writes=self.oT.tl(4 + hd, tt))

        def chain(blk):
            args = prep_out.pop(blk)
            for ch in range(2):
                chunk(blk, ch, *args)
            if blk % 4 == 3:
                finish(blk // 4)

        halves = {}

        def cap_prep(b):
            if b < NB and b not in halves:
                lst = S.capture(lambda: prep(b))
                h = len(lst) // 2
                halves[b] = (lst[:h], lst[h:])
        cap_prep(0)
        cap_prep(1)
        S.merge(halves[0][0])
        S.merge(halves[0][1], halves[1][0] if NB > 1 else [])
        for blk in range(NB):
            cap_prep(blk + 2)
            la = halves[blk + 1][1] if blk + 1 < NB else []
            lc = halves[blk + 2][0] if blk + 2 < NB else []
            lb = S.capture(lambda: chain(blk))
            S.merge(la, lc, lb)

    def out_proj(self, l, s):
        S, NT = self.S, self.NT
        xT, oT, mod = self.xT, self.oT, self.mod

        def one(wo, cc, c, tt):
            ts = slice(tt * 512, (tt + 1) * 512)
            ps = self.mmps()
            for kc in range(NCH):
                S.op("pe", lambda e, kc=kc: e.matmul(ps.t[:, :], wo.t[:, kc, cc * 128:(cc + 1) * 128], oT.t[:, kc, ts],
                                                     start=(kc == 0), stop=(kc == NCH - 1)),
                     reads=wo.tl() + oT.tl(kc, tt), writes=ps.tl())
            S.op("dve", lambda e: e.scalar_tensor_tensor(xT.t[:, c, ts], ps.t[:, :], mod.t[:, l, 16 + c, s:s + 1], xT.t[:, c, ts], ALU.mult, ALU.add),
                 reads=ps.tl() + mod.tl() + xT.tl(c, tt), writes=xT.tl(c, tt))
        for half in range(2):
            wo = self.wo[half]
            S.dma("pool", wo.t[:], self.d["wout"][l, :, :, half * 512:(half + 1) * 512], writes=wo.tl())
        for half in range(2):
            for cc in range(4):
                for tt in range(NT):
                    one(self.wo[half], cc, half * 4 + cc, tt)

    def mlp(self, l, s):
        S, NT, HT = self.S, self.NT, self.HT
        xT, hT, mod = self.xT, self.hT, self.mod
        nh = self.Tn // HT
        tph = HT // 512

        def ff1(wb, j, tt, t2):
            ts = slice(tt * 512, (tt + 1) * 512)
            ps = self.mmps()
            for kc in range(NCH):
                S.op("pe", lambda e, kc=kc: e.matmul(ps.t[:, :], wb.t[:, kc, :], hT.t[:, kc, ts], start=(kc == 0), stop=(kc == NCH - 1)),
                     reads=wb.tl() + hT.tl(kc, tt), writes=ps.tl())
            rl = self.ring(self.relu, "relu")
            hb, jj = self.hid(j)
            S.op("act", lambda e: e.activation(rl.t[:], ps.t[:, :], AF.Relu), reads=ps.tl(), writes=rl.tl())
            S.op("pool", lambda e: e.tensor_tensor(hb.t[:, jj, t2 * 512:(t2 + 1) * 512], rl.t[:], rl.t[:], ALU.mult),
                 reads=rl.tl(), writes=hb.tl(jj, t2))

        def ff2(w2, c, tt, t2):
            ts = slice(tt * 512, (tt + 1) * 512)
            ps = self.mmps()
            for j in range(32):
                hb, jj = self.hid(j)
                S.op("pe", lambda e, j=j, hb=hb, jj=jj: e.matmul(ps.t[:, :], w2.t[:, j, :], hb.t[:, jj, t2 * 512:(t2 + 1) * 512],
                                                                start=(j == 0), stop=(j == 31)),
                     reads=w2.tl() + hb.tl(jj, t2), writes=ps.tl())
            S.op("dve", lambda e: e.scalar_tensor_tensor(xT.t[:, c, ts], ps.t[:, :], mod.t[:, l, 40 + c, s:s + 1], xT.t[:, c, ts], ALU.mult, ALU.add),
                 reads=ps.tl() + mod.tl() + xT.tl(c, tt), writes=xT.tl(c, tt))

        s1 = WStream(self, self.wring, [self.d["ff1"][l, :, :, j * 128:(j + 1) * 128] for hf in range(nh) for j in range(32)], 2)
        s2 = WStream(self, self.w2, [self.d["ff2"][l, :, :, c * 128:(c + 1) * 128] for hf in range(nh) for c in range(NCH)], 1)
        for hf in range(nh):
            for j in range(32):
                wb = s1.next()
                for t2 in range(tph):
                    ff1(wb, j, hf * tph + t2, t2)
            for c in range(NCH):
                w2 = s2.next()
                for t2 in range(tph):
                    ff2(w2, c, hf * tph + t2, t2)


def _chunked(w, L):
    Lk, K, N = w.shape
    return np.ascontiguousarray(w.reshape(Lk, K // 128, 128, N).transpose(0, 2, 1, 3))


def prep_shared(inp, L):
    f = lambda a: np.ascontiguousarray(np.asarray(a, dtype=np.float32))
    sh = {}
    sh["w_ada"] = _chunked(f(inp["w_ada"])[:L], L)
    sh["b_ada"] = f(f(inp["b_ada"])[:L].reshape(L, 48, 128).transpose(2, 0, 1))
    sh["norm_mix"] = f(f(inp["norm_mix"])[:L].reshape(L, NCH, 128).transpose(2, 0, 1))
    sh["norm_mlp"] = f(f(inp["norm_mlp"])[:L].reshape(L, NCH, 128).transpose(2, 0, 1))
    sh["w_in"] = _chunked(f(inp["w_in"])[:L], L)
    sh["sb_q_norm"] = f(np.tile(f(inp["sb_q_norm"])[:L], (1, 2)).T)
    sh["sb_k_norm"] = f(np.tile(f(inp["sb_k_norm"])[:L], (1, 2)).T)
    sh["conv_w"] = f(f(inp["conv_w"])[:L].reshape(L, 4, 12, 128).transpose(3, 0, 2, 1))
    gp = np.concatenate([f(inp["a_log"])[:L], f(inp["dt_bias"])[:L]], axis=1)
    sh["gate_p"] = f(np.broadcast_to(gp[None], (128, L, 8)))
    sh["dn_out_norm"] = f(f(inp["dn_out_norm"])[:L].T)
    sh["w_out"] = _chunked(f(inp["w_out"])[:L], L)
    sh["w_ff1"] = _chunked(f(inp["w_ff1"])[:L], L)
    sh["w_ff2"] = _chunked(f(inp["w_ff2"])[:L], L)
    cc = _consts()
    sh["consts32"], sh["consts16"] = cc[1], cc[3]
    return sh


def prep_core(x2, c2, T):
    ns = x2.shape[0]
    xT = np.ascontiguousarray(np.asarray(x2, np.float32).reshape(ns, T, NCH, 128).transpose(0, 3, 2, 1))
    cT = np.ascontiguousarray(np.asarray(c2, np.float32).reshape(2, NCH, 128).transpose(2, 1, 0))
    return {"xT": xT, "cT": cT}


_CACHE = {}


def kernel(**inputs):
    x = np.asarray(inputs["x"], np.float32)
    c = np.asarray(inputs["c"], np.float32)
    B, T, _ = x.shape
    key = (T, DEPTH, 2)
    if key not in _CACHE:
        _CACHE[key] = Builder(T=T, L=DEPTH, NSEQ=2)
    bld = _CACHE[key]
    sh = prep_shared(inputs, DEPTH)
    in_maps = []
    for core in range(NCORES):
        m = dict(sh)
        m.update(prep_core(x[2 * core:2 * core + 2], c[2 * core:2 * core + 2], T))
        in_maps.append(m)
    res = run_bass_kernel_spmd(bld.nc, in_maps, core_ids=list(range(NCORES)))
    out = np.empty((B, T, D), np.float32)
    for core in range(NCORES):
        oT = res.results[core]["outT"]
        out[2 * core:2 * core + 2] = oT.transpose(0, 3, 2, 1).reshape(2, T, D)
    return out
```

```python
import math
import numpy as np
import concourse.bass as bass
import concourse.mybir as mybir
from concourse.bass_utils import run_bass_kernel_spmd

F32 = mybir.dt.float32
BF16 = mybir.dt.bfloat16
AF = mybir.ActivationFunctionType
ALU = mybir.AluOpType

D = 1024
NCH = 8
DEPTH = 4
SEQ = 2048
NCORES = 8
SB_H, SB_D = 8, 64
DN_H, DN_D = 4, 128
IN_W = 3592
DFF = 4096
EPS = 1e-6
ENGS = ("pe", "act", "dve", "pool", "sp")


class Tile:
    __slots__ = ("name", "lw", "rd", "sem", "semcnt", "psum")

    def __init__(self, name):
        self.name = name
        self.psum = False
        self.lw = None
        self.rd = []
        self.sem = None
        self.semcnt = 0


class Op:
    __slots__ = ("eng", "fn", "deps", "signal", "sigval", "dma", "dsem", "dval")

    def __init__(self, eng, fn):
        self.eng = eng
        self.fn = fn
        self.deps = []
        self.signal = False
        self.sigval = 0
        self.dma = False
        self.dsem = None
        self.dval = 0


class Sched:
    def __init__(self, nc):
        self.nc = nc
        self.ops = {e: [] for e in ENGS}
        self.ndsem = 0

    def _add(self, op, reads, writes):
        deps = []
        for t in reads:
            if t.lw is not None:
                deps.append(t.lw)
        for t in writes:
            if t.lw is not None:
                deps.append(t.lw)
            deps.extend(t.rd)
        for t in reads:
            t.rd.append(op)
        for t in writes:
            t.lw = op
            t.rd = []
        seen = set()
        for d in deps:
            if d is op or id(d) in seen:
                continue
            seen.add(id(d))
            if (not d.dma) and (not op.dma) and d.eng == "pe" and op.eng == "pe":
                continue
            op.deps.append(d)
            d.signal = True
        self.ops[op.eng].append(op)
        return op

    def op(self, eng, fn, reads=(), writes=()):
        if getattr(self, "cap", None) is not None:
            self.cap.append((eng, fn, list(reads), list(writes)))
            return None
        reads, writes = list(reads), list(writes)
        pr = [t for t in reads if t.psum]
        if pr:
            reads = [t for t in reads if not t.psum]
            writes = writes + [t for t in pr if t not in writes]
        return self._add(Op(eng, fn), reads, writes)

    def dma(self, eng, out, in_, reads=(), writes=()):
        o = Op(eng, None)
        o.dma = True
        tiles = list(writes) + list(reads)
        st = tiles[0]
        if st.sem is None:
            st.sem = self.nc.alloc_semaphore(f"dsem{self.ndsem}")
            self.ndsem += 1
        st.semcnt += 16
        o.dsem = st.sem
        o.dval = st.semcnt
        o.signal = True
        o.fn = lambda e, out=out, in_=in_: e.dma_start(out=out, in_=in_)
        return self._add(o, [], tiles)

    def capture(self, f):
        assert getattr(self, "cap", None) is None
        self.cap = []
        f()
        lst, self.cap = self.cap, None
        return lst

    def merge(self, *lists):
        idx = [0] * len(lists)
        total = sum(len(x) for x in lists)
        for _ in range(total):
            best, bf = None, None
            for k, lst in enumerate(lists):
                if idx[k] < len(lst):
                    fr = idx[k] / len(lst)
                    if bf is None or fr < bf:
                        best, bf = k, fr
            eng, fn, reads, writes = lists[best][idx[best]]
            idx[best] += 1
            self.op(eng, fn, reads, writes)

    def emit(self, final_wait_ops=()):
        nc = self.nc
        esem = {e: nc.alloc_semaphore(f"sem_{e}") for e in ENGS}
        for e in ENGS:
            c = 0
            for o in self.ops[e]:
                if (not o.dma) and o.signal:
                    c += 1
                    o.sigval = c
        stats = {}

        def run(e, eng):
            known = {}
            nw = 0
            for o in self.ops[e]:
                for d in o.deps:
                    if d.dma:
                        key, val = d.dsem, d.dval
                    else:
                        key, val = esem[d.eng], d.sigval
                    if known.get(key.num, 0) >= val:
                        continue
                    known[key.num] = val
                    eng.wait_ge(key, val)
                    nw += 1
                ins = o.fn(eng)
                if o.dma:
                    ins.then_inc(o.dsem, 16)
                elif o.signal:
                    ins.then_inc(esem[e], 1)
            if e == "sp":
                for o in final_wait_ops:
                    if o.dma:
                        eng.wait_ge(o.dsem, o.dval)
                    else:
                        eng.wait_ge(esem[o.eng], o.sigval)
            stats[e] = (len(self.ops[e]), nw)

        with nc.Block() as block:
            @block.tensor
            def _(eng):
                run("pe", eng)

            @block.scalar
            def _(eng):
                run("act", eng)

            @block.vector
            def _(eng):
                run("dve", eng)

            @block.gpsimd
            def _(eng):
                run("pool", eng)

            @block.sync
            def _(eng):
                run("sp", eng)
        return stats


class Buf:
    def __init__(self, t, name, grid=None):
        self.t = t
        if grid is None:
            self.T = Tile(name)
        else:
            self.T = np.empty(grid, dtype=object)
            for idx in np.ndindex(*grid):
                self.T[idx] = Tile(f"{name}{idx}")

    def tl(self, *idx):
        if isinstance(self.T, Tile):
            return [self.T]
        sub = self.T[idx] if idx else self.T
        if isinstance(sub, Tile):
            return [sub]
        return list(sub.ravel())


def _consts():
    i = np.arange(128)
    same = (i[:, None] // 64) == (i[None, :] // 64)
    c = {}
    c["ident"] = np.eye(128, dtype=np.float32)
    c["ones"] = np.ones((128, 128), np.float32)
    blk = np.zeros((128, 128), np.float32)
    blk[:64, :64] = 1.0
    blk[64:, 64:] = 1.0
    c["blk64"] = blk
    c["tril"] = (i[:, None] >= i[None, :]).astype(np.float32)
    c["mincl"] = ((i[:, None] <= i[None, :]) & same).astype(np.float32)
    c["mrev"] = ((i[:, None] > i[None, :]) & same).astype(np.float32)
    c["mbI"] = np.where((i[None, :] <= i[:, None]) & same, 0.0, 30000.0).astype(np.float32)
    c["strict"] = (i[None, :] < i[:, None]).astype(np.float32)
    b16 = (i[:, None] // 16) == (i[None, :] // 16)
    b32 = (i[:, None] // 32) == (i[None, :] // 32)
    low = i[None, :] < i[:, None]
    c["m16"] = (b16 & low).astype(np.float32)
    c["m32"] = (b32 & ~b16 & low).astype(np.float32)
    c["m64"] = (same & ~b32 & low).astype(np.float32)
    c["m16T"] = np.ascontiguousarray(c["m16"].T)
    c["m32T"] = np.ascontiguousarray(c["m32"].T)
    n32 = ["ident", "mincl", "mrev", "mbI", "strict", "m16", "m32", "m64", "m16T", "m32T"]
    n16 = ["ident", "ones", "blk64", "tril"]
    return (n32, np.concatenate([c[n] for n in n32], axis=1), n16, np.concatenate([c[n] for n in n16], axis=1))


class WStream:
    def __init__(self, bld, bufs, srcs, depth):
        self.b, self.bufs, self.srcs, self.depth = bld, bufs, list(srcs), depth
        assert len(bufs) > depth
        self.n = 0
        self.issued = 0
        self.live = []
        for _ in range(min(depth, len(self.srcs))):
            self._issue()

    def _issue(self):
        buf = self.bufs[self.issued % len(self.bufs)]
        self.b.S.dma("pool", buf.t[:], self.srcs[self.issued], writes=buf.tl())
        self.live.append(buf)
        self.issued += 1

    def next(self):
        buf = self.live[self.n]
        self.n += 1
        if self.issued < len(self.srcs):
            self._issue()
        return buf


AR_EL = 30208


class Builder:
    def __init__(self, T=SEQ, L=DEPTH, NSEQ=2, dbg=(), stop=None):
        self.stop = stop
        self.Tn = T
        self.L = L
        self.NSEQ = NSEQ
        self.NT = T // 512
        self.NB = T // 128
        self.dbg_names = dbg
        nc = self.nc = bass.Bass("TRN2", target_bir_lowering=False)
        self.S = Sched(nc)
        self.cnt = {}
        self.final = []
        self.phase = {}
        self.build()

    def sb(self, name, shape, dt, grid=None):
        return Buf(self.nc.alloc_sbuf_tensor("s_" + name, list(shape), dt), name, grid)

    def av(self, phase, name, shape, dt, grid=None, base=None):
        ph = self.phase.setdefault(phase, {"off": 0, "tiles": []})
        nel = int(np.prod(shape[1:]))
        nbf = nel * (2 if dt == F32 else 1)
        off = (ph["off"] + 15) // 16 * 16
        ph["off"] = off + nbf
        assert ph["off"] <= AR_EL, (phase, name, ph["off"])
        ap = self.arena[:, off:off + nbf]
        if dt == F32:
            ap = ap.bitcast(F32)
        if len(shape) == 3:
            ap = ap.rearrange("p (a b) -> p a b", a=shape[1], b=shape[2])
        if shape[0] < 128:
            ap = ap[0:shape[0]]
        b = Buf(ap, name, grid)
        ph["tiles"].extend(b.tl())
        return b

    def ring(self, lst, key):
        v = self.cnt.get(key, 0)
        self.cnt[key] = v + 1
        return lst[v % len(lst)]

    def barrier(self, *phases, extra=()):
        tiles = list(extra)
        for p in phases:
            tiles.extend(self.phase[p]["tiles"])
        self.S.op("sp", lambda e: e.nop(), writes=tiles)

    def din(self, name, shape, dt=F32):
        return self.nc.dram_tensor(name, list(shape), dt, kind="ExternalInput").ap()

    def dump(self, name, ap, tiles, shape, dt=F32):
        if name not in self.dbg_names:
            return
        d = self.nc.dram_tensor("dbg_" + name, list(shape), dt, kind="ExternalOutput").ap()
        self.final.append(self.S.dma("sp", d, ap, reads=tiles))

    def build(self):
        nc, S, T, L, NSEQ, NT, NB = self.nc, self.S, self.Tn, self.L, self.NSEQ, self.NT, self.NB
        d_xT = self.din("xT", [NSEQ, 128, NCH, T])
        d_cT = self.din("cT", [128, NCH, 2])
        d_wada = self.din("w_ada", [L, 128, NCH, 6 * D])
        d_bada = self.din("b_ada", [128, L, 48])
        d_nmix = self.din("norm_mix", [128, L, NCH])
        d_nmlp = self.din("norm_mlp", [128, L, NCH])
        d_win = self.din("w_in", [L, 128, NCH, IN_W])
        d_sbq = self.din("sb_q_norm", [128, L])
        d_sbk = self.din("sb_k_norm", [128, L])
        d_conv = self.din("conv_w", [128, L, 12, 4])
        d_gate = self.din("gate_p", [128, L, 8])
        d_dno = self.din("dn_out_norm", [128, L])
        d_wout = self.din("w_out", [L, 128, NCH, D])
        d_ff1 = self.din("w_ff1", [L, 128, NCH, DFF])
        d_ff2 = self.din("w_ff2", [L, 128, 32, D])
        n32, c32m, n16, c16m = _consts()
        d_c32 = self.din("consts32", [128, c32m.shape[1]])
        d_c16 = self.din("consts16", [128, c16m.shape[1]])
        d_out = nc.dram_tensor("outT", [NSEQ, 128, NCH, T], F32, kind="ExternalOutput").ap()
        self.d = dict(win=d_win, wout=d_wout, ff1=d_ff1, ff2=d_ff2)

        self.arena = nc.alloc_sbuf_tensor("arena", [128, AR_EL], BF16)

        c32 = self.sb("c32", [128, c32m.shape[1]], F32)
        c16 = self.sb("c16", [128, c16m.shape[1]], BF16)
        S.dma("sp", c32.t[:], d_c32, writes=c32.tl())
        S.dma("pool", c16.t[:], d_c16, writes=c16.tl())
        self.c32, self.c16 = c32, c16
        self.C32 = lambda n: c32.t[:, n32.index(n) * 128:(n32.index(n) + 1) * 128]
        self.C16 = lambda n: c16.t[:, n16.index(n) * 128:(n16.index(n) + 1) * 128]

        prm = self.sb("prm", [128, 512], F32)
        PT = prm.tl()
        self.prm, self.PT = prm, PT
        o = [0]

        def pslot(n):
            r = (o[0], o[0] + n)
            o[0] += n
            return r
        s_bada, s_nmix, s_nmlp = pslot(L * 48), pslot(L * NCH), pslot(L * NCH)
        s_sbq, s_sbk, s_conv, s_dno, s_gate = pslot(L), pslot(L), pslot(L * 48), pslot(L), pslot(L * 8)
        assert o[0] <= 512

        def pv(s, *shape):
            ap = prm.t[:, s[0]:s[1]]
            if len(shape) == 2:
                ap = ap.rearrange("p (a b) -> p a b", a=shape[0], b=shape[1])
            elif len(shape) == 3:
                ap = ap.rearrange("p (a b c) -> p a b c", a=shape[0], b=shape[1], c=shape[2])
            return ap
        self.pv = pv
        self.slots = dict(sbq=s_sbq, sbk=s_sbk, conv=s_conv, dno=s_dno, gate=s_gate)
        S.dma("sp", pv(s_bada, L, 48), d_bada, writes=PT)
        S.dma("sp", pv(s_nmix, L, NCH), d_nmix, writes=PT)
        S.dma("sp", pv(s_nmlp, L, NCH), d_nmlp, writes=PT)
        S.dma("sp", prm.t[:, s_sbq[0]:s_sbq[1]], d_sbq, writes=PT)
        S.dma("sp", prm.t[:, s_sbk[0]:s_sbk[1]], d_sbk, writes=PT)
        S.dma("sp", pv(s_conv, L, 12, 4), d_conv, writes=PT)
        S.dma("sp", prm.t[:, s_dno[0]:s_dno[1]], d_dno, writes=PT)
        S.dma("sp", pv(s_gate, L, 8), d_gate, writes=PT)
        nexpA = self.sb("nexpA", [128, L, 4], F32)
        self.nexpA = nexpA
        S.op("act", lambda e: e.activation(nexpA.t[:], pv(s_gate, L, 8)[:, :, 0:4], AF.Exp), reads=PT, writes=nexpA.tl())
        S.op("dve", lambda e: e.tensor_scalar(nexpA.t[:], nexpA.t[:], -1.0, None, ALU.mult), reads=nexpA.tl(), writes=nexpA.tl())

        self.PS = PS = [Buf(nc.alloc_psum_tensor(f"ps{i}", [128, 512], F32), f"ps{i}") for i in range(8)]
        for b in PS:
            b.T.psum = True

        cT = self.sb("cT", [128, NCH, 2], F32)
        ctmp = self.sb("ctmp", [128, NCH, 2], F32)
        cond = self.sb("cond", [128, NCH, 2], BF16)
        S.dma("sp", cT.t[:], d_cT, writes=cT.tl())
        S.op("act", lambda e: e.activation(ctmp.t[:], cT.t[:], AF.Exp, scale=-1.0), reads=cT.tl(), writes=ctmp.tl())
        S.op("dve", lambda e: e.tensor_scalar(ctmp.t[:], ctmp.t[:], 1.0, None, ALU.add), reads=ctmp.tl(), writes=ctmp.tl())
        S.op("dve", lambda e: e.reciprocal(ctmp.t[:], ctmp.t[:]), reads=ctmp.tl(), writes=ctmp.tl())
        S.op("dve", lambda e: e.tensor_tensor(cond.t[:], cT.t[:], ctmp.t[:], ALU.mult), reads=cT.tl() + ctmp.tl(), writes=cond.tl())
        mod = self.sb("mod", [128, L, 48, 2], F32)
        self.mod = mod
        wada = [self.av("setup", f"wada{i}", [128, NCH, 512], BF16) for i in range(2)]

        def ada_piece(l, pc):
            wb = self.ring(wada, "wada")
            S.dma("pool", wb.t[:], d_wada[l, :, :, pc * 512:(pc + 1) * 512], writes=wb.tl())
            for jj in range(4):
                j = pc * 4 + jj
                for kc in range(NCH):
                    S.op("pe", lambda e, jj=jj, kc=kc, j=j: e.matmul(
                        PS[0].t[:, j * 2:j * 2 + 2], wb.t[:, kc, jj * 128:(jj + 1) * 128], cond.t[:, kc, :],
                        start=(kc == 0), stop=(kc == NCH - 1)), reads=wb.tl() + cond.tl(), writes=PS[0].tl())

        def ada_evac(l, b):
            S.op("dve", lambda e: e.tensor_tensor(
                mod.t[:, l, :, b], PS[0].t[:, 0:96].rearrange("p (j b) -> p j b", b=2)[:, :, b],
                pv(s_bada, L, 48)[:, l, :], ALU.add), reads=PS[0].tl() + PT, writes=mod.tl())
        for l in range(L):
            for pc in range(12):
                ada_piece(l, pc)
            for b in range(2):
                ada_evac(l, b)
        gains = self.sb("gains", [128, L, 2, NCH, 2], F32)
        self.gains = gains

        def gain_op(l, which, sl, m, b):
            S.op("dve", lambda e: e.scalar_tensor_tensor(
                gains.t[:, l, which, :, b], mod.t[:, l, m * 8:(m + 1) * 8, b], 1.0, pv(sl, L, NCH)[:, l, :], ALU.add, ALU.mult),
                reads=mod.tl() + PT, writes=gains.tl())
        for l in range(L):
            for which, (sl, m) in enumerate(((s_nmix, 1), (s_nmlp, 4))):
                for b in range(2):
                    gain_op(l, which, sl, m, b)
        self.dump("mod", mod.t[:], mod.tl(), [128, L, 48, 2])

        self.xT = self.sb("xT", [128, NCH, T], F32, grid=(NCH, NT))
        self.hT = self.sb("hT", [128, NCH, T], BF16, grid=(NCH, NT))
        self.oT = self.sb("oT", [128, NCH, T], BF16, grid=(NCH, NT))
        self.sqb = [self.sb(f"sqb{i}", [128, 512], BF16) for i in range(2)]
        self.wring = [self.sb(f"wring{i}", [128, NCH, 128], BF16) for i in range(3)]
        self.alloc_phases()

        xT = self.xT
        self.cur = "setup"
        for s in range(NSEQ):
            for c in range(NCH):
                S.dma("sp", xT.t[:, c, :], d_xT[s, :, c, :], writes=xT.tl(c))
            for l in range(L):
                self.layer(l, s)
            for c in range(NCH):
                self.final.append(S.dma("sp", d_out[s, :, c, :], xT.t[:, c, :], reads=xT.tl(c)))
        self.stats = S.emit(final_wait_ops=self.final)

    def switch(self, new, extra=()):
        self.barrier(self.cur, new, extra=extra)
        self.cur = new

    def alloc_phases(self):
        T, NB, NT = self.Tn, self.NB, self.NT
        av = self.av
        self.qa = av("sb", "qa", [128, 2, T], BF16, grid=(2, NT))
        self.ka = av("sb", "ka", [128, 2, T], BF16, grid=(2, NT))
        self.va = av("sb", "va", [128, NB, 256], BF16, grid=(NB,))
        self.wv = av("sb", "wv", [128, NCH, 256], BF16)
        self.a_e = [av("sb", f"a_e{i}", [128, 512], F32) for i in range(3)]
        self.a_x = [av("sb", f"a_x{i}", [128, 512], F32) for i in range(2)]
        self.a_sp = [av("sb", f"a_sp{i}", [128, 512], BF16) for i in range(2)]
        self.a_att = [av("sb", f"a_att{i}", [128, 512], BF16) for i in range(2)]
        self.a_R = [av("sb", f"a_R{i}", [1, 512], BF16) for i in range(2)]
        self.raw32 = [av("sb", f"raw32_{i}", [128, 512], F32) for i in range(2)]
        g = "gdn"
        self.raw = av(g, "raw", [128, T + 4], F32)
        self.CH = min(T, 1024)
        self.cacc = av(g, "cacc", [128, self.CH], F32)
        self.tA = av(g, "tA", [128, self.CH], F32)
        self.acc_t = [Tile(f"acc_t{i}") for i in range(self.CH // 512)]
        self.tA_t = [Tile(f"tA_t{i}") for i in range(self.CH // 512)]
        self.phase[g]["tiles"].extend(self.acc_t + self.tA_t)
        self.kT = av(g, "kT", [128, T], BF16)
        self.qT = av(g, "qT", [128, T], BF16)
        self.vT = av(g, "vT", [128, T], BF16)
        self.gtok = av(g, "gtok", [128, NB, 8], F32)
        self.gcrc = av(g, "gcrc", [128, NB, 8], F32)
        self.gsc = av(g, "gsc", [128, NB, 8], F32)
        self.wgate = av(g, "wgate", [128, NCH, 8], BF16)

        def mk(n, k, dt):
            return [av(g, f"{n}{i}", [128, 128], dt) for i in range(k)]
        self.g_ebc, self.g_tI, self.g_DL, self.g_DLs = mk("ebc", 4, F32), mk("tI", 1, F32), mk("DL", 2, F32), mk("DLs", 1, F32)
        self.g_L32, self.g_Lc, self.g_Uc = mk("L32", 2, F32), mk("Lc", 4, F32), mk("Uc", 4, F32)
        self.g_Xc, self.g_Yc = mk("Xc", 4, F32), mk("Yc", 4, F32)
        self.g_C32, self.g_C32T, self.g_C64 = mk("C32", 3, BF16), mk("C32T", 3, BF16), mk("C64", 3, BF16)
        self.g_T16T, self.g_T16, self.g_Ya, self.g_Yb = mk("T16T", 2, BF16), mk("T16", 2, BF16), mk("Ya", 2, BF16), mk("Yb", 2, BF16)
        self.g_T32T, self.g_T32, self.g_Yd, self.g_TT = mk("T32T", 2, BF16), mk("T32", 2, BF16), mk("Yd", 2, BF16), mk("TT", 2, BF16)
        self.g_A, self.g_AT, self.g_kbg, self.g_kd = mk("A", 2, BF16), mk("AT", 4, BF16), mk("kbg", 3, BF16), mk("kd", 4, BF16)
        self.g_vb, self.g_u, self.g_wT = mk("vb", 3, BF16), mk("u", 3, F32), mk("wT", 3, BF16)
        self.g_qg, self.g_vn = mk("qg", 4, BF16), mk("vn", 2, BF16)
        self.S32 = av(g, "S32", [128, 128], F32)
        self.Sbf = av(g, "Sbf", [128, 128], BF16)
        self.wo = [av("op", f"wo{i}", [128, NCH, 512], BF16) for i in range(2)]
        self.HT = min(1024, T)
        tph = self.HT // 512
        self.hidA = av("mlp", "hidA", [128, 16, self.HT], BF16, grid=(16, tph))
        self.w2 = [av("mlp", f"w2_{i}", [128, 32, 128], BF16) for i in range(2)]
        self.relu = [av("mlp", f"relu{i}", [128, 512], F32) for i in range(2)]
        if T == SEQ:
            ap = self.oT.t[:].rearrange("p c t -> p (c t)").rearrange("p (a b) -> p a b", a=16, b=self.HT)
            self.hidB = Buf(ap, "hidB", grid=(16, tph))
        else:
            self.hidB = self.sb("hidB", [128, 16, self.HT], BF16, grid=(16, tph))
        self.phase["mlp"]["tiles"].extend(self.hidB.tl())

    def hid(self, j):
        return (self.hidA, j) if j < 16 else (self.hidB, j - 16)

    def mmps(self):
        return self.ring([self.PS[0], self.PS[1]], "mmps")


    def norm_to_hT(self, l, s, which):
        S, NT, PS = self.S, self.NT, self.PS
        xT, hT, gains, mod = self.xT, self.hT, self.gains, self.mod
        msh = 0 if which == 0 else 3
        for tt in range(NT):
            ts = slice(tt * 512, (tt + 1) * 512)
            ps = self.mmps()
            for c in range(NCH):
                self._sq_mm(xT.t[:, c, ts], xT.tl(c, tt), ps, "ones", c == 0, c == NCH - 1, eng=("act" if c % 2 else "pool"))
            S.op("act", lambda e, ps=ps: e.activation(ps.t[:, :], ps.t[:, :], AF.Ln, bias=EPS, scale=1.0 / D), reads=ps.tl(), writes=ps.tl())
            S.op("act", lambda e, ps=ps: e.activation(ps.t[:, :], ps.t[:, :], AF.Exp, scale=-0.5), reads=ps.tl(), writes=ps.tl())
            for c in range(NCH):
                tmp = PS[2 + c % 2]
                S.op("dve", lambda e, c=c, ts=ts, ps=ps, tmp=tmp: e.tensor_tensor(tmp.t[:, :], xT.t[:, c, ts], ps.t[:, :], ALU.mult),
                     reads=xT.tl(c, tt) + ps.tl(), writes=tmp.tl())
                S.op("dve", lambda e, c=c, ts=ts, tmp=tmp: e.tensor_scalar(
                    hT.t[:, c, ts], tmp.t[:, :], gains.t[:, l, which, c, s:s + 1], mod.t[:, l, msh * 8 + c, s:s + 1], ALU.mult, ALU.add),
                    reads=tmp.tl() + gains.tl() + mod.tl(), writes=hT.tl(c, tt))

    def _sq_mm(self, src_ap, src_tiles, ps, ones_name, start, stop, eng="pool"):
        S = self.S
        sq = self.ring(self.sqb, "sqb")
        if eng == "act":
            S.op("act", lambda e: e.activation(sq.t[:], src_ap, AF.Square), reads=src_tiles, writes=sq.tl())
        else:
            S.op("pool", lambda e: e.tensor_tensor(sq.t[:], src_ap, src_ap, ALU.mult), reads=src_tiles, writes=sq.tl())
        S.op("pe", lambda e: e.matmul(ps.t[:, :], self.C16(ones_name), sq.t[:], start=start, stop=stop),
             reads=sq.tl() + self.c16.tl(), writes=ps.tl())

    def group_norm(self, src_ap, src_tiles, ones_name, nfeat, gain_ap, out_ap, out_tiles, bias2=0.0):
        S = self.S
        ps = self.mmps()
        self._sq_mm(src_ap, src_tiles, ps, ones_name, True, True, eng="act")
        sc = 1.0 if nfeat is None else 1.0 / nfeat
        S.op("act", lambda e: e.activation(ps.t[:, :], ps.t[:, :], AF.Ln, bias=EPS, scale=sc), reads=ps.tl(), writes=ps.tl())
        S.op("act", lambda e: e.activation(ps.t[:, :], ps.t[:, :], AF.Exp, scale=-0.5, bias=bias2), reads=ps.tl(), writes=ps.tl())
        if gain_ap is None:
            S.op("dve", lambda e: e.tensor_tensor(out_ap, src_ap, ps.t[:, :], ALU.mult), reads=src_tiles + ps.tl(), writes=out_tiles)
        else:
            S.op("dve", lambda e: e.scalar_tensor_tensor(out_ap, src_ap, gain_ap, ps.t[:, :], ALU.mult, ALU.mult),
                 reads=src_tiles + ps.tl() + self.PT, writes=out_tiles)

    def proj_chunk(self, l, col0, evac):
        S, hT = self.S, self.hT
        assert self.win_cols[self.win_stream.n] == col0
        wb = self.win_stream.next()
        for tt in range(self.NT):
            ts = slice(tt * 512, (tt + 1) * 512)
            ps = self.mmps()
            for kc in range(NCH):
                S.op("pe", lambda e, ps=ps, kc=kc, ts=ts: e.matmul(ps.t[:, :], wb.t[:, kc, :], hT.t[:, kc, ts],
                                                                  start=(kc == 0), stop=(kc == NCH - 1)),
                     reads=wb.tl() + hT.tl(kc, tt), writes=ps.tl())
            evac(tt, ps)

    def layer(self, l, s):
        T = self.Tn
        cols = []
        for half in range(2):
            for which in range(2):
                for cc in range(2):
                    cols.append(which * 512 + (half * 2 + cc) * 128)
        for hd in range(DN_H):
            for base in (1536, 2048, 2560, 3072):
                cols.append(base + hd * 128)
        self.win_cols = cols
        self.win_stream = WStream(self, self.wring, [self.d["win"][l, :, :, c0:c0 + 128] for c0 in cols], 2)
        if self.stop == "setup":
            return
        self.norm_to_hT(l, s, 0)
        self.dump(f"h1_{l}", self.hT.t[:], self.hT.tl(), [128, NCH, T], BF16)
        if self.stop == "norm":
            return
        self.switch("sb", extra=self.oT.tl())
        for half in range(2):
            self.sb_proj(l, half)
            if self.stop == "sbproj":
                return
            self.sb_attn(l, half)
        if self.stop == "attn":
            return
        self.switch("gdn")
        self.gdn(l)
        self.dump(f"oT_{l}", self.oT.t[:], self.oT.tl(), [128, NCH, T], BF16)
        if self.stop == "gdn":
            return
        self.switch("op")
        self.out_proj(l, s)
        self.dump(f"x1_{l}", self.xT.t[:], self.xT.tl(), [128, NCH, T])
        self.norm_to_hT(l, s, 1)
        self.switch("mlp", extra=self.oT.tl())
        self.mlp(l, s)
        self.dump(f"x2_{l}", self.xT.t[:], self.xT.tl(), [128, NCH, T])

    def sb_proj(self, l, half):
        S, NT, NB, hT = self.S, self.NT, self.NB, self.hT
        for which, dst, slot in ((0, self.qa, self.slots["sbq"]), (1, self.ka, self.slots["sbk"])):
            for cc in range(2):
                def evac(tt, ps, cc=cc, dst=dst, slot=slot):
                    r = self.raw32[tt % 2]
                    S.op("act", lambda e: e.activation(r.t[:], ps.t[:, :], AF.Copy), reads=ps.tl(), writes=r.tl())
                    self.group_norm(r.t[:], r.tl(), "blk64", SB_D, self.prm.t[:, slot[0] + l:slot[0] + l + 1],
                                    dst.t[:, cc, tt * 512:(tt + 1) * 512], dst.tl(cc, tt))
                self.proj_chunk(l, which * 512 + (half * 2 + cc) * 128, evac)
        wv, va = self.wv, self.va
        S.dma("pool", wv.t[:], self.d["win"][l, :, :, 1024 + half * 256:1024 + (half + 1) * 256], writes=wv.tl())

        def vblk(blk):
            ps = self.mmps()
            tt = blk // 4
            for kc in range(NCH):
                S.op("pe", lambda e, kc=kc: e.matmul(ps.t[:, 0:256], hT.t[:, kc, blk * 128:(blk + 1) * 128], wv.t[:, kc, :],
                                                     start=(kc == 0), stop=(kc == NCH - 1)),
                     reads=wv.tl() + hT.tl(kc, tt), writes=ps.tl())
            if blk % 2 == 0:
                S.op("dve", lambda e: e.tensor_copy(va.t[:, blk, :], ps.t[:, 0:256]), reads=ps.tl(), writes=va.tl(blk))
            else:
                S.op("act", lambda e: e.activation(va.t[:, blk, :], ps.t[:, 0:256], AF.Copy), reads=ps.tl(), writes=va.tl(blk))
        for blk in range(NB):
            vblk(blk)
        self.dump(f"qa_{l}_{half}", self.qa.t[:], self.qa.tl(), [128, 2, self.Tn], BF16)
        self.dump(f"ka_{l}_{half}", self.ka.t[:], self.ka.tl(), [128, 2, self.Tn], BF16)
        self.dump(f"va_{l}_{half}", self.va.t[:], self.va.tl(), [128, NB, 256], BF16)

    def sb_attn(self, l, half):
        S, NT = self.S, self.NT
        PS, C16 = self.PS, self.C16
        qa, ka, va, oT = self.qa, self.ka, self.va, self.oT
        c16t = self.c16.tl()
        scale = SB_D ** -0.5
        items = []
        for cc in range(2):
            for qt in range(NT):
                for kb in range(4 * qt + 3, -1, -1):
                    for hh in (2 * cc, 2 * cc + 1):
                        items.append((hh, qt, kb))
        n_it = len(items)
        grp = {n: items[n][0] % 2 for n in range(n_it)}

        def geom(n):
            hh, qt, kb = items[n]
            i = kb - 4 * qt
            c0 = 128 * i if i > 0 else 0
            return hh, qt, kb, i, c0, 512 - c0

        def s1(n):
            hh, qt, kb, i, c0, w = geom(n)
            cc, p0 = hh // 2, (hh % 2) * 64
            zp = PS[2 + n % 2]
            S.op("pe", lambda e: e.matmul(zp.t[:, 0:w], ka.t[p0:p0 + 64, cc, kb * 128:(kb + 1) * 128],
                                          qa.t[p0:p0 + 64, cc, qt * 512 + c0:(qt + 1) * 512], start=True, stop=True),
                 reads=ka.tl(cc, kb // 4) + qa.tl(cc, qt), writes=zp.tl())

        import os
        nfill = int(os.environ.get("FILL", "0"))

        def fill():
            for _ in range(nfill):
                S.op("pe", lambda e: e.matmul(PS[0].t[:, :], C16("ones"), self.c16.t[:, 0:512], start=True, stop=True),
                     reads=c16t, writes=PS[0].tl())

        def s2(n):
            hh, qt, kb, i, c0, w = geom(n)
            zp = PS[2 + n % 2]
            eb, sp = self.a_e[n % 3], self.a_sp[n % 2]
            S.op("act", lambda e: e.activation(eb.t[:, 0:w], zp.t[:, 0:w], AF.Exp, scale=scale), reads=zp.tl(), writes=eb.tl())
            S.op("act", lambda e: e.activation(sp.t[:, 0:w], eb.t[:, 0:w], AF.Ln, bias=1.0), reads=eb.tl(), writes=sp.tl())
            if i >= 0:
                S.op("pool", lambda e: e.affine_select(sp.t[:, 0:128], sp.t[:, 0:128], [[1, 128]], ALU.is_gt, 0.0,
                                                       base=0, channel_multiplier=-1), reads=sp.tl(), writes=sp.tl())

        def s3(n):
            hh, qt, kb, i, c0, w = geom(n)
            cp = PS[4 + n % 2]
            sp = self.a_sp[n % 2]
            R = self.a_R[grp[n] % 2]
            first = (kb == 4 * qt + 3)
            S.op("pe", lambda e: e.matmul(cp.t[:, 0:w], C16("tril"), sp.t[:, 0:w], start=True, stop=first),
                 reads=sp.tl() + c16t, writes=cp.tl())
            if not first:
                r0 = 128 if i >= 0 else 0
                S.op("pe", lambda e: e.matmul(cp.t[:, r0:w], C16("ones")[0:1, :], R.t[0:1, c0 + r0:512], start=False, stop=True),
                     reads=R.tl() + c16t, writes=cp.tl())

        def s4(n):
            hh, qt, kb, i, c0, w = geom(n)
            cp = PS[4 + n % 2]
            eb, xb, at = self.a_e[n % 3], self.a_x[n % 2], self.a_att[n % 2]
            import os
            sub = os.environ.get("S4SUB", "abc")
            if "a" in sub:
                S.op("act", lambda e: e.activation(xb.t[:, 0:w], cp.t[:, 0:w], AF.Exp, scale=-1.0), reads=cp.tl(), writes=xb.tl())
            if kb > 0:
                R = self.a_R[grp[n] % 2]
                S.op("dve", lambda e: e.tensor_copy(R.t[0:1, c0:512], cp.t[0:1, 0:w]), reads=cp.tl(), writes=R.tl())
            if "b" in sub:
                S.op("dve", lambda e: e.tensor_tensor(at.t[:, 0:w], eb.t[:, 0:w], xb.t[:, 0:w], ALU.mult),
                     reads=eb.tl() + xb.tl(), writes=at.tl())
            if i >= 0 and "c" in sub:
                S.op("pool", lambda e: e.affine_select(at.t[:, 0:128], at.t[:, 0:128], [[1, 128]], ALU.is_gt, 0.0,
                                                       base=0, channel_multiplier=-1), reads=at.tl(), writes=at.tl())

        def s5(n):
            hh, qt, kb, i, c0, w = geom(n)
            cc, p0 = hh // 2, (hh % 2) * 64
            c = half * 2 + cc
            op_ = PS[6 + grp[n] % 2]
            at = self.a_att[n % 2]
            last = (kb == 0)
            first = (kb == 4 * qt + 3)
            vl = va.t[:, kb, cc * 128:(cc + 1) * 128]
            if i >= 0 and w > 128:
                S.op("pe", lambda e: e.matmul(op_.t[:, c0:c0 + 128], vl, at.t[:, 0:128], start=first, stop=False),
                     reads=at.tl() + va.tl(kb), writes=op_.tl())
                S.op("pe", lambda e: e.matmul(op_.t[:, c0 + 128:512], vl, at.t[:, 128:w], start=False, stop=last),
                     reads=at.tl() + va.tl(kb), writes=op_.tl())
            else:
                S.op("pe", lambda e: e.matmul(op_.t[:, c0:512], vl, at.t[:, 0:w], start=first, stop=last),
                     reads=at.tl() + va.tl(kb), writes=op_.tl())
            if last:
                S.op("dve", lambda e: e.tensor_copy(oT.t[p0:p0 + 64, c, qt * 512:(qt + 1) * 512], op_.t[p0:p0 + 64, :]),
                     reads=op_.tl(), writes=oT.tl(c, qt))

        import os
        stg = os.environ.get("ATT_STAGES", "12345")
        n_it = min(n_it, int(os.environ.get("ATT_ITEMS", n_it)))
        for n in range(n_it + 2):
            if n < n_it:
                if "1" in stg:
                    s1(n)
                if "2" in stg:
                    s2(n)
            if 0 <= n - 1 < n_it:
                if "3" in stg:
                    s3(n - 1)
                if "4" in stg:
                    s4(n - 1)
            if 0 <= n - 2 < n_it:
                if "5" in stg:
                    s5(n - 2)
            fill()

    def gdn(self, l):
        S, NT, NB, T = self.S, self.NT, self.NB, self.Tn
        PS, C32, hT = self.PS, self.C32, self.hT
        c32t = self.c32.tl()
        gtok, gcrc, gsc, wg = self.gtok, self.gcrc, self.gsc, self.wgate
        PT = self.PT
        gpar = self.pv(self.slots["gate"], self.L, 8)
        nexpA = self.nexpA
        S.dma("pool", wg.t[:], self.d["win"][l, :, :, 3584:3592], writes=wg.tl())
        ps = self.mmps()

        def gproj(blk):
            for kc in range(NCH):
                S.op("pe", lambda e, kc=kc: e.matmul(ps.t[:, blk * 8:blk * 8 + 8], hT.t[:, kc, blk * 128:(blk + 1) * 128], wg.t[:, kc, :],
                                                     start=(kc == 0), stop=(kc == NCH - 1)),
                     reads=wg.tl() + hT.tl(kc, blk // 4), writes=ps.tl())
        for blk in range(NB):
            gproj(blk)
        psv = ps.t[:, 0:NB * 8].rearrange("p (a b) -> p a b", b=8)

        def ghead(h):
            S.op("act", lambda e: e.activation(gtok.t[:, :, h], psv[:, :, h], AF.Exp, bias=gpar[:, l, 4 + h:5 + h]),
                 reads=ps.tl() + PT, writes=gtok.tl())
            S.op("act", lambda e: e.activation(gtok.t[:, :, h], gtok.t[:, :, h], AF.Ln, bias=1.0), reads=gtok.tl(), writes=gtok.tl())
            S.op("dve", lambda e: e.tensor_scalar(gtok.t[:, :, h], gtok.t[:, :, h], nexpA.t[:, l, h:h + 1], None, ALU.mult),
                 reads=gtok.tl() + nexpA.tl(), writes=gtok.tl())
        for h in range(DN_H):
            ghead(h)
        S.op("act", lambda e: e.activation(gtok.t[:, :, 4:8], psv[:, :, 4:8], AF.Exp, scale=-1.0), reads=ps.tl(), writes=gtok.tl())
        S.op("dve", lambda e: e.tensor_scalar(gtok.t[:, :, 4:8], gtok.t[:, :, 4:8], 1.0, None, ALU.add), reads=gtok.tl(), writes=gtok.tl())
        S.op("dve", lambda e: e.reciprocal(gtok.t[:, :, 4:8], gtok.t[:, :, 4:8]), reads=gtok.tl(), writes=gtok.tl())
        ps2 = self.mmps()

        def gcs(blk):
            S.op("pe", lambda e: e.matmul(ps2.t[:, blk * 8:blk * 8 + 4], C32("mincl"), gtok.t[:, blk, 0:4], start=True, stop=True),
                 reads=gtok.tl() + c32t, writes=ps2.tl())
            S.op("pe", lambda e: e.matmul(ps2.t[:, blk * 8 + 4:blk * 8 + 8], C32("mrev"), gtok.t[:, blk, 0:4], start=True, stop=True),
                 reads=gtok.tl() + c32t, writes=ps2.tl())
        for blk in range(NB):
            gcs(blk)
        S.op("dve", lambda e: e.tensor_copy(gcrc.t[:].rearrange("p a b -> p (a b)"), ps2.t[:, 0:NB * 8]), reads=ps2.tl(), writes=gcrc.tl())
        S.op("act", lambda e: e.activation(gsc.t[:], gcrc.t[:], AF.Exp), reads=gcrc.tl(), writes=gsc.tl())
        S.op("dve", lambda e: e.tensor_tensor(gsc.t[:, :, 0:4], gsc.t[:, :, 0:4], gtok.t[:, :, 4:8], ALU.mult),
             reads=gsc.tl() + gtok.tl(), writes=gsc.tl())
        self.dump(f"gtok_{l}", gtok.t[:], gtok.tl(), [128, NB, 8])
        self.dump(f"gcrc_{l}", gcrc.t[:], gcrc.tl(), [128, NB, 8])
        S.op("pool", lambda e: e.memset(self.raw.t[:, 0:3], 0.0), writes=self.raw.tl())
        for hd in range(DN_H):
            self.gdn_head(l, hd)

    def sigmoid_to(self, dst_ap, dst_tiles, src_ap, src_tiles):
        S = self.S
        S.op("act", lambda e: e.activation(dst_ap, src_ap, AF.Exp, scale=-1.0), reads=src_tiles, writes=dst_tiles)
        S.op("act", lambda e: e.activation(dst_ap, dst_ap, AF.Ln, bias=1.0), reads=dst_tiles, writes=dst_tiles)
        S.op("act", lambda e: e.activation(dst_ap, dst_ap, AF.Exp, scale=-1.0), reads=dst_tiles, writes=dst_tiles)

    def gdn_stream(self, l, hd, kind):
        S, NT, T = self.S, self.NT, self.Tn
        raw = self.raw
        col0 = {"q": 1536, "k": 2048, "v": 2560, "z": 3072}[kind] + hd * 128

        def evac(tt, ps):
            if tt % 2 == 0:
                S.op("dve", lambda e: e.tensor_copy(raw.t[:, 3 + tt * 512:3 + (tt + 1) * 512], ps.t[:, :]), reads=ps.tl(), writes=raw.tl())
            else:
                S.op("act", lambda e: e.activation(raw.t[:, 3 + tt * 512:3 + (tt + 1) * 512], ps.t[:, :], AF.Copy), reads=ps.tl(), writes=raw.tl())
        self.proj_chunk(l, col0, evac)
        cw = self.pv(self.slots["conv"], self.L, 12, 4)
        PT = self.PT
        j = {"q": 0, "k": 4, "v": 8, "z": 0}[kind] + hd
        b2 = math.log(DN_D ** -0.5) if kind == "q" else 0.0
        nsub = self.CH // 512

        def piece(tt):
            sub = tt % nsub
            t0 = tt * 512
            acc = self.cacc.t[:, sub * 512:(sub + 1) * 512]
            tA = self.tA.t[:, sub * 512:(sub + 1) * 512]
            acct, tAt = [self.acc_t[sub]], [self.tA_t[sub]]
            if kind == "z":
                self.sigmoid_to(tA, tAt, raw.t[:, 3 + t0:3 + t0 + 512], raw.tl())
                S.op("dve", lambda e: e.tensor_tensor(self.oT.t[:, 4 + hd, t0:t0 + 512], raw.t[:, 3 + t0:3 + t0 + 512], tA, ALU.mult),
                     reads=raw.tl() + tAt, writes=self.oT.tl(4 + hd, tt))
                return
            S.op("act", lambda e: e.activation(acc, raw.t[:, t0:t0 + 512], AF.Copy, scale=cw[:, l, j, 0:1]),
                 reads=raw.tl() + PT, writes=acct)
            for k in range(1, 4):
                S.op("dve", lambda e, k=k: e.scalar_tensor_tensor(
                    acc, raw.t[:, t0 + k:t0 + k + 512], cw[:, l, j, k:k + 1], acc, ALU.mult, ALU.add),
                    reads=raw.tl() + acct + PT, writes=acct)
            self.sigmoid_to(tA, tAt, acc, acct)
            if kind == "v":
                S.op("dve", lambda e: e.tensor_tensor(self.vT.t[:, t0:t0 + 512], acc, tA, ALU.mult),
                     reads=acct + tAt, writes=self.vT.tl())
                return
            S.op("dve", lambda e: e.tensor_tensor(acc, acc, tA, ALU.mult), reads=acct + tAt, writes=acct)
            dst = self.qT if kind == "q" else self.kT
            self.group_norm(acc, acct, "ones", None, None, dst.t[:, t0:t0 + 512], dst.tl(), bias2=b2)

        for t2 in range(0, NT, nsub):
            lists = [S.capture(lambda tt=tt: piece(tt)) for tt in range(t2, min(t2 + nsub, NT))]
            S.merge(*lists)

    def tri_inverse(self, L32):
        S, ring, C32c, C16c = self.S, self.ring, self.C32, self.C16
        c32t = self.c32.tl()
        ident32 = C32c("ident")
        PS = self.PS
        pu = PS[3]
        S.op("pe", lambda e: e.matmul(pu.t[:, 0:128], L32.t[:], ident32, start=True, stop=True), reads=L32.tl() + c32t, writes=pu.tl())
        L16, U16 = ring(self.g_Lc, "Lc"), ring(self.g_Uc, "Uc")
        LB, UB = ring(self.g_Lc, "Lc"), ring(self.g_Uc, "Uc")
        C32, C32T, C64 = ring(self.g_C32, "C32"), ring(self.g_C32T, "C32T"), ring(self.g_C64, "C64")
        X0, Y0 = ring(self.g_Xc, "Xc"), ring(self.g_Yc, "Yc")
        XB, YB = ring(self.g_Xc, "Xc"), ring(self.g_Yc, "Yc")
        S.op("dve", lambda e: e.tensor_tensor(L16.t[:], L32.t[:], C32c("m16"), ALU.mult), reads=L32.tl() + c32t, writes=L16.tl())
        S.op("dve", lambda e: e.tensor_tensor(C32.t[:], L32.t[:], C32c("m32"), ALU.mult), reads=L32.tl() + c32t, writes=C32.tl())
        S.op("pool", lambda e: e.tensor_tensor(C64.t[:], L32.t[:], C32c("m64"), ALU.mult), reads=L32.tl() + c32t, writes=C64.tl())
        S.op("dve", lambda e: e.tensor_tensor(U16.t[:], pu.t[:, 0:128], C32c("m16T"), ALU.mult), reads=pu.tl() + c32t, writes=U16.tl())
        S.op("dve", lambda e: e.tensor_tensor(C32T.t[:], pu.t[:, 0:128], C32c("m32T"), ALU.mult), reads=pu.tl() + c32t, writes=C32T.tl())
        S.op("pool", lambda e: e.tensor_tensor(Y0.t[:], ident32, L16.t[:], ALU.subtract), reads=L16.tl() + c32t, writes=Y0.tl())
        S.op("dve", lambda e: e.tensor_tensor(X0.t[:], ident32, U16.t[:], ALU.subtract), reads=U16.tl() + c32t, writes=X0.tl())
        Lk, Uk, Xk, Yk = L16, U16, X0, Y0
        if getattr(S, "cap", None) is not None:
            self._marks.append(len(S.cap))
        first_call = not getattr(self, "_tri_dbg", False)
        self._tri_dbg = True
        if first_call:
            self.dump("dbgX0", Xk.t[:], Xk.tl(), [128, 128])
            self.dump("dbgU16", Uk.t[:], Uk.tl(), [128, 128])
            self.dump("dbgL16", Lk.t[:], Lk.tl(), [128, 128])
        T16T = T16 = None
        for k in range(3):
            last = (k == 2)
            pq = PS[4]
            S.op("pe", lambda e, Uk=Uk, Lk=Lk, pq=pq: e.matmul(pq.t[:, 0:128], Uk.t[:], Lk.t[:], start=True, stop=True),
                 reads=Uk.tl() + Lk.tl(), writes=pq.tl())
            S.op("pe", lambda e, Uk=Uk, Lk=Lk, pq=pq: e.matmul(pq.t[:, 128:256], Lk.t[:], Uk.t[:], start=True, stop=True),
                 reads=Uk.tl() + Lk.tl(), writes=pq.tl())
            Ln_, Un_ = (LB, UB) if k % 2 == 0 else (L16, U16)
            S.op("act", lambda e, Ln_=Ln_, pq=pq: e.activation(Ln_.t[:], pq.t[:, 0:128], AF.Copy), reads=pq.tl(), writes=Ln_.tl())
            S.op("dve", lambda e, Un_=Un_, pq=pq: e.tensor_copy(Un_.t[:], pq.t[:, 128:256]), reads=pq.tl(), writes=Un_.tl())
            S.op("pe", lambda e, Ln_=Ln_, Xk=Xk, pq=pq: e.matmul(pq.t[:, 256:384], Ln_.t[:], Xk.t[:], start=True, stop=True),
                 reads=Ln_.tl() + Xk.tl(), writes=pq.tl())
            S.op("pe", lambda e, Un_=Un_, Yk=Yk, pq=pq: e.matmul(pq.t[:, 384:512], Un_.t[:], Yk.t[:], start=True, stop=True),
                 reads=Un_.tl() + Yk.tl(), writes=pq.tl())
            if last:
                Xn, Yn = ring(self.g_T16T, "T16T"), ring(self.g_T16, "T16")
            else:
                Xn, Yn = (XB, YB) if k % 2 == 0 else (X0, Y0)
            S.op("dve", lambda e, Xn=Xn, Xk=Xk, pq=pq: e.tensor_tensor(Xn.t[:], pq.t[:, 256:384], Xk.t[:], ALU.add),
                 reads=pq.tl() + Xk.tl(), writes=Xn.tl())
            S.op("dve", lambda e, Yn=Yn, Yk=Yk, pq=pq: e.tensor_tensor(Yn.t[:], pq.t[:, 384:512], Yk.t[:], ALU.add),
                 reads=pq.tl() + Yk.tl(), writes=Yn.tl())
            Uk, Lk, Xk, Yk = Un_, Ln_, Xn, Yn
        T16T, T16 = Xk, Yk
        if getattr(S, "cap", None) is not None:
            self._marks.append(len(S.cap))
        if first_call:
            self.dump("dbgT16T", T16T.t[:], T16T.tl(), [128, 128], BF16)
        pd = PS[5]
        S.op("pe", lambda e: e.matmul(pd.t[:, 0:128], C32.t[:], T16T.t[:], start=True, stop=True), reads=C32.tl() + T16T.tl(), writes=pd.tl())
        S.op("pe", lambda e: e.matmul(pd.t[:, 128:256], C32T.t[:], T16.t[:], start=True, stop=True), reads=C32T.tl() + T16.tl(), writes=pd.tl())
        Ya, Yb = ring(self.g_Ya, "Ya"), ring(self.g_Yb, "Yb")
        S.op("act", lambda e: e.activation(Ya.t[:], pd.t[:, 0:128], AF.Copy), reads=pd.tl(), writes=Ya.tl())
        S.op("dve", lambda e: e.tensor_copy(Yb.t[:], pd.t[:, 128:256]), reads=pd.tl(), writes=Yb.tl())
        pd2 = PS[5]
        S.op("pe", lambda e: e.matmul(pd2.t[:, 0:128], T16.t[:], Ya.t[:], start=True, stop=True), reads=T16.tl() + Ya.tl(), writes=pd2.tl())
        S.op("pe", lambda e: e.matmul(pd2.t[:, 128:256], T16T.t[:], Yb.t[:], start=True, stop=True), reads=T16T.tl() + Yb.tl(), writes=pd2.tl())
        T32T, T32 = ring(self.g_T32T, "T32T"), ring(self.g_T32, "T32")
        S.op("dve", lambda e: e.scalar_tensor_tensor(T32T.t[:], pd2.t[:, 0:128], -1.0, T16T.t[:], ALU.mult, ALU.add),
             reads=pd2.tl() + T16T.tl(), writes=T32T.tl())
        S.op("dve", lambda e: e.scalar_tensor_tensor(T32.t[:], pd2.t[:, 128:256], -1.0, T16.t[:], ALU.mult, ALU.add),
             reads=pd2.tl() + T16.tl(), writes=T32.tl())
        pe1 = PS[5]
        S.op("pe", lambda e: e.matmul(pe1.t[:, 0:128], C64.t[:], T32T.t[:], start=True, stop=True), reads=C64.tl() + T32T.tl(), writes=pe1.tl())
        Yd = ring(self.g_Yd, "Yd")
        S.op("act", lambda e: e.activation(Yd.t[:], pe1.t[:, 0:128], AF.Copy), reads=pe1.tl(), writes=Yd.tl())
        S.op("pe", lambda e: e.matmul(pe1.t[:, 128:256], T32.t[:], Yd.t[:], start=True, stop=True), reads=T32.tl() + Yd.tl(), writes=pe1.tl())
        TT = ring(self.g_TT, "TT")
        S.op("dve", lambda e: e.scalar_tensor_tensor(TT.t[:], pe1.t[:, 128:256], -1.0, T32T.t[:], ALU.mult, ALU.add),
             reads=pe1.tl() + T32T.tl(), writes=TT.tl())
        return TT

    def gdn_head(self, l, hd):
        S, NT, NB, T = self.S, self.NT, self.NB, self.Tn
        PS, C16, C32 = self.PS, self.C16, self.C32
        c16t, c32t = self.c16.tl(), self.c32.tl()
        kT, qT, vT = self.kT, self.qT, self.vT
        gtok, gcrc, gsc = self.gtok, self.gcrc, self.gsc
        ring = self.ring
        for kind in ("q", "k", "v", "z"):
            self.gdn_stream(l, hd, kind)
        self.dump(f"qT_{l}_{hd}", qT.t[:], qT.tl(), [128, T], BF16)
        self.dump(f"kT_{l}_{hd}", kT.t[:], kT.tl(), [128, T], BF16)
        self.dump(f"vT_{l}_{hd}", vT.t[:], vT.tl(), [128, T], BF16)

        S32, Sbf = self.S32, self.Sbf
        S.op("dve", lambda e: e.memset(S32.t[:], 0.0), writes=S32.tl())
        S.op("pool", lambda e: e.memset(Sbf.t[:], 0.0), writes=Sbf.tl())
        PB = [PS[2], PS[3], PS[4], PS[5]]
        ident16 = C16("ident")

        def psr():
            return ring(PB, "gps")
        prep_out = {}

        def prep(blk):
            bs = slice(blk * 128, (blk + 1) * 128)
            gcol = gtok.t[:, blk, hd:hd + 1]
            bcol = gtok.t[:, blk, 4 + hd:5 + hd]
            gccol = gcrc.t[:, blk, hd:hd + 1]
            bgcol = gsc.t[:, blk, hd:hd + 1]
            erccol = gsc.t[:, blk, 4 + hd:5 + hd]
            p1 = PS[2]
            S.op("pe", lambda e: e.matmul(p1.t[:, 0:128], gcol.to_broadcast([128, 128]), C32("mincl"), start=True, stop=True),
                 reads=gtok.tl() + c32t, writes=p1.tl())
            ebc, tI, DL, DLs = ring(self.g_ebc, "ebc"), ring(self.g_tI, "tI"), ring(self.g_DL, "DL"), ring(self.g_DLs, "DLs")
            S.op("act", lambda e: e.activation(ebc.t[:], p1.t[:, 0:128], AF.Exp), reads=p1.tl(), writes=ebc.tl())
            S.op("dve", lambda e: e.scalar_tensor_tensor(tI.t[:], p1.t[:, 0:128], gccol, C32("mbI"), ALU.subtract, ALU.add),
                 reads=p1.tl() + gcrc.tl() + c32t, writes=tI.tl())
            S.op("act", lambda e: e.activation(DL.t[:], tI.t[:], AF.Exp, scale=-1.0), reads=tI.tl(), writes=DL.tl())
            S.op("dve", lambda e: e.tensor_tensor(DLs.t[:], DL.t[:], C32("strict"), ALU.mult), reads=DL.tl() + c32t, writes=DLs.tl())
            p2 = PS[3]
            S.op("pe", lambda e: e.matmul(p2.t[:, 0:128], kT.t[:, bs], kT.t[:, bs], start=True, stop=True), reads=kT.tl(), writes=p2.tl())
            S.op("pe", lambda e: e.matmul(p2.t[:, 128:256], qT.t[:, bs], kT.t[:, bs], start=True, stop=True), reads=kT.tl() + qT.tl(), writes=p2.tl())
            L32, Ab = ring(self.g_L32, "L32"), ring(self.g_A, "A")
            S.op("dve", lambda e: e.scalar_tensor_tensor(L32.t[:], p2.t[:, 0:128], bcol, DLs.t[:], ALU.mult, ALU.mult),
                 reads=p2.tl() + gtok.tl() + DLs.tl(), writes=L32.tl())
            S.op("dve", lambda e: e.tensor_tensor(Ab.t[:], p2.t[:, 128:256], DL.t[:], ALU.mult), reads=p2.tl() + DL.tl(), writes=Ab.tl())
            p3 = PS[2]
            p3b = p3.t[:, :].bitcast(BF16)
            S.op("pe", lambda e: e.transpose(p3b[:, 128:256], Ab.t[:], ident16), reads=Ab.tl() + c16t, writes=p3.tl())
            S.op("pe", lambda e: e.transpose(p3b[:, 256:384], kT.t[:, bs], ident16), reads=kT.tl() + c16t, writes=p3.tl())
            S.op("pe", lambda e: e.transpose(p3b[:, 384:512], vT.t[:, bs], ident16), reads=vT.tl() + c16t, writes=p3.tl())
            AT = ring(self.g_AT, "AT")
            kbg, kd, vb = ring(self.g_kbg, "kbg"), ring(self.g_kd, "kd"), ring(self.g_vb, "vb")
            S.op("act", lambda e: e.activation(AT.t[:], p3b[:, 128:256], AF.Copy), reads=p3.tl(), writes=AT.tl())
            S.op("act", lambda e: e.activation(kbg.t[:], p3b[:, 256:384], AF.Copy, scale=bgcol), reads=p3.tl() + gsc.tl(), writes=kbg.tl())
            S.op("dve", lambda e: e.tensor_scalar(kd.t[:], p3b[:, 256:384], erccol, None, ALU.mult), reads=p3.tl() + gsc.tl(), writes=kd.tl())
            S.op("act", lambda e: e.activation(vb.t[:], p3b[:, 384:512], AF.Copy, scale=bcol), reads=p3.tl() + gtok.tl(), writes=vb.tl())
            qg = ring(self.g_qg, "qg")
            S.op("dve", lambda e: e.tensor_tensor(qg.t[:], qT.t[:, bs], ebc.t[:], ALU.mult), reads=qT.tl() + ebc.tl(), writes=qg.tl())
            TT = self.tri_inverse(L32)
            if blk == 0:
                self.dump(f"L32_{l}_{hd}", L32.t[:], L32.tl(), [128, 128])
                self.dump(f"TT_{l}_{hd}", TT.t[:], TT.tl(), [128, 128], BF16)
            p4 = PS[5]
            S.op("pe", lambda e: e.matmul(p4.t[:, 0:128], TT.t[:], vb.t[:], start=True, stop=True), reads=TT.tl() + vb.tl(), writes=p4.tl())
            S.op("pe", lambda e: e.matmul(p4.t[:, 128:256], kbg.t[:], TT.t[:], start=True, stop=True), reads=TT.tl() + kbg.tl(), writes=p4.tl())
            u, wT = ring(self.g_u, "u"), ring(self.g_wT, "wT")
            S.op("dve", lambda e: e.tensor_copy(u.t[:], p4.t[:, 0:128]), reads=p4.tl(), writes=u.tl())
            S.op("act", lambda e: e.activation(wT.t[:], p4.t[:, 128:256], AF.Copy), reads=p4.tl(), writes=wT.tl())
            prep_out[blk] = (u, wT, kd, AT, qg, ebc)

        opsum, pw = PS[6], PS[7]

        def chunk(blk, ch, u, wT, kd, AT, qg, ebc):
            r0 = ch * 64
            rs = slice(r0, r0 + 64)
            cs = slice((blk % 4) * 128 + r0, (blk % 4) * 128 + r0 + 64)
            S.op("pe", lambda e: e.matmul(pw.t[:, 0:128], wT.t[:], Sbf.t[:], start=True, stop=True), reads=wT.tl() + Sbf.tl(), writes=pw.tl())
            vn = ring(self.g_vn, "vn")
            S.op("dve", lambda e: e.tensor_tensor(vn.t[rs, :], u.t[rs, :], pw.t[rs, 0:128], ALU.subtract), reads=u.tl() + pw.tl(), writes=vn.tl())
            S.op("pe", lambda e: e.matmul(opsum.t[:, cs], Sbf.t[:], qg.t[:, rs], start=True, stop=False), reads=Sbf.tl() + qg.tl(), writes=opsum.tl())
            S.op("pe", lambda e: e.matmul(opsum.t[:, cs], vn.t[rs, :], AT.t[rs, rs], start=False, stop=True), reads=vn.tl() + AT.tl(), writes=opsum.tl())
            S.op("pe", lambda e: e.matmul(pw.t[:, 128:256], kd.t[rs, :], vn.t[rs, :], start=True, stop=True), reads=kd.tl() + vn.tl(), writes=pw.tl())
            S.op("dve", lambda e: e.scalar_tensor_tensor(S32.t[:], S32.t[:], ebc.t[:, r0 + 63:r0 + 64], pw.t[:, 128:256], ALU.mult, ALU.add),
                 reads=S32.tl() + ebc.tl() + pw.tl(), writes=S32.tl())
            S.op("act", lambda e: e.activation(Sbf.t[:], S32.t[:], AF.Copy), reads=S32.tl(), writes=Sbf.tl())

        def finish(tt):
            ts = slice(tt * 512, (tt + 1) * 512)
            go = self.tA.t[:, 0:512]
            got = [self.tA_t[0]]
            S.op("act", lambda e: e.activation(go, opsum.t[:, :], AF.Copy), reads=opsum.tl(), writes=got)
            dno = self.slots["dno"]
            self.group_norm(go, got, "ones", DN_D, self.prm.t[:, dno[0] + l:dno[0] + l + 1], go, got)
            S.op("dve", lambda e: e.tensor_tensor(self.oT.t[:, 4 + hd, ts], go, self.oT.t[:, 4 + hd, ts], ALU.mult),
                 reads=got, writes=self.oT.tl(4 + hd, tt))

        def chain(blk):
            args = prep_out.pop(blk)
            for ch in range(2):
                chunk(blk, ch, *args)
            if blk % 4 == 3:
                finish(blk // 4)

        parts = {}

        def cap_prep(b):
            if b < NB and b not in parts:
                self._marks = []
                lst = S.capture(lambda: prep(b))
                m1, m2 = self._marks
                parts[b] = (lst[:m1], lst[m1:m2], lst[m2:])

        def part(b, i):
            cap_prep(b)
            return parts[b][i] if b < NB else []
        S.merge(part(0, 0))
        S.merge(part(0, 1), part(1, 0))
        S.merge(part(0, 2), part(1, 1), part(2, 0))
        for blk in range(NB):
            lb = S.capture(lambda: chain(blk))
            S.merge(part(blk + 1, 2), part(blk + 2, 1), part(blk + 3, 0), lb)
            parts.pop(blk, None)

    def out_proj(self, l, s):
        S, NT = self.S, self.NT
        xT, oT, mod = self.xT, self.oT, self.mod

        def one(wo, cc, c, tt):
            ts = slice(tt * 512, (tt + 1) * 512)
            ps = self.mmps()
            for kc in range(NCH):
                S.op("pe", lambda e, kc=kc: e.matmul(ps.t[:, :], wo.t[:, kc, cc * 128:(cc + 1) * 128], oT.t[:, kc, ts],
                                                     start=(kc == 0), stop=(kc == NCH - 1)),
                     reads=wo.tl() + oT.tl(kc, tt), writes=ps.tl())
            S.op("dve", lambda e: e.scalar_tensor_tensor(xT.t[:, c, ts], ps.t[:, :], mod.t[:, l, 16 + c, s:s + 1], xT.t[:, c, ts], ALU.mult, ALU.add),
                 reads=ps.tl() + mod.tl() + xT.tl(c, tt), writes=xT.tl(c, tt))
        for half in range(2):
            wo = self.wo[half]
            S.dma("pool", wo.t[:], self.d["wout"][l, :, :, half * 512:(half + 1) * 512], writes=wo.tl())
        for half in range(2):
            for cc in range(4):
                for tt in range(NT):
                    one(self.wo[half], cc, half * 4 + cc, tt)

    def mlp(self, l, s):
        S, NT, HT = self.S, self.NT, self.HT
        xT, hT, mod = self.xT, self.hT, self.mod
        nh = self.Tn // HT
        tph = HT // 512

        def ff1(wb, j, tt, t2):
            ts = slice(tt * 512, (tt + 1) * 512)
            ps = self.mmps()
            for kc in range(NCH):
                S.op("pe", lambda e, kc=kc: e.matmul(ps.t[:, :], wb.t[:, kc, :], hT.t[:, kc, ts], start=(kc == 0), stop=(kc == NCH - 1)),
                     reads=wb.tl() + hT.tl(kc, tt), writes=ps.tl())
            rl = self.ring(self.relu, "relu")
            hb, jj = self.hid(j)
            S.op("act", lambda e: e.activation(rl.t[:], ps.t[:, :], AF.Relu), reads=ps.tl(), writes=rl.tl())
            S.op("pool", lambda e: e.tensor_tensor(hb.t[:, jj, t2 * 512:(t2 + 1) * 512], rl.t[:], rl.t[:], ALU.mult),
                 reads=rl.tl(), writes=hb.tl(jj, t2))

        def ff2(w2, c, tt, t2):
            ts = slice(tt * 512, (tt + 1) * 512)
            ps = self.mmps()
            for j in range(32):
                hb, jj = self.hid(j)
                S.op("pe", lambda e, j=j, hb=hb, jj=jj: e.matmul(ps.t[:, :], w2.t[:, j, :], hb.t[:, jj, t2 * 512:(t2 + 1) * 512],
                                                                start=(j == 0), stop=(j == 31)),
                     reads=w2.tl() + hb.tl(jj, t2), writes=ps.tl())
            S.op("dve", lambda e: e.scalar_tensor_tensor(xT.t[:, c, ts], ps.t[:, :], mod.t[:, l, 40 + c, s:s + 1], xT.t[:, c, ts], ALU.mult, ALU.add),
                 reads=ps.tl() + mod.tl() + xT.tl(c, tt), writes=xT.tl(c, tt))

        s1 = WStream(self, self.wring, [self.d["ff1"][l, :, :, j * 128:(j + 1) * 128] for hf in range(nh) for j in range(32)], 2)
        s2 = WStream(self, self.w2, [self.d["ff2"][l, :, :, c * 128:(c + 1) * 128] for hf in range(nh) for c in range(NCH)], 1)
        for hf in range(nh):
            for j in range(32):
                wb = s1.next()
                for t2 in range(tph):
                    ff1(wb, j, hf * tph + t2, t2)
            for c in range(NCH):
                w2 = s2.next()
                for t2 in range(tph):
                    ff2(w2, c, hf * tph + t2, t2)


def _chunked(w, L):
    Lk, K, N = w.shape
    return np.ascontiguousarray(w.reshape(Lk, K // 128, 128, N).transpose(0, 2, 1, 3))


def prep_shared(inp, L):
    f = lambda a: np.ascontiguousarray(np.asarray(a, dtype=np.float32))
    sh = {}
    sh["w_ada"] = _chunked(f(inp["w_ada"])[:L], L)
    sh["b_ada"] = f(f(inp["b_ada"])[:L].reshape(L, 48, 128).transpose(2, 0, 1))
    sh["norm_mix"] = f(f(inp["norm_mix"])[:L].reshape(L, NCH, 128).transpose(2, 0, 1))
    sh["norm_mlp"] = f(f(inp["norm_mlp"])[:L].reshape(L, NCH, 128).transpose(2, 0, 1))
    sh["w_in"] = _chunked(f(inp["w_in"])[:L], L)
    sh["sb_q_norm"] = f(np.tile(f(inp["sb_q_norm"])[:L], (1, 2)).T)
    sh["sb_k_norm"] = f(np.tile(f(inp["sb_k_norm"])[:L], (1, 2)).T)
    sh["conv_w"] = f(f(inp["conv_w"])[:L].reshape(L, 4, 12, 128).transpose(3, 0, 2, 1))
    gp = np.concatenate([f(inp["a_log"])[:L], f(inp["dt_bias"])[:L]], axis=1)
    sh["gate_p"] = f(np.broadcast_to(gp[None], (128, L, 8)))
    sh["dn_out_norm"] = f(f(inp["dn_out_norm"])[:L].T)
    sh["w_out"] = _chunked(f(inp["w_out"])[:L], L)
    sh["w_ff1"] = _chunked(f(inp["w_ff1"])[:L], L)
    sh["w_ff2"] = _chunked(f(inp["w_ff2"])[:L], L)
    cc = _consts()
    sh["consts32"], sh["consts16"] = cc[1], cc[3]
    return sh


def prep_core(x2, c2, T):
    ns = x2.shape[0]
    xT = np.ascontiguousarray(np.asarray(x2, np.float32).reshape(ns, T, NCH, 128).transpose(0, 3, 2, 1))
    cT = np.ascontiguousarray(np.asarray(c2, np.float32).reshape(2, NCH, 128).transpose(2, 1, 0))
    return {"xT": xT, "cT": cT}


_CACHE = {}


def kernel(**inputs):
    x = np.asarray(inputs["x"], np.float32)
    c = np.asarray(inputs["c"], np.float32)
    B, T, _ = x.shape
    key = (T, DEPTH, 2)
    if key not in _CACHE:
        _CACHE[key] = Builder(T=T, L=DEPTH, NSEQ=2)
    bld = _CACHE[key]
    sh = prep_shared(inputs, DEPTH)
    in_maps = []
    for core in range(NCORES):
        m = dict(sh)
        m.update(prep_core(x[2 * core:2 * core + 2], c[2 * core:2 * core + 2], T))
        in_maps.append(m)
    res = run_bass_kernel_spmd(bld.nc, in_maps, core_ids=list(range(NCORES)))
    out = np.empty((B, T, D), np.float32)
    for core in range(NCORES):
        oT = res.results[core]["outT"]
        out[2 * core:2 * core + 2] = oT.transpose(0, 3, 2, 1).reshape(2, T, D)
    return out
```

```python
import math
import numpy as np
import concourse.bass as bass
import concourse.mybir as mybir
from concourse.bass_utils import run_bass_kernel_spmd

F32 = mybir.dt.float32
BF16 = mybir.dt.bfloat16
AF = mybir.ActivationFunctionType
ALU = mybir.AluOpType

D = 1024
NCH = 8
DEPTH = 4
SEQ = 2048
NCORES = 8
SB_H, SB_D = 8, 64
DN_H, DN_D = 4, 128
IN_W = 3592
DFF = 4096
EPS = 1e-6
ENGS = ("pe", "act", "dve", "pool", "sp")
ATTACH_WAIT = True


class Tile:
    __slots__ = ("name", "lw", "rd", "sem", "semcnt", "psum")

    def __init__(self, name):
        self.name = name
        self.psum = False
        self.lw = None
        self.rd = []
        self.sem = None
        self.semcnt = 0


class Op:
    __slots__ = ("eng", "fn", "deps", "signal", "sigval", "dma", "dsem", "dval")

    def __init__(self, eng, fn):
        self.eng = eng
        self.fn = fn
        self.deps = []
        self.signal = False
        self.sigval = 0
        self.dma = False
        self.dsem = None
        self.dval = 0


class Sched:
    def __init__(self, nc):
        self.nc = nc
        self.ops = {e: [] for e in ENGS}
        self.ndsem = 0

    def _add(self, op, reads, writes):
        deps = []
        for t in reads:
            if t.lw is not None:
                deps.append(t.lw)
        for t in writes:
            if t.lw is not None:
                deps.append(t.lw)
            deps.extend(t.rd)
        for t in reads:
            t.rd.append(op)
        for t in writes:
            t.lw = op
            t.rd = []
        seen = set()
        for d in deps:
            if d is op or id(d) in seen:
                continue
            seen.add(id(d))
            if (not d.dma) and (not op.dma) and d.eng == "pe" and op.eng == "pe":
                continue
            op.deps.append(d)
            d.signal = True
        self.ops[op.eng].append(op)
        return op

    def op(self, eng, fn, reads=(), writes=()):
        if getattr(self, "cap", None) is not None:
            self.cap.append((eng, fn, list(reads), list(writes)))
            return None
        reads, writes = list(reads), list(writes)
        pr = [t for t in reads if t.psum]
        if pr:
            reads = [t for t in reads if not t.psum]
            writes = writes + [t for t in pr if t not in writes]
        return self._add(Op(eng, fn), reads, writes)

    def dma(self, eng, out, in_, reads=(), writes=()):
        o = Op(eng, None)
        o.dma = True
        tiles = list(writes) + list(reads)
        st = tiles[0]
        if st.sem is None:
            st.sem = self.nc.alloc_semaphore(f"dsem{self.ndsem}")
            self.ndsem += 1
        st.semcnt += 16
        o.dsem = st.sem
        o.dval = st.semcnt
        o.signal = True
        o.fn = lambda e, out=out, in_=in_: e.dma_start(out=out, in_=in_)
        return self._add(o, [], tiles)

    def capture(self, f):
        assert getattr(self, "cap", None) is None
        self.cap = []
        f()
        lst, self.cap = self.cap, None
        return lst

    def merge(self, *lists):
        idx = [0] * len(lists)
        total = sum(len(x) for x in lists)
        for _ in range(total):
            best, bf = None, None
            for k, lst in enumerate(lists):
                if idx[k] < len(lst):
                    fr = idx[k] / len(lst)
                    if bf is None or fr < bf:
                        best, bf = k, fr
            eng, fn, reads, writes = lists[best][idx[best]]
            idx[best] += 1
            self.op(eng, fn, reads, writes)

    def emit(self, final_wait_ops=()):
        nc = self.nc
        esem = {e: nc.alloc_semaphore(f"sem_{e}") for e in ENGS}
        for e in ENGS:
            c = 0
            for o in self.ops[e]:
                if (not o.dma) and o.signal:
                    c += 1
                    o.sigval = c
        stats = {}

        def run(e, eng):
            known = {}
            nw = 0
            for o in self.ops[e]:
                need = []
                for d in o.deps:
                    if d.dma:
                        key, val = d.dsem, d.dval
                    else:
                        key, val = esem[d.eng], d.sigval
                    if known.get(key.num, 0) >= val:
                        continue
                    known[key.num] = val
                    need.append((key, val))
                attach = need.pop() if (need and ATTACH_WAIT) else None
                for key, val in need:
                    eng.wait_ge(key, val)
                    nw += 1
                ins = o.fn(eng)
                if attach is not None:
                    ins._wait_ge(attach[0], attach[1])
                if o.dma:
                    ins.then_inc(o.dsem, 16)
                elif o.signal:
                    ins.then_inc(esem[e], 1)
            if e == "sp":
                for o in final_wait_ops:
                    if o.dma:
                        eng.wait_ge(o.dsem, o.dval)
                    else:
                        eng.wait_ge(esem[o.eng], o.sigval)
            stats[e] = (len(self.ops[e]), nw)

        with nc.Block() as block:
            @block.tensor
            def _(eng):
                run("pe", eng)

            @block.scalar
            def _(eng):
                run("act", eng)

            @block.vector
            def _(eng):
                run("dve", eng)

            @block.gpsimd
            def _(eng):
                run("pool", eng)

            @block.sync
            def _(eng):
                run("sp", eng)
        return stats


class Buf:
    def __init__(self, t, name, grid=None):
        self.t = t
        if grid is None:
            self.T = Tile(name)
        else:
            self.T = np.empty(grid, dtype=object)
            for idx in np.ndindex(*grid):
                self.T[idx] = Tile(f"{name}{idx}")

    def tl(self, *idx):
        if isinstance(self.T, Tile):
            return [self.T]
        sub = self.T[idx] if idx else self.T
        if isinstance(sub, Tile):
            return [sub]
        return list(sub.ravel())


def _consts():
    i = np.arange(128)
    same = (i[:, None] // 64) == (i[None, :] // 64)
    c = {}
    c["ident"] = np.eye(128, dtype=np.float32)
    c["ones"] = np.ones((128, 128), np.float32)
    blk = np.zeros((128, 128), np.float32)
    blk[:64, :64] = 1.0
    blk[64:, 64:] = 1.0
    c["blk64"] = blk
    c["tril"] = (i[:, None] >= i[None, :]).astype(np.float32)
    c["mincl"] = ((i[:, None] <= i[None, :]) & same).astype(np.float32)
    c["mrev"] = ((i[:, None] > i[None, :]) & same).astype(np.float32)
    c["mbI"] = np.where((i[None, :] <= i[:, None]) & same, 0.0, 30000.0).astype(np.float32)
    c["strict"] = (i[None, :] < i[:, None]).astype(np.float32)
    b16 = (i[:, None] // 16) == (i[None, :] // 16)
    b32 = (i[:, None] // 32) == (i[None, :] // 32)
    low = i[None, :] < i[:, None]
    c["m16"] = (b16 & low).astype(np.float32)
    c["m32"] = (b32 & ~b16 & low).astype(np.float32)
    c["m64"] = (same & ~b32 & low).astype(np.float32)
    c["m16T"] = np.ascontiguousarray(c["m16"].T)
    c["m32T"] = np.ascontiguousarray(c["m32"].T)
    n32 = ["ident", "mincl", "mrev", "mbI", "strict", "m16", "m32", "m64", "m16T", "m32T"]
    n16 = ["ident", "ones", "blk64", "tril"]
    return (n32, np.concatenate([c[n] for n in n32], axis=1), n16, np.concatenate([c[n] for n in n16], axis=1))


class WStream:
    def __init__(self, bld, bufs, srcs, depth):
        self.b, self.bufs, self.srcs, self.depth = bld, bufs, list(srcs), depth
        assert len(bufs) > depth
        self.n = 0
        self.issued = 0
        self.live = []
        for _ in range(min(depth, len(self.srcs))):
            self._issue()

    def _issue(self):
        buf = self.bufs[self.issued % len(self.bufs)]
        self.b.S.dma("pool", buf.t[:], self.srcs[self.issued], writes=buf.tl())
        self.live.append(buf)
        self.issued += 1

    def next(self):
        buf = self.live[self.n]
        self.n += 1
        if self.issued < len(self.srcs):
            self._issue()
        return buf


AR_EL = 30208


class Builder:
    def __init__(self, T=SEQ, L=DEPTH, NSEQ=2, dbg=(), stop=None):
        self.stop = stop
        self.Tn = T
        self.L = L
        self.NSEQ = NSEQ
        self.NT = T // 512
        self.NB = T // 128
        self.dbg_names = dbg
        nc = self.nc = bass.Bass("TRN2", target_bir_lowering=False)
        self.S = Sched(nc)
        self.cnt = {}
        self.final = []
        self.phase = {}
        self.build()

    def sb(self, name, shape, dt, grid=None):
        return Buf(self.nc.alloc_sbuf_tensor("s_" + name, list(shape), dt), name, grid)

    def av(self, phase, name, shape, dt, grid=None, base=None):
        ph = self.phase.setdefault(phase, {"off": 0, "tiles": []})
        nel = int(np.prod(shape[1:]))
        nbf = nel * (2 if dt == F32 else 1)
        off = (ph["off"] + 15) // 16 * 16
        ph["off"] = off + nbf
        assert ph["off"] <= AR_EL, (phase, name, ph["off"])
        ap = self.arena[:, off:off + nbf]
        if dt == F32:
            ap = ap.bitcast(F32)
        if len(shape) == 3:
            ap = ap.rearrange("p (a b) -> p a b", a=shape[1], b=shape[2])
        if shape[0] < 128:
            ap = ap[0:shape[0]]
        b = Buf(ap, name, grid)
        ph["tiles"].extend(b.tl())
        return b

    def ring(self, lst, key):
        v = self.cnt.get(key, 0)
        self.cnt[key] = v + 1
        return lst[v % len(lst)]

    def barrier(self, *phases, extra=()):
        tiles = list(extra)
        for p in phases:
            tiles.extend(self.phase[p]["tiles"])
        self.S.op("sp", lambda e: e.nop(), writes=tiles)

    def din(self, name, shape, dt=F32):
        return self.nc.dram_tensor(name, list(shape), dt, kind="ExternalInput").ap()

    def dump(self, name, ap, tiles, shape, dt=F32):
        if name not in self.dbg_names:
            return
        d = self.nc.dram_tensor("dbg_" + name, list(shape), dt, kind="ExternalOutput").ap()
        self.final.append(self.S.dma("sp", d, ap, reads=tiles))

    def build(self):
        nc, S, T, L, NSEQ, NT, NB = self.nc, self.S, self.Tn, self.L, self.NSEQ, self.NT, self.NB
        d_xT = self.din("xT", [NSEQ, 128, NCH, T])
        d_cT = self.din("cT", [128, NCH, 2])
        d_wada = self.din("w_ada", [L, 128, NCH, 6 * D])
        d_bada = self.din("b_ada", [128, L, 48])
        d_nmix = self.din("norm_mix", [128, L, NCH])
        d_nmlp = self.din("norm_mlp", [128, L, NCH])
        d_win = self.din("w_in", [L, 128, NCH, IN_W])
        d_sbq = self.din("sb_q_norm", [128, L])
        d_sbk = self.din("sb_k_norm", [128, L])
        d_conv = self.din("conv_w", [128, L, 12, 4])
        d_gate = self.din("gate_p", [128, L, 8])
        d_dno = self.din("dn_out_norm", [128, L])
        d_wout = self.din("w_out", [L, 128, NCH, D])
        d_ff1 = self.din("w_ff1", [L, 128, NCH, DFF])
        d_ff2 = self.din("w_ff2", [L, 128, 32, D])
        n32, c32m, n16, c16m = _consts()
        d_c32 = self.din("consts32", [128, c32m.shape[1]])
        d_c16 = self.din("consts16", [128, c16m.shape[1]])
        d_out = nc.dram_tensor("outT", [NSEQ, 128, NCH, T], F32, kind="ExternalOutput").ap()
        self.d = dict(win=d_win, wout=d_wout, ff1=d_ff1, ff2=d_ff2)

        self.arena = nc.alloc_sbuf_tensor("arena", [128, AR_EL], BF16)

        c32 = self.sb("c32", [128, c32m.shape[1]], F32)
        c16 = self.sb("c16", [128, c16m.shape[1]], BF16)
        S.dma("sp", c32.t[:], d_c32, writes=c32.tl())
        S.dma("pool", c16.t[:], d_c16, writes=c16.tl())
        self.c32, self.c16 = c32, c16
        self.C32 = lambda n: c32.t[:, n32.index(n) * 128:(n32.index(n) + 1) * 128]
        self.C16 = lambda n: c16.t[:, n16.index(n) * 128:(n16.index(n) + 1) * 128]

        prm = self.sb("prm", [128, 512], F32)
        PT = prm.tl()
        self.prm, self.PT = prm, PT
        o = [0]

        def pslot(n):
            r = (o[0], o[0] + n)
            o[0] += n
            return r
        s_bada, s_nmix, s_nmlp = pslot(L * 48), pslot(L * NCH), pslot(L * NCH)
        s_sbq, s_sbk, s_conv, s_dno, s_gate = pslot(L), pslot(L), pslot(L * 48), pslot(L), pslot(L * 8)
        assert o[0] <= 512

        def pv(s, *shape):
            ap = prm.t[:, s[0]:s[1]]
            if len(shape) == 2:
                ap = ap.rearrange("p (a b) -> p a b", a=shape[0], b=shape[1])
            elif len(shape) == 3:
                ap = ap.rearrange("p (a b c) -> p a b c", a=shape[0], b=shape[1], c=shape[2])
            return ap
        self.pv = pv
        self.slots = dict(sbq=s_sbq, sbk=s_sbk, conv=s_conv, dno=s_dno, gate=s_gate)
        S.dma("sp", pv(s_bada, L, 48), d_bada, writes=PT)
        S.dma("sp", pv(s_nmix, L, NCH), d_nmix, writes=PT)
        S.dma("sp", pv(s_nmlp, L, NCH), d_nmlp, writes=PT)
        S.dma("sp", prm.t[:, s_sbq[0]:s_sbq[1]], d_sbq, writes=PT)
        S.dma("sp", prm.t[:, s_sbk[0]:s_sbk[1]], d_sbk, writes=PT)
        S.dma("sp", pv(s_conv, L, 12, 4), d_conv, writes=PT)
        S.dma("sp", prm.t[:, s_dno[0]:s_dno[1]], d_dno, writes=PT)
        S.dma("sp", pv(s_gate, L, 8), d_gate, writes=PT)
        nexpA = self.sb("nexpA", [128, L, 4], F32)
        self.nexpA = nexpA
        S.op("act", lambda e: e.activation(nexpA.t[:], pv(s_gate, L, 8)[:, :, 0:4], AF.Exp), reads=PT, writes=nexpA.tl())
        S.op("dve", lambda e: e.tensor_scalar(nexpA.t[:], nexpA.t[:], -1.0, None, ALU.mult), reads=nexpA.tl(), writes=nexpA.tl())

        self.PS = PS = [Buf(nc.alloc_psum_tensor(f"ps{i}", [128, 512], F32), f"ps{i}") for i in range(8)]
        for b in PS:
            b.T.psum = True

        cT = self.sb("cT", [128, NCH, 2], F32)
        ctmp = self.sb("ctmp", [128, NCH, 2], F32)
        cond = self.sb("cond", [128, NCH, 2], BF16)
        S.dma("sp", cT.t[:], d_cT, writes=cT.tl())
        S.op("act", lambda e: e.activation(ctmp.t[:], cT.t[:], AF.Exp, scale=-1.0), reads=cT.tl(), writes=ctmp.tl())
        S.op("dve", lambda e: e.tensor_scalar(ctmp.t[:], ctmp.t[:], 1.0, None, ALU.add), reads=ctmp.tl(), writes=ctmp.tl())
        S.op("dve", lambda e: e.reciprocal(ctmp.t[:], ctmp.t[:]), reads=ctmp.tl(), writes=ctmp.tl())
        S.op("dve", lambda e: e.tensor_tensor(cond.t[:], cT.t[:], ctmp.t[:], ALU.mult), reads=cT.tl() + ctmp.tl(), writes=cond.tl())
        mod = self.sb("mod", [128, L, 48, 2], F32)
        self.mod = mod
        wada = [self.av("setup", f"wada{i}", [128, NCH, 512], BF16) for i in range(2)]

        def ada_piece(l, pc):
            wb = self.ring(wada, "wada")
            S.dma("pool", wb.t[:], d_wada[l, :, :, pc * 512:(pc + 1) * 512], writes=wb.tl())
            for jj in range(4):
                j = pc * 4 + jj
                for kc in range(NCH):
                    S.op("pe", lambda e, jj=jj, kc=kc, j=j: e.matmul(
                        PS[0].t[:, j * 2:j * 2 + 2], wb.t[:, kc, jj * 128:(jj + 1) * 128], cond.t[:, kc, :],
                        start=(kc == 0), stop=(kc == NCH - 1)), reads=wb.tl() + cond.tl(), writes=PS[0].tl())

        def ada_evac(l, b):
            S.op("dve", lambda e: e.tensor_tensor(
                mod.t[:, l, :, b], PS[0].t[:, 0:96].rearrange("p (j b) -> p j b", b=2)[:, :, b],
                pv(s_bada, L, 48)[:, l, :], ALU.add), reads=PS[0].tl() + PT, writes=mod.tl())
        for l in range(L):
            for pc in range(12):
                ada_piece(l, pc)
            for b in range(2):
                ada_evac(l, b)
        gains = self.sb("gains", [128, L, 2, NCH, 2], F32)
        self.gains = gains

        def gain_op(l, which, sl, m, b):
            S.op("dve", lambda e: e.scalar_tensor_tensor(
                gains.t[:, l, which, :, b], mod.t[:, l, m * 8:(m + 1) * 8, b], 1.0, pv(sl, L, NCH)[:, l, :], ALU.add, ALU.mult),
                reads=mod.tl() + PT, writes=gains.tl())
        for l in range(L):
            for which, (sl, m) in enumerate(((s_nmix, 1), (s_nmlp, 4))):
                for b in range(2):
                    gain_op(l, which, sl, m, b)
        self.dump("mod", mod.t[:], mod.tl(), [128, L, 48, 2])

        self.xT = self.sb("xT", [128, NCH, T], F32, grid=(NCH, NT))
        self.hT = self.sb("hT", [128, NCH, T], BF16, grid=(NCH, NT))
        self.oT = self.sb("oT", [128, NCH, T], BF16, grid=(NCH, NT))
        self.sqb = [self.sb(f"sqb{i}", [128, 512], BF16) for i in range(2)]
        self.wring = [self.sb(f"wring{i}", [128, NCH, 128], BF16) for i in range(3)]
        self.alloc_phases()

        xT = self.xT
        self.cur = "setup"
        for s in range(NSEQ):
            for c in range(NCH):
                S.dma("sp", xT.t[:, c, :], d_xT[s, :, c, :], writes=xT.tl(c))
            for l in range(L):
                self.layer(l, s)
            for c in range(NCH):
                self.final.append(S.dma("sp", d_out[s, :, c, :], xT.t[:, c, :], reads=xT.tl(c)))
        self.stats = S.emit(final_wait_ops=self.final)

    def switch(self, new, extra=()):
        self.barrier(self.cur, new, extra=extra)
        self.cur = new

    def alloc_phases(self):
        T, NB, NT = self.Tn, self.NB, self.NT
        av = self.av
        self.qa = av("sb", "qa", [128, 2, T], BF16, grid=(2, NT))
        self.ka = av("sb", "ka", [128, 2, T], BF16, grid=(2, NT))
        self.va = av("sb", "va", [128, NB, 256], BF16, grid=(NB,))
        self.wv = av("sb", "wv", [128, NCH, 256], BF16)
        self.a_e = [av("sb", f"a_e{i}", [128, 512], F32) for i in range(3)]
        self.a_x = [av("sb", f"a_x{i}", [128, 512], F32) for i in range(2)]
        self.a_sp = [av("sb", f"a_sp{i}", [128, 512], BF16) for i in range(2)]
        self.a_att = [av("sb", f"a_att{i}", [128, 512], BF16) for i in range(2)]
        self.a_R = [av("sb", f"a_R{i}", [1, 512], BF16) for i in range(2)]
        self.raw32 = [av("sb", f"raw32_{i}", [128, 512], F32) for i in range(2)]
        g = "gdn"
        self.raw = av(g, "raw", [128, T + 4], F32)
        self.CH = min(T, 1024)
        self.cacc = av(g, "cacc", [128, self.CH], F32)
        self.tA = av(g, "tA", [128, self.CH], F32)
        self.acc_t = [Tile(f"acc_t{i}") for i in range(self.CH // 512)]
        self.tA_t = [Tile(f"tA_t{i}") for i in range(self.CH // 512)]
        self.phase[g]["tiles"].extend(self.acc_t + self.tA_t)
        self.kT = av(g, "kT", [128, T], BF16)
        self.qT = av(g, "qT", [128, T], BF16)
        self.vT = av(g, "vT", [128, T], BF16)
        self.gtok = av(g, "gtok", [128, NB, 8], F32)
        self.gcrc = av(g, "gcrc", [128, NB, 8], F32)
        self.gsc = av(g, "gsc", [128, NB, 8], F32)
        self.wgate = av(g, "wgate", [128, NCH, 8], BF16)

        def mk(n, k, dt):
            return [av(g, f"{n}{i}", [128, 128], dt) for i in range(k)]
        self.g_ebc, self.g_tI, self.g_DL, self.g_DLs = mk("ebc", 4, F32), mk("tI", 1, F32), mk("DL", 2, F32), mk("DLs", 1, F32)
        self.g_L32, self.g_Lc, self.g_Uc = mk("L32", 2, F32), mk("Lc", 4, F32), mk("Uc", 4, F32)
        self.g_Xc, self.g_Yc = mk("Xc", 4, F32), mk("Yc", 4, F32)
        self.g_C32, self.g_C32T, self.g_C64 = mk("C32", 3, BF16), mk("C32T", 3, BF16), mk("C64", 3, BF16)
        self.g_T16T, self.g_T16, self.g_Ya, self.g_Yb = mk("T16T", 2, BF16), mk("T16", 2, BF16), mk("Ya", 2, BF16), mk("Yb", 2, BF16)
        self.g_T32T, self.g_T32, self.g_Yd, self.g_TT = mk("T32T", 2, BF16), mk("T32", 2, BF16), mk("Yd", 2, BF16), mk("TT", 2, BF16)
        self.g_A, self.g_AT, self.g_kbg, self.g_kd = mk("A", 2, BF16), mk("AT", 4, BF16), mk("kbg", 3, BF16), mk("kd", 4, BF16)
        self.g_vb, self.g_u, self.g_wT = mk("vb", 3, BF16), mk("u", 3, F32), mk("wT", 3, BF16)
        self.g_qg, self.g_vn = mk("qg", 4, BF16), mk("vn", 2, BF16)
        self.S32 = av(g, "S32", [128, 128], F32)
        self.Sbf = av(g, "Sbf", [128, 128], BF16)
        self.wo = [av("op", f"wo{i}", [128, NCH, 512], BF16) for i in range(2)]
        self.HT = min(1024, T)
        tph = self.HT // 512
        self.hidA = av("mlp", "hidA", [128, 16, self.HT], BF16, grid=(16, tph))
        self.w2 = [av("mlp", f"w2_{i}", [128, 32, 128], BF16) for i in range(2)]
        self.relu = [av("mlp", f"relu{i}", [128, 512], F32) for i in range(2)]
        if T == SEQ:
            ap = self.oT.t[:].rearrange("p c t -> p (c t)").rearrange("p (a b) -> p a b", a=16, b=self.HT)
            self.hidB = Buf(ap, "hidB", grid=(16, tph))
        else:
            self.hidB = self.sb("hidB", [128, 16, self.HT], BF16, grid=(16, tph))
        self.phase["mlp"]["tiles"].extend(self.hidB.tl())

    def hid(self, j):
        return (self.hidA, j) if j < 16 else (self.hidB, j - 16)

    def mmps(self):
        return self.ring([self.PS[0], self.PS[1]], "mmps")


    def norm_to_hT(self, l, s, which):
        S, NT, PS = self.S, self.NT, self.PS
        xT, hT, gains, mod = self.xT, self.hT, self.gains, self.mod
        msh = 0 if which == 0 else 3
        for tt in range(NT):
            ts = slice(tt * 512, (tt + 1) * 512)
            ps = self.mmps()
            for c in range(NCH):
                self._sq_mm(xT.t[:, c, ts], xT.tl(c, tt), ps, "ones", c == 0, c == NCH - 1, eng=("act" if c % 2 else "pool"))
            S.op("act", lambda e, ps=ps: e.activation(ps.t[:, :], ps.t[:, :], AF.Ln, bias=EPS, scale=1.0 / D), reads=ps.tl(), writes=ps.tl())
            S.op("act", lambda e, ps=ps: e.activation(ps.t[:, :], ps.t[:, :], AF.Exp, scale=-0.5), reads=ps.tl(), writes=ps.tl())
            for c in range(NCH):
                tmp = PS[2 + c % 2]
                S.op("dve", lambda e, c=c, ts=ts, ps=ps, tmp=tmp: e.tensor_tensor(tmp.t[:, :], xT.t[:, c, ts], ps.t[:, :], ALU.mult),
                     reads=xT.tl(c, tt) + ps.tl(), writes=tmp.tl())
                S.op("dve", lambda e, c=c, ts=ts, tmp=tmp: e.tensor_scalar(
                    hT.t[:, c, ts], tmp.t[:, :], gains.t[:, l, which, c, s:s + 1], mod.t[:, l, msh * 8 + c, s:s + 1], ALU.mult, ALU.add),
                    reads=tmp.tl() + gains.tl() + mod.tl(), writes=hT.tl(c, tt))

    def _sq_mm(self, src_ap, src_tiles, ps, ones_name, start, stop, eng="pool"):
        S = self.S
        sq = self.ring(self.sqb, "sqb")
        if eng == "act":
            S.op("act", lambda e: e.activation(sq.t[:], src_ap, AF.Square), reads=src_tiles, writes=sq.tl())
        else:
            S.op("pool", lambda e: e.tensor_tensor(sq.t[:], src_ap, src_ap, ALU.mult), reads=src_tiles, writes=sq.tl())
        S.op("pe", lambda e: e.matmul(ps.t[:, :], self.C16(ones_name), sq.t[:], start=start, stop=stop),
             reads=sq.tl() + self.c16.tl(), writes=ps.tl())

    def group_norm(self, src_ap, src_tiles, ones_name, nfeat, gain_ap, out_ap, out_tiles, bias2=0.0):
        S = self.S
        ps = self.mmps()
        self._sq_mm(src_ap, src_tiles, ps, ones_name, True, True, eng="act")
        sc = 1.0 if nfeat is None else 1.0 / nfeat
        S.op("act", lambda e: e.activation(ps.t[:, :], ps.t[:, :], AF.Ln, bias=EPS, scale=sc), reads=ps.tl(), writes=ps.tl())
        S.op("act", lambda e: e.activation(ps.t[:, :], ps.t[:, :], AF.Exp, scale=-0.5, bias=bias2), reads=ps.tl(), writes=ps.tl())
        if gain_ap is None:
            S.op("dve", lambda e: e.tensor_tensor(out_ap, src_ap, ps.t[:, :], ALU.mult), reads=src_tiles + ps.tl(), writes=out_tiles)
        else:
            S.op("dve", lambda e: e.scalar_tensor_tensor(out_ap, src_ap, gain_ap, ps.t[:, :], ALU.mult, ALU.mult),
                 reads=src_tiles + ps.tl() + self.PT, writes=out_tiles)

    def proj_chunk(self, l, col0, evac):
        S, hT = self.S, self.hT
        assert self.win_cols[self.win_stream.n] == col0
        wb = self.win_stream.next()
        for tt in range(self.NT):
            ts = slice(tt * 512, (tt + 1) * 512)
            ps = self.mmps()
            for kc in range(NCH):
                S.op("pe", lambda e, ps=ps, kc=kc, ts=ts: e.matmul(ps.t[:, :], wb.t[:, kc, :], hT.t[:, kc, ts],
                                                                  start=(kc == 0), stop=(kc == NCH - 1)),
                     reads=wb.tl() + hT.tl(kc, tt), writes=ps.tl())
            evac(tt, ps)

    def layer(self, l, s):
        T = self.Tn
        cols = []
        for half in range(2):
            for which in range(2):
                for cc in range(2):
                    cols.append(which * 512 + (half * 2 + cc) * 128)
        for hd in range(DN_H):
            for base in (1536, 2048, 2560, 3072):
                cols.append(base + hd * 128)
        self.win_cols = cols
        self.win_stream = WStream(self, self.wring, [self.d["win"][l, :, :, c0:c0 + 128] for c0 in cols], 2)
        if self.stop == "setup":
            return
        self.norm_to_hT(l, s, 0)
        self.dump(f"h1_{l}", self.hT.t[:], self.hT.tl(), [128, NCH, T], BF16)
        if self.stop == "norm":
            return
        self.switch("sb", extra=self.oT.tl())
        for half in range(2):
            self.sb_proj(l, half)
            if self.stop == "sbproj":
                return
            self.sb_attn(l, half)
        if self.stop == "attn":
            return
        self.switch("gdn")
        self.gdn(l)
        self.dump(f"oT_{l}", self.oT.t[:], self.oT.tl(), [128, NCH, T], BF16)
        if self.stop == "gdn":
            return
        self.switch("op")
        self.out_proj(l, s)
        self.dump(f"x1_{l}", self.xT.t[:], self.xT.tl(), [128, NCH, T])
        self.norm_to_hT(l, s, 1)
        self.switch("mlp", extra=self.oT.tl())
        self.mlp(l, s)
        self.dump(f"x2_{l}", self.xT.t[:], self.xT.tl(), [128, NCH, T])

    def sb_proj(self, l, half):
        S, NT, NB, hT = self.S, self.NT, self.NB, self.hT
        for which, dst, slot in ((0, self.qa, self.slots["sbq"]), (1, self.ka, self.slots["sbk"])):
            for cc in range(2):
                def evac(tt, ps, cc=cc, dst=dst, slot=slot):
                    r = self.raw32[tt % 2]
                    S.op("act", lambda e: e.activation(r.t[:], ps.t[:, :], AF.Copy), reads=ps.tl(), writes=r.tl())
                    self.group_norm(r.t[:], r.tl(), "blk64", SB_D, self.prm.t[:, slot[0] + l:slot[0] + l + 1],
                                    dst.t[:, cc, tt * 512:(tt + 1) * 512], dst.tl(cc, tt))
                self.proj_chunk(l, which * 512 + (half * 2 + cc) * 128, evac)
        wv, va = self.wv, self.va
        S.dma("pool", wv.t[:], self.d["win"][l, :, :, 1024 + half * 256:1024 + (half + 1) * 256], writes=wv.tl())

        def vblk(blk):
            ps = self.mmps()
            tt = blk // 4
            for kc in range(NCH):
                S.op("pe", lambda e, kc=kc: e.matmul(ps.t[:, 0:256], hT.t[:, kc, blk * 128:(blk + 1) * 128], wv.t[:, kc, :],
                                                     start=(kc == 0), stop=(kc == NCH - 1)),
                     reads=wv.tl() + hT.tl(kc, tt), writes=ps.tl())
            if blk % 2 == 0:
                S.op("dve", lambda e: e.tensor_copy(va.t[:, blk, :], ps.t[:, 0:256]), reads=ps.tl(), writes=va.tl(blk))
            else:
                S.op("act", lambda e: e.activation(va.t[:, blk, :], ps.t[:, 0:256], AF.Copy), reads=ps.tl(), writes=va.tl(blk))
        for blk in range(NB):
            vblk(blk)
        self.dump(f"qa_{l}_{half}", self.qa.t[:], self.qa.tl(), [128, 2, self.Tn], BF16)
        self.dump(f"ka_{l}_{half}", self.ka.t[:], self.ka.tl(), [128, 2, self.Tn], BF16)
        self.dump(f"va_{l}_{half}", self.va.t[:], self.va.tl(), [128, NB, 256], BF16)

    def sb_attn(self, l, half):
        S, NT = self.S, self.NT
        PS, C16 = self.PS, self.C16
        qa, ka, va, oT = self.qa, self.ka, self.va, self.oT
        c16t = self.c16.tl()
        scale = SB_D ** -0.5
        items = []
        for cc in range(2):
            for qt in range(NT):
                for kb in range(4 * qt + 3, -1, -1):
                    for hh in (2 * cc, 2 * cc + 1):
                        items.append((hh, qt, kb))
        n_it = len(items)
        grp = {n: items[n][0] % 2 for n in range(n_it)}

        def geom(n):
            hh, qt, kb = items[n]
            i = kb - 4 * qt
            c0 = 128 * i if i > 0 else 0
            return hh, qt, kb, i, c0, 512 - c0

        def s1(n):
            hh, qt, kb, i, c0, w = geom(n)
            cc, p0 = hh // 2, (hh % 2) * 64
            zp = PS[2 + n % 2]
            S.op("pe", lambda e: e.matmul(zp.t[:, 0:w], ka.t[p0:p0 + 64, cc, kb * 128:(kb + 1) * 128],
                                          qa.t[p0:p0 + 64, cc, qt * 512 + c0:(qt + 1) * 512], start=True, stop=True),
                 reads=ka.tl(cc, kb // 4) + qa.tl(cc, qt), writes=zp.tl())

        def s2(n):
            hh, qt, kb, i, c0, w = geom(n)
            zp = PS[2 + n % 2]
            eb, sp = self.a_e[n % 3], self.a_sp[n % 2]
            S.op("act", lambda e: e.activation(eb.t[:, 0:w], zp.t[:, 0:w], AF.Exp, scale=scale), reads=zp.tl(), writes=eb.tl())
            S.op("act", lambda e: e.activation(sp.t[:, 0:w], eb.t[:, 0:w], AF.Ln, bias=1.0), reads=eb.tl(), writes=sp.tl())
            if i >= 0:
                S.op("pool", lambda e: e.affine_select(sp.t[:, 0:128], sp.t[:, 0:128], [[1, 128]], ALU.is_gt, 0.0,
                                                       base=0, channel_multiplier=-1), reads=sp.tl(), writes=sp.tl())

        def s3(n):
            hh, qt, kb, i, c0, w = geom(n)
            cp = PS[4 + n % 2]
            sp = self.a_sp[n % 2]
            R = self.a_R[grp[n] % 2]
            first = (kb == 4 * qt + 3)
            S.op("pe", lambda e: e.matmul(cp.t[:, 0:w], C16("tril"), sp.t[:, 0:w], start=True, stop=first),
                 reads=sp.tl() + c16t, writes=cp.tl())
            if not first:
                r0 = 128 if i >= 0 else 0
                S.op("pe", lambda e: e.matmul(cp.t[:, r0:w], C16("ones")[0:1, :], R.t[0:1, c0 + r0:512], start=False, stop=True),
                     reads=R.tl() + c16t, writes=cp.tl())

        def s4(n):
            hh, qt, kb, i, c0, w = geom(n)
            cp = PS[4 + n % 2]
            eb, xb, at = self.a_e[n % 3], self.a_x[n % 2], self.a_att[n % 2]
            S.op("act", lambda e: e.activation(xb.t[:, 0:w], cp.t[:, 0:w], AF.Exp, scale=-1.0), reads=cp.tl(), writes=xb.tl())
            if kb > 0:
                R = self.a_R[grp[n] % 2]
                S.op("dve", lambda e: e.tensor_copy(R.t[0:1, c0:512], cp.t[0:1, 0:w]), reads=cp.tl(), writes=R.tl())
            S.op("dve", lambda e: e.tensor_tensor(at.t[:, 0:w], eb.t[:, 0:w], xb.t[:, 0:w], ALU.mult),
                 reads=eb.tl() + xb.tl(), writes=at.tl())
            if i >= 0:
                S.op("pool", lambda e: e.affine_select(at.t[:, 0:128], at.t[:, 0:128], [[1, 128]], ALU.is_gt, 0.0,
                                                       base=0, channel_multiplier=-1), reads=at.tl(), writes=at.tl())

        def s5(n):
            hh, qt, kb, i, c0, w = geom(n)
            cc, p0 = hh // 2, (hh % 2) * 64
            c = half * 2 + cc
            op_ = PS[6 + grp[n] % 2]
            at = self.a_att[n % 2]
            last = (kb == 0)
            first = (kb == 4 * qt + 3)
            vl = va.t[:, kb, cc * 128:(cc + 1) * 128]
            if i >= 0 and w > 128:
                S.op("pe", lambda e: e.matmul(op_.t[:, c0:c0 + 128], vl, at.t[:, 0:128], start=first, stop=False),
                     reads=at.tl() + va.tl(kb), writes=op_.tl())
                S.op("pe", lambda e: e.matmul(op_.t[:, c0 + 128:512], vl, at.t[:, 128:w], start=False, stop=last),
                     reads=at.tl() + va.tl(kb), writes=op_.tl())
            else:
                S.op("pe", lambda e: e.matmul(op_.t[:, c0:512], vl, at.t[:, 0:w], start=first, stop=last),
                     reads=at.tl() + va.tl(kb), writes=op_.tl())
            if last:
                S.op("dve", lambda e: e.tensor_copy(oT.t[p0:p0 + 64, c, qt * 512:(qt + 1) * 512], op_.t[p0:p0 + 64, :]),
                     reads=op_.tl(), writes=oT.tl(c, qt))

        for n in range(n_it + 2):
            if n < n_it:
                s1(n)
                s2(n)
            if 0 <= n - 1 < n_it:
                s3(n - 1)
                s4(n - 1)
            if 0 <= n - 2 < n_it:
                s5(n - 2)

    def gdn(self, l):
        S, NT, NB, T = self.S, self.NT, self.NB, self.Tn
        PS, C32, hT = self.PS, self.C32, self.hT
        c32t = self.c32.tl()
        gtok, gcrc, gsc, wg = self.gtok, self.gcrc, self.gsc, self.wgate
        PT = self.PT
        gpar = self.pv(self.slots["gate"], self.L, 8)
        nexpA = self.nexpA
        S.dma("pool", wg.t[:], self.d["win"][l, :, :, 3584:3592], writes=wg.tl())
        ps = self.mmps()

        def gproj(blk):
            for kc in range(NCH):
                S.op("pe", lambda e, kc=kc: e.matmul(ps.t[:, blk * 8:blk * 8 + 8], hT.t[:, kc, blk * 128:(blk + 1) * 128], wg.t[:, kc, :],
                                                     start=(kc == 0), stop=(kc == NCH - 1)),
                     reads=wg.tl() + hT.tl(kc, blk // 4), writes=ps.tl())
        for blk in range(NB):
            gproj(blk)
        psv = ps.t[:, 0:NB * 8].rearrange("p (a b) -> p a b", b=8)

        def ghead(h):
            S.op("act", lambda e: e.activation(gtok.t[:, :, h], psv[:, :, h], AF.Exp, bias=gpar[:, l, 4 + h:5 + h]),
                 reads=ps.tl() + PT, writes=gtok.tl())
            S.op("act", lambda e: e.activation(gtok.t[:, :, h], gtok.t[:, :, h], AF.Ln, bias=1.0), reads=gtok.tl(), writes=gtok.tl())
            S.op("dve", lambda e: e.tensor_scalar(gtok.t[:, :, h], gtok.t[:, :, h], nexpA.t[:, l, h:h + 1], None, ALU.mult),
                 reads=gtok.tl() + nexpA.tl(), writes=gtok.tl())
        for h in range(DN_H):
            ghead(h)
        S.op("act", lambda e: e.activation(gtok.t[:, :, 4:8], psv[:, :, 4:8], AF.Exp, scale=-1.0), reads=ps.tl(), writes=gtok.tl())
        S.op("dve", lambda e: e.tensor_scalar(gtok.t[:, :, 4:8], gtok.t[:, :, 4:8], 1.0, None, ALU.add), reads=gtok.tl(), writes=gtok.tl())
        S.op("dve", lambda e: e.reciprocal(gtok.t[:, :, 4:8], gtok.t[:, :, 4:8]), reads=gtok.tl(), writes=gtok.tl())
        ps2 = self.mmps()

        def gcs(blk):
            S.op("pe", lambda e: e.matmul(ps2.t[:, blk * 8:blk * 8 + 4], C32("mincl"), gtok.t[:, blk, 0:4], start=True, stop=True),
                 reads=gtok.tl() + c32t, writes=ps2.tl())
            S.op("pe", lambda e: e.matmul(ps2.t[:, blk * 8 + 4:blk * 8 + 8], C32("mrev"), gtok.t[:, blk, 0:4], start=True, stop=True),
                 reads=gtok.tl() + c32t, writes=ps2.tl())
        for blk in range(NB):
            gcs(blk)
        S.op("dve", lambda e: e.tensor_copy(gcrc.t[:].rearrange("p a b -> p (a b)"), ps2.t[:, 0:NB * 8]), reads=ps2.tl(), writes=gcrc.tl())
        S.op("act", lambda e: e.activation(gsc.t[:], gcrc.t[:], AF.Exp), reads=gcrc.tl(), writes=gsc.tl())
        S.op("dve", lambda e: e.tensor_tensor(gsc.t[:, :, 0:4], gsc.t[:, :, 0:4], gtok.t[:, :, 4:8], ALU.mult),
             reads=gsc.tl() + gtok.tl(), writes=gsc.tl())
        self.dump(f"gtok_{l}", gtok.t[:], gtok.tl(), [128, NB, 8])
        self.dump(f"gcrc_{l}", gcrc.t[:], gcrc.tl(), [128, NB, 8])
        S.op("pool", lambda e: e.memset(self.raw.t[:, 0:3], 0.0), writes=self.raw.tl())
        for hd in range(DN_H):
            self.gdn_head(l, hd)

    def sigmoid_to(self, dst_ap, dst_tiles, src_ap, src_tiles):
        S = self.S
        S.op("act", lambda e: e.activation(dst_ap, src_ap, AF.Exp, scale=-1.0), reads=src_tiles, writes=dst_tiles)
        S.op("act", lambda e: e.activation(dst_ap, dst_ap, AF.Ln, bias=1.0), reads=dst_tiles, writes=dst_tiles)
        S.op("act", lambda e: e.activation(dst_ap, dst_ap, AF.Exp, scale=-1.0), reads=dst_tiles, writes=dst_tiles)

    def gdn_stream(self, l, hd, kind):
        S, NT, T = self.S, self.NT, self.Tn
        raw = self.raw
        col0 = {"q": 1536, "k": 2048, "v": 2560, "z": 3072}[kind] + hd * 128

        def evac(tt, ps):
            if tt % 2 == 0:
                S.op("dve", lambda e: e.tensor_copy(raw.t[:, 3 + tt * 512:3 + (tt + 1) * 512], ps.t[:, :]), reads=ps.tl(), writes=raw.tl())
            else:
                S.op("act", lambda e: e.activation(raw.t[:, 3 + tt * 512:3 + (tt + 1) * 512], ps.t[:, :], AF.Copy), reads=ps.tl(), writes=raw.tl())
        self.proj_chunk(l, col0, evac)
        cw = self.pv(self.slots["conv"], self.L, 12, 4)
        PT = self.PT
        j = {"q": 0, "k": 4, "v": 8, "z": 0}[kind] + hd
        b2 = math.log(DN_D ** -0.5) if kind == "q" else 0.0
        nsub = self.CH // 512

        def piece(tt):
            sub = tt % nsub
            t0 = tt * 512
            acc = self.cacc.t[:, sub * 512:(sub + 1) * 512]
            tA = self.tA.t[:, sub * 512:(sub + 1) * 512]
            acct, tAt = [self.acc_t[sub]], [self.tA_t[sub]]
            if kind == "z":
                self.sigmoid_to(tA, tAt, raw.t[:, 3 + t0:3 + t0 + 512], raw.tl())
                S.op("dve", lambda e: e.tensor_tensor(self.oT.t[:, 4 + hd, t0:t0 + 512], raw.t[:, 3 + t0:3 + t0 + 512], tA, ALU.mult),
                     reads=raw.tl() + tAt, writes=self.oT.tl(4 + hd, tt))
                return
            S.op("act", lambda e: e.activation(acc, raw.t[:, t0:t0 + 512], AF.Copy, scale=cw[:, l, j, 0:1]),
                 reads=raw.tl() + PT, writes=acct)
            for k in range(1, 4):
                S.op("dve", lambda e, k=k: e.scalar_tensor_tensor(
                    acc, raw.t[:, t0 + k:t0 + k + 512], cw[:, l, j, k:k + 1], acc, ALU.mult, ALU.add),
                    reads=raw.tl() + acct + PT, writes=acct)
            self.sigmoid_to(tA, tAt, acc, acct)
            if kind == "v":
                S.op("dve", lambda e: e.tensor_tensor(self.vT.t[:, t0:t0 + 512], acc, tA, ALU.mult),
                     reads=acct + tAt, writes=self.vT.tl())
                return
            S.op("dve", lambda e: e.tensor_tensor(acc, acc, tA, ALU.mult), reads=acct + tAt, writes=acct)
            dst = self.qT if kind == "q" else self.kT
            self.group_norm(acc, acct, "ones", None, None, dst.t[:, t0:t0 + 512], dst.tl(), bias2=b2)

        for t2 in range(0, NT, nsub):
            lists = [S.capture(lambda tt=tt: piece(tt)) for tt in range(t2, min(t2 + nsub, NT))]
            S.merge(*lists)

    def tri_inverse(self, L32):
        S, ring, C32c, C16c = self.S, self.ring, self.C32, self.C16
        c32t = self.c32.tl()
        ident32 = C32c("ident")
        PS = self.PS
        pu = PS[3]
        S.op("pe", lambda e: e.matmul(pu.t[:, 0:128], L32.t[:], ident32, start=True, stop=True), reads=L32.tl() + c32t, writes=pu.tl())
        L16, U16 = ring(self.g_Lc, "Lc"), ring(self.g_Uc, "Uc")
        LB, UB = ring(self.g_Lc, "Lc"), ring(self.g_Uc, "Uc")
        C32, C32T, C64 = ring(self.g_C32, "C32"), ring(self.g_C32T, "C32T"), ring(self.g_C64, "C64")
        X0, Y0 = ring(self.g_Xc, "Xc"), ring(self.g_Yc, "Yc")
        XB, YB = ring(self.g_Xc, "Xc"), ring(self.g_Yc, "Yc")
        S.op("dve", lambda e: e.tensor_tensor(L16.t[:], L32.t[:], C32c("m16"), ALU.mult), reads=L32.tl() + c32t, writes=L16.tl())
        S.op("dve", lambda e: e.tensor_tensor(C32.t[:], L32.t[:], C32c("m32"), ALU.mult), reads=L32.tl() + c32t, writes=C32.tl())
        S.op("pool", lambda e: e.tensor_tensor(C64.t[:], L32.t[:], C32c("m64"), ALU.mult), reads=L32.tl() + c32t, writes=C64.tl())
        S.op("dve", lambda e: e.tensor_tensor(U16.t[:], pu.t[:, 0:128], C32c("m16T"), ALU.mult), reads=pu.tl() + c32t, writes=U16.tl())
        S.op("dve", lambda e: e.tensor_tensor(C32T.t[:], pu.t[:, 0:128], C32c("m32T"), ALU.mult), reads=pu.tl() + c32t, writes=C32T.tl())
        S.op("pool", lambda e: e.tensor_tensor(Y0.t[:], ident32, L16.t[:], ALU.subtract), reads=L16.tl() + c32t, writes=Y0.tl())
        S.op("dve", lambda e: e.tensor_tensor(X0.t[:], ident32, U16.t[:], ALU.subtract), reads=U16.tl() + c32t, writes=X0.tl())
        Lk, Uk, Xk, Yk = L16, U16, X0, Y0
        if getattr(S, "cap", None) is not None:
            self._marks.append(len(S.cap))
        first_call = not getattr(self, "_tri_dbg", False)
        self._tri_dbg = True
        if first_call:
            self.dump("dbgX0", Xk.t[:], Xk.tl(), [128, 128])
            self.dump("dbgU16", Uk.t[:], Uk.tl(), [128, 128])
            self.dump("dbgL16", Lk.t[:], Lk.tl(), [128, 128])
        T16T = T16 = None
        for k in range(3):
            last = (k == 2)
            pq = PS[4]
            S.op("pe", lambda e, Uk=Uk, Lk=Lk, pq=pq: e.matmul(pq.t[:, 0:128], Uk.t[:], Lk.t[:], start=True, stop=True),
                 reads=Uk.tl() + Lk.tl(), writes=pq.tl())
            S.op("pe", lambda e, Uk=Uk, Lk=Lk, pq=pq: e.matmul(pq.t[:, 128:256], Lk.t[:], Uk.t[:], start=True, stop=True),
                 reads=Uk.tl() + Lk.tl(), writes=pq.tl())
            Ln_, Un_ = (LB, UB) if k % 2 == 0 else (L16, U16)
            S.op("act", lambda e, Ln_=Ln_, pq=pq: e.activation(Ln_.t[:], pq.t[:, 0:128], AF.Copy), reads=pq.tl(), writes=Ln_.tl())
            S.op("dve", lambda e, Un_=Un_, pq=pq: e.tensor_copy(Un_.t[:], pq.t[:, 128:256]), reads=pq.tl(), writes=Un_.tl())
            S.op("pe", lambda e, Ln_=Ln_, Xk=Xk, pq=pq: e.matmul(pq.t[:, 256:384], Ln_.t[:], Xk.t[:], start=True, stop=True),
                 reads=Ln_.tl() + Xk.tl(), writes=pq.tl())
            S.op("pe", lambda e, Un_=Un_, Yk=Yk, pq=pq: e.matmul(pq.t[:, 384:512], Un_.t[:], Yk.t[:], start=True, stop=True),
                 reads=Un_.tl() + Yk.tl(), writes=pq.tl())
            if last:
                Xn, Yn = ring(self.g_T16T, "T16T"), ring(self.g_T16, "T16")
            else:
                Xn, Yn = (XB, YB) if k % 2 == 0 else (X0, Y0)
            S.op("dve", lambda e, Xn=Xn, Xk=Xk, pq=pq: e.tensor_tensor(Xn.t[:], pq.t[:, 256:384], Xk.t[:], ALU.add),
                 reads=pq.tl() + Xk.tl(), writes=Xn.tl())
            S.op("dve", lambda e, Yn=Yn, Yk=Yk, pq=pq: e.tensor_tensor(Yn.t[:], pq.t[:, 384:512], Yk.t[:], ALU.add),
                 reads=pq.tl() + Yk.tl(), writes=Yn.tl())
            Uk, Lk, Xk, Yk = Un_, Ln_, Xn, Yn
        T16T, T16 = Xk, Yk
        if getattr(S, "cap", None) is not None:
            self._marks.append(len(S.cap))
        if first_call:
            self.dump("dbgT16T", T16T.t[:], T16T.tl(), [128, 128], BF16)
        pd = PS[5]
        S.op("pe", lambda e: e.matmul(pd.t[:, 0:128], C32.t[:], T16T.t[:], start=True, stop=True), reads=C32.tl() + T16T.tl(), writes=pd.tl())
        S.op("pe", lambda e: e.matmul(pd.t[:, 128:256], C32T.t[:], T16.t[:], start=True, stop=True), reads=C32T.tl() + T16.tl(), writes=pd.tl())
        Ya, Yb = ring(self.g_Ya, "Ya"), ring(self.g_Yb, "Yb")
        S.op("act", lambda e: e.activation(Ya.t[:], pd.t[:, 0:128], AF.Copy), reads=pd.tl(), writes=Ya.tl())
        S.op("dve", lambda e: e.tensor_copy(Yb.t[:], pd.t[:, 128:256]), reads=pd.tl(), writes=Yb.tl())
        pd2 = PS[5]
        S.op("pe", lambda e: e.matmul(pd2.t[:, 0:128], T16.t[:], Ya.t[:], start=True, stop=True), reads=T16.tl() + Ya.tl(), writes=pd2.tl())
        S.op("pe", lambda e: e.matmul(pd2.t[:, 128:256], T16T.t[:], Yb.t[:], start=True, stop=True), reads=T16T.tl() + Yb.tl(), writes=pd2.tl())
        T32T, T32 = ring(self.g_T32T, "T32T"), ring(self.g_T32, "T32")
        S.op("dve", lambda e: e.scalar_tensor_tensor(T32T.t[:], pd2.t[:, 0:128], -1.0, T16T.t[:], ALU.mult, ALU.add),
             reads=pd2.tl() + T16T.tl(), writes=T32T.tl())
        S.op("dve", lambda e: e.scalar_tensor_tensor(T32.t[:], pd2.t[:, 128:256], -1.0, T16.t[:], ALU.mult, ALU.add),
             reads=pd2.tl() + T16.tl(), writes=T32.tl())
        pe1 = PS[5]
        S.op("pe", lambda e: e.matmul(pe1.t[:, 0:128], C64.t[:], T32T.t[:], start=True, stop=True), reads=C64.tl() + T32T.tl(), writes=pe1.tl())
        Yd = ring(self.g_Yd, "Yd")
        S.op("act", lambda e: e.activation(Yd.t[:], pe1.t[:, 0:128], AF.Copy), reads=pe1.tl(), writes=Yd.tl())
        S.op("pe", lambda e: e.matmul(pe1.t[:, 128:256], T32.t[:], Yd.t[:], start=True, stop=True), reads=T32.tl() + Yd.tl(), writes=pe1.tl())
        TT = ring(self.g_TT, "TT")
        S.op("dve", lambda e: e.scalar_tensor_tensor(TT.t[:], pe1.t[:, 128:256], -1.0, T32T.t[:], ALU.mult, ALU.add),
             reads=pe1.tl() + T32T.tl(), writes=TT.tl())
        return TT

    def gdn_head(self, l, hd):
        S, NT, NB, T = self.S, self.NT, self.NB, self.Tn
        PS, C16, C32 = self.PS, self.C16, self.C32
        c16t, c32t = self.c16.tl(), self.c32.tl()
        kT, qT, vT = self.kT, self.qT, self.vT
        gtok, gcrc, gsc = self.gtok, self.gcrc, self.gsc
        ring = self.ring
        for kind in ("q", "k", "v", "z"):
            self.gdn_stream(l, hd, kind)
        self.dump(f"qT_{l}_{hd}", qT.t[:], qT.tl(), [128, T], BF16)
        self.dump(f"kT_{l}_{hd}", kT.t[:], kT.tl(), [128, T], BF16)
        self.dump(f"vT_{l}_{hd}", vT.t[:], vT.tl(), [128, T], BF16)

        S32, Sbf = self.S32, self.Sbf
        S.op("dve", lambda e: e.memset(S32.t[:], 0.0), writes=S32.tl())
        S.op("pool", lambda e: e.memset(Sbf.t[:], 0.0), writes=Sbf.tl())
        PB = [PS[2], PS[3], PS[4], PS[5]]
        ident16 = C16("ident")

        def psr():
            return ring(PB, "gps")
        prep_out = {}

        def prep(blk):
            bs = slice(blk * 128, (blk + 1) * 128)
            gcol = gtok.t[:, blk, hd:hd + 1]
            bcol = gtok.t[:, blk, 4 + hd:5 + hd]
            gccol = gcrc.t[:, blk, hd:hd + 1]
            bgcol = gsc.t[:, blk, hd:hd + 1]
            erccol = gsc.t[:, blk, 4 + hd:5 + hd]
            p1 = PS[2]
            S.op("pe", lambda e: e.matmul(p1.t[:, 0:128], gcol.to_broadcast([128, 128]), C32("mincl"), start=True, stop=True),
                 reads=gtok.tl() + c32t, writes=p1.tl())
            ebc, tI, DL, DLs = ring(self.g_ebc, "ebc"), ring(self.g_tI, "tI"), ring(self.g_DL, "DL"), ring(self.g_DLs, "DLs")
            S.op("act", lambda e: e.activation(ebc.t[:], p1.t[:, 0:128], AF.Exp), reads=p1.tl(), writes=ebc.tl())
            S.op("dve", lambda e: e.scalar_tensor_tensor(tI.t[:], p1.t[:, 0:128], gccol, C32("mbI"), ALU.subtract, ALU.add),
                 reads=p1.tl() + gcrc.tl() + c32t, writes=tI.tl())
            S.op("act", lambda e: e.activation(DL.t[:], tI.t[:], AF.Exp, scale=-1.0), reads=tI.tl(), writes=DL.tl())
            S.op("dve", lambda e: e.tensor_tensor(DLs.t[:], DL.t[:], C32("strict"), ALU.mult), reads=DL.tl() + c32t, writes=DLs.tl())
            p2 = PS[3]
            S.op("pe", lambda e: e.matmul(p2.t[:, 0:128], kT.t[:, bs], kT.t[:, bs], start=True, stop=True), reads=kT.tl(), writes=p2.tl())
            S.op("pe", lambda e: e.matmul(p2.t[:, 128:256], qT.t[:, bs], kT.t[:, bs], start=True, stop=True), reads=kT.tl() + qT.tl(), writes=p2.tl())
            L32, Ab = ring(self.g_L32, "L32"), ring(self.g_A, "A")
            S.op("dve", lambda e: e.scalar_tensor_tensor(L32.t[:], p2.t[:, 0:128], bcol, DLs.t[:], ALU.mult, ALU.mult),
                 reads=p2.tl() + gtok.tl() + DLs.tl(), writes=L32.tl())
            S.op("dve", lambda e: e.tensor_tensor(Ab.t[:], p2.t[:, 128:256], DL.t[:], ALU.mult), reads=p2.tl() + DL.tl(), writes=Ab.tl())
            p3 = PS[2]
            p3b = p3.t[:, :].bitcast(BF16)
            S.op("pe", lambda e: e.transpose(p3b[:, 128:256], Ab.t[:], ident16), reads=Ab.tl() + c16t, writes=p3.tl())
            S.op("pe", lambda e: e.transpose(p3b[:, 256:384], kT.t[:, bs], ident16), reads=kT.tl() + c16t, writes=p3.tl())
            S.op("pe", lambda e: e.transpose(p3b[:, 384:512], vT.t[:, bs], ident16), reads=vT.tl() + c16t, writes=p3.tl())
            AT = ring(self.g_AT, "AT")
            kbg, kd, vb = ring(self.g_kbg, "kbg"), ring(self.g_kd, "kd"), ring(self.g_vb, "vb")
            S.op("act", lambda e: e.activation(AT.t[:], p3b[:, 128:256], AF.Copy), reads=p3.tl(), writes=AT.tl())
            S.op("act", lambda e: e.activation(kbg.t[:], p3b[:, 256:384], AF.Copy, scale=bgcol), reads=p3.tl() + gsc.tl(), writes=kbg.tl())
            S.op("dve", lambda e: e.tensor_scalar(kd.t[:], p3b[:, 256:384], erccol, None, ALU.mult), reads=p3.tl() + gsc.tl(), writes=kd.tl())
            S.op("act", lambda e: e.activation(vb.t[:], p3b[:, 384:512], AF.Copy, scale=bcol), reads=p3.tl() + gtok.tl(), writes=vb.tl())
            qg = ring(self.g_qg, "qg")
            S.op("dve", lambda e: e.tensor_tensor(qg.t[:], qT.t[:, bs], ebc.t[:], ALU.mult), reads=qT.tl() + ebc.tl(), writes=qg.tl())
            TT = self.tri_inverse(L32)
            if blk == 0:
                self.dump(f"L32_{l}_{hd}", L32.t[:], L32.tl(), [128, 128])
                self.dump(f"TT_{l}_{hd}", TT.t[:], TT.tl(), [128, 128], BF16)
            p4 = PS[5]
            S.op("pe", lambda e: e.matmul(p4.t[:, 0:128], TT.t[:], vb.t[:], start=True, stop=True), reads=TT.tl() + vb.tl(), writes=p4.tl())
            S.op("pe", lambda e: e.matmul(p4.t[:, 128:256], kbg.t[:], TT.t[:], start=True, stop=True), reads=TT.tl() + kbg.tl(), writes=p4.tl())
            u, wT = ring(self.g_u, "u"), ring(self.g_wT, "wT")
            S.op("dve", lambda e: e.tensor_copy(u.t[:], p4.t[:, 0:128]), reads=p4.tl(), writes=u.tl())
            S.op("act", lambda e: e.activation(wT.t[:], p4.t[:, 128:256], AF.Copy), reads=p4.tl(), writes=wT.tl())
            prep_out[blk] = (u, wT, kd, AT, qg, ebc)

        opsum, pw = PS[6], PS[7]

        def chunk(blk, ch, u, wT, kd, AT, qg, ebc):
            r0 = ch * 64
            rs = slice(r0, r0 + 64)
            cs = slice((blk % 4) * 128 + r0, (blk % 4) * 128 + r0 + 64)
            S.op("pe", lambda e: e.matmul(pw.t[:, 0:128], wT.t[:], Sbf.t[:], start=True, stop=True), reads=wT.tl() + Sbf.tl(), writes=pw.tl())
            vn = ring(self.g_vn, "vn")
            S.op("dve", lambda e: e.tensor_tensor(vn.t[rs, :], u.t[rs, :], pw.t[rs, 0:128], ALU.subtract), reads=u.tl() + pw.tl(), writes=vn.tl())
            S.op("pe", lambda e: e.matmul(opsum.t[:, cs], Sbf.t[:], qg.t[:, rs], start=True, stop=False), reads=Sbf.tl() + qg.tl(), writes=opsum.tl())
            S.op("pe", lambda e: e.matmul(opsum.t[:, cs], vn.t[rs, :], AT.t[rs, rs], start=False, stop=True), reads=vn.tl() + AT.tl(), writes=opsum.tl())
            S.op("pe", lambda e: e.matmul(pw.t[:, 128:256], kd.t[rs, :], vn.t[rs, :], start=True, stop=True), reads=kd.tl() + vn.tl(), writes=pw.tl())
            S.op("dve", lambda e: e.scalar_tensor_tensor(S32.t[:], S32.t[:], ebc.t[:, r0 + 63:r0 + 64], pw.t[:, 128:256], ALU.mult, ALU.add),
                 reads=S32.tl() + ebc.tl() + pw.tl(), writes=S32.tl())
            S.op("act", lambda e: e.activation(Sbf.t[:], S32.t[:], AF.Copy), reads=S32.tl(), writes=Sbf.tl())

        def finish(tt):
            ts = slice(tt * 512, (tt + 1) * 512)
            go = self.tA.t[:, 0:512]
            got = [self.tA_t[0]]
            S.op("act", lambda e: e.activation(go, opsum.t[:, :], AF.Copy), reads=opsum.tl(), writes=got)
            dno = self.slots["dno"]
            self.group_norm(go, got, "ones", DN_D, self.prm.t[:, dno[0] + l:dno[0] + l + 1], go, got)
            S.op("dve", lambda e: e.tensor_tensor(self.oT.t[:, 4 + hd, ts], go, self.oT.t[:, 4 + hd, ts], ALU.mult),
                 reads=got, writes=self.oT.tl(4 + hd, tt))

        def chain(blk):
            args = prep_out.pop(blk)
            for ch in range(2):
                chunk(blk, ch, *args)
            if blk % 4 == 3:
                finish(blk // 4)

        parts = {}

        def cap_prep(b):
            if b < NB and b not in parts:
                self._marks = []
                lst = S.capture(lambda: prep(b))
                m1, m2 = self._marks
                parts[b] = (lst[:m1], lst[m1:m2], lst[m2:])

        def part(b, i):
            cap_prep(b)
            return parts[b][i] if b < NB else []
        S.merge(part(0, 0))
        S.merge(part(0, 1), part(1, 0))
        S.merge(part(0, 2), part(1, 1), part(2, 0))
        for blk in range(NB):
            lb = S.capture(lambda: chain(blk))
            S.merge(part(blk + 1, 2), part(blk + 2, 1), part(blk + 3, 0), lb)
            parts.pop(blk, None)

    def out_proj(self, l, s):
        S, NT = self.S, self.NT
        xT, oT, mod = self.xT, self.oT, self.mod

        def one(wo, cc, c, tt):
            ts = slice(tt * 512, (tt + 1) * 512)
            ps = self.mmps()
            for kc in range(NCH):
                S.op("pe", lambda e, kc=kc: e.matmul(ps.t[:, :], wo.t[:, kc, cc * 128:(cc + 1) * 128], oT.t[:, kc, ts],
                                                     start=(kc == 0), stop=(kc == NCH - 1)),
                     reads=wo.tl() + oT.tl(kc, tt), writes=ps.tl())
            S.op("dve", lambda e: e.scalar_tensor_tensor(xT.t[:, c, ts], ps.t[:, :], mod.t[:, l, 16 + c, s:s + 1], xT.t[:, c, ts], ALU.mult, ALU.add),
                 reads=ps.tl() + mod.tl() + xT.tl(c, tt), writes=xT.tl(c, tt))
        for half in range(2):
            wo = self.wo[half]
            S.dma("pool", wo.t[:], self.d["wout"][l, :, :, half * 512:(half + 1) * 512], writes=wo.tl())
        for half in range(2):
            for cc in range(4):
                for tt in range(NT):
                    one(self.wo[half], cc, half * 4 + cc, tt)

    def mlp(self, l, s):
        S, NT, HT = self.S, self.NT, self.HT
        xT, hT, mod = self.xT, self.hT, self.mod
        nh = self.Tn // HT
        tph = HT // 512

        def ff1(wb, j, tt, t2):
            ts = slice(tt * 512, (tt + 1) * 512)
            ps = self.mmps()
            for kc in range(NCH):
                S.op("pe", lambda e, kc=kc: e.matmul(ps.t[:, :], wb.t[:, kc, :], hT.t[:, kc, ts], start=(kc == 0), stop=(kc == NCH - 1)),
                     reads=wb.tl() + hT.tl(kc, tt), writes=ps.tl())
            rl = self.ring(self.relu, "relu")
            hb, jj = self.hid(j)
            S.op("act", lambda e: e.activation(rl.t[:], ps.t[:, :], AF.Relu), reads=ps.tl(), writes=rl.tl())
            S.op("pool", lambda e: e.tensor_tensor(hb.t[:, jj, t2 * 512:(t2 + 1) * 512], rl.t[:], rl.t[:], ALU.mult),
                 reads=rl.tl(), writes=hb.tl(jj, t2))

        def ff2(w2, c, tt, t2):
            ts = slice(tt * 512, (tt + 1) * 512)
            ps = self.mmps()
            for j in range(32):
                hb, jj = self.hid(j)
                S.op("pe", lambda e, j=j, hb=hb, jj=jj: e.matmul(ps.t[:, :], w2.t[:, j, :], hb.t[:, jj, t2 * 512:(t2 + 1) * 512],
                                                                start=(j == 0), stop=(j == 31)),
                     reads=w2.tl() + hb.tl(jj, t2), writes=ps.tl())
            S.op("dve", lambda e: e.scalar_tensor_tensor(xT.t[:, c, ts], ps.t[:, :], mod.t[:, l, 40 + c, s:s + 1], xT.t[:, c, ts], ALU.mult, ALU.add),
                 reads=ps.tl() + mod.tl() + xT.tl(c, tt), writes=xT.tl(c, tt))

        s1 = WStream(self, self.wring, [self.d["ff1"][l, :, :, j * 128:(j + 1) * 128] for hf in range(nh) for j in range(32)], 2)
        s2 = WStream(self, self.w2, [self.d["ff2"][l, :, :, c * 128:(c + 1) * 128] for hf in range(nh) for c in range(NCH)], 1)
        for hf in range(nh):
            for j in range(32):
                wb = s1.next()
                for t2 in range(tph):
                    ff1(wb, j, hf * tph + t2, t2)
            for c in range(NCH):
                w2 = s2.next()
                for t2 in range(tph):
                    ff2(w2, c, hf * tph + t2, t2)


def _chunked(w, L):
    Lk, K, N = w.shape
    return np.ascontiguousarray(w.reshape(Lk, K // 128, 128, N).transpose(0, 2, 1, 3))


def prep_shared(inp, L):
    f = lambda a: np.ascontiguousarray(np.asarray(a, dtype=np.float32))
    sh = {}
    sh["w_ada"] = _chunked(f(inp["w_ada"])[:L], L)
    sh["b_ada"] = f(f(inp["b_ada"])[:L].reshape(L, 48, 128).transpose(2, 0, 1))
    sh["norm_mix"] = f(f(inp["norm_mix"])[:L].reshape(L, NCH, 128).transpose(2, 0, 1))
    sh["norm_mlp"] = f(f(inp["norm_mlp"])[:L].reshape(L, NCH, 128).transpose(2, 0, 1))
    sh["w_in"] = _chunked(f(inp["w_in"])[:L], L)
    sh["sb_q_norm"] = f(np.tile(f(inp["sb_q_norm"])[:L], (1, 2)).T)
    sh["sb_k_norm"] = f(np.tile(f(inp["sb_k_norm"])[:L], (1, 2)).T)
    sh["conv_w"] = f(f(inp["conv_w"])[:L].reshape(L, 4, 12, 128).transpose(3, 0, 2, 1))
    gp = np.concatenate([f(inp["a_log"])[:L], f(inp["dt_bias"])[:L]], axis=1)
    sh["gate_p"] = f(np.broadcast_to(gp[None], (128, L, 8)))
    sh["dn_out_norm"] = f(f(inp["dn_out_norm"])[:L].T)
    sh["w_out"] = _chunked(f(inp["w_out"])[:L], L)
    sh["w_ff1"] = _chunked(f(inp["w_ff1"])[:L], L)
    sh["w_ff2"] = _chunked(f(inp["w_ff2"])[:L], L)
    cc = _consts()
    sh["consts32"], sh["consts16"] = cc[1], cc[3]
    return sh


def prep_core(x2, c2, T):
    ns = x2.shape[0]
    xT = np.ascontiguousarray(np.asarray(x2, np.float32).reshape(ns, T, NCH, 128).transpose(0, 3, 2, 1))
    cT = np.ascontiguousarray(np.asarray(c2, np.float32).reshape(2, NCH, 128).transpose(2, 1, 0))
    return {"xT": xT, "cT": cT}


_CACHE = {}


def kernel(**inputs):
    x = np.asarray(inputs["x"], np.float32)
    c = np.asarray(inputs["c"], np.float32)
    B, T, _ = x.shape
    key = (T, DEPTH, 2)
    if key not in _CACHE:
        _CACHE[key] = Builder(T=T, L=DEPTH, NSEQ=2)
    bld = _CACHE[key]
    sh = prep_shared(inputs, DEPTH)
    in_maps = []
    for core in range(NCORES):
        m = dict(sh)
        m.update(prep_core(x[2 * core:2 * core + 2], c[2 * core:2 * core + 2], T))
        in_maps.append(m)
    res = run_bass_kernel_spmd(bld.nc, in_maps, core_ids=list(range(NCORES)))
    out = np.empty((B, T, D), np.float32)
    for core in range(NCORES):
        oT = res.results[core]["outT"]
        out[2 * core:2 * core + 2] = oT.transpose(0, 3, 2, 1).reshape(2, T, D)
    return out
```

```python
import math
import numpy as np
import concourse.bass as bass
import concourse.mybir as mybir
from concourse.bass_utils import run_bass_kernel_spmd

F32 = mybir.dt.float32
BF16 = mybir.dt.bfloat16
F32R = mybir.dt.float32r
NEUMANN_F32R = False
AF = mybir.ActivationFunctionType
ALU = mybir.AluOpType

D = 1024
NCH = 8
DEPTH = 4
SEQ = 2048
NCORES = 8
SB_H, SB_D = 8, 64
DN_H, DN_D = 4, 128
IN_W = 3592
DFF = 4096
EPS = 1e-6
ENGS = ("pe", "act", "dve", "pool", "sp")
ATTACH_WAIT = True


class Tile:
    __slots__ = ("name", "lw", "rd", "sem", "semcnt", "psum")

    def __init__(self, name):
        self.name = name
        self.psum = False
        self.lw = None
        self.rd = []
        self.sem = None
        self.semcnt = 0


class Op:
    __slots__ = ("eng", "fn", "deps", "signal", "sigval", "dma", "dsem", "dval", "seq", "need", "know")

    def __init__(self, eng, fn):
        self.eng = eng
        self.fn = fn
        self.deps = []
        self.signal = False
        self.sigval = 0
        self.dma = False
        self.dsem = None
        self.dval = 0


class Sched:
    def __init__(self, nc):
        self.nc = nc
        self.ops = {e: [] for e in ENGS}
        self.ndsem = 0
        self.nseq = 0

    def _add(self, op, reads, writes):
        deps = []
        for t in reads:
            if t.lw is not None:
                deps.append(t.lw)
        for t in writes:
            if t.lw is not None:
                deps.append(t.lw)
            deps.extend(t.rd)
        for t in reads:
            t.rd.append(op)
        for t in writes:
            t.lw = op
            t.rd = []
        seen = set()
        for d in deps:
            if d is op or id(d) in seen:
                continue
            seen.add(id(d))
            if (not d.dma) and (not op.dma) and d.eng == "pe" and op.eng == "pe":
                continue
            op.deps.append(d)
            d.signal = True
        op.seq = self.nseq
        self.nseq += 1
        self.ops[op.eng].append(op)
        return op

    def op(self, eng, fn, reads=(), writes=()):
        if getattr(self, "cap", None) is not None:
            self.cap.append((eng, fn, list(reads), list(writes)))
            return None
        reads, writes = list(reads), list(writes)
        pr = [t for t in reads if t.psum]
        if pr:
            reads = [t for t in reads if not t.psum]
            writes = writes + [t for t in pr if t not in writes]
        return self._add(Op(eng, fn), reads, writes)

    def dma(self, eng, out, in_, reads=(), writes=()):
        o = Op(eng, None)
        o.dma = True
        tiles = list(writes) + list(reads)
        st = tiles[0]
        if st.sem is None:
            st.sem = self.nc.alloc_semaphore(f"dsem{self.ndsem}")
            self.ndsem += 1
        st.semcnt += 16
        o.dsem = st.sem
        o.dval = st.semcnt
        o.signal = True
        o.fn = lambda e, out=out, in_=in_: e.dma_start(out=out, in_=in_)
        return self._add(o, [], tiles)

    def capture(self, f):
        assert getattr(self, "cap", None) is None
        self.cap = []
        f()
        lst, self.cap = self.cap, None
        return lst

    def merge(self, *lists):
        idx = [0] * len(lists)
        total = sum(len(x) for x in lists)
        for _ in range(total):
            best, bf = None, None
            for k, lst in enumerate(lists):
                if idx[k] < len(lst):
                    fr = idx[k] / len(lst)
                    if bf is None or fr < bf:
                        best, bf = k, fr
            eng, fn, reads, writes = lists[best][idx[best]]
            idx[best] += 1
            self.op(eng, fn, reads, writes)

    def emit(self, final_wait_ops=()):
        nc = self.nc
        esem = {e: nc.alloc_semaphore(f"sem_{e}") for e in ENGS}
        for e in ENGS:
            c = 0
            for o in self.ops[e]:
                if (not o.dma) and o.signal:
                    c += 1
                    o.sigval = c
        stats = {}
        known = {e: {} for e in ENGS}
        allops = sorted((o for e in ENGS for o in self.ops[e]), key=lambda o: o.seq)
        for o in allops:
            kn = known[o.eng]
            o.need = []
            for d in o.deps:
                if d.dma:
                    key, val = d.dsem, d.dval
                else:
                    key, val = esem[d.eng], d.sigval
                if kn.get(key.num, 0) >= val:
                    continue
                o.need.append((key, val))
                for k2, v2 in d.know.items():
                    if kn.get(k2, 0) < v2:
                        kn[k2] = v2
                kn[key.num] = val
            if o.dma:
                o.know = dict(kn)
                o.know[o.dsem.num] = o.dval
            elif o.signal:
                o.know = dict(kn)
                o.know[esem[o.eng].num] = o.sigval
            else:
                o.know = None

        def run(e, eng):
            nw = 0
            for o in self.ops[e]:
                need = list(o.need)
                attach = need.pop() if (need and ATTACH_WAIT) else None
                for key, val in need:
                    eng.wait_ge(key, val)
                    nw += 1
                ins = o.fn(eng)
                if attach is not None:
                    ins._wait_ge(attach[0], attach[1])
                if o.dma:
                    ins.then_inc(o.dsem, 16)
                elif o.signal:
                    ins.then_inc(esem[e], 1)
            if e == "sp":
                for o in final_wait_ops:
                    if o.dma:
                        eng.wait_ge(o.dsem, o.dval)
                    else:
                        eng.wait_ge(esem[o.eng], o.sigval)
            stats[e] = (len(self.ops[e]), nw)

        with nc.Block() as block:
            @block.tensor
            def _(eng):
                run("pe", eng)

            @block.scalar
            def _(eng):
                run("act", eng)

            @block.vector
            def _(eng):
                run("dve", eng)

            @block.gpsimd
            def _(eng):
                run("pool", eng)

            @block.sync
            def _(eng):
                run("sp", eng)
        return stats


class Buf:
    def __init__(self, t, name, grid=None):
        self.t = t
        if grid is None:
            self.T = Tile(name)
        else:
            self.T = np.empty(grid, dtype=object)
            for idx in np.ndindex(*grid):
                self.T[idx] = Tile(f"{name}{idx}")

    def tl(self, *idx):
        if isinstance(self.T, Tile):
            return [self.T]
        sub = self.T[idx] if idx else self.T
        if isinstance(sub, Tile):
            return [sub]
        return list(sub.ravel())


def _consts():
    i = np.arange(128)
    same = (i[:, None] // 64) == (i[None, :] // 64)
    c = {}
    c["ident"] = np.eye(128, dtype=np.float32)
    c["ones"] = np.ones((128, 128), np.float32)
    blk = np.zeros((128, 128), np.float32)
    blk[:64, :64] = 1.0
    blk[64:, 64:] = 1.0
    c["blk64"] = blk
    c["tril"] = (i[:, None] >= i[None, :]).astype(np.float32)
    c["mincl"] = ((i[:, None] <= i[None, :]) & same).astype(np.float32)
    c["mrev"] = ((i[:, None] > i[None, :]) & same).astype(np.float32)
    c["mbI"] = np.where((i[None, :] <= i[:, None]) & same, 0.0, 30000.0).astype(np.float32)
    c["strict"] = (i[None, :] < i[:, None]).astype(np.float32)
    b16 = (i[:, None] // 16) == (i[None, :] // 16)
    b32 = (i[:, None] // 32) == (i[None, :] // 32)
    low = i[None, :] < i[:, None]
    c["m16"] = (b16 & low).astype(np.float32)
    c["m32"] = (b32 & ~b16 & low).astype(np.float32)
    c["m64"] = (same & ~b32 & low).astype(np.float32)
    c["m16T"] = np.ascontiguousarray(c["m16"].T)
    c["m32T"] = np.ascontiguousarray(c["m32"].T)
    n32 = ["ident", "mincl", "mrev", "mbI", "strict", "m16", "m32", "m64", "m16T", "m32T"]
    n16 = ["ident", "ones", "blk64", "tril"]
    return (n32, np.concatenate([c[n] for n in n32], axis=1), n16, np.concatenate([c[n] for n in n16], axis=1))


class WStream:
    def __init__(self, bld, bufs, srcs, depth):
        self.b, self.bufs, self.srcs, self.depth = bld, bufs, list(srcs), depth
        assert len(bufs) > depth
        self.n = 0
        self.issued = 0
        self.live = []
        for _ in range(min(depth, len(self.srcs))):
            self._issue()

    def _issue(self):
        buf = self.bufs[self.issued % len(self.bufs)]
        self.b.S.dma("pool", buf.t[:], self.srcs[self.issued], writes=buf.tl())
        self.live.append(buf)
        self.issued += 1

    def next(self):
        buf = self.live[self.n]
        self.n += 1
        if self.issued < len(self.srcs):
            self._issue()
        return buf


AR_EL = 30208


class Builder:
    def __init__(self, T=SEQ, L=DEPTH, NSEQ=2, dbg=(), stop=None):
        self.stop = stop
        self.Tn = T
        self.L = L
        self.NSEQ = NSEQ
        self.NT = T // 512
        self.NB = T // 128
        self.dbg_names = dbg
        nc = self.nc = bass.Bass("TRN2", target_bir_lowering=False)
        self.S = Sched(nc)
        self.cnt = {}
        self.final = []
        self.phase = {}
        self.build()

    def sb(self, name, shape, dt, grid=None):
        return Buf(self.nc.alloc_sbuf_tensor("s_" + name, list(shape), dt), name, grid)

    def av(self, phase, name, shape, dt, grid=None, base=None):
        ph = self.phase.setdefault(phase, {"off": 0, "tiles": []})
        nel = int(np.prod(shape[1:]))
        nbf = nel * (2 if dt == F32 else 1)
        off = (ph["off"] + 15) // 16 * 16
        ph["off"] = off + nbf
        assert ph["off"] <= AR_EL, (phase, name, ph["off"])
        ap = self.arena[:, off:off + nbf]
        if dt == F32:
            ap = ap.bitcast(F32)
        if len(shape) == 3:
            ap = ap.rearrange("p (a b) -> p a b", a=shape[1], b=shape[2])
        if shape[0] < 128:
            ap = ap[0:shape[0]]
        b = Buf(ap, name, grid)
        ph["tiles"].extend(b.tl())
        return b

    def ring(self, lst, key):
        v = self.cnt.get(key, 0)
        self.cnt[key] = v + 1
        return lst[v % len(lst)]

    def barrier(self, *phases, extra=()):
        tiles = list(extra)
        for p in phases:
            tiles.extend(self.phase[p]["tiles"])
        self.S.op("sp", lambda e: e.nop(), writes=tiles)

    def din(self, name, shape, dt=F32):
        return self.nc.dram_tensor(name, list(shape), dt, kind="ExternalInput").ap()

    def dump(self, name, ap, tiles, shape, dt=F32):
        if name not in self.dbg_names:
            return
        d = self.nc.dram_tensor("dbg_" + name, list(shape), dt, kind="ExternalOutput").ap()
        self.final.append(self.S.dma("sp", d, ap, reads=tiles))

    def build(self):
        nc, S, T, L, NSEQ, NT, NB = self.nc, self.S, self.Tn, self.L, self.NSEQ, self.NT, self.NB
        d_xT = self.din("xT", [NSEQ, 128, NCH, T])
        d_cT = self.din("cT", [128, NCH, 2])
        d_wada = self.din("w_ada", [L, 128, NCH, 6 * D])
        d_bada = self.din("b_ada", [128, L, 48])
        d_nmix = self.din("norm_mix", [128, L, NCH])
        d_nmlp = self.din("norm_mlp", [128, L, NCH])
        d_win = self.din("w_in", [L, 128, NCH, IN_W])
        d_sbq = self.din("sb_q_norm", [128, L])
        d_sbk = self.din("sb_k_norm", [128, L])
        d_conv = self.din("conv_w", [128, L, 12, 4])
        d_gate = self.din("gate_p", [128, L, 8])
        d_dno = self.din("dn_out_norm", [128, L])
        d_wout = self.din("w_out", [L, 128, NCH, D])
        d_ff1 = self.din("w_ff1", [L, 128, NCH, DFF])
        d_ff2 = self.din("w_ff2", [L, 128, 32, D])
        n32, c32m, n16, c16m = _consts()
        d_c32 = self.din("consts32", [128, c32m.shape[1]])
        d_c16 = self.din("consts16", [128, c16m.shape[1]])
        d_out = nc.dram_tensor("outT", [NSEQ, 128, NCH, T], F32, kind="ExternalOutput").ap()
        self.d = dict(win=d_win, wout=d_wout, ff1=d_ff1, ff2=d_ff2)

        self.arena = nc.alloc_sbuf_tensor("arena", [128, AR_EL], BF16)

        c32 = self.sb("c32", [128, c32m.shape[1]], F32)
        c16 = self.sb("c16", [128, c16m.shape[1]], BF16)
        S.dma("sp", c32.t[:], d_c32, writes=c32.tl())
        S.dma("pool", c16.t[:], d_c16, writes=c16.tl())
        self.c32, self.c16 = c32, c16
        self.C32 = lambda n: c32.t[:, n32.index(n) * 128:(n32.index(n) + 1) * 128]
        self.C16 = lambda n: c16.t[:, n16.index(n) * 128:(n16.index(n) + 1) * 128]

        prm = self.sb("prm", [128, 512], F32)
        PT = prm.tl()
        self.prm, self.PT = prm, PT
        o = [0]

        def pslot(n):
            r = (o[0], o[0] + n)
            o[0] += n
            return r
        s_bada, s_nmix, s_nmlp = pslot(L * 48), pslot(L * NCH), pslot(L * NCH)
        s_sbq, s_sbk, s_conv, s_dno, s_gate = pslot(L), pslot(L), pslot(L * 48), pslot(L), pslot(L * 8)
        assert o[0] <= 512

        def pv(s, *shape):
            ap = prm.t[:, s[0]:s[1]]
            if len(shape) == 2:
                ap = ap.rearrange("p (a b) -> p a b", a=shape[0], b=shape[1])
            elif len(shape) == 3:
                ap = ap.rearrange("p (a b c) -> p a b c", a=shape[0], b=shape[1], c=shape[2])
            return ap
        self.pv = pv
        self.slots = dict(sbq=s_sbq, sbk=s_sbk, conv=s_conv, dno=s_dno, gate=s_gate)
        S.dma("sp", pv(s_bada, L, 48), d_bada, writes=PT)
        S.dma("sp", pv(s_nmix, L, NCH), d_nmix, writes=PT)
        S.dma("sp", pv(s_nmlp, L, NCH), d_nmlp, writes=PT)
        S.dma("sp", prm.t[:, s_sbq[0]:s_sbq[1]], d_sbq, writes=PT)
        S.dma("sp", prm.t[:, s_sbk[0]:s_sbk[1]], d_sbk, writes=PT)
        S.dma("sp", pv(s_conv, L, 12, 4), d_conv, writes=PT)
        S.dma("sp", prm.t[:, s_dno[0]:s_dno[1]], d_dno, writes=PT)
        S.dma("sp", pv(s_gate, L, 8), d_gate, writes=PT)
        nexpA = self.sb("nexpA", [128, L, 4], F32)
        self.nexpA = nexpA
        S.op("act", lambda e: e.activation(nexpA.t[:], pv(s_gate, L, 8)[:, :, 0:4], AF.Exp), reads=PT, writes=nexpA.tl())
        S.op("dve", lambda e: e.tensor_scalar(nexpA.t[:], nexpA.t[:], -1.0, None, ALU.mult), reads=nexpA.tl(), writes=nexpA.tl())

        self.PS = PS = [Buf(nc.alloc_psum_tensor(f"ps{i}", [128, 512], F32), f"ps{i}") for i in range(8)]
        for b in PS:
            b.T.psum = True

        cT = self.sb("cT", [128, NCH, 2], F32)
        ctmp = self.sb("ctmp", [128, NCH, 2], F32)
        cond = self.sb("cond", [128, NCH, 2], BF16)
        S.dma("sp", cT.t[:], d_cT, writes=cT.tl())
        S.op("act", lambda e: e.activation(ctmp.t[:], cT.t[:], AF.Exp, scale=-1.0), reads=cT.tl(), writes=ctmp.tl())
        S.op("dve", lambda e: e.tensor_scalar(ctmp.t[:], ctmp.t[:], 1.0, None, ALU.add), reads=ctmp.tl(), writes=ctmp.tl())
        S.op("dve", lambda e: e.reciprocal(ctmp.t[:], ctmp.t[:]), reads=ctmp.tl(), writes=ctmp.tl())
        S.op("dve", lambda e: e.tensor_tensor(cond.t[:], cT.t[:], ctmp.t[:], ALU.mult), reads=cT.tl() + ctmp.tl(), writes=cond.tl())
        mod = self.sb("mod", [128, L, 48, 2], F32)
        self.mod = mod
        wada = [self.av("setup", f"wada{i}", [128, NCH, 512], BF16) for i in range(2)]

        def ada_piece(l, pc):
            wb = self.ring(wada, "wada")
            S.dma("pool", wb.t[:], d_wada[l, :, :, pc * 512:(pc + 1) * 512], writes=wb.tl())
            for jj in range(4):
                j = pc * 4 + jj
                for kc in range(NCH):
                    S.op("pe", lambda e, jj=jj, kc=kc, j=j: e.matmul(
                        PS[0].t[:, j * 2:j * 2 + 2], wb.t[:, kc, jj * 128:(jj + 1) * 128], cond.t[:, kc, :],
                        start=(kc == 0), stop=(kc == NCH - 1)), reads=wb.tl() + cond.tl(), writes=PS[0].tl())

        def ada_evac(l, b):
            S.op("dve", lambda e: e.tensor_tensor(
                mod.t[:, l, :, b], PS[0].t[:, 0:96].rearrange("p (j b) -> p j b", b=2)[:, :, b],
                pv(s_bada, L, 48)[:, l, :], ALU.add), reads=PS[0].tl() + PT, writes=mod.tl())
        for l in range(L):
            for pc in range(12):
                ada_piece(l, pc)
            for b in range(2):
                ada_evac(l, b)
        gains = self.sb("gains", [128, L, 2, NCH, 2], F32)
        self.gains = gains

        def gain_op(l, which, sl, m, b):
            S.op("dve", lambda e: e.scalar_tensor_tensor(
                gains.t[:, l, which, :, b], mod.t[:, l, m * 8:(m + 1) * 8, b], 1.0, pv(sl, L, NCH)[:, l, :], ALU.add, ALU.mult),
                reads=mod.tl() + PT, writes=gains.tl())
        for l in range(L):
            for which, (sl, m) in enumerate(((s_nmix, 1), (s_nmlp, 4))):
                for b in range(2):
                    gain_op(l, which, sl, m, b)
        self.dump("mod", mod.t[:], mod.tl(), [128, L, 48, 2])

        self.xT = self.sb("xT", [128, NCH, T], F32, grid=(NCH, NT))
        self.hT = self.sb("hT", [128, NCH, T], BF16, grid=(NCH, NT))
        self.oT = self.sb("oT", [128, NCH, T], BF16, grid=(NCH, NT))
        self.sqb = [self.sb(f"sqb{i}", [128, 512], BF16) for i in range(2)]
        self.wring = [self.sb(f"wring{i}", [128, NCH, 128], BF16) for i in range(3)]
        self.alloc_phases()

        xT = self.xT
        self.cur = "setup"
        for s in range(NSEQ):
            for c in range(NCH):
                S.dma("sp", xT.t[:, c, :], d_xT[s, :, c, :], writes=xT.tl(c))
            for l in range(L):
                self.layer(l, s)
            for c in range(NCH):
                self.final.append(S.dma("sp", d_out[s, :, c, :], xT.t[:, c, :], reads=xT.tl(c)))
        self.stats = S.emit(final_wait_ops=self.final)

    def switch(self, new, extra=()):
        self.barrier(self.cur, new, extra=extra)
        self.cur = new

    def alloc_phases(self):
        T, NB, NT = self.Tn, self.NB, self.NT
        av = self.av
        self.qa = av("sb", "qa", [128, 2, T], BF16, grid=(2, NT))
        self.ka = av("sb", "ka", [128, 2, T], BF16, grid=(2, NT))
        self.va = av("sb", "va", [128, NB, 256], BF16, grid=(NB,))
        self.wv = av("sb", "wv", [128, NCH, 256], BF16)
        self.a_e = [av("sb", f"a_e{i}", [128, 512], F32) for i in range(3)]
        self.a_x = [av("sb", f"a_x{i}", [128, 512], F32) for i in range(2)]
        self.a_sp = [av("sb", f"a_sp{i}", [128, 512], BF16) for i in range(2)]
        self.a_att = [av("sb", f"a_att{i}", [128, 512], BF16) for i in range(2)]
        self.a_R = [av("sb", f"a_R{i}", [1, 512], BF16) for i in range(2)]
        self.raw32 = [av("sb", f"raw32_{i}", [128, 512], F32) for i in range(2)]
        g = "gdn"
        self.raw = av(g, "raw", [128, T + 4], F32)
        self.CH = min(T, 1024)
        self.cacc = av(g, "cacc", [128, self.CH], F32)
        self.tA = av(g, "tA", [128, self.CH], F32)
        self.acc_t = [Tile(f"acc_t{i}") for i in range(self.CH // 512)]
        self.tA_t = [Tile(f"tA_t{i}") for i in range(self.CH // 512)]
        self.phase[g]["tiles"].extend(self.acc_t + self.tA_t)
        self.kT = av(g, "kT", [128, T], BF16)
        self.qT = av(g, "qT", [128, T], BF16)
        self.vT = av(g, "vT", [128, T], BF16)
        self.gtok = av(g, "gtok", [128, NB, 8], F32)
        self.gcrc = av(g, "gcrc", [128, NB, 8], F32)
        self.gsc = av(g, "gsc", [128, NB, 8], F32)
        self.wgate = av(g, "wgate", [128, NCH, 8], BF16)

        def mk(n, k, dt):
            return [av(g, f"{n}{i}", [128, 128], dt) for i in range(k)]
        self.g_ebc, self.g_tI, self.g_DL, self.g_DLs = mk("ebc", 4, F32), mk("tI", 1, F32), mk("DL", 2, F32), mk("DLs", 1, F32)
        self.g_L32, self.g_Lc, self.g_Uc = mk("L32", 2, F32), mk("Lc", 4, F32), mk("Uc", 4, F32)
        self.g_Xc, self.g_Yc = mk("Xc", 4, F32), mk("Yc", 4, F32)
        self.g_C32, self.g_C32T, self.g_C64 = mk("C32", 3, BF16), mk("C32T", 3, BF16), mk("C64", 3, BF16)
        self.g_T16T, self.g_T16, self.g_Ya, self.g_Yb = mk("T16T", 2, BF16), mk("T16", 2, BF16), mk("Ya", 2, BF16), mk("Yb", 2, BF16)
        self.g_T32T, self.g_T32, self.g_Yd, self.g_TT = mk("T32T", 2, BF16), mk("T32", 2, BF16), mk("Yd", 2, BF16), mk("TT", 2, BF16)
        self.g_A, self.g_AT, self.g_kbg, self.g_kd = mk("A", 2, BF16), mk("AT", 4, BF16), mk("kbg", 3, BF16), mk("kd", 4, BF16)
        self.g_vb, self.g_u, self.g_wT = mk("vb", 3, BF16), mk("u", 3, F32), mk("wT", 3, BF16)
        self.g_qg, self.g_vn = mk("qg", 4, BF16), mk("vn", 2, BF16)
        self.S32 = av(g, "S32", [128, 128], F32)
        self.Sbf = av(g, "Sbf", [128, 128], BF16)
        self.wo = [av("op", f"wo{i}", [128, NCH, 512], BF16) for i in range(2)]
        self.HT = min(1024, T)
        tph = self.HT // 512
        self.hidA = av("mlp", "hidA", [128, 16, self.HT], BF16, grid=(16, tph))
        self.w2 = [av("mlp", f"w2_{i}", [128, 32, 128], BF16) for i in range(2)]
        self.relu = [av("mlp", f"relu{i}", [128, 512], F32) for i in range(2)]
        if T == SEQ:
            ap = self.oT.t[:].rearrange("p c t -> p (c t)").rearrange("p (a b) -> p a b", a=16, b=self.HT)
            self.hidB = Buf(ap, "hidB", grid=(16, tph))
        else:
            self.hidB = self.sb("hidB", [128, 16, self.HT], BF16, grid=(16, tph))
        self.phase["mlp"]["tiles"].extend(self.hidB.tl())

    def hid(self, j):
        return (self.hidA, j) if j < 16 else (self.hidB, j - 16)

    def mmps(self):
        return self.ring([self.PS[0], self.PS[1]], "mmps")


    def norm_to_hT(self, l, s, which):
        S, NT, PS = self.S, self.NT, self.PS
        xT, hT, gains, mod = self.xT, self.hT, self.gains, self.mod
        msh = 0 if which == 0 else 3
        for tt in range(NT):
            ts = slice(tt * 512, (tt + 1) * 512)
            ps = self.mmps()
            for c in range(NCH):
                self._sq_mm(xT.t[:, c, ts], xT.tl(c, tt), ps, "ones", c == 0, c == NCH - 1, eng=("act" if c % 2 else "pool"))
            S.op("act", lambda e, ps=ps: e.activation(ps.t[:, :], ps.t[:, :], AF.Ln, bias=EPS, scale=1.0 / D), reads=ps.tl(), writes=ps.tl())
            S.op("act", lambda e, ps=ps: e.activation(ps.t[:, :], ps.t[:, :], AF.Exp, scale=-0.5), reads=ps.tl(), writes=ps.tl())
            for c in range(NCH):
                tmp = PS[2 + c % 2]
                S.op("dve", lambda e, c=c, ts=ts, ps=ps, tmp=tmp: e.tensor_tensor(tmp.t[:, :], xT.t[:, c, ts], ps.t[:, :], ALU.mult),
                     reads=xT.tl(c, tt) + ps.tl(), writes=tmp.tl())
                S.op("dve", lambda e, c=c, ts=ts, tmp=tmp: e.tensor_scalar(
                    hT.t[:, c, ts], tmp.t[:, :], gains.t[:, l, which, c, s:s + 1], mod.t[:, l, msh * 8 + c, s:s + 1], ALU.mult, ALU.add),
                    reads=tmp.tl() + gains.tl() + mod.tl(), writes=hT.tl(c, tt))

    def _sq_mm(self, src_ap, src_tiles, ps, ones_name, start, stop, eng="pool"):
        S = self.S
        sq = self.ring(self.sqb, "sqb")
        if eng == "act":
            S.op("act", lambda e: e.activation(sq.t[:], src_ap, AF.Square), reads=src_tiles, writes=sq.tl())
        else:
            S.op("pool", lambda e: e.tensor_tensor(sq.t[:], src_ap, src_ap, ALU.mult), reads=src_tiles, writes=sq.tl())
        S.op("pe", lambda e: e.matmul(ps.t[:, :], self.C16(ones_name), sq.t[:], start=start, stop=stop),
             reads=sq.tl() + self.c16.tl(), writes=ps.tl())

    def group_norm(self, src_ap, src_tiles, ones_name, nfeat, gain_ap, out_ap, out_tiles, bias2=0.0):
        S = self.S
        ps = self.mmps()
        self._sq_mm(src_ap, src_tiles, ps, ones_name, True, True, eng="act")
        sc = 1.0 if nfeat is None else 1.0 / nfeat
        S.op("act", lambda e: e.activation(ps.t[:, :], ps.t[:, :], AF.Ln, bias=EPS, scale=sc), reads=ps.tl(), writes=ps.tl())
        S.op("act", lambda e: e.activation(ps.t[:, :], ps.t[:, :], AF.Exp, scale=-0.5, bias=bias2), reads=ps.tl(), writes=ps.tl())
        if gain_ap is None:
            S.op("dve", lambda e: e.tensor_tensor(out_ap, src_ap, ps.t[:, :], ALU.mult), reads=src_tiles + ps.tl(), writes=out_tiles)
        else:
            S.op("dve", lambda e: e.scalar_tensor_tensor(out_ap, src_ap, gain_ap, ps.t[:, :], ALU.mult, ALU.mult),
                 reads=src_tiles + ps.tl() + self.PT, writes=out_tiles)

    def proj_chunk(self, l, col0, evac):
        S, hT = self.S, self.hT
        assert self.win_cols[self.win_stream.n] == col0
        wb = self.win_stream.next()
        for tt in range(self.NT):
            ts = slice(tt * 512, (tt + 1) * 512)
            ps = self.mmps()
            for kc in range(NCH):
                S.op("pe", lambda e, ps=ps, kc=kc, ts=ts: e.matmul(ps.t[:, :], wb.t[:, kc, :], hT.t[:, kc, ts],
                                                                  start=(kc == 0), stop=(kc == NCH - 1)),
                     reads=wb.tl() + hT.tl(kc, tt), writes=ps.tl())
            evac(tt, ps)

    def layer(self, l, s):
        T = self.Tn
        cols = []
        for half in range(2):
            for which in range(2):
                for cc in range(2):
                    cols.append(which * 512 + (half * 2 + cc) * 128)
        for hd in range(DN_H):
            for base in (1536, 2048, 2560, 3072):
                cols.append(base + hd * 128)
        self.win_cols = cols
        self.win_stream = WStream(self, self.wring, [self.d["win"][l, :, :, c0:c0 + 128] for c0 in cols], 2)
        if self.stop == "setup":
            return
        self.norm_to_hT(l, s, 0)
        self.dump(f"h1_{l}", self.hT.t[:], self.hT.tl(), [128, NCH, T], BF16)
        if self.stop == "norm":
            return
        self.switch("sb", extra=self.oT.tl())
        for half in range(2):
            self.sb_proj(l, half)
            if self.stop == "sbproj":
                return
            self.sb_attn(l, half)
        if self.stop == "attn":
            return
        self.switch("gdn")
        self.gdn(l)
        self.dump(f"oT_{l}", self.oT.t[:], self.oT.tl(), [128, NCH, T], BF16)
        if self.stop == "gdn":
            return
        self.switch("op")
        self.out_proj(l, s)
        self.dump(f"x1_{l}", self.xT.t[:], self.xT.tl(), [128, NCH, T])
        self.norm_to_hT(l, s, 1)
        self.switch("mlp", extra=self.oT.tl())
        self.mlp(l, s)
        self.dump(f"x2_{l}", self.xT.t[:], self.xT.tl(), [128, NCH, T])

    def sb_proj(self, l, half):
        S, NT, NB, hT = self.S, self.NT, self.NB, self.hT
        for which, dst, slot in ((0, self.qa, self.slots["sbq"]), (1, self.ka, self.slots["sbk"])):
            for cc in range(2):
                def evac(tt, ps, cc=cc, dst=dst, slot=slot):
                    r = self.raw32[tt % 2]
                    S.op("act", lambda e: e.activation(r.t[:], ps.t[:, :], AF.Copy), reads=ps.tl(), writes=r.tl())
                    self.group_norm(r.t[:], r.tl(), "blk64", SB_D, self.prm.t[:, slot[0] + l:slot[0] + l + 1],
                                    dst.t[:, cc, tt * 512:(tt + 1) * 512], dst.tl(cc, tt))
                self.proj_chunk(l, which * 512 + (half * 2 + cc) * 128, evac)
        wv, va = self.wv, self.va
        S.dma("pool", wv.t[:], self.d["win"][l, :, :, 1024 + half * 256:1024 + (half + 1) * 256], writes=wv.tl())

        def vblk(blk):
            ps = self.mmps()
            tt = blk // 4
            for kc in range(NCH):
                S.op("pe", lambda e, kc=kc: e.matmul(ps.t[:, 0:256], hT.t[:, kc, blk * 128:(blk + 1) * 128], wv.t[:, kc, :],
                                                     start=(kc == 0), stop=(kc == NCH - 1)),
                     reads=wv.tl() + hT.tl(kc, tt), writes=ps.tl())
            if blk % 2 == 0:
                S.op("dve", lambda e: e.tensor_copy(va.t[:, blk, :], ps.t[:, 0:256]), reads=ps.tl(), writes=va.tl(blk))
            else:
                S.op("act", lambda e: e.activation(va.t[:, blk, :], ps.t[:, 0:256], AF.Copy), reads=ps.tl(), writes=va.tl(blk))
        for blk in range(NB):
            vblk(blk)
        self.dump(f"qa_{l}_{half}", self.qa.t[:], self.qa.tl(), [128, 2, self.Tn], BF16)
        self.dump(f"ka_{l}_{half}", self.ka.t[:], self.ka.tl(), [128, 2, self.Tn], BF16)
        self.dump(f"va_{l}_{half}", self.va.t[:], self.va.tl(), [128, NB, 256], BF16)

    def sb_attn(self, l, half):
        S, NT = self.S, self.NT
        PS, C16 = self.PS, self.C16
        qa, ka, va, oT = self.qa, self.ka, self.va, self.oT
        c16t = self.c16.tl()
        scale = SB_D ** -0.5
        items = []
        for cc in range(2):
            for qt in range(NT):
                for kb in range(4 * qt + 3, -1, -1):
                    for hh in (2 * cc, 2 * cc + 1):
                        items.append((hh, qt, kb))
        n_it = len(items)
        grp = {n: items[n][0] % 2 for n in range(n_it)}

        def geom(n):
            hh, qt, kb = items[n]
            i = kb - 4 * qt
            c0 = 128 * i if i > 0 else 0
            return hh, qt, kb, i, c0, 512 - c0

        def s1(n):
            hh, qt, kb, i, c0, w = geom(n)
            cc, p0 = hh // 2, (hh % 2) * 64
            zp = PS[2 + n % 2]
            S.op("pe", lambda e: e.matmul(zp.t[:, 0:w], ka.t[p0:p0 + 64, cc, kb * 128:(kb + 1) * 128],
                                          qa.t[p0:p0 + 64, cc, qt * 512 + c0:(qt + 1) * 512], start=True, stop=True),
                 reads=ka.tl(cc, kb // 4) + qa.tl(cc, qt), writes=zp.tl())

        def s2(n):
            hh, qt, kb, i, c0, w = geom(n)
            zp = PS[2 + n % 2]
            eb, sp = self.a_e[n % 3], self.a_sp[n % 2]
            S.op("act", lambda e: e.activation(eb.t[:, 0:w], zp.t[:, 0:w], AF.Exp, scale=scale), reads=zp.tl(), writes=eb.tl())
            S.op("act", lambda e: e.activation(sp.t[:, 0:w], eb.t[:, 0:w], AF.Ln, bias=1.0), reads=eb.tl(), writes=sp.tl())
            if i >= 0:
                S.op("pool", lambda e: e.affine_select(sp.t[:, 0:128], sp.t[:, 0:128], [[1, 128]], ALU.is_gt, 0.0,
                                                       base=0, channel_multiplier=-1), reads=sp.tl(), writes=sp.tl())

        def s3(n):
            hh, qt, kb, i, c0, w = geom(n)
            cp = PS[4 + n % 2]
            sp = self.a_sp[n % 2]
            R = self.a_R[grp[n] % 2]
            first = (kb == 4 * qt + 3)
            S.op("pe", lambda e: e.matmul(cp.t[:, 0:w], C16("tril"), sp.t[:, 0:w], start=True, stop=first),
                 reads=sp.tl() + c16t, writes=cp.tl())
            if not first:
                r0 = 128 if i >= 0 else 0
                S.op("pe", lambda e: e.matmul(cp.t[:, r0:w], C16("ones")[0:1, :], R.t[0:1, c0 + r0:512], start=False, stop=True),
                     reads=R.tl() + c16t, writes=cp.tl())

        def s4(n):
            hh, qt, kb, i, c0, w = geom(n)
            cp = PS[4 + n % 2]
            eb, xb, at = self.a_e[n % 3], self.a_x[n % 2], self.a_att[n % 2]
            S.op("act", lambda e: e.activation(xb.t[:, 0:w], cp.t[:, 0:w], AF.Exp, scale=-1.0), reads=cp.tl(), writes=xb.tl())
            if kb > 0:
                R = self.a_R[grp[n] % 2]
                S.op("dve", lambda e: e.tensor_copy(R.t[0:1, c0:512], cp.t[0:1, 0:w]), reads=cp.tl(), writes=R.tl())
            S.op("dve", lambda e: e.tensor_tensor(at.t[:, 0:w], eb.t[:, 0:w], xb.t[:, 0:w], ALU.mult),
                 reads=eb.tl() + xb.tl(), writes=at.tl())
            if i >= 0:
                S.op("pool", lambda e: e.affine_select(at.t[:, 0:128], at.t[:, 0:128], [[1, 128]], ALU.is_gt, 0.0,
                                                       base=0, channel_multiplier=-1), reads=at.tl(), writes=at.tl())

        def s5(n):
            hh, qt, kb, i, c0, w = geom(n)
            cc, p0 = hh // 2, (hh % 2) * 64
            c = half * 2 + cc
            op_ = PS[6 + grp[n] % 2]
            at = self.a_att[n % 2]
            last = (kb == 0)
            first = (kb == 4 * qt + 3)
            vl = va.t[:, kb, cc * 128:(cc + 1) * 128]
            if i >= 0 and w > 128:
                S.op("pe", lambda e: e.matmul(op_.t[:, c0:c0 + 128], vl, at.t[:, 0:128], start=first, stop=False),
                     reads=at.tl() + va.tl(kb), writes=op_.tl())
                S.op("pe", lambda e: e.matmul(op_.t[:, c0 + 128:512], vl, at.t[:, 128:w], start=False, stop=last),
                     reads=at.tl() + va.tl(kb), writes=op_.tl())
            else:
                S.op("pe", lambda e: e.matmul(op_.t[:, c0:512], vl, at.t[:, 0:w], start=first, stop=last),
                     reads=at.tl() + va.tl(kb), writes=op_.tl())
            if last:
                S.op("dve", lambda e: e.tensor_copy(oT.t[p0:p0 + 64, c, qt * 512:(qt + 1) * 512], op_.t[p0:p0 + 64, :]),
                     reads=op_.tl(), writes=oT.tl(c, qt))

        for n in range(n_it + 2):
            if n < n_it:
                s1(n)
                s2(n)
            if 0 <= n - 1 < n_it:
                s3(n - 1)
                s4(n - 1)
            if 0 <= n - 2 < n_it:
                s5(n - 2)

    def gdn(self, l):
        S, NT, NB, T = self.S, self.NT, self.NB, self.Tn
        PS, C32, hT = self.PS, self.C32, self.hT
        c32t = self.c32.tl()
        gtok, gcrc, gsc, wg = self.gtok, self.gcrc, self.gsc, self.wgate
        PT = self.PT
        gpar = self.pv(self.slots["gate"], self.L, 8)
        nexpA = self.nexpA
        S.dma("pool", wg.t[:], self.d["win"][l, :, :, 3584:3592], writes=wg.tl())
        ps = self.mmps()

        def gproj(blk):
            for kc in range(NCH):
                S.op("pe", lambda e, kc=kc: e.matmul(ps.t[:, blk * 8:blk * 8 + 8], hT.t[:, kc, blk * 128:(blk + 1) * 128], wg.t[:, kc, :],
                                                     start=(kc == 0), stop=(kc == NCH - 1)),
                     reads=wg.tl() + hT.tl(kc, blk // 4), writes=ps.tl())
        for blk in range(NB):
            gproj(blk)
        psv = ps.t[:, 0:NB * 8].rearrange("p (a b) -> p a b", b=8)

        def ghead(h):
            S.op("act", lambda e: e.activation(gtok.t[:, :, h], psv[:, :, h], AF.Exp, bias=gpar[:, l, 4 + h:5 + h]),
                 reads=ps.tl() + PT, writes=gtok.tl())
            S.op("act", lambda e: e.activation(gtok.t[:, :, h], gtok.t[:, :, h], AF.Ln, bias=1.0), reads=gtok.tl(), writes=gtok.tl())
            S.op("dve", lambda e: e.tensor_scalar(gtok.t[:, :, h], gtok.t[:, :, h], nexpA.t[:, l, h:h + 1], None, ALU.mult),
                 reads=gtok.tl() + nexpA.tl(), writes=gtok.tl())
        for h in range(DN_H):
            ghead(h)
        S.op("act", lambda e: e.activation(gtok.t[:, :, 4:8], psv[:, :, 4:8], AF.Exp, scale=-1.0), reads=ps.tl(), writes=gtok.tl())
        S.op("dve", lambda e: e.tensor_scalar(gtok.t[:, :, 4:8], gtok.t[:, :, 4:8], 1.0, None, ALU.add), reads=gtok.tl(), writes=gtok.tl())
        S.op("dve", lambda e: e.reciprocal(gtok.t[:, :, 4:8], gtok.t[:, :, 4:8]), reads=gtok.tl(), writes=gtok.tl())
        ps2 = self.mmps()

        def gcs(blk):
            S.op("pe", lambda e: e.matmul(ps2.t[:, blk * 8:blk * 8 + 4], C32("mincl"), gtok.t[:, blk, 0:4], start=True, stop=True),
                 reads=gtok.tl() + c32t, writes=ps2.tl())
            S.op("pe", lambda e: e.matmul(ps2.t[:, blk * 8 + 4:blk * 8 + 8], C32("mrev"), gtok.t[:, blk, 0:4], start=True, stop=True),
                 reads=gtok.tl() + c32t, writes=ps2.tl())
        for blk in range(NB):
            gcs(blk)
        S.op("dve", lambda e: e.tensor_copy(gcrc.t[:].rearrange("p a b -> p (a b)"), ps2.t[:, 0:NB * 8]), reads=ps2.tl(), writes=gcrc.tl())
        S.op("act", lambda e: e.activation(gsc.t[:], gcrc.t[:], AF.Exp), reads=gcrc.tl(), writes=gsc.tl())
        S.op("dve", lambda e: e.tensor_tensor(gsc.t[:, :, 0:4], gsc.t[:, :, 0:4], gtok.t[:, :, 4:8], ALU.mult),
             reads=gsc.tl() + gtok.tl(), writes=gsc.tl())
        self.dump(f"gtok_{l}", gtok.t[:], gtok.tl(), [128, NB, 8])
        self.dump(f"gcrc_{l}", gcrc.t[:], gcrc.tl(), [128, NB, 8])
        S.op("pool", lambda e: e.memset(self.raw.t[:, 0:3], 0.0), writes=self.raw.tl())
        for hd in range(DN_H):
            self.gdn_head(l, hd)

    def sigmoid_to(self, dst_ap, dst_tiles, src_ap, src_tiles):
        S = self.S
        S.op("act", lambda e: e.activation(dst_ap, src_ap, AF.Exp, scale=-1.0), reads=src_tiles, writes=dst_tiles)
        S.op("act", lambda e: e.activation(dst_ap, dst_ap, AF.Ln, bias=1.0), reads=dst_tiles, writes=dst_tiles)
        S.op("act", lambda e: e.activation(dst_ap, dst_ap, AF.Exp, scale=-1.0), reads=dst_tiles, writes=dst_tiles)

    def gdn_stream(self, l, hd, kind):
        S, NT, T = self.S, self.NT, self.Tn
        raw = self.raw
        col0 = {"q": 1536, "k": 2048, "v": 2560, "z": 3072}[kind] + hd * 128

        def evac(tt, ps):
            if tt % 2 == 0:
                S.op("dve", lambda e: e.tensor_copy(raw.t[:, 3 + tt * 512:3 + (tt + 1) * 512], ps.t[:, :]), reads=ps.tl(), writes=raw.tl())
            else:
                S.op("act", lambda e: e.activation(raw.t[:, 3 + tt * 512:3 + (tt + 1) * 512], ps.t[:, :], AF.Copy), reads=ps.tl(), writes=raw.tl())
        self.proj_chunk(l, col0, evac)
        cw = self.pv(self.slots["conv"], self.L, 12, 4)
        PT = self.PT
        j = {"q": 0, "k": 4, "v": 8, "z": 0}[kind] + hd
        b2 = math.log(DN_D ** -0.5) if kind == "q" else 0.0
        nsub = self.CH // 512

        def piece(tt):
            sub = tt % nsub
            t0 = tt * 512
            acc = self.cacc.t[:, sub * 512:(sub + 1) * 512]
            tA = self.tA.t[:, sub * 512:(sub + 1) * 512]
            acct, tAt = [self.acc_t[sub]], [self.tA_t[sub]]
            if kind == "z":
                self.sigmoid_to(tA, tAt, raw.t[:, 3 + t0:3 + t0 + 512], raw.tl())
                S.op("dve", lambda e: e.tensor_tensor(self.oT.t[:, 4 + hd, t0:t0 + 512], raw.t[:, 3 + t0:3 + t0 + 512], tA, ALU.mult),
                     reads=raw.tl() + tAt, writes=self.oT.tl(4 + hd, tt))
                return
            S.op("act", lambda e: e.activation(acc, raw.t[:, t0:t0 + 512], AF.Copy, scale=cw[:, l, j, 0:1]),
                 reads=raw.tl() + PT, writes=acct)
            for k in range(1, 4):
                S.op("dve", lambda e, k=k: e.scalar_tensor_tensor(
                    acc, raw.t[:, t0 + k:t0 + k + 512], cw[:, l, j, k:k + 1], acc, ALU.mult, ALU.add),
                    reads=raw.tl() + acct + PT, writes=acct)
            self.sigmoid_to(tA, tAt, acc, acct)
            if kind == "v":
                S.op("dve", lambda e: e.tensor_tensor(self.vT.t[:, t0:t0 + 512], acc, tA, ALU.mult),
                     reads=acct + tAt, writes=self.vT.tl())
                return
            S.op("dve", lambda e: e.tensor_tensor(acc, acc, tA, ALU.mult), reads=acct + tAt, writes=acct)
            dst = self.qT if kind == "q" else self.kT
            self.group_norm(acc, acct, "ones", None, None, dst.t[:, t0:t0 + 512], dst.tl(), bias2=b2)

        for t2 in range(0, NT, nsub):
            lists = [S.capture(lambda tt=tt: piece(tt)) for tt in range(t2, min(t2 + nsub, NT))]
            S.merge(*lists)

    def tri_inverse(self, L32):
        S, ring, C32c, C16c = self.S, self.ring, self.C32, self.C16
        c32t = self.c32.tl()
        ident32 = C32c("ident")
        PS = self.PS
        pu = PS[3]
        S.op("pe", lambda e: e.matmul(pu.t[:, 0:128], L32.t[:], ident32, start=True, stop=True), reads=L32.tl() + c32t, writes=pu.tl())
        def fr(buf):
            return buf.t[:].bitcast(F32R) if NEUMANN_F32R else buf.t[:]
        L16, U16 = ring(self.g_Lc, "Lc"), ring(self.g_Uc, "Uc")
        LB, UB = ring(self.g_Lc, "Lc"), ring(self.g_Uc, "Uc")
        C32, C32T, C64 = ring(self.g_C32, "C32"), ring(self.g_C32T, "C32T"), ring(self.g_C64, "C64")
        X0, Y0 = ring(self.g_Xc, "Xc"), ring(self.g_Yc, "Yc")
        XB, YB = ring(self.g_Xc, "Xc"), ring(self.g_Yc, "Yc")
        S.op("dve", lambda e: e.tensor_tensor(fr(L16), L32.t[:], C32c("m16"), ALU.mult), reads=L32.tl() + c32t, writes=L16.tl())
        S.op("dve", lambda e: e.tensor_tensor(C32.t[:], L32.t[:], C32c("m32"), ALU.mult), reads=L32.tl() + c32t, writes=C32.tl())
        S.op("pool", lambda e: e.tensor_tensor(C64.t[:], L32.t[:], C32c("m64"), ALU.mult), reads=L32.tl() + c32t, writes=C64.tl())
        S.op("dve", lambda e: e.tensor_tensor(fr(U16), pu.t[:, 0:128], C32c("m16T"), ALU.mult), reads=pu.tl() + c32t, writes=U16.tl())
        S.op("dve", lambda e: e.tensor_tensor(C32T.t[:], pu.t[:, 0:128], C32c("m32T"), ALU.mult), reads=pu.tl() + c32t, writes=C32T.tl())
        S.op("dve", lambda e: e.tensor_tensor(fr(Y0), ident32, L16.t[:], ALU.subtract), reads=L16.tl() + c32t, writes=Y0.tl())
        S.op("dve", lambda e: e.tensor_tensor(fr(X0), ident32, U16.t[:], ALU.subtract), reads=U16.tl() + c32t, writes=X0.tl())
        Lk, Uk, Xk, Yk = L16, U16, X0, Y0
        if getattr(S, "cap", None) is not None:
            self._marks.append(len(S.cap))
        first_call = not getattr(self, "_tri_dbg", False)
        self._tri_dbg = True
        if first_call:
            self.dump("dbgX0", Xk.t[:], Xk.tl(), [128, 128])
            self.dump("dbgU16", Uk.t[:], Uk.tl(), [128, 128])
            self.dump("dbgL16", Lk.t[:], Lk.tl(), [128, 128])
        T16T = T16 = None
        for k in range(3):
            last = (k == 2)
            pq = PS[4]
            S.op("pe", lambda e, Uk=Uk, Lk=Lk, pq=pq: e.matmul(pq.t[:, 0:128], fr(Uk), fr(Lk), start=True, stop=True),
                 reads=Uk.tl() + Lk.tl(), writes=pq.tl())
            S.op("pe", lambda e, Uk=Uk, Lk=Lk, pq=pq: e.matmul(pq.t[:, 128:256], fr(Lk), fr(Uk), start=True, stop=True),
                 reads=Uk.tl() + Lk.tl(), writes=pq.tl())
            Ln_, Un_ = (LB, UB) if k % 2 == 0 else (L16, U16)
            S.op("act", lambda e, Ln_=Ln_, pq=pq: e.activation(fr(Ln_), pq.t[:, 0:128], AF.Copy), reads=pq.tl(), writes=Ln_.tl())
            S.op("dve", lambda e, Un_=Un_, pq=pq: e.tensor_copy(fr(Un_), pq.t[:, 128:256]), reads=pq.tl(), writes=Un_.tl())
            S.op("pe", lambda e, Ln_=Ln_, Xk=Xk, pq=pq: e.matmul(pq.t[:, 256:384], fr(Ln_), fr(Xk), start=True, stop=True),
                 reads=Ln_.tl() + Xk.tl(), writes=pq.tl())
            S.op("pe", lambda e, Un_=Un_, Yk=Yk, pq=pq: e.matmul(pq.t[:, 384:512], fr(Un_), fr(Yk), start=True, stop=True),
                 reads=Un_.tl() + Yk.tl(), writes=pq.tl())
            if last:
                Xn, Yn = ring(self.g_T16T, "T16T"), ring(self.g_T16, "T16")
            else:
                Xn, Yn = (XB, YB) if k % 2 == 0 else (X0, Y0)
            S.op("dve", lambda e, Xn=Xn, Xk=Xk, pq=pq, last=last: e.tensor_tensor(Xn.t[:] if last else fr(Xn), pq.t[:, 256:384], Xk.t[:], ALU.add),
                 reads=pq.tl() + Xk.tl(), writes=Xn.tl())
            S.op("dve", lambda e, Yn=Yn, Yk=Yk, pq=pq, last=last: e.tensor_tensor(Yn.t[:] if last else fr(Yn), pq.t[:, 384:512], Yk.t[:], ALU.add),
                 reads=pq.tl() + Yk.tl(), writes=Yn.tl())
            Uk, Lk, Xk, Yk = Un_, Ln_, Xn, Yn
        T16T, T16 = Xk, Yk
        if getattr(S, "cap", None) is not None:
            self._marks.append(len(S.cap))
        if first_call:
            self.dump("dbgT16T", T16T.t[:], T16T.tl(), [128, 128], BF16)
        pd = PS[5]
        S.op("pe", lambda e: e.matmul(pd.t[:, 0:128], C32.t[:], T16T.t[:], start=True, stop=True), reads=C32.tl() + T16T.tl(), writes=pd.tl())
        S.op("pe", lambda e: e.matmul(pd.t[:, 128:256], C32T.t[:], T16.t[:], start=True, stop=True), reads=C32T.tl() + T16.tl(), writes=pd.tl())
        Ya, Yb = ring(self.g_Ya, "Ya"), ring(self.g_Yb, "Yb")
        S.op("act", lambda e: e.activation(Ya.t[:], pd.t[:, 0:128], AF.Copy), reads=pd.tl(), writes=Ya.tl())
        S.op("dve", lambda e: e.tensor_copy(Yb.t[:], pd.t[:, 128:256]), reads=pd.tl(), writes=Yb.tl())
        pd2 = PS[5]
        S.op("pe", lambda e: e.matmul(pd2.t[:, 0:128], T16.t[:], Ya.t[:], start=True, stop=True), reads=T16.tl() + Ya.tl(), writes=pd2.tl())
        S.op("pe", lambda e: e.matmul(pd2.t[:, 128:256], T16T.t[:], Yb.t[:], start=True, stop=True), reads=T16T.tl() + Yb.tl(), writes=pd2.tl())
        T32T, T32 = ring(self.g_T32T, "T32T"), ring(self.g_T32, "T32")
        S.op("dve", lambda e: e.scalar_tensor_tensor(T32T.t[:], pd2.t[:, 0:128], -1.0, T16T.t[:], ALU.mult, ALU.add),
             reads=pd2.tl() + T16T.tl(), writes=T32T.tl())
        S.op("dve", lambda e: e.scalar_tensor_tensor(T32.t[:], pd2.t[:, 128:256], -1.0, T16.t[:], ALU.mult, ALU.add),
             reads=pd2.tl() + T16.tl(), writes=T32.tl())
        pe1 = PS[5]
        S.op("pe", lambda e: e.matmul(pe1.t[:, 0:128], C64.t[:], T32T.t[:], start=True, stop=True), reads=C64.tl() + T32T.tl(), writes=pe1.tl())
        Yd = ring(self.g_Yd, "Yd")
        S.op("act", lambda e: e.activation(Yd.t[:], pe1.t[:, 0:128], AF.Copy), reads=pe1.tl(), writes=Yd.tl())
        S.op("pe", lambda e: e.matmul(pe1.t[:, 128:256], T32.t[:], Yd.t[:], start=True, stop=True), reads=T32.tl() + Yd.tl(), writes=pe1.tl())
        TT = ring(self.g_TT, "TT")
        S.op("dve", lambda e: e.scalar_tensor_tensor(TT.t[:], pe1.t[:, 128:256], -1.0, T32T.t[:], ALU.mult, ALU.add),
             reads=pe1.tl() + T32T.tl(), writes=TT.tl())
        return TT

    def gdn_head(self, l, hd):
        S, NT, NB, T = self.S, self.NT, self.NB, self.Tn
        PS, C16, C32 = self.PS, self.C16, self.C32
        c16t, c32t = self.c16.tl(), self.c32.tl()
        kT, qT, vT = self.kT, self.qT, self.vT
        gtok, gcrc, gsc = self.gtok, self.gcrc, self.gsc
        ring = self.ring
        for kind in ("q", "k", "v", "z"):
            self.gdn_stream(l, hd, kind)
        self.dump(f"qT_{l}_{hd}", qT.t[:], qT.tl(), [128, T], BF16)
        self.dump(f"kT_{l}_{hd}", kT.t[:], kT.tl(), [128, T], BF16)
        self.dump(f"vT_{l}_{hd}", vT.t[:], vT.tl(), [128, T], BF16)

        S32, Sbf = self.S32, self.Sbf
        S.op("dve", lambda e: e.memset(S32.t[:], 0.0), writes=S32.tl())
        S.op("pool", lambda e: e.memset(Sbf.t[:], 0.0), writes=Sbf.tl())
        PB = [PS[2], PS[3], PS[4], PS[5]]
        ident16 = C16("ident")

        def psr():
            return ring(PB, "gps")
        prep_out = {}

        def prep(blk):
            bs = slice(blk * 128, (blk + 1) * 128)
            gcol = gtok.t[:, blk, hd:hd + 1]
            bcol = gtok.t[:, blk, 4 + hd:5 + hd]
            gccol = gcrc.t[:, blk, hd:hd + 1]
            bgcol = gsc.t[:, blk, hd:hd + 1]
            erccol = gsc.t[:, blk, 4 + hd:5 + hd]
            p1 = PS[2]
            S.op("pe", lambda e: e.matmul(p1.t[:, 0:128], gcol.to_broadcast([128, 128]), C32("mincl"), start=True, stop=True),
                 reads=gtok.tl() + c32t, writes=p1.tl())
            ebc, tI, DL, DLs = ring(self.g_ebc, "ebc"), ring(self.g_tI, "tI"), ring(self.g_DL, "DL"), ring(self.g_DLs, "DLs")
            S.op("act", lambda e: e.activation(ebc.t[:], p1.t[:, 0:128], AF.Exp), reads=p1.tl(), writes=ebc.tl())
            S.op("dve", lambda e: e.scalar_tensor_tensor(tI.t[:], p1.t[:, 0:128], gccol, C32("mbI"), ALU.subtract, ALU.add),
                 reads=p1.tl() + gcrc.tl() + c32t, writes=tI.tl())
            S.op("act", lambda e: e.activation(DL.t[:], tI.t[:], AF.Exp, scale=-1.0), reads=tI.tl(), writes=DL.tl())
            S.op("dve", lambda e: e.tensor_tensor(DLs.t[:], DL.t[:], C32("strict"), ALU.mult), reads=DL.tl() + c32t, writes=DLs.tl())
            p2 = PS[3]
            S.op("pe", lambda e: e.matmul(p2.t[:, 0:128], kT.t[:, bs], kT.t[:, bs], start=True, stop=True), reads=kT.tl(), writes=p2.tl())
            S.op("pe", lambda e: e.matmul(p2.t[:, 128:256], qT.t[:, bs], kT.t[:, bs], start=True, stop=True), reads=kT.tl() + qT.tl(), writes=p2.tl())
            L32, Ab = ring(self.g_L32, "L32"), ring(self.g_A, "A")
            S.op("dve", lambda e: e.scalar_tensor_tensor(L32.t[:], p2.t[:, 0:128], bcol, DLs.t[:], ALU.mult, ALU.mult),
                 reads=p2.tl() + gtok.tl() + DLs.tl(), writes=L32.tl())
            S.op("dve", lambda e: e.tensor_tensor(Ab.t[:], p2.t[:, 128:256], DL.t[:], ALU.mult), reads=p2.tl() + DL.tl(), writes=Ab.tl())
            p3 = PS[2]
            p3b = p3.t[:, :].bitcast(BF16)
            S.op("pe", lambda e: e.transpose(p3b[:, 128:256], Ab.t[:], ident16), reads=Ab.tl() + c16t, writes=p3.tl())
            S.op("pe", lambda e: e.transpose(p3b[:, 256:384], kT.t[:, bs], ident16), reads=kT.tl() + c16t, writes=p3.tl())
            S.op("pe", lambda e: e.transpose(p3b[:, 384:512], vT.t[:, bs], ident16), reads=vT.tl() + c16t, writes=p3.tl())
            AT = ring(self.g_AT, "AT")
            kbg, kd, vb = ring(self.g_kbg, "kbg"), ring(self.g_kd, "kd"), ring(self.g_vb, "vb")
            S.op("act", lambda e: e.activation(AT.t[:], p3b[:, 128:256], AF.Copy), reads=p3.tl(), writes=AT.tl())
            S.op("act", lambda e: e.activation(kbg.t[:], p3b[:, 256:384], AF.Copy, scale=bgcol), reads=p3.tl() + gsc.tl(), writes=kbg.tl())
            S.op("dve", lambda e: e.tensor_scalar(kd.t[:], p3b[:, 256:384], erccol, None, ALU.mult), reads=p3.tl() + gsc.tl(), writes=kd.tl())
            S.op("act", lambda e: e.activation(vb.t[:], p3b[:, 384:512], AF.Copy, scale=bcol), reads=p3.tl() + gtok.tl(), writes=vb.tl())
            qg = ring(self.g_qg, "qg")
            S.op("dve", lambda e: e.tensor_tensor(qg.t[:], qT.t[:, bs], ebc.t[:], ALU.mult), reads=qT.tl() + ebc.tl(), writes=qg.tl())
            TT = self.tri_inverse(L32)
            if blk == 0:
                self.dump(f"L32_{l}_{hd}", L32.t[:], L32.tl(), [128, 128])
                self.dump(f"TT_{l}_{hd}", TT.t[:], TT.tl(), [128, 128], BF16)
            p4 = PS[5]
            S.op("pe", lambda e: e.matmul(p4.t[:, 0:128], TT.t[:], vb.t[:], start=True, stop=True), reads=TT.tl() + vb.tl(), writes=p4.tl())
            S.op("pe", lambda e: e.matmul(p4.t[:, 128:256], kbg.t[:], TT.t[:], start=True, stop=True), reads=TT.tl() + kbg.tl(), writes=p4.tl())
            u, wT = ring(self.g_u, "u"), ring(self.g_wT, "wT")
            S.op("dve", lambda e: e.tensor_copy(u.t[:], p4.t[:, 0:128]), reads=p4.tl(), writes=u.tl())
            S.op("act", lambda e: e.activation(wT.t[:], p4.t[:, 128:256], AF.Copy), reads=p4.tl(), writes=wT.tl())
            prep_out[blk] = (u, wT, kd, AT, qg, ebc)

        opsum, pw = PS[6], PS[7]

        def chunk(blk, ch, u, wT, kd, AT, qg, ebc):
            r0 = ch * 64
            rs = slice(r0, r0 + 64)
            cs = slice((blk % 4) * 128 + r0, (blk % 4) * 128 + r0 + 64)
            S.op("pe", lambda e: e.matmul(pw.t[:, 0:128], wT.t[:], Sbf.t[:], start=True, stop=True), reads=wT.tl() + Sbf.tl(), writes=pw.tl())
            vn = ring(self.g_vn, "vn")
            S.op("dve", lambda e: e.tensor_tensor(vn.t[rs, :], u.t[rs, :], pw.t[rs, 0:128], ALU.subtract), reads=u.tl() + pw.tl(), writes=vn.tl())
            S.op("pe", lambda e: e.matmul(opsum.t[:, cs], Sbf.t[:], qg.t[:, rs], start=True, stop=False), reads=Sbf.tl() + qg.tl(), writes=opsum.tl())
            S.op("pe", lambda e: e.matmul(opsum.t[:, cs], vn.t[rs, :], AT.t[rs, rs], start=False, stop=True), reads=vn.tl() + AT.tl(), writes=opsum.tl())
            S.op("pe", lambda e: e.matmul(pw.t[:, 128:256], kd.t[rs, :], vn.t[rs, :], start=True, stop=True), reads=kd.tl() + vn.tl(), writes=pw.tl())
            S.op("dve", lambda e: e.scalar_tensor_tensor(S32.t[:], S32.t[:], ebc.t[:, r0 + 63:r0 + 64], pw.t[:, 128:256], ALU.mult, ALU.add),
                 reads=S32.tl() + ebc.tl() + pw.tl(), writes=S32.tl())
            S.op("act", lambda e: e.activation(Sbf.t[:], S32.t[:], AF.Copy), reads=S32.tl(), writes=Sbf.tl())

        def finish(tt):
            ts = slice(tt * 512, (tt + 1) * 512)
            go = self.tA.t[:, 0:512]
            got = [self.tA_t[0]]
            S.op("act", lambda e: e.activation(go, opsum.t[:, :], AF.Copy), reads=opsum.tl(), writes=got)
            dno = self.slots["dno"]
            self.group_norm(go, got, "ones", DN_D, self.prm.t[:, dno[0] + l:dno[0] + l + 1], go, got)
            S.op("dve", lambda e: e.tensor_tensor(self.oT.t[:, 4 + hd, ts], go, self.oT.t[:, 4 + hd, ts], ALU.mult),
                 reads=got, writes=self.oT.tl(4 + hd, tt))

        def chain(blk):
            args = prep_out.pop(blk)
            for ch in range(2):
                chunk(blk, ch, *args)
            if blk % 4 == 3:
                finish(blk // 4)

        parts = {}

        def cap_prep(b):
            if b < NB and b not in parts:
                self._marks = []
                lst = S.capture(lambda: prep(b))
                m1, m2 = self._marks
                parts[b] = (lst[:m1], lst[m1:m2], lst[m2:])

        def part(b, i):
            cap_prep(b)
            return parts[b][i] if b < NB else []
        S.merge(part(0, 0))
        S.merge(part(0, 1), part(1, 0))
        S.merge(part(0, 2), part(1, 1), part(2, 0))
        for blk in range(NB):
            lb = S.capture(lambda: chain(blk))
            S.merge(part(blk + 1, 2), part(blk + 2, 1), part(blk + 3, 0), lb)
            parts.pop(blk, None)

    def out_proj(self, l, s):
        S, NT = self.S, self.NT
        xT, oT, mod = self.xT, self.oT, self.mod

        def one(wo, cc, c, tt):
            ts = slice(tt * 512, (tt + 1) * 512)
            ps = self.mmps()
            for kc in range(NCH):
                S.op("pe", lambda e, kc=kc: e.matmul(ps.t[:, :], wo.t[:, kc, cc * 128:(cc + 1) * 128], oT.t[:, kc, ts],
                                                     start=(kc == 0), stop=(kc == NCH - 1)),
                     reads=wo.tl() + oT.tl(kc, tt), writes=ps.tl())
            S.op("dve", lambda e: e.scalar_tensor_tensor(xT.t[:, c, ts], ps.t[:, :], mod.t[:, l, 16 + c, s:s + 1], xT.t[:, c, ts], ALU.mult, ALU.add),
                 reads=ps.tl() + mod.tl() + xT.tl(c, tt), writes=xT.tl(c, tt))
        for half in range(2):
            wo = self.wo[half]
            S.dma("pool", wo.t[:], self.d["wout"][l, :, :, half * 512:(half + 1) * 512], writes=wo.tl())
        for half in range(2):
            for cc in range(4):
                for tt in range(NT):
                    one(self.wo[half], cc, half * 4 + cc, tt)

    def mlp(self, l, s):
        S, NT, HT = self.S, self.NT, self.HT
        xT, hT, mod = self.xT, self.hT, self.mod
        nh = self.Tn // HT
        tph = HT // 512

        def ff1(wb, j, tt, t2):
            ts = slice(tt * 512, (tt + 1) * 512)
            ps = self.mmps()
            for kc in range(NCH):
                S.op("pe", lambda e, kc=kc: e.matmul(ps.t[:, :], wb.t[:, kc, :], hT.t[:, kc, ts], start=(kc == 0), stop=(kc == NCH - 1)),
                     reads=wb.tl() + hT.tl(kc, tt), writes=ps.tl())
            rl = self.ring(self.relu, "relu")
            hb, jj = self.hid(j)
            S.op("act", lambda e: e.activation(rl.t[:], ps.t[:, :], AF.Relu), reads=ps.tl(), writes=rl.tl())
            S.op("pool", lambda e: e.tensor_tensor(hb.t[:, jj, t2 * 512:(t2 + 1) * 512], rl.t[:], rl.t[:], ALU.mult),
                 reads=rl.tl(), writes=hb.tl(jj, t2))

        def ff2(w2, c, tt, t2):
            ts = slice(tt * 512, (tt + 1) * 512)
            ps = self.mmps()
            for j in range(32):
                hb, jj = self.hid(j)
                S.op("pe", lambda e, j=j, hb=hb, jj=jj: e.matmul(ps.t[:, :], w2.t[:, j, :], hb.t[:, jj, t2 * 512:(t2 + 1) * 512],
                                                                start=(j == 0), stop=(j == 31)),
                     reads=w2.tl() + hb.tl(jj, t2), writes=ps.tl())
            S.op("dve", lambda e: e.scalar_tensor_tensor(xT.t[:, c, ts], ps.t[:, :], mod.t[:, l, 40 + c, s:s + 1], xT.t[:, c, ts], ALU.mult, ALU.add),
                 reads=ps.tl() + mod.tl() + xT.tl(c, tt), writes=xT.tl(c, tt))

        s1 = WStream(self, self.wring, [self.d["ff1"][l, :, :, j * 128:(j + 1) * 128] for hf in range(nh) for j in range(32)], 2)
        s2 = WStream(self, self.w2, [self.d["ff2"][l, :, :, c * 128:(c + 1) * 128] for hf in range(nh) for c in range(NCH)], 1)
        for hf in range(nh):
            for j in range(32):
                wb = s1.next()
                for t2 in range(tph):
                    ff1(wb, j, hf * tph + t2, t2)
            for c in range(NCH):
                w2 = s2.next()
                for t2 in range(tph):
                    ff2(w2, c, hf * tph + t2, t2)


def _chunked(w, L):
    Lk, K, N = w.shape
    return np.ascontiguousarray(w.reshape(Lk, K // 128, 128, N).transpose(0, 2, 1, 3))


def prep_shared(inp, L):
    f = lambda a: np.ascontiguousarray(np.asarray(a, dtype=np.float32))
    sh = {}
    sh["w_ada"] = _chunked(f(inp["w_ada"])[:L], L)
    sh["b_ada"] = f(f(inp["b_ada"])[:L].reshape(L, 48, 128).transpose(2, 0, 1))
    sh["norm_mix"] = f(f(inp["norm_mix"])[:L].reshape(L, NCH, 128).transpose(2, 0, 1))
    sh["norm_mlp"] = f(f(inp["norm_mlp"])[:L].reshape(L, NCH, 128).transpose(2, 0, 1))
    sh["w_in"] = _chunked(f(inp["w_in"])[:L], L)
    sh["sb_q_norm"] = f(np.tile(f(inp["sb_q_norm"])[:L], (1, 2)).T)
    sh["sb_k_norm"] = f(np.tile(f(inp["sb_k_norm"])[:L], (1, 2)).T)
    sh["conv_w"] = f(f(inp["conv_w"])[:L].reshape(L, 4, 12, 128).transpose(3, 0, 2, 1))
    gp = np.concatenate([f(inp["a_log"])[:L], f(inp["dt_bias"])[:L]], axis=1)
    sh["gate_p"] = f(np.broadcast_to(gp[None], (128, L, 8)))
    sh["dn_out_norm"] = f(f(inp["dn_out_norm"])[:L].T)
    sh["w_out"] = _chunked(f(inp["w_out"])[:L], L)
    sh["w_ff1"] = _chunked(f(inp["w_ff1"])[:L], L)
    sh["w_ff2"] = _chunked(f(inp["w_ff2"])[:L], L)
    cc = _consts()
    sh["consts32"], sh["consts16"] = cc[1], cc[3]
    return sh


def prep_core(x2, c2, T):
    ns = x2.shape[0]
    xT = np.ascontiguousarray(np.asarray(x2, np.float32).reshape(ns, T, NCH, 128).transpose(0, 3, 2, 1))
    cT = np.ascontiguousarray(np.asarray(c2, np.float32).reshape(2, NCH, 128).transpose(2, 1, 0))
    return {"xT": xT, "cT": cT}


_CACHE = {}


def kernel(**inputs):
    x = np.asarray(inputs["x"], np.float32)
    c = np.asarray(inputs["c"], np.float32)
    B, T, _ = x.shape
    key = (T, DEPTH, 2)
    if key not in _CACHE:
        _CACHE[key] = Builder(T=T, L=DEPTH, NSEQ=2)
    bld = _CACHE[key]
    sh = prep_shared(inputs, DEPTH)
    in_maps = []
    for core in range(NCORES):
        m = dict(sh)
        m.update(prep_core(x[2 * core:2 * core + 2], c[2 * core:2 * core + 2], T))
        in_maps.append(m)
    res = run_bass_kernel_spmd(bld.nc, in_maps, core_ids=list(range(NCORES)))
    out = np.empty((B, T, D), np.float32)
    for core in range(NCORES):
        oT = res.results[core]["outT"]
        out[2 * core:2 * core + 2] = oT.transpose(0, 3, 2, 1).reshape(2, T, D)
    return out
```

```python
import math
import numpy as np
import concourse.bass as bass
import concourse.mybir as mybir
from concourse.bass_utils import run_bass_kernel_spmd

F32 = mybir.dt.float32
BF16 = mybir.dt.bfloat16
F32R = mybir.dt.float32r
NEUMANN_F32R = False
AF = mybir.ActivationFunctionType
ALU = mybir.AluOpType

D = 1024
NCH = 8
DEPTH = 4
SEQ = 2048
NCORES = 8
SB_H, SB_D = 8, 64
DN_H, DN_D = 4, 128
IN_W = 3592
DFF = 4096
EPS = 1e-6
ENGS = ("pe", "act", "dve", "pool", "sp")
ATTACH_WAIT = True


class Tile:
    __slots__ = ("name", "lw", "rd", "sem", "semcnt", "psum")

    def __init__(self, name):
        self.name = name
        self.psum = False
        self.lw = None
        self.rd = []
        self.sem = None
        self.semcnt = 0


class Op:
    __slots__ = ("eng", "fn", "deps", "signal", "sigval", "dma", "dsem", "dval", "seq", "need", "know")

    def __init__(self, eng, fn):
        self.eng = eng
        self.fn = fn
        self.deps = []
        self.signal = False
        self.sigval = 0
        self.dma = False
        self.dsem = None
        self.dval = 0


class Sched:
    def __init__(self, nc):
        self.nc = nc
        self.ops = {e: [] for e in ENGS}
        self.ndsem = 0
        self.nseq = 0

    def _add(self, op, reads, writes):
        deps = []
        for t in reads:
            if t.lw is not None:
                deps.append(t.lw)
        for t in writes:
            if t.lw is not None:
                deps.append(t.lw)
            deps.extend(t.rd)
        for t in reads:
            t.rd.append(op)
        for t in writes:
            t.lw = op
            t.rd = []
        seen = set()
        for d in deps:
            if d is op or id(d) in seen:
                continue
            seen.add(id(d))
            if (not d.dma) and (not op.dma) and d.eng == "pe" and op.eng == "pe":
                continue
            op.deps.append(d)
            d.signal = True
        op.seq = self.nseq
        self.nseq += 1
        self.ops[op.eng].append(op)
        return op

    def op(self, eng, fn, reads=(), writes=()):
        if getattr(self, "cap", None) is not None:
            self.cap.append((eng, fn, list(reads), list(writes)))
            return None
        reads, writes = list(reads), list(writes)
        pr = [t for t in reads if t.psum]
        if pr:
            reads = [t for t in reads if not t.psum]
            writes = writes + [t for t in pr if t not in writes]
        return self._add(Op(eng, fn), reads, writes)

    def dma(self, eng, out, in_, reads=(), writes=()):
        o = Op(eng, None)
        o.dma = True
        tiles = list(writes) + list(reads)
        st = tiles[0]
        if st.sem is None:
            st.sem = self.nc.alloc_semaphore(f"dsem{self.ndsem}")
            self.ndsem += 1
        st.semcnt += 16
        o.dsem = st.sem
        o.dval = st.semcnt
        o.signal = True
        o.fn = lambda e, out=out, in_=in_: e.dma_start(out=out, in_=in_)
        return self._add(o, [], tiles)

    def capture(self, f):
        assert getattr(self, "cap", None) is None
        self.cap = []
        f()
        lst, self.cap = self.cap, None
        return lst

    def merge(self, *lists):
        idx = [0] * len(lists)
        total = sum(len(x) for x in lists)
        for _ in range(total):
            best, bf = None, None
            for k, lst in enumerate(lists):
                if idx[k] < len(lst):
                    fr = idx[k] / len(lst)
                    if bf is None or fr < bf:
                        best, bf = k, fr
            eng, fn, reads, writes = lists[best][idx[best]]
            idx[best] += 1
            self.op(eng, fn, reads, writes)

    def emit(self, final_wait_ops=()):
        nc = self.nc
        esem = {e: nc.alloc_semaphore(f"sem_{e}") for e in ENGS}
        for e in ENGS:
            c = 0
            for o in self.ops[e]:
                if (not o.dma) and o.signal:
                    c += 1
                    o.sigval = c
        stats = {}
        known = {e: {} for e in ENGS}
        allops = sorted((o for e in ENGS for o in self.ops[e]), key=lambda o: o.seq)
        for o in allops:
            kn = known[o.eng]
            o.need = []
            for d in o.deps:
                if d.dma:
                    key, val = d.dsem, d.dval
                else:
                    key, val = esem[d.eng], d.sigval
                if kn.get(key.num, 0) >= val:
                    continue
                o.need.append((key, val))
                for k2, v2 in d.know.items():
                    if kn.get(k2, 0) < v2:
                        kn[k2] = v2
                kn[key.num] = val
            if o.dma:
                o.know = dict(kn)
                o.know[o.dsem.num] = o.dval
            elif o.signal:
                o.know = dict(kn)
                o.know[esem[o.eng].num] = o.sigval
            else:
                o.know = None

        def run(e, eng):
            nw = 0
            for o in self.ops[e]:
                need = list(o.need)
                attach = need.pop() if (need and ATTACH_WAIT) else None
                for key, val in need:
                    eng.wait_ge(key, val)
                    nw += 1
                ins = o.fn(eng)
                if attach is not None:
                    ins._wait_ge(attach[0], attach[1])
                if o.dma:
                    ins.then_inc(o.dsem, 16)
                elif o.signal:
                    ins.then_inc(esem[e], 1)
            if e == "sp":
                for o in final_wait_ops:
                    if o.dma:
                        eng.wait_ge(o.dsem, o.dval)
                    else:
                        eng.wait_ge(esem[o.eng], o.sigval)
            stats[e] = (len(self.ops[e]), nw)

        with nc.Block() as block:
            @block.tensor
            def _(eng):
                run("pe", eng)

            @block.scalar
            def _(eng):
                run("act", eng)

            @block.vector
            def _(eng):
                run("dve", eng)

            @block.gpsimd
            def _(eng):
                run("pool", eng)

            @block.sync
            def _(eng):
                run("sp", eng)
        return stats


class Buf:
    def __init__(self, t, name, grid=None):
        self.t = t
        if grid is None:
            self.T = Tile(name)
        else:
            self.T = np.empty(grid, dtype=object)
            for idx in np.ndindex(*grid):
                self.T[idx] = Tile(f"{name}{idx}")

    def tl(self, *idx):
        if isinstance(self.T, Tile):
            return [self.T]
        sub = self.T[idx] if idx else self.T
        if isinstance(sub, Tile):
            return [sub]
        return list(sub.ravel())


def _consts():
    i = np.arange(128)
    same = (i[:, None] // 64) == (i[None, :] // 64)
    c = {}
    c["ident"] = np.eye(128, dtype=np.float32)
    c["ones"] = np.ones((128, 128), np.float32)
    blk = np.zeros((128, 128), np.float32)
    blk[:64, :64] = 1.0
    blk[64:, 64:] = 1.0
    c["blk64"] = blk
    c["tril"] = (i[:, None] >= i[None, :]).astype(np.float32)
    c["mincl"] = ((i[:, None] <= i[None, :]) & same).astype(np.float32)
    c["mrev"] = ((i[:, None] > i[None, :]) & same).astype(np.float32)
    c["mbI"] = np.where((i[None, :] <= i[:, None]) & same, 0.0, 30000.0).astype(np.float32)
    c["strict"] = (i[None, :] < i[:, None]).astype(np.float32)
    b16 = (i[:, None] // 16) == (i[None, :] // 16)
    b32 = (i[:, None] // 32) == (i[None, :] // 32)
    low = i[None, :] < i[:, None]
    c["m16"] = (b16 & low).astype(np.float32)
    c["m32"] = (b32 & ~b16 & low).astype(np.float32)
    c["m64"] = (same & ~b32 & low).astype(np.float32)
    c["m16T"] = np.ascontiguousarray(c["m16"].T)
    c["m32T"] = np.ascontiguousarray(c["m32"].T)
    n32 = ["ident", "mincl", "mrev", "mbI", "strict", "m16", "m32", "m64", "m16T", "m32T"]
    n16 = ["ident", "ones", "blk64", "tril"]
    return (n32, np.concatenate([c[n] for n in n32], axis=1), n16, np.concatenate([c[n] for n in n16], axis=1))


class WStream:
    def __init__(self, bld, bufs, srcs, depth):
        self.b, self.bufs, self.srcs, self.depth = bld, bufs, list(srcs), depth
        assert len(bufs) > depth
        self.n = 0
        self.issued = 0
        self.live = []
        for _ in range(min(depth, len(self.srcs))):
            self._issue()

    def _issue(self):
        buf = self.bufs[self.issued % len(self.bufs)]
        self.b.S.dma("pool", buf.t[:], self.srcs[self.issued], writes=buf.tl())
        self.live.append(buf)
        self.issued += 1

    def next(self):
        buf = self.live[self.n]
        self.n += 1
        if self.issued < len(self.srcs):
            self._issue()
        return buf


AR_EL = 30208


class Builder:
    def __init__(self, T=SEQ, L=DEPTH, NSEQ=2, dbg=(), stop=None):
        self.stop = stop
        self.Tn = T
        self.L = L
        self.NSEQ = NSEQ
        self.NT = T // 512
        self.NB = T // 128
        self.dbg_names = dbg
        nc = self.nc = bass.Bass("TRN2", target_bir_lowering=False)
        self.S = Sched(nc)
        self.cnt = {}
        self.final = []
        self.phase = {}
        self.build()

    def sb(self, name, shape, dt, grid=None):
        return Buf(self.nc.alloc_sbuf_tensor("s_" + name, list(shape), dt), name, grid)

    def av(self, phase, name, shape, dt, grid=None, base=None):
        ph = self.phase.setdefault(phase, {"off": 0, "tiles": []})
        nel = int(np.prod(shape[1:]))
        nbf = nel * (2 if dt == F32 else 1)
        off = (ph["off"] + 15) // 16 * 16
        ph["off"] = off + nbf
        assert ph["off"] <= AR_EL, (phase, name, ph["off"])
        ap = self.arena[:, off:off + nbf]
        if dt == F32:
            ap = ap.bitcast(F32)
        if len(shape) == 3:
            ap = ap.rearrange("p (a b) -> p a b", a=shape[1], b=shape[2])
        if shape[0] < 128:
            ap = ap[0:shape[0]]
        b = Buf(ap, name, grid)
        ph["tiles"].extend(b.tl())
        return b

    def ring(self, lst, key):
        v = self.cnt.get(key, 0)
        self.cnt[key] = v + 1
        return lst[v % len(lst)]

    def barrier(self, *phases, extra=()):
        tiles = list(extra)
        for p in phases:
            tiles.extend(self.phase[p]["tiles"])
        self.S.op("sp", lambda e: e.nop(), writes=tiles)

    def din(self, name, shape, dt=F32):
        return self.nc.dram_tensor(name, list(shape), dt, kind="ExternalInput").ap()

    def dump(self, name, ap, tiles, shape, dt=F32):
        if name not in self.dbg_names:
            return
        d = self.nc.dram_tensor("dbg_" + name, list(shape), dt, kind="ExternalOutput").ap()
        self.final.append(self.S.dma("sp", d, ap, reads=tiles))

    def build(self):
        nc, S, T, L, NSEQ, NT, NB = self.nc, self.S, self.Tn, self.L, self.NSEQ, self.NT, self.NB
        d_xT = self.din("xT", [NSEQ, 128, NCH, T])
        d_cT = self.din("cT", [128, NCH, 2])
        d_wada = self.din("w_ada", [L, 128, NCH, 6 * D])
        d_bada = self.din("b_ada", [128, L, 48])
        d_nmix = self.din("norm_mix", [128, L, NCH])
        d_nmlp = self.din("norm_mlp", [128, L, NCH])
        d_win = self.din("w_in", [L, 128, NCH, IN_W])
        d_sbq = self.din("sb_q_norm", [128, L])
        d_sbk = self.din("sb_k_norm", [128, L])
        d_conv = self.din("conv_w", [128, L, 12, 4])
        d_gate = self.din("gate_p", [128, L, 8])
        d_dno = self.din("dn_out_norm", [128, L])
        d_wout = self.din("w_out", [L, 128, NCH, D])
        d_ff1 = self.din("w_ff1", [L, 128, NCH, DFF])
        d_ff2 = self.din("w_ff2", [L, 128, 32, D])
        n32, c32m, n16, c16m = _consts()
        d_c32 = self.din("consts32", [128, c32m.shape[1]])
        d_c16 = self.din("consts16", [128, c16m.shape[1]])
        d_out = nc.dram_tensor("outT", [NSEQ, 128, NCH, T], F32, kind="ExternalOutput").ap()
        self.d = dict(win=d_win, wout=d_wout, ff1=d_ff1, ff2=d_ff2)

        self.arena = nc.alloc_sbuf_tensor("arena", [128, AR_EL], BF16)

        c32 = self.sb("c32", [128, c32m.shape[1]], F32)
        c16 = self.sb("c16", [128, c16m.shape[1]], BF16)
        S.dma("sp", c32.t[:], d_c32, writes=c32.tl())
        S.dma("pool", c16.t[:], d_c16, writes=c16.tl())
        self.c32, self.c16 = c32, c16
        self.C32 = lambda n: c32.t[:, n32.index(n) * 128:(n32.index(n) + 1) * 128]
        self.C16 = lambda n: c16.t[:, n16.index(n) * 128:(n16.index(n) + 1) * 128]

        prm = self.sb("prm", [128, 512], F32)
        PT = prm.tl()
        self.prm, self.PT = prm, PT
        o = [0]

        def pslot(n):
            r = (o[0], o[0] + n)
            o[0] += n
            return r
        s_bada, s_nmix, s_nmlp = pslot(L * 48), pslot(L * NCH), pslot(L * NCH)
        s_sbq, s_sbk, s_conv, s_dno, s_gate = pslot(L), pslot(L), pslot(L * 48), pslot(L), pslot(L * 8)
        assert o[0] <= 512

        def pv(s, *shape):
            ap = prm.t[:, s[0]:s[1]]
            if len(shape) == 2:
                ap = ap.rearrange("p (a b) -> p a b", a=shape[0], b=shape[1])
            elif len(shape) == 3:
                ap = ap.rearrange("p (a b c) -> p a b c", a=shape[0], b=shape[1], c=shape[2])
            return ap
        self.pv = pv
        self.slots = dict(sbq=s_sbq, sbk=s_sbk, conv=s_conv, dno=s_dno, gate=s_gate)
        S.dma("sp", pv(s_bada, L, 48), d_bada, writes=PT)
        S.dma("sp", pv(s_nmix, L, NCH), d_nmix, writes=PT)
        S.dma("sp", pv(s_nmlp, L, NCH), d_nmlp, writes=PT)
        S.dma("sp", prm.t[:, s_sbq[0]:s_sbq[1]], d_sbq, writes=PT)
        S.dma("sp", prm.t[:, s_sbk[0]:s_sbk[1]], d_sbk, writes=PT)
        S.dma("sp", pv(s_conv, L, 12, 4), d_conv, writes=PT)
        S.dma("sp", prm.t[:, s_dno[0]:s_dno[1]], d_dno, writes=PT)
        S.dma("sp", pv(s_gate, L, 8), d_gate, writes=PT)
        nexpA = self.sb("nexpA", [128, L, 4], F32)
        self.nexpA = nexpA
        S.op("act", lambda e: e.activation(nexpA.t[:], pv(s_gate, L, 8)[:, :, 0:4], AF.Exp), reads=PT, writes=nexpA.tl())
        S.op("dve", lambda e: e.tensor_scalar(nexpA.t[:], nexpA.t[:], -1.0, None, ALU.mult), reads=nexpA.tl(), writes=nexpA.tl())

        self.PS = PS = [Buf(nc.alloc_psum_tensor(f"ps{i}", [128, 512], F32), f"ps{i}") for i in range(8)]
        for b in PS:
            b.T.psum = True

        cT = self.sb("cT", [128, NCH, 2], F32)
        ctmp = self.sb("ctmp", [128, NCH, 2], F32)
        cond = self.sb("cond", [128, NCH, 2], BF16)
        S.dma("sp", cT.t[:], d_cT, writes=cT.tl())
        S.op("act", lambda e: e.activation(ctmp.t[:], cT.t[:], AF.Exp, scale=-1.0), reads=cT.tl(), writes=ctmp.tl())
        S.op("dve", lambda e: e.tensor_scalar(ctmp.t[:], ctmp.t[:], 1.0, None, ALU.add), reads=ctmp.tl(), writes=ctmp.tl())
        S.op("dve", lambda e: e.reciprocal(ctmp.t[:], ctmp.t[:]), reads=ctmp.tl(), writes=ctmp.tl())
        S.op("dve", lambda e: e.tensor_tensor(cond.t[:], cT.t[:], ctmp.t[:], ALU.mult), reads=cT.tl() + ctmp.tl(), writes=cond.tl())
        mod = self.sb("mod", [128, L, 48, 2], F32)
        self.mod = mod
        wada = [self.av("setup", f"wada{i}", [128, NCH, 512], BF16) for i in range(2)]

        def ada_piece(l, pc):
            wb = self.ring(wada, "wada")
            S.dma("pool", wb.t[:], d_wada[l, :, :, pc * 512:(pc + 1) * 512], writes=wb.tl())
            for jj in range(4):
                j = pc * 4 + jj
                for kc in range(NCH):
                    S.op("pe", lambda e, jj=jj, kc=kc, j=j: e.matmul(
                        PS[0].t[:, j * 2:j * 2 + 2], wb.t[:, kc, jj * 128:(jj + 1) * 128], cond.t[:, kc, :],
                        start=(kc == 0), stop=(kc == NCH - 1)), reads=wb.tl() + cond.tl(), writes=PS[0].tl())

        def ada_evac(l, b):
            S.op("dve", lambda e: e.tensor_tensor(
                mod.t[:, l, :, b], PS[0].t[:, 0:96].rearrange("p (j b) -> p j b", b=2)[:, :, b],
                pv(s_bada, L, 48)[:, l, :], ALU.add), reads=PS[0].tl() + PT, writes=mod.tl())
        for l in range(L):
            for pc in range(12):
                ada_piece(l, pc)
            for b in range(2):
                ada_evac(l, b)
        gains = self.sb("gains", [128, L, 2, NCH, 2], F32)
        self.gains = gains

        def gain_op(l, which, sl, m, b):
            S.op("dve", lambda e: e.scalar_tensor_tensor(
                gains.t[:, l, which, :, b], mod.t[:, l, m * 8:(m + 1) * 8, b], 1.0, pv(sl, L, NCH)[:, l, :], ALU.add, ALU.mult),
                reads=mod.tl() + PT, writes=gains.tl())
        for l in range(L):
            for which, (sl, m) in enumerate(((s_nmix, 1), (s_nmlp, 4))):
                for b in range(2):
                    gain_op(l, which, sl, m, b)
        self.dump("mod", mod.t[:], mod.tl(), [128, L, 48, 2])

        self.xT = self.sb("xT", [128, NCH, T], F32, grid=(NCH, NT))
        self.hT = self.sb("hT", [128, NCH, T], BF16, grid=(NCH, NT))
        self.oT = self.sb("oT", [128, NCH, T], BF16, grid=(NCH, NT))
        self.sqb = [self.sb(f"sqb{i}", [128, 512], BF16) for i in range(2)]
        self.wring = [self.sb(f"wring{i}", [128, NCH, 128], BF16) for i in range(3)]
        self.alloc_phases()

        xT = self.xT
        self.cur = "setup"
        for s in range(NSEQ):
            for c in range(NCH):
                S.dma("sp", xT.t[:, c, :], d_xT[s, :, c, :], writes=xT.tl(c))
            for l in range(L):
                self.layer(l, s)
            for c in range(NCH):
                self.final.append(S.dma("sp", d_out[s, :, c, :], xT.t[:, c, :], reads=xT.tl(c)))
        self.stats = S.emit(final_wait_ops=self.final)

    def switch(self, new, extra=()):
        self.barrier(self.cur, new, extra=extra)
        self.cur = new

    def alloc_phases(self):
        T, NB, NT = self.Tn, self.NB, self.NT
        av = self.av
        self.qa = av("sb", "qa", [128, 2, T], BF16, grid=(2, NT))
        self.ka = av("sb", "ka", [128, 2, T], BF16, grid=(2, NT))
        self.va = av("sb", "va", [128, NB, 256], BF16, grid=(NB,))
        self.wv = av("sb", "wv", [128, NCH, 256], BF16)
        self.a_e = [av("sb", f"a_e{i}", [128, 512], F32) for i in range(4)]
        self.a_x = [av("sb", f"a_x{i}", [128, 512], F32) for i in range(2)]
        self.a_sp = [av("sb", f"a_sp{i}", [128, 512], BF16) for i in range(4)]
        self.a_att = [av("sb", f"a_att{i}", [128, 512], BF16) for i in range(4)]
        self.a_R = [av("sb", f"a_R{i}", [1, 512], BF16) for i in range(2)]
        self.raw32 = [av("sb", f"raw32_{i}", [128, 512], F32) for i in range(2)]
        g = "gdn"
        self.raw = av(g, "raw", [128, T + 4], F32)
        self.CH = min(T, 1024)
        self.cacc = av(g, "cacc", [128, self.CH], F32)
        self.tA = av(g, "tA", [128, self.CH], F32)
        self.acc_t = [Tile(f"acc_t{i}") for i in range(self.CH // 512)]
        self.tA_t = [Tile(f"tA_t{i}") for i in range(self.CH // 512)]
        self.phase[g]["tiles"].extend(self.acc_t + self.tA_t)
        self.kT = av(g, "kT", [128, T], BF16)
        self.qT = av(g, "qT", [128, T], BF16)
        self.vT = av(g, "vT", [128, T], BF16)
        self.gtok = av(g, "gtok", [128, NB, 8], F32)
        self.gcrc = av(g, "gcrc", [128, NB, 8], F32)
        self.gsc = av(g, "gsc", [128, NB, 8], F32)
        self.wgate = av(g, "wgate", [128, NCH, 8], BF16)

        def mk(n, k, dt):
            return [av(g, f"{n}{i}", [128, 128], dt) for i in range(k)]
        self.g_ebc, self.g_tI, self.g_DL, self.g_DLs = mk("ebc", 4, F32), mk("tI", 1, F32), mk("DL", 2, F32), mk("DLs", 1, F32)
        self.g_L32, self.g_Lc, self.g_Uc = mk("L32", 2, F32), mk("Lc", 4, F32), mk("Uc", 4, F32)
        self.g_Xc, self.g_Yc = mk("Xc", 4, F32), mk("Yc", 4, F32)
        self.g_C32, self.g_C32T, self.g_C64 = mk("C32", 3, BF16), mk("C32T", 3, BF16), mk("C64", 3, BF16)
        self.g_T16T, self.g_T16, self.g_Ya, self.g_Yb = mk("T16T", 2, BF16), mk("T16", 2, BF16), mk("Ya", 2, BF16), mk("Yb", 2, BF16)
        self.g_T32T, self.g_T32, self.g_Yd, self.g_TT = mk("T32T", 2, BF16), mk("T32", 2, BF16), mk("Yd", 2, BF16), mk("TT", 2, BF16)
        self.g_A, self.g_AT, self.g_kbg, self.g_kd = mk("A", 2, BF16), mk("AT", 4, BF16), mk("kbg", 3, BF16), mk("kd", 4, BF16)
        self.g_vb, self.g_u, self.g_wT = mk("vb", 3, BF16), mk("u", 3, F32), mk("wT", 3, BF16)
        self.g_qg, self.g_vn = mk("qg", 4, BF16), mk("vn", 2, BF16)
        self.S32 = av(g, "S32", [128, 128], F32)
        self.Sbf = av(g, "Sbf", [128, 128], BF16)
        self.wo = [av("op", f"wo{i}", [128, NCH, 512], BF16) for i in range(2)]
        self.HT = min(1024, T)
        tph = self.HT // 512
        self.hidA = av("mlp", "hidA", [128, 16, self.HT], BF16, grid=(16, tph))
        self.w2 = [av("mlp", f"w2_{i}", [128, 32, 128], BF16) for i in range(2)]
        self.relu = [av("mlp", f"relu{i}", [128, 512], F32) for i in range(2)]
        if T == SEQ:
            ap = self.oT.t[:].rearrange("p c t -> p (c t)").rearrange("p (a b) -> p a b", a=16, b=self.HT)
            self.hidB = Buf(ap, "hidB", grid=(16, tph))
        else:
            self.hidB = self.sb("hidB", [128, 16, self.HT], BF16, grid=(16, tph))
        self.phase["mlp"]["tiles"].extend(self.hidB.tl())

    def hid(self, j):
        return (self.hidA, j) if j < 16 else (self.hidB, j - 16)

    def mmps(self):
        return self.ring([self.PS[0], self.PS[1]], "mmps")


    def norm_to_hT(self, l, s, which):
        S, NT, PS = self.S, self.NT, self.PS
        xT, hT, gains, mod = self.xT, self.hT, self.gains, self.mod
        msh = 0 if which == 0 else 3
        for tt in range(NT):
            ts = slice(tt * 512, (tt + 1) * 512)
            ps = self.mmps()
            for c in range(NCH):
                self._sq_mm(xT.t[:, c, ts], xT.tl(c, tt), ps, "ones", c == 0, c == NCH - 1, eng=("act" if c % 2 else "pool"))
            S.op("act", lambda e, ps=ps: e.activation(ps.t[:, :], ps.t[:, :], AF.Ln, bias=EPS, scale=1.0 / D), reads=ps.tl(), writes=ps.tl())
            S.op("act", lambda e, ps=ps: e.activation(ps.t[:, :], ps.t[:, :], AF.Exp, scale=-0.5), reads=ps.tl(), writes=ps.tl())
            for c in range(NCH):
                tmp = PS[2 + c % 2]
                S.op("dve", lambda e, c=c, ts=ts, ps=ps, tmp=tmp: e.tensor_tensor(tmp.t[:, :], xT.t[:, c, ts], ps.t[:, :], ALU.mult),
                     reads=xT.tl(c, tt) + ps.tl(), writes=tmp.tl())
                S.op("dve", lambda e, c=c, ts=ts, tmp=tmp: e.tensor_scalar(
                    hT.t[:, c, ts], tmp.t[:, :], gains.t[:, l, which, c, s:s + 1], mod.t[:, l, msh * 8 + c, s:s + 1], ALU.mult, ALU.add),
                    reads=tmp.tl() + gains.tl() + mod.tl(), writes=hT.tl(c, tt))

    def _sq_mm(self, src_ap, src_tiles, ps, ones_name, start, stop, eng="pool"):
        S = self.S
        sq = self.ring(self.sqb, "sqb")
        if eng == "act":
            S.op("act", lambda e: e.activation(sq.t[:], src_ap, AF.Square), reads=src_tiles, writes=sq.tl())
        else:
            S.op("pool", lambda e: e.tensor_tensor(sq.t[:], src_ap, src_ap, ALU.mult), reads=src_tiles, writes=sq.tl())
        S.op("pe", lambda e: e.matmul(ps.t[:, :], self.C16(ones_name), sq.t[:], start=start, stop=stop),
             reads=sq.tl() + self.c16.tl(), writes=ps.tl())

    def group_norm(self, src_ap, src_tiles, ones_name, nfeat, gain_ap, out_ap, out_tiles, bias2=0.0):
        S = self.S
        ps = self.mmps()
        self._sq_mm(src_ap, src_tiles, ps, ones_name, True, True, eng="act")
        sc = 1.0 if nfeat is None else 1.0 / nfeat
        S.op("act", lambda e: e.activation(ps.t[:, :], ps.t[:, :], AF.Ln, bias=EPS, scale=sc), reads=ps.tl(), writes=ps.tl())
        S.op("act", lambda e: e.activation(ps.t[:, :], ps.t[:, :], AF.Exp, scale=-0.5, bias=bias2), reads=ps.tl(), writes=ps.tl())
        if gain_ap is None:
            S.op("dve", lambda e: e.tensor_tensor(out_ap, src_ap, ps.t[:, :], ALU.mult), reads=src_tiles + ps.tl(), writes=out_tiles)
        else:
            S.op("dve", lambda e: e.scalar_tensor_tensor(out_ap, src_ap, gain_ap, ps.t[:, :], ALU.mult, ALU.mult),
                 reads=src_tiles + ps.tl() + self.PT, writes=out_tiles)

    def proj_chunk(self, l, col0, evac):
        S, hT = self.S, self.hT
        assert self.win_cols[self.win_stream.n] == col0
        wb = self.win_stream.next()
        for tt in range(self.NT):
            ts = slice(tt * 512, (tt + 1) * 512)
            ps = self.mmps()
            for kc in range(NCH):
                S.op("pe", lambda e, ps=ps, kc=kc, ts=ts: e.matmul(ps.t[:, :], wb.t[:, kc, :], hT.t[:, kc, ts],
                                                                  start=(kc == 0), stop=(kc == NCH - 1)),
                     reads=wb.tl() + hT.tl(kc, tt), writes=ps.tl())
            evac(tt, ps)

    def layer(self, l, s):
        T = self.Tn
        cols = []
        for half in range(2):
            for which in range(2):
                for cc in range(2):
                    cols.append(which * 512 + (half * 2 + cc) * 128)
        for hd in range(DN_H):
            for base in (1536, 2048, 2560, 3072):
                cols.append(base + hd * 128)
        self.win_cols = cols
        self.win_stream = WStream(self, self.wring, [self.d["win"][l, :, :, c0:c0 + 128] for c0 in cols], 2)
        if self.stop == "setup":
            return
        self.norm_to_hT(l, s, 0)
        self.dump(f"h1_{l}", self.hT.t[:], self.hT.tl(), [128, NCH, T], BF16)
        if self.stop == "norm":
            return
        self.switch("sb", extra=self.oT.tl())
        for half in range(2):
            self.sb_proj(l, half)
            if self.stop == "sbproj":
                return
            self.sb_attn(l, half)
        if self.stop == "attn":
            return
        self.switch("gdn")
        self.gdn(l)
        self.dump(f"oT_{l}", self.oT.t[:], self.oT.tl(), [128, NCH, T], BF16)
        if self.stop == "gdn":
            return
        self.switch("op")
        self.out_proj(l, s)
        self.dump(f"x1_{l}", self.xT.t[:], self.xT.tl(), [128, NCH, T])
        self.norm_to_hT(l, s, 1)
        self.switch("mlp", extra=self.oT.tl())
        self.mlp(l, s)
        self.dump(f"x2_{l}", self.xT.t[:], self.xT.tl(), [128, NCH, T])

    def sb_proj(self, l, half):
        S, NT, NB, hT = self.S, self.NT, self.NB, self.hT
        for which, dst, slot in ((0, self.qa, self.slots["sbq"]), (1, self.ka, self.slots["sbk"])):
            for cc in range(2):
                def evac(tt, ps, cc=cc, dst=dst, slot=slot):
                    r = self.raw32[tt % 2]
                    S.op("act", lambda e: e.activation(r.t[:], ps.t[:, :], AF.Copy), reads=ps.tl(), writes=r.tl())
                    self.group_norm(r.t[:], r.tl(), "blk64", SB_D, self.prm.t[:, slot[0] + l:slot[0] + l + 1],
                                    dst.t[:, cc, tt * 512:(tt + 1) * 512], dst.tl(cc, tt))
                self.proj_chunk(l, which * 512 + (half * 2 + cc) * 128, evac)
        wv, va = self.wv, self.va
        S.dma("pool", wv.t[:], self.d["win"][l, :, :, 1024 + half * 256:1024 + (half + 1) * 256], writes=wv.tl())

        def vblk(blk):
            ps = self.mmps()
            tt = blk // 4
            for kc in range(NCH):
                S.op("pe", lambda e, kc=kc: e.matmul(ps.t[:, 0:256], hT.t[:, kc, blk * 128:(blk + 1) * 128], wv.t[:, kc, :],
                                                     start=(kc == 0), stop=(kc == NCH - 1)),
                     reads=wv.tl() + hT.tl(kc, tt), writes=ps.tl())
            if blk % 2 == 0:
                S.op("dve", lambda e: e.tensor_copy(va.t[:, blk, :], ps.t[:, 0:256]), reads=ps.tl(), writes=va.tl(blk))
            else:
                S.op("act", lambda e: e.activation(va.t[:, blk, :], ps.t[:, 0:256], AF.Copy), reads=ps.tl(), writes=va.tl(blk))
        for blk in range(NB):
            vblk(blk)
        self.dump(f"qa_{l}_{half}", self.qa.t[:], self.qa.tl(), [128, 2, self.Tn], BF16)
        self.dump(f"ka_{l}_{half}", self.ka.t[:], self.ka.tl(), [128, 2, self.Tn], BF16)
        self.dump(f"va_{l}_{half}", self.va.t[:], self.va.tl(), [128, NB, 256], BF16)

    def sb_attn(self, l, half):
        S, NT = self.S, self.NT
        PS, C16 = self.PS, self.C16
        qa, ka, va, oT = self.qa, self.ka, self.va, self.oT
        c16t = self.c16.tl()
        scale = SB_D ** -0.5
        items = []
        for cc in range(2):
            for qt in range(NT):
                for kb in range(4 * qt + 3, -1, -1):
                    for hh in (2 * cc, 2 * cc + 1):
                        items.append((hh, qt, kb))
        n_it = len(items)
        grp = {n: items[n][0] % 2 for n in range(n_it)}

        ZP = [PS[2], PS[3], PS[0], PS[1]]

        def geom(n):
            hh, qt, kb = items[n]
            i = kb - 4 * qt
            c0 = 128 * i if i > 0 else 0
            return hh, qt, kb, i, c0, 512 - c0

        def s1(n):
            hh, qt, kb, i, c0, w = geom(n)
            cc, p0 = hh // 2, (hh % 2) * 64
            zp = ZP[n % 4]
            S.op("pe", lambda e: e.matmul(zp.t[:, 0:w], ka.t[p0:p0 + 64, cc, kb * 128:(kb + 1) * 128],
                                          qa.t[p0:p0 + 64, cc, qt * 512 + c0:(qt + 1) * 512], start=True, stop=True),
                 reads=ka.tl(cc, kb // 4) + qa.tl(cc, qt), writes=zp.tl())

        def s2(n):
            hh, qt, kb, i, c0, w = geom(n)
            zp = ZP[n % 4]
            eb, sp = self.a_e[n % 4], self.a_sp[n % 4]
            S.op("act", lambda e: e.activation(eb.t[:, 0:w], zp.t[:, 0:w], AF.Exp, scale=scale), reads=zp.tl(), writes=eb.tl())
            S.op("act", lambda e: e.activation(sp.t[:, 0:w], eb.t[:, 0:w], AF.Ln, bias=1.0), reads=eb.tl(), writes=sp.tl())
            if i >= 0:
                S.op("pool", lambda e: e.affine_select(sp.t[:, 0:128], sp.t[:, 0:128], [[1, 128]], ALU.is_gt, 0.0,
                                                       base=0, channel_multiplier=-1), reads=sp.tl(), writes=sp.tl())

        def s3(n):
            hh, qt, kb, i, c0, w = geom(n)
            cp = PS[4 + n % 2]
            sp = self.a_sp[n % 4]
            R = self.a_R[grp[n] % 2]
            first = (kb == 4 * qt + 3)
            S.op("pe", lambda e: e.matmul(cp.t[:, 0:w], C16("tril"), sp.t[:, 0:w], start=True, stop=first),
                 reads=sp.tl() + c16t, writes=cp.tl())
            if not first:
                r0 = 128 if i >= 0 else 0
                S.op("pe", lambda e: e.matmul(cp.t[:, r0:w], C16("ones")[0:1, :], R.t[0:1, c0 + r0:512], start=False, stop=True),
                     reads=R.tl() + c16t, writes=cp.tl())

        def s4(n):
            hh, qt, kb, i, c0, w = geom(n)
            cp = PS[4 + n % 2]
            eb, xb, at = self.a_e[n % 4], self.a_x[n % 2], self.a_att[n % 4]
            S.op("act", lambda e: e.activation(xb.t[:, 0:w], cp.t[:, 0:w], AF.Exp, scale=-1.0), reads=cp.tl(), writes=xb.tl())
            if kb > 0:
                R = self.a_R[grp[n] % 2]
                S.op("dve", lambda e: e.tensor_copy(R.t[0:1, c0:512], cp.t[0:1, 0:w]), reads=cp.tl(), writes=R.tl())
            S.op("dve", lambda e: e.tensor_tensor(at.t[:, 0:w], eb.t[:, 0:w], xb.t[:, 0:w], ALU.mult),
                 reads=eb.tl() + xb.tl(), writes=at.tl())
            if i >= 0:
                S.op("pool", lambda e: e.affine_select(at.t[:, 0:128], at.t[:, 0:128], [[1, 128]], ALU.is_gt, 0.0,
                                                       base=0, channel_multiplier=-1), reads=at.tl(), writes=at.tl())

        def s5(n):
            hh, qt, kb, i, c0, w = geom(n)
            cc, p0 = hh // 2, (hh % 2) * 64
            c = half * 2 + cc
            op_ = PS[6 + grp[n] % 2]
            at = self.a_att[n % 4]
            last = (kb == 0)
            first = (kb == 4 * qt + 3)
            vl = va.t[:, kb, cc * 128:(cc + 1) * 128]
            if i >= 0 and w > 128:
                S.op("pe", lambda e: e.matmul(op_.t[:, c0:c0 + 128], vl, at.t[:, 0:128], start=first, stop=False),
                     reads=at.tl() + va.tl(kb), writes=op_.tl())
                S.op("pe", lambda e: e.matmul(op_.t[:, c0 + 128:512], vl, at.t[:, 128:w], start=False, stop=last),
                     reads=at.tl() + va.tl(kb), writes=op_.tl())
            else:
                S.op("pe", lambda e: e.matmul(op_.t[:, c0:512], vl, at.t[:, 0:w], start=first, stop=last),
                     reads=at.tl() + va.tl(kb), writes=op_.tl())
            if last:
                S.op("dve", lambda e: e.tensor_copy(oT.t[p0:p0 + 64, c, qt * 512:(qt + 1) * 512], op_.t[p0:p0 + 64, :]),
                     reads=op_.tl(), writes=oT.tl(c, qt))

        npair = n_it // 2
        for m in range(npair + 2):
            if m < npair:
                s1(2 * m)
                s1(2 * m + 1)
                s2(2 * m)
                s2(2 * m + 1)
            if 0 <= m - 1 < npair:
                s3(2 * m - 2)
                s3(2 * m - 1)
                s4(2 * m - 2)
                s4(2 * m - 1)
            if 0 <= m - 2 < npair:
                s5(2 * m - 4)
                s5(2 * m - 3)

    def gdn(self, l):
        S, NT, NB, T = self.S, self.NT, self.NB, self.Tn
        PS, C32, hT = self.PS, self.C32, self.hT
        c32t = self.c32.tl()
        gtok, gcrc, gsc, wg = self.gtok, self.gcrc, self.gsc, self.wgate
        PT = self.PT
        gpar = self.pv(self.slots["gate"], self.L, 8)
        nexpA = self.nexpA
        S.dma("pool", wg.t[:], self.d["win"][l, :, :, 3584:3592], writes=wg.tl())
        ps = self.mmps()

        def gproj(blk):
            for kc in range(NCH):
                S.op("pe", lambda e, kc=kc: e.matmul(ps.t[:, blk * 8:blk * 8 + 8], hT.t[:, kc, blk * 128:(blk + 1) * 128], wg.t[:, kc, :],
                                                     start=(kc == 0), stop=(kc == NCH - 1)),
                     reads=wg.tl() + hT.tl(kc, blk // 4), writes=ps.tl())
        for blk in range(NB):
            gproj(blk)
        psv = ps.t[:, 0:NB * 8].rearrange("p (a b) -> p a b", b=8)

        def ghead(h):
            S.op("act", lambda e: e.activation(gtok.t[:, :, h], psv[:, :, h], AF.Exp, bias=gpar[:, l, 4 + h:5 + h]),
                 reads=ps.tl() + PT, writes=gtok.tl())
            S.op("act", lambda e: e.activation(gtok.t[:, :, h], gtok.t[:, :, h], AF.Ln, bias=1.0), reads=gtok.tl(), writes=gtok.tl())
            S.op("dve", lambda e: e.tensor_scalar(gtok.t[:, :, h], gtok.t[:, :, h], nexpA.t[:, l, h:h + 1], None, ALU.mult),
                 reads=gtok.tl() + nexpA.tl(), writes=gtok.tl())
        for h in range(DN_H):
            ghead(h)
        S.op("act", lambda e: e.activation(gtok.t[:, :, 4:8], psv[:, :, 4:8], AF.Exp, scale=-1.0), reads=ps.tl(), writes=gtok.tl())
        S.op("dve", lambda e: e.tensor_scalar(gtok.t[:, :, 4:8], gtok.t[:, :, 4:8], 1.0, None, ALU.add), reads=gtok.tl(), writes=gtok.tl())
        S.op("dve", lambda e: e.reciprocal(gtok.t[:, :, 4:8], gtok.t[:, :, 4:8]), reads=gtok.tl(), writes=gtok.tl())
        ps2 = self.mmps()

        def gcs(blk):
            S.op("pe", lambda e: e.matmul(ps2.t[:, blk * 8:blk * 8 + 4], C32("mincl"), gtok.t[:, blk, 0:4], start=True, stop=True),
                 reads=gtok.tl() + c32t, writes=ps2.tl())
            S.op("pe", lambda e: e.matmul(ps2.t[:, blk * 8 + 4:blk * 8 + 8], C32("mrev"), gtok.t[:, blk, 0:4], start=True, stop=True),
                 reads=gtok.tl() + c32t, writes=ps2.tl())
        for blk in range(NB):
            gcs(blk)
        S.op("dve", lambda e: e.tensor_copy(gcrc.t[:].rearrange("p a b -> p (a b)"), ps2.t[:, 0:NB * 8]), reads=ps2.tl(), writes=gcrc.tl())
        S.op("act", lambda e: e.activation(gsc.t[:], gcrc.t[:], AF.Exp), reads=gcrc.tl(), writes=gsc.tl())
        S.op("dve", lambda e: e.tensor_tensor(gsc.t[:, :, 0:4], gsc.t[:, :, 0:4], gtok.t[:, :, 4:8], ALU.mult),
             reads=gsc.tl() + gtok.tl(), writes=gsc.tl())
        self.dump(f"gtok_{l}", gtok.t[:], gtok.tl(), [128, NB, 8])
        self.dump(f"gcrc_{l}", gcrc.t[:], gcrc.tl(), [128, NB, 8])
        S.op("pool", lambda e: e.memset(self.raw.t[:, 0:3], 0.0), writes=self.raw.tl())
        for hd in range(DN_H):
            self.gdn_head(l, hd)

    def sigmoid_to(self, dst_ap, dst_tiles, src_ap, src_tiles):
        S = self.S
        S.op("act", lambda e: e.activation(dst_ap, src_ap, AF.Exp, scale=-1.0), reads=src_tiles, writes=dst_tiles)
        S.op("act", lambda e: e.activation(dst_ap, dst_ap, AF.Ln, bias=1.0), reads=dst_tiles, writes=dst_tiles)
        S.op("act", lambda e: e.activation(dst_ap, dst_ap, AF.Exp, scale=-1.0), reads=dst_tiles, writes=dst_tiles)

    def gdn_stream(self, l, hd, kind):
        S, NT, T = self.S, self.NT, self.Tn
        raw = self.raw
        col0 = {"q": 1536, "k": 2048, "v": 2560, "z": 3072}[kind] + hd * 128

        def evac(tt, ps):
            if tt % 2 == 0:
                S.op("dve", lambda e: e.tensor_copy(raw.t[:, 3 + tt * 512:3 + (tt + 1) * 512], ps.t[:, :]), reads=ps.tl(), writes=raw.tl())
            else:
                S.op("act", lambda e: e.activation(raw.t[:, 3 + tt * 512:3 + (tt + 1) * 512], ps.t[:, :], AF.Copy), reads=ps.tl(), writes=raw.tl())
        self.proj_chunk(l, col0, evac)
        cw = self.pv(self.slots["conv"], self.L, 12, 4)
        PT = self.PT
        j = {"q": 0, "k": 4, "v": 8, "z": 0}[kind] + hd
        b2 = math.log(DN_D ** -0.5) if kind == "q" else 0.0
        nsub = self.CH // 512

        def piece(tt):
            sub = tt % nsub
            t0 = tt * 512
            acc = self.cacc.t[:, sub * 512:(sub + 1) * 512]
            tA = self.tA.t[:, sub * 512:(sub + 1) * 512]
            acct, tAt = [self.acc_t[sub]], [self.tA_t[sub]]
            if kind == "z":
                self.sigmoid_to(tA, tAt, raw.t[:, 3 + t0:3 + t0 + 512], raw.tl())
                S.op("dve", lambda e: e.tensor_tensor(self.oT.t[:, 4 + hd, t0:t0 + 512], raw.t[:, 3 + t0:3 + t0 + 512], tA, ALU.mult),
                     reads=raw.tl() + tAt, writes=self.oT.tl(4 + hd, tt))
                return
            S.op("act", lambda e: e.activation(acc, raw.t[:, t0:t0 + 512], AF.Copy, scale=cw[:, l, j, 0:1]),
                 reads=raw.tl() + PT, writes=acct)
            for k in range(1, 4):
                S.op("dve", lambda e, k=k: e.scalar_tensor_tensor(
                    acc, raw.t[:, t0 + k:t0 + k + 512], cw[:, l, j, k:k + 1], acc, ALU.mult, ALU.add),
                    reads=raw.tl() + acct + PT, writes=acct)
            self.sigmoid_to(tA, tAt, acc, acct)
            if kind == "v":
                S.op("dve", lambda e: e.tensor_tensor(self.vT.t[:, t0:t0 + 512], acc, tA, ALU.mult),
                     reads=acct + tAt, writes=self.vT.tl())
                return
            S.op("dve", lambda e: e.tensor_tensor(acc, acc, tA, ALU.mult), reads=acct + tAt, writes=acct)
            dst = self.qT if kind == "q" else self.kT
            self.group_norm(acc, acct, "ones", None, None, dst.t[:, t0:t0 + 512], dst.tl(), bias2=b2)

        for t2 in range(0, NT, nsub):
            lists = [S.capture(lambda tt=tt: piece(tt)) for tt in range(t2, min(t2 + nsub, NT))]
            S.merge(*lists)

    def tri_inverse(self, L32):
        S, ring, C32c, C16c = self.S, self.ring, self.C32, self.C16
        c32t = self.c32.tl()
        ident32 = C32c("ident")
        PS = self.PS
        pu = PS[3]
        S.op("pe", lambda e: e.matmul(pu.t[:, 0:128], L32.t[:], ident32, start=True, stop=True), reads=L32.tl() + c32t, writes=pu.tl())
        def fr(buf):
            return buf.t[:].bitcast(F32R) if NEUMANN_F32R else buf.t[:]
        L16, U16 = ring(self.g_Lc, "Lc"), ring(self.g_Uc, "Uc")
        LB, UB = ring(self.g_Lc, "Lc"), ring(self.g_Uc, "Uc")
        C32, C32T, C64 = ring(self.g_C32, "C32"), ring(self.g_C32T, "C32T"), ring(self.g_C64, "C64")
        X0, Y0 = ring(self.g_Xc, "Xc"), ring(self.g_Yc, "Yc")
        XB, YB = ring(self.g_Xc, "Xc"), ring(self.g_Yc, "Yc")
        S.op("dve", lambda e: e.tensor_tensor(fr(L16), L32.t[:], C32c("m16"), ALU.mult), reads=L32.tl() + c32t, writes=L16.tl())
        S.op("dve", lambda e: e.tensor_tensor(C32.t[:], L32.t[:], C32c("m32"), ALU.mult), reads=L32.tl() + c32t, writes=C32.tl())
        S.op("pool", lambda e: e.tensor_tensor(C64.t[:], L32.t[:], C32c("m64"), ALU.mult), reads=L32.tl() + c32t, writes=C64.tl())
        S.op("dve", lambda e: e.tensor_tensor(fr(U16), pu.t[:, 0:128], C32c("m16T"), ALU.mult), reads=pu.tl() + c32t, writes=U16.tl())
        S.op("dve", lambda e: e.tensor_tensor(C32T.t[:], pu.t[:, 0:128], C32c("m32T"), ALU.mult), reads=pu.tl() + c32t, writes=C32T.tl())
        S.op("dve", lambda e: e.tensor_tensor(fr(Y0), ident32, L16.t[:], ALU.subtract), reads=L16.tl() + c32t, writes=Y0.tl())
        S.op("dve", lambda e: e.tensor_tensor(fr(X0), ident32, U16.t[:], ALU.subtract), reads=U16.tl() + c32t, writes=X0.tl())
        Lk, Uk, Xk, Yk = L16, U16, X0, Y0
        if getattr(S, "cap", None) is not None:
            self._marks.append(len(S.cap))
        first_call = not getattr(self, "_tri_dbg", False)
        self._tri_dbg = True
        if first_call:
            self.dump("dbgX0", Xk.t[:], Xk.tl(), [128, 128])
            self.dump("dbgU16", Uk.t[:], Uk.tl(), [128, 128])
            self.dump("dbgL16", Lk.t[:], Lk.tl(), [128, 128])
        T16T = T16 = None
        for k in range(3):
            last = (k == 2)
            pq = PS[4]
            S.op("pe", lambda e, Uk=Uk, Lk=Lk, pq=pq: e.matmul(pq.t[:, 0:128], fr(Uk), fr(Lk), start=True, stop=True),
                 reads=Uk.tl() + Lk.tl(), writes=pq.tl())
            S.op("pe", lambda e, Uk=Uk, Lk=Lk, pq=pq: e.matmul(pq.t[:, 128:256], fr(Lk), fr(Uk), start=True, stop=True),
                 reads=Uk.tl() + Lk.tl(), writes=pq.tl())
            Ln_, Un_ = (LB, UB) if k % 2 == 0 else (L16, U16)
            S.op("act", lambda e, Ln_=Ln_, pq=pq: e.activation(fr(Ln_), pq.t[:, 0:128], AF.Copy), reads=pq.tl(), writes=Ln_.tl())
            S.op("dve", lambda e, Un_=Un_, pq=pq: e.tensor_copy(fr(Un_), pq.t[:, 128:256]), reads=pq.tl(), writes=Un_.tl())
            S.op("pe", lambda e, Ln_=Ln_, Xk=Xk, pq=pq: e.matmul(pq.t[:, 256:384], fr(Ln_), fr(Xk), start=True, stop=True),
                 reads=Ln_.tl() + Xk.tl(), writes=pq.tl())
            S.op("pe", lambda e, Un_=Un_, Yk=Yk, pq=pq: e.matmul(pq.t[:, 384:512], fr(Un_), fr(Yk), start=True, stop=True),
                 reads=Un_.tl() + Yk.tl(), writes=pq.tl())
            if last:
                Xn, Yn = ring(self.g_T16T, "T16T"), ring(self.g_T16, "T16")
            else:
                Xn, Yn = (XB, YB) if k % 2 == 0 else (X0, Y0)
            S.op("dve", lambda e, Xn=Xn, Xk=Xk, pq=pq, last=last: e.tensor_tensor(Xn.t[:] if last else fr(Xn), pq.t[:, 256:384], Xk.t[:], ALU.add),
                 reads=pq.tl() + Xk.tl(), writes=Xn.tl())
            S.op("dve", lambda e, Yn=Yn, Yk=Yk, pq=pq, last=last: e.tensor_tensor(Yn.t[:] if last else fr(Yn), pq.t[:, 384:512], Yk.t[:], ALU.add),
                 reads=pq.tl() + Yk.tl(), writes=Yn.tl())
            Uk, Lk, Xk, Yk = Un_, Ln_, Xn, Yn
        T16T, T16 = Xk, Yk
        if getattr(S, "cap", None) is not None:
            self._marks.append(len(S.cap))
        if first_call:
            self.dump("dbgT16T", T16T.t[:], T16T.tl(), [128, 128], BF16)
        pd = PS[5]
        S.op("pe", lambda e: e.matmul(pd.t[:, 0:128], C32.t[:], T16T.t[:], start=True, stop=True), reads=C32.tl() + T16T.tl(), writes=pd.tl())
        S.op("pe", lambda e: e.matmul(pd.t[:, 128:256], C32T.t[:], T16.t[:], start=True, stop=True), reads=C32T.tl() + T16.tl(), writes=pd.tl())
        Ya, Yb = ring(self.g_Ya, "Ya"), ring(self.g_Yb, "Yb")
        S.op("act", lambda e: e.activation(Ya.t[:], pd.t[:, 0:128], AF.Copy), reads=pd.tl(), writes=Ya.tl())
        S.op("dve", lambda e: e.tensor_copy(Yb.t[:], pd.t[:, 128:256]), reads=pd.tl(), writes=Yb.tl())
        pd2 = PS[5]
        S.op("pe", lambda e: e.matmul(pd2.t[:, 0:128], T16.t[:], Ya.t[:], start=True, stop=True), reads=T16.tl() + Ya.tl(), writes=pd2.tl())
        S.op("pe", lambda e: e.matmul(pd2.t[:, 128:256], T16T.t[:], Yb.t[:], start=True, stop=True), reads=T16T.tl() + Yb.tl(), writes=pd2.tl())
        T32T, T32 = ring(self.g_T32T, "T32T"), ring(self.g_T32, "T32")
        S.op("dve", lambda e: e.scalar_tensor_tensor(T32T.t[:], pd2.t[:, 0:128], -1.0, T16T.t[:], ALU.mult, ALU.add),
             reads=pd2.tl() + T16T.tl(), writes=T32T.tl())
        S.op("dve", lambda e: e.scalar_tensor_tensor(T32.t[:], pd2.t[:, 128:256], -1.0, T16.t[:], ALU.mult, ALU.add),
             reads=pd2.tl() + T16.tl(), writes=T32.tl())
        pe1 = PS[5]
        S.op("pe", lambda e: e.matmul(pe1.t[:, 0:128], C64.t[:], T32T.t[:], start=True, stop=True), reads=C64.tl() + T32T.tl(), writes=pe1.tl())
        Yd = ring(self.g_Yd, "Yd")
        S.op("act", lambda e: e.activation(Yd.t[:], pe1.t[:, 0:128], AF.Copy), reads=pe1.tl(), writes=Yd.tl())
        S.op("pe", lambda e: e.matmul(pe1.t[:, 128:256], T32.t[:], Yd.t[:], start=True, stop=True), reads=T32.tl() + Yd.tl(), writes=pe1.tl())
        TT = ring(self.g_TT, "TT")
        S.op("dve", lambda e: e.scalar_tensor_tensor(TT.t[:], pe1.t[:, 128:256], -1.0, T32T.t[:], ALU.mult, ALU.add),
             reads=pe1.tl() + T32T.tl(), writes=TT.tl())
        return TT

    def gdn_head(self, l, hd):
        S, NT, NB, T = self.S, self.NT, self.NB, self.Tn
        PS, C16, C32 = self.PS, self.C16, self.C32
        c16t, c32t = self.c16.tl(), self.c32.tl()
        kT, qT, vT = self.kT, self.qT, self.vT
        gtok, gcrc, gsc = self.gtok, self.gcrc, self.gsc
        ring = self.ring
        for kind in ("q", "k", "v", "z"):
            self.gdn_stream(l, hd, kind)
        self.dump(f"qT_{l}_{hd}", qT.t[:], qT.tl(), [128, T], BF16)
        self.dump(f"kT_{l}_{hd}", kT.t[:], kT.tl(), [128, T], BF16)
        self.dump(f"vT_{l}_{hd}", vT.t[:], vT.tl(), [128, T], BF16)

        S32, Sbf = self.S32, self.Sbf
        S.op("dve", lambda e: e.memset(S32.t[:], 0.0), writes=S32.tl())
        S.op("pool", lambda e: e.memset(Sbf.t[:], 0.0), writes=Sbf.tl())
        PB = [PS[2], PS[3], PS[4], PS[5]]
        ident16 = C16("ident")

        def psr():
            return ring(PB, "gps")
        prep_out = {}

        def prep(blk):
            bs = slice(blk * 128, (blk + 1) * 128)
            gcol = gtok.t[:, blk, hd:hd + 1]
            bcol = gtok.t[:, blk, 4 + hd:5 + hd]
            gccol = gcrc.t[:, blk, hd:hd + 1]
            bgcol = gsc.t[:, blk, hd:hd + 1]
            erccol = gsc.t[:, blk, 4 + hd:5 + hd]
            p1 = PS[2]
            S.op("pe", lambda e: e.matmul(p1.t[:, 0:128], gcol.to_broadcast([128, 128]), C32("mincl"), start=True, stop=True),
                 reads=gtok.tl() + c32t, writes=p1.tl())
            ebc, tI, DL, DLs = ring(self.g_ebc, "ebc"), ring(self.g_tI, "tI"), ring(self.g_DL, "DL"), ring(self.g_DLs, "DLs")
            S.op("act", lambda e: e.activation(ebc.t[:], p1.t[:, 0:128], AF.Exp), reads=p1.tl(), writes=ebc.tl())
            S.op("dve", lambda e: e.scalar_tensor_tensor(tI.t[:], p1.t[:, 0:128], gccol, C32("mbI"), ALU.subtract, ALU.add),
                 reads=p1.tl() + gcrc.tl() + c32t, writes=tI.tl())
            S.op("act", lambda e: e.activation(DL.t[:], tI.t[:], AF.Exp, scale=-1.0), reads=tI.tl(), writes=DL.tl())
            S.op("dve", lambda e: e.tensor_tensor(DLs.t[:], DL.t[:], C32("strict"), ALU.mult), reads=DL.tl() + c32t, writes=DLs.tl())
            p2 = PS[3]
            S.op("pe", lambda e: e.matmul(p2.t[:, 0:128], kT.t[:, bs], kT.t[:, bs], start=True, stop=True), reads=kT.tl(), writes=p2.tl())
            S.op("pe", lambda e: e.matmul(p2.t[:, 128:256], qT.t[:, bs], kT.t[:, bs], start=True, stop=True), reads=kT.tl() + qT.tl(), writes=p2.tl())
            L32, Ab = ring(self.g_L32, "L32"), ring(self.g_A, "A")
            S.op("dve", lambda e: e.scalar_tensor_tensor(L32.t[:], p2.t[:, 0:128], bcol, DLs.t[:], ALU.mult, ALU.mult),
                 reads=p2.tl() + gtok.tl() + DLs.tl(), writes=L32.tl())
            S.op("dve", lambda e: e.tensor_tensor(Ab.t[:], p2.t[:, 128:256], DL.t[:], ALU.mult), reads=p2.tl() + DL.tl(), writes=Ab.tl())
            p3 = PS[2]
            p3b = p3.t[:, :].bitcast(BF16)
            S.op("pe", lambda e: e.transpose(p3b[:, 128:256], Ab.t[:], ident16), reads=Ab.tl() + c16t, writes=p3.tl())
            S.op("pe", lambda e: e.transpose(p3b[:, 256:384], kT.t[:, bs], ident16), reads=kT.tl() + c16t, writes=p3.tl())
            S.op("pe", lambda e: e.transpose(p3b[:, 384:512], vT.t[:, bs], ident16), reads=vT.tl() + c16t, writes=p3.tl())
            AT = ring(self.g_AT, "AT")
            kbg, kd, vb = ring(self.g_kbg, "kbg"), ring(self.g_kd, "kd"), ring(self.g_vb, "vb")
            S.op("act", lambda e: e.activation(AT.t[:], p3b[:, 128:256], AF.Copy), reads=p3.tl(), writes=AT.tl())
            S.op("act", lambda e: e.activation(kbg.t[:], p3b[:, 256:384], AF.Copy, scale=bgcol), reads=p3.tl() + gsc.tl(), writes=kbg.tl())
            S.op("dve", lambda e: e.tensor_scalar(kd.t[:], p3b[:, 256:384], erccol, None, ALU.mult), reads=p3.tl() + gsc.tl(), writes=kd.tl())
            S.op("act", lambda e: e.activation(vb.t[:], p3b[:, 384:512], AF.Copy, scale=bcol), reads=p3.tl() + gtok.tl(), writes=vb.tl())
            qg = ring(self.g_qg, "qg")
            S.op("dve", lambda e: e.tensor_tensor(qg.t[:], qT.t[:, bs], ebc.t[:], ALU.mult), reads=qT.tl() + ebc.tl(), writes=qg.tl())
            TT = self.tri_inverse(L32)
            if blk == 0:
                self.dump(f"L32_{l}_{hd}", L32.t[:], L32.tl(), [128, 128])
                self.dump(f"TT_{l}_{hd}", TT.t[:], TT.tl(), [128, 128], BF16)
            p4 = PS[5]
            S.op("pe", lambda e: e.matmul(p4.t[:, 0:128], TT.t[:], vb.t[:], start=True, stop=True), reads=TT.tl() + vb.tl(), writes=p4.tl())
            S.op("pe", lambda e: e.matmul(p4.t[:, 128:256], kbg.t[:], TT.t[:], start=True, stop=True), reads=TT.tl() + kbg.tl(), writes=p4.tl())
            u, wT = ring(self.g_u, "u"), ring(self.g_wT, "wT")
            S.op("dve", lambda e: e.tensor_copy(u.t[:], p4.t[:, 0:128]), reads=p4.tl(), writes=u.tl())
            S.op("act", lambda e: e.activation(wT.t[:], p4.t[:, 128:256], AF.Copy), reads=p4.tl(), writes=wT.tl())
            prep_out[blk] = (u, wT, kd, AT, qg, ebc)

        opsum, pw = PS[6], PS[7]

        def chunk(blk, ch, u, wT, kd, AT, qg, ebc):
            r0 = ch * 64
            rs = slice(r0, r0 + 64)
            cs = slice((blk % 4) * 128 + r0, (blk % 4) * 128 + r0 + 64)
            S.op("pe", lambda e: e.matmul(pw.t[:, 0:128], wT.t[:], Sbf.t[:], start=True, stop=True), reads=wT.tl() + Sbf.tl(), writes=pw.tl())
            vn = ring(self.g_vn, "vn")
            S.op("dve", lambda e: e.tensor_tensor(vn.t[rs, :], u.t[rs, :], pw.t[rs, 0:128], ALU.subtract), reads=u.tl() + pw.tl(), writes=vn.tl())
            S.op("pe", lambda e: e.matmul(opsum.t[:, cs], Sbf.t[:], qg.t[:, rs], start=True, stop=False), reads=Sbf.tl() + qg.tl(), writes=opsum.tl())
            S.op("pe", lambda e: e.matmul(opsum.t[:, cs], vn.t[rs, :], AT.t[rs, rs], start=False, stop=True), reads=vn.tl() + AT.tl(), writes=opsum.tl())
            S.op("pe", lambda e: e.matmul(pw.t[:, 128:256], kd.t[rs, :], vn.t[rs, :], start=True, stop=True), reads=kd.tl() + vn.tl(), writes=pw.tl())
            S.op("dve", lambda e: e.scalar_tensor_tensor(S32.t[:], S32.t[:], ebc.t[:, r0 + 63:r0 + 64], pw.t[:, 128:256], ALU.mult, ALU.add),
                 reads=S32.tl() + ebc.tl() + pw.tl(), writes=S32.tl())
            S.op("act", lambda e: e.activation(Sbf.t[:], S32.t[:], AF.Copy), reads=S32.tl(), writes=Sbf.tl())

        def finish(tt):
            ts = slice(tt * 512, (tt + 1) * 512)
            go = self.tA.t[:, 0:512]
            got = [self.tA_t[0]]
            S.op("act", lambda e: e.activation(go, opsum.t[:, :], AF.Copy), reads=opsum.tl(), writes=got)
            dno = self.slots["dno"]
            self.group_norm(go, got, "ones", DN_D, self.prm.t[:, dno[0] + l:dno[0] + l + 1], go, got)
            S.op("dve", lambda e: e.tensor_tensor(self.oT.t[:, 4 + hd, ts], go, self.oT.t[:, 4 + hd, ts], ALU.mult),
                 reads=got, writes=self.oT.tl(4 + hd, tt))

        def chain(blk):
            args = prep_out.pop(blk)
            for ch in range(2):
                chunk(blk, ch, *args)
            if blk % 4 == 3:
                finish(blk // 4)

        parts = {}

        def cap_prep(b):
            if b < NB and b not in parts:
                self._marks = []
                lst = S.capture(lambda: prep(b))
                m1, m2 = self._marks
                parts[b] = (lst[:m1], lst[m1:m2], lst[m2:])

        def part(b, i):
            cap_prep(b)
            return parts[b][i] if b < NB else []
        S.merge(part(0, 0))
        S.merge(part(0, 1), part(1, 0))
        S.merge(part(0, 2), part(1, 1), part(2, 0))
        for blk in range(NB):
            lb = S.capture(lambda: chain(blk))
            S.merge(part(blk + 1, 2), part(blk + 2, 1), part(blk + 3, 0), lb)
            parts.pop(blk, None)

    def out_proj(self, l, s):
        S, NT = self.S, self.NT
        xT, oT, mod = self.xT, self.oT, self.mod

        def one(wo, cc, c, tt):
            ts = slice(tt * 512, (tt + 1) * 512)
            ps = self.mmps()
            for kc in range(NCH):
                S.op("pe", lambda e, kc=kc: e.matmul(ps.t[:, :], wo.t[:, kc, cc * 128:(cc + 1) * 128], oT.t[:, kc, ts],
                                                     start=(kc == 0), stop=(kc == NCH - 1)),
                     reads=wo.tl() + oT.tl(kc, tt), writes=ps.tl())
            S.op("dve", lambda e: e.scalar_tensor_tensor(xT.t[:, c, ts], ps.t[:, :], mod.t[:, l, 16 + c, s:s + 1], xT.t[:, c, ts], ALU.mult, ALU.add),
                 reads=ps.tl() + mod.tl() + xT.tl(c, tt), writes=xT.tl(c, tt))
        for half in range(2):
            wo = self.wo[half]
            S.dma("pool", wo.t[:], self.d["wout"][l, :, :, half * 512:(half + 1) * 512], writes=wo.tl())
        for half in range(2):
            for cc in range(4):
                for tt in range(NT):
                    one(self.wo[half], cc, half * 4 + cc, tt)

    def mlp(self, l, s):
        S, NT, HT = self.S, self.NT, self.HT
        xT, hT, mod = self.xT, self.hT, self.mod
        nh = self.Tn // HT
        tph = HT // 512

        def ff1(wb, j, tt, t2):
            ts = slice(tt * 512, (tt + 1) * 512)
            ps = self.mmps()
            for kc in range(NCH):
                S.op("pe", lambda e, kc=kc: e.matmul(ps.t[:, :], wb.t[:, kc, :], hT.t[:, kc, ts], start=(kc == 0), stop=(kc == NCH - 1)),
                     reads=wb.tl() + hT.tl(kc, tt), writes=ps.tl())
            rl = self.ring(self.relu, "relu")
            hb, jj = self.hid(j)
            S.op("act", lambda e: e.activation(rl.t[:], ps.t[:, :], AF.Relu), reads=ps.tl(), writes=rl.tl())
            S.op("pool", lambda e: e.tensor_tensor(hb.t[:, jj, t2 * 512:(t2 + 1) * 512], rl.t[:], rl.t[:], ALU.mult),
                 reads=rl.tl(), writes=hb.tl(jj, t2))

        def ff2(w2, c, tt, t2):
            ts = slice(tt * 512, (tt + 1) * 512)
            ps = self.mmps()
            for j in range(32):
                hb, jj = self.hid(j)
                S.op("pe", lambda e, j=j, hb=hb, jj=jj: e.matmul(ps.t[:, :], w2.t[:, j, :], hb.t[:, jj, t2 * 512:(t2 + 1) * 512],
                                                                start=(j == 0), stop=(j == 31)),
                     reads=w2.tl() + hb.tl(jj, t2), writes=ps.tl())
            S.op("dve", lambda e: e.scalar_tensor_tensor(xT.t[:, c, ts], ps.t[:, :], mod.t[:, l, 40 + c, s:s + 1], xT.t[:, c, ts], ALU.mult, ALU.add),
                 reads=ps.tl() + mod.tl() + xT.tl(c, tt), writes=xT.tl(c, tt))

        s1 = WStream(self, self.wring, [self.d["ff1"][l, :, :, j * 128:(j + 1) * 128] for hf in range(nh) for j in range(32)], 2)
        s2 = WStream(self, self.w2, [self.d["ff2"][l, :, :, c * 128:(c + 1) * 128] for hf in range(nh) for c in range(NCH)], 1)
        for hf in range(nh):
            for j in range(32):
                wb = s1.next()
                for t2 in range(tph):
                    ff1(wb, j, hf * tph + t2, t2)
            for c in range(NCH):
                w2 = s2.next()
                for t2 in range(tph):
                    ff2(w2, c, hf * tph + t2, t2)


def _chunked(w, L):
    Lk, K, N = w.shape
    return np.ascontiguousarray(w.reshape(Lk, K // 128, 128, N).transpose(0, 2, 1, 3))


def prep_shared(inp, L):
    f = lambda a: np.ascontiguousarray(np.asarray(a, dtype=np.float32))
    sh = {}
    sh["w_ada"] = _chunked(f(inp["w_ada"])[:L], L)
    sh["b_ada"] = f(f(inp["b_ada"])[:L].reshape(L, 48, 128).transpose(2, 0, 1))
    sh["norm_mix"] = f(f(inp["norm_mix"])[:L].reshape(L, NCH, 128).transpose(2, 0, 1))
    sh["norm_mlp"] = f(f(inp["norm_mlp"])[:L].reshape(L, NCH, 128).transpose(2, 0, 1))
    sh["w_in"] = _chunked(f(inp["w_in"])[:L], L)
    sh["sb_q_norm"] = f(np.tile(f(inp["sb_q_norm"])[:L], (1, 2)).T)
    sh["sb_k_norm"] = f(np.tile(f(inp["sb_k_norm"])[:L], (1, 2)).T)
    sh["conv_w"] = f(f(inp["conv_w"])[:L].reshape(L, 4, 12, 128).transpose(3, 0, 2, 1))
    gp = np.concatenate([f(inp["a_log"])[:L], f(inp["dt_bias"])[:L]], axis=1)
    sh["gate_p"] = f(np.broadcast_to(gp[None], (128, L, 8)))
    sh["dn_out_norm"] = f(f(inp["dn_out_norm"])[:L].T)
    sh["w_out"] = _chunked(f(inp["w_out"])[:L], L)
    sh["w_ff1"] = _chunked(f(inp["w_ff1"])[:L], L)
    sh["w_ff2"] = _chunked(f(inp["w_ff2"])[:L], L)
    cc = _consts()
    sh["consts32"], sh["consts16"] = cc[1], cc[3]
    return sh


def prep_core(x2, c2, T):
    ns = x2.shape[0]
    xT = np.ascontiguousarray(np.asarray(x2, np.float32).reshape(ns, T, NCH, 128).transpose(0, 3, 2, 1))
    cT = np.ascontiguousarray(np.asarray(c2, np.float32).reshape(2, NCH, 128).transpose(2, 1, 0))
    return {"xT": xT, "cT": cT}


_CACHE = {}


def kernel(**inputs):
    x = np.asarray(inputs["x"], np.float32)
    c = np.asarray(inputs["c"], np.float32)
    B, T, _ = x.shape
    key = (T, DEPTH, 2)
    if key not in _CACHE:
        _CACHE[key] = Builder(T=T, L=DEPTH, NSEQ=2)
    bld = _CACHE[key]
    sh = prep_shared(inputs, DEPTH)
    in_maps = []
    for core in range(NCORES):
        m = dict(sh)
        m.update(prep_core(x[2 * core:2 * core + 2], c[2 * core:2 * core + 2], T))
        in_maps.append(m)
    res = run_bass_kernel_spmd(bld.nc, in_maps, core_ids=list(range(NCORES)))
    out = np.empty((B, T, D), np.float32)
    for core in range(NCORES):
        oT = res.results[core]["outT"]
        out[2 * core:2 * core + 2] = oT.transpose(0, 3, 2, 1).reshape(2, T, D)
    return out
```
